# Optimizing a Trainium2 kernel written in Bass

```python
import math
import jax
import jax.numpy as jnp
from jax import lax
import numpy as np

D_MODEL = 2048
BATCH = 1
SEQ = 16384
DEPTH = 4

GRID_W = 64
CTX_LEN = 256
N_MOD = 6
EPS = 1e-6
ROPE_THETA = 10000.0
Q_BLOCK = 128

A_DK = 128
A_DV = 128
A_HEADS = D_MODEL // (2 * A_DV)
A_CONV = 3
A_CHUNK = 64
B_DH = 128
B_DV = 2 * B_DH
B_HEADS = D_MODEL // (2 * B_DV)
C_DH = 128
C_HEADS = D_MODEL // C_DH
C_KV_HEADS = C_HEADS // 4
D_FF = 256 * ((8 * D_MODEL // 3 + 255) // 256)
FFN_CONV = 3

A_QK = A_HEADS * A_DK
A_VD = A_HEADS * A_DV
A_CONV_CH = 2 * A_QK + A_VD
A_IN = A_CONV_CH + A_VD + 4 * A_HEADS
B_QK = B_HEADS * 2 * B_DH
B_VD = B_HEADS * B_DV
B_IN = 2 * B_QK + B_VD
EVEN_IN = A_IN + B_IN
EVEN_OUT = A_VD + B_VD
C_Q = C_HEADS * C_DH
C_KV = C_KV_HEADS * C_DH
ODD_IN = C_Q + 2 * C_KV
ODD_OUT = C_Q
N_EVEN = (DEPTH + 1) // 2
N_ODD = DEPTH // 2

kernel_name = "hybrid_deltanet_diffattn_gqa_dit"


def rms_norm(x, g):
    xf = x.astype(jnp.float32)
    y = xf * lax.rsqrt(jnp.mean(xf * xf, axis=-1, keepdims=True) + EPS)
    return (y * g.astype(jnp.float32)).astype(x.dtype)


def modulate(x, g, shift, scale):
    return rms_norm(x, g) * (1 + scale) + shift


def l2_normalize(x):
    xf = x.astype(jnp.float32)
    return xf * lax.rsqrt(jnp.sum(xf * xf, axis=-1, keepdims=True) + EPS)


def dwconv_seq(x, w, b=None):
    k, ch = w.shape
    y = lax.conv_general_dilated(x, w[:, None, :].astype(x.dtype), (1,), [(k // 2, k // 2)],
                                 dimension_numbers=("NWC", "WIO", "NWC"), feature_group_count=ch)
    return y if b is None else y + b


def axial_rope(rows, head_dim):
    axis_dim = head_dim // 2
    freqs = ROPE_THETA ** (-jnp.arange(0, axis_dim, 2, dtype=jnp.float32) / axis_dim)
    row = jnp.repeat(jnp.arange(rows, dtype=jnp.float32), GRID_W)
    col = jnp.tile(jnp.arange(GRID_W, dtype=jnp.float32), rows)
    ang_r = row[:, None] * freqs
    ang_c = col[:, None] * freqs
    ang = jnp.concatenate([ang_r, ang_r, ang_c, ang_c], axis=-1)
    return jnp.cos(ang), jnp.sin(ang)


def apply_rope(x, cos, sin):
    shape = (x.shape[1],) + (1,) * (x.ndim - 3) + (x.shape[-1],)
    cos = cos.reshape(shape).astype(x.dtype)
    sin = sin.reshape(shape).astype(x.dtype)
    x1, x2, x3, x4 = jnp.split(x, 4, axis=-1)
    rot = jnp.concatenate([-x2, x1, -x4, x3], axis=-1)
    return x * cos + rot * sin


def sweep_query_blocks(q, attend):
    b, s = q.shape[:2]
    nb = s // Q_BLOCK
    qb = jnp.moveaxis(q.reshape((b, nb, Q_BLOCK) + q.shape[2:]), 1, 0)
    ob = lax.map(attend, qb)
    return jnp.moveaxis(ob, 0, 1).reshape((b, s) + ob.shape[3:])


def gqa_attend(q, k, v):
    s = jnp.einsum("bqhgd,bkhd->bhgqk", q, k, preferred_element_type=jnp.float32) * (q.shape[-1] ** -0.5)
    p = jax.nn.softmax(s, axis=-1).astype(v.dtype)
    return jnp.einsum("bhgqk,bkhd->bqhgd", p, v)


def diff_attend(q, k, v, lam):
    s = jnp.einsum("bqhmd,bkhmd->bhmqk", q, k, preferred_element_type=jnp.float32) * (q.shape[-1] ** -0.5)
    p = jax.nn.softmax(s, axis=-1)
    a = p[:, :, 0] - lam * p[:, :, 1]
    return jnp.einsum("bhqk,bkhe->bqhe", a.astype(v.dtype), v)


def gated_delta_chunked(q, k, v, g, beta, s0):
    f32 = jnp.float32
    b, l, h, _ = q.shape
    dv = v.shape[-1]
    n = l // A_CHUNK

    def chunks(t):
        t = t.astype(f32).reshape((b, n, A_CHUNK, h) + t.shape[3:])
        return jnp.moveaxis(t, 3, 1)

    qc, kc, vc, gc, bc = chunks(q), chunks(k), chunks(v), chunks(g), chunks(beta)
    decay = jnp.cumsum(gc, axis=-1)
    idx = jnp.arange(A_CHUNK)
    incl = idx[:, None] >= idx[None, :]
    strict = idx[:, None] > idx[None, :]
    dmask = jnp.exp(jnp.where(incl, decay[..., :, None] - decay[..., None, :], -jnp.inf))
    kk = jnp.einsum("bhncd,bhnjd->bhncj", kc, kc)
    a_strict = jnp.where(strict, bc[..., :, None] * kk * dmask, 0.0)
    eye = jnp.eye(A_CHUNK, dtype=f32)
    t_inv = lax.linalg.triangular_solve(eye + a_strict, jnp.broadcast_to(eye, a_strict.shape),
                                        left_side=True, lower=True, unit_diagonal=True)
    u_base = t_inv @ (bc[..., None] * vc)
    w_dec = t_inv @ ((bc * jnp.exp(decay))[..., None] * kc)
    qk = jnp.einsum("bhncd,bhnjd->bhncj", qc, kc) * dmask
    q_dec = qc * jnp.exp(decay)[..., None]
    k_dec = kc * jnp.exp(decay[..., -1:] - decay)[..., None]
    chunk_decay = jnp.exp(decay[..., -1])[..., None, None]

    def step(state, inp):
        u_b, w_c, qk_c, qd_c, kd_c, cd_c = inp
        u = u_b - w_c @ state
        o = qd_c @ state + qk_c @ u
        state = state * cd_c + jnp.swapaxes(kd_c, -1, -2) @ u
        return state, o

    xs = tuple(jnp.moveaxis(t, 2, 0) for t in (u_base, w_dec, qk, q_dec, k_dec, chunk_decay))
    s_fin, o = lax.scan(step, s0.astype(f32), xs)
    o = jnp.transpose(o, (1, 0, 3, 2, 4)).reshape(b, l, h, dv)
    return o, s_fin


def _a_stream(p, conv_w):
    b, l = p.shape[:2]
    qkv = jax.nn.silu(dwconv_seq(p[..., :A_CONV_CH], conv_w))
    q = l2_normalize(qkv[..., :A_QK].reshape(b, l, A_HEADS, A_DK)) * (A_DK ** -0.5)
    k = l2_normalize(qkv[..., A_QK:2 * A_QK].reshape(b, l, A_HEADS, A_DK))
    v = qkv[..., 2 * A_QK:].reshape(b, l, A_HEADS, A_DV)
    z = p[..., A_CONV_CH:A_CONV_CH + A_VD].reshape(b, l, A_HEADS, A_DV)
    gates = p[..., A_CONV_CH + A_VD:].reshape(b, l, 4, A_HEADS).astype(jnp.float32)
    return q, k, v, z, gates


def _a_output(o, z, norm_g):
    b, l = o.shape[:2]
    return (rms_norm(o.astype(z.dtype), norm_g) * jax.nn.silu(z)).reshape(b, l, A_VD)


def mixer_a(p_lat, p_ctx, conv_w, a_log, dt_bias, norm_g, with_ctx):
    lat = _a_stream(p_lat, conv_w)
    ctx = _a_stream(p_ctx, conv_w)
    s0 = jnp.zeros((p_lat.shape[0], A_HEADS, A_DK, A_DV), jnp.float32)
    outs_lat, outs_ctx = [], []
    for d in range(2):
        order = (lambda t: t) if d == 0 else (lambda t: jnp.flip(t, axis=1))

        def run(stream, s_init):
            q, k, v, _, gates = stream
            g = -jnp.exp(a_log[d].astype(jnp.float32)) * jax.nn.softplus(gates[:, :, d] + dt_bias[d].astype(jnp.float32))
            beta = jax.nn.sigmoid(gates[:, :, 2 + d])
            o, s_fin = gated_delta_chunked(order(q), order(k), order(v), order(g), order(beta), s_init)
            return order(o), s_fin

        o_c, s_c = run(ctx, s0)
        o_l, _ = run(lat, s_c)
        outs_lat.append(o_l)
        outs_ctx.append(o_c)
    out_lat = _a_output(outs_lat[0] + outs_lat[1], lat[3], norm_g)
    out_ctx = _a_output(outs_ctx[0] + outs_ctx[1], ctx[3], norm_g) if with_ctx else None
    return out_lat, out_ctx


def mixer_b(p_lat, p_ctx, cos, sin, lam_p, norm_g, lambda_init, with_ctx):
    def split(p):
        b, l = p.shape[:2]
        q = p[..., :B_QK].reshape(b, l, B_HEADS, 2, B_DH)
        k = p[..., B_QK:2 * B_QK].reshape(b, l, B_HEADS, 2, B_DH)
        v = p[..., 2 * B_QK:].reshape(b, l, B_HEADS, B_DV)
        return q, k, v

    def finish(o):
        b, l = o.shape[:2]
        return (rms_norm(o, norm_g) * (1.0 - lambda_init)).reshape(b, l, B_VD)

    lp = lam_p.astype(jnp.float32)
    lam = jnp.exp(jnp.sum(lp[0] * lp[1])) - jnp.exp(jnp.sum(lp[2] * lp[3])) + lambda_init
    ql, kl, vl = split(p_lat)
    qc, kc, vc = split(p_ctx)
    ql = apply_rope(ql, cos, sin)
    kl = apply_rope(kl, cos, sin)
    k_all = jnp.concatenate([kc, kl], axis=1)
    v_all = jnp.concatenate([vc, vl], axis=1)
    out_lat = finish(sweep_query_blocks(ql, lambda qb: diff_attend(qb, k_all, v_all, lam)))
    out_ctx = finish(diff_attend(qc, kc, vc, lam)) if with_ctx else None
    return out_lat, out_ctx


def mixer_c(p_lat, q_ctx, kv_ctx, cos, sin, q_norm, k_norm):
    group = C_HEADS // C_KV_HEADS

    def heads_q(t):
        return rms_norm(t.reshape(t.shape[:2] + (C_HEADS, C_DH)), q_norm)

    def heads_kv(t):
        k = rms_norm(t[..., :C_KV].reshape(t.shape[:2] + (C_KV_HEADS, C_DH)), k_norm)
        v = t[..., C_KV:].reshape(t.shape[:2] + (C_KV_HEADS, C_DH))
        return k, v

    b, s = p_lat.shape[:2]
    ql = apply_rope(heads_q(p_lat[..., :C_Q]), cos, sin).reshape(b, s, C_KV_HEADS, group, C_DH)
    kl, vl = heads_kv(p_lat[..., C_Q:])
    kl = apply_rope(kl, cos, sin)
    kc, vc = heads_kv(kv_ctx)
    k_all = jnp.concatenate([kc, kl], axis=1)
    v_all = jnp.concatenate([vc, vl], axis=1)
    out_lat = sweep_query_blocks(ql, lambda qb: gqa_attend(qb, k_all, v_all)).reshape(b, s, C_Q)
    out_ctx = None
    if q_ctx is not None:
        lc = q_ctx.shape[1]
        qc = heads_q(q_ctx).reshape(b, lc, C_KV_HEADS, group, C_DH)
        out_ctx = gqa_attend(qc, kc, vc).reshape(b, lc, C_Q)
    return out_lat, out_ctx


def conv_ffn(h, w_up, conv_w, conv_b, w_down):
    u = dwconv_seq(h @ w_up, conv_w, conv_b)
    gate, val = jnp.split(u, 2, axis=-1)
    return (jax.nn.silu(gate) * val) @ w_down


def diff_lambda_init(layer):
    return 0.8 - 0.6 * math.exp(-0.3 * layer)


def setup_inputs(seed: int = 0) -> dict:
    key = jax.random.key(seed)
    ks = iter(jax.random.split(key, 32))

    def nrm(shape, std):
        return std * jax.random.normal(next(ks), shape, jnp.float32)

    def gain(shape):
        return 1.0 + nrm(shape, 0.02)

    dt = jnp.exp(jax.random.uniform(next(ks), (N_EVEN, 2, A_HEADS), jnp.float32,
                                    minval=math.log(1e-3), maxval=math.log(1e-1)))
    a_A_log = jnp.log(jax.random.uniform(next(ks), (N_EVEN, 2, A_HEADS), jnp.float32, minval=1.0, maxval=16.0))
    return {
        "x": nrm((BATCH, SEQ, D_MODEL), 1.0),
        "c": nrm((BATCH, D_MODEL), 1.0),
        "ctx": nrm((BATCH, CTX_LEN, D_MODEL), 1.0),
        "c_ctx": nrm((D_MODEL,), 1.0),
        "w_mod": nrm((DEPTH, D_MODEL, N_MOD * D_MODEL), 0.5 * D_MODEL ** -0.5),
        "b_mod": nrm((DEPTH, N_MOD * D_MODEL), 0.02),
        "norm1_g": gain((DEPTH, D_MODEL)),
        "norm2_g": gain((DEPTH, D_MODEL)),
        "w_in_even": nrm((N_EVEN, D_MODEL, EVEN_IN), D_MODEL ** -0.5),
        "a_conv_w": nrm((N_EVEN, A_CONV, A_CONV_CH), A_CONV ** -0.5),
        "a_A_log": a_A_log,
        "a_dt_bias": dt + jnp.log(-jnp.expm1(-dt)),
        "a_norm_g": gain((N_EVEN, A_DV)),
        "b_lambda": nrm((N_EVEN, 4, B_DH), 0.1),
        "b_norm_g": gain((N_EVEN, B_DV)),
        "w_out_even": nrm((N_EVEN, EVEN_OUT, D_MODEL), EVEN_OUT ** -0.5),
        "w_in_odd": nrm((N_ODD, D_MODEL, ODD_IN), D_MODEL ** -0.5),
        "c_q_norm": gain((N_ODD, C_DH)),
        "c_k_norm": gain((N_ODD, C_DH)),
        "w_out_odd": nrm((N_ODD, ODD_OUT, D_MODEL), ODD_OUT ** -0.5),
        "ffn_up": nrm((DEPTH, D_MODEL, 2 * D_FF), D_MODEL ** -0.5),
        "ffn_conv_w": nrm((DEPTH, FFN_CONV, 2 * D_FF), FFN_CONV ** -0.5),
        "ffn_conv_b": nrm((DEPTH, 2 * D_FF), 0.02),
        "ffn_down": nrm((DEPTH, D_FF, D_MODEL), D_FF ** -0.5),
        "final_g": gain((D_MODEL,)),
    }


def reference(x, c, ctx, c_ctx, w_mod, b_mod, norm1_g, norm2_g, w_in_even, a_conv_w, a_A_log, a_dt_bias,
              a_norm_g, b_lambda, b_norm_g, w_out_even, w_in_odd, c_q_norm, c_k_norm, w_out_odd,
              ffn_up, ffn_conv_w, ffn_conv_b, ffn_down, final_g):
    rows = x.shape[1] // GRID_W
    cos_b, sin_b = axial_rope(rows, B_DH)
    cos_c, sin_c = (cos_b, sin_b) if C_DH == B_DH else axial_rope(rows, C_DH)
    xc = ctx
    for l in range(DEPTH):
        last = l == DEPTH - 1
        mod_l = (jax.nn.silu(c) @ w_mod[l] + b_mod[l])[:, None, :]
        mod_c = (jax.nn.silu(c_ctx) @ w_mod[l] + b_mod[l])[None, None, :]
        sh1, sc1, g1, sh2, sc2, g2 = jnp.split(mod_l, N_MOD, axis=-1)
        csh1, csc1, cg1, csh2, csc2, cg2 = jnp.split(mod_c, N_MOD, axis=-1)
        h_l = modulate(x, norm1_g[l], sh1, sc1)
        h_c = modulate(xc, norm1_g[l], csh1, csc1)
        if l % 2 == 0:
            e = l // 2
            p_l = h_l @ w_in_even[e]
            p_c = h_c @ w_in_even[e]
            a_l, a_c = mixer_a(p_l[..., :A_IN], p_c[..., :A_IN], a_conv_w[e], a_A_log[e], a_dt_bias[e],
                               a_norm_g[e], not last)
            b_l, b_c = mixer_b(p_l[..., A_IN:], p_c[..., A_IN:], cos_b, sin_b, b_lambda[e], b_norm_g[e],
                               diff_lambda_init(l), not last)
            m_l = jnp.concatenate([a_l, b_l], axis=-1) @ w_out_even[e]
            m_c = None if last else jnp.concatenate([a_c, b_c], axis=-1) @ w_out_even[e]
        else:
            o = l // 2
            p_l = h_l @ w_in_odd[o]
            kv_c = h_c @ w_in_odd[o][:, C_Q:]
            q_c = None if last else h_c @ w_in_odd[o][:, :C_Q]
            c_l, c_c = mixer_c(p_l, q_c, kv_c, cos_c, sin_c, c_q_norm[o], c_k_norm[o])
            m_l = c_l @ w_out_odd[o]
            m_c = None if last else c_c @ w_out_odd[o]
        x = x + g1 * m_l
        x = x + g2 * conv_ffn(modulate(x, norm2_g[l], sh2, sc2), ffn_up[l], ffn_conv_w[l], ffn_conv_b[l], ffn_down[l])
        if not last:
            xc = xc + cg1 * m_c
            xc = xc + cg2 * conv_ffn(modulate(xc, norm2_g[l], csh2, csc2), ffn_up[l], ffn_conv_w[l],
                                     ffn_conv_b[l], ffn_down[l])
    return rms_norm(x, final_g)
```

```python
import math
from contextlib import ExitStack

import ml_dtypes
import numpy as np
import concourse.bass as bass
import concourse.mybir as mybir
from concourse.bass_utils import run_bass_kernel_spmd

F32 = mybir.dt.float32
BF16 = mybir.dt.bfloat16
AF = mybir.ActivationFunctionType
ALU = mybir.AluOpType
AX = mybir.AxisListType
NPBF = ml_dtypes.bfloat16

D = 2048
KC = 16
NCORE = 8
SEQ = 16384
CTX = 256
DEPTH = 4
EPS = 1e-6
DFF = 5632
FC = 44
A_IN = 4128
EVEN_IN = 7200
ODD_IN = 3072


class Ctx:
    def __init__(self, nc, stack):
        self.nc = nc
        self.stack = stack
        self.E = {"pe": nc.tensor, "act": nc.scalar, "dve": nc.vector, "pool": nc.gpsimd, "sp": nc.sync}
        self.sems = {}
        self.cnt = {}
        self.known = {e: {} for e in self.E}
        self.lastw = {}
        self.readers = {}
        self.ninst = 0

    def sb(self, name, shape, dt):
        return self.stack.enter_context(self.nc.sbuf_tensor("sb_" + name, list(shape), dt))

    def ps(self, name, shape, dt=F32):
        return self.stack.enter_context(self.nc.psum_tensor("ps_" + name, list(shape), dt))

    def sem(self, key):
        if key not in self.sems:
            self.sems[key] = self.stack.enter_context(self.nc.semaphore("s_" + key.replace(":", "_")))
            self.cnt[key] = 0
        return self.sems[key]

    def _waits(self, eng, reads, writes):
        need = {}

        def add(ev):
            if ev is not None and need.get(ev[0], 0) < ev[1]:
                need[ev[0]] = ev[1]

        for t in reads:
            add(self.lastw.get(t))
        for t in writes:
            add(self.lastw.get(t))
            for k, v in self.readers.get(t, {}).items():
                add((k, v))
        E = self.E[eng]
        for k, v in need.items():
            if k == "pe" and eng == "pe":
                continue
            if k.startswith("d:"):
                v = self.cnt[k]
            if self.known[eng].get(k, 0) < v:
                E.wait_ge(self.sems[k], v)
                self.known[eng][k] = v
                self.ninst += 1

    def _record(self, ev, reads, writes):
        k, v = ev
        for t in reads:
            d = self.readers.setdefault(t, {})
            if d.get(k, 0) < v:
                d[k] = v
        for t in writes:
            self.lastw[t] = ev
            self.readers[t] = {}

    def op(self, eng, emit, reads=(), writes=()):
        ex = [t for t in reads if t.startswith("bank")]
        if ex:
            writes = list(writes) + ex
        self._waits(eng, reads, writes)
        s = self.sem(eng)
        ins = emit(self.E[eng])
        ins.then_inc(s, 1)
        self.cnt[eng] += 1
        self.ninst += 1
        self._record((eng, self.cnt[eng]), reads, writes)
        return ins

    def dma(self, eng, stream, out, in_, reads=(), writes=()):
        key = "d:" + stream
        s = self.sem(key)
        self._waits(eng, reads, writes)
        ins = self.E[eng].dma_start(out=out, in_=in_)
        ins.then_inc(s, 16)
        self.cnt[key] += 16
        self.ninst += 1
        self._record((key, self.cnt[key]), reads, writes)
        return ins

    def push_scope(self):
        self._outer = self.stack
        self.stack = ExitStack()

    def pop_scope(self):
        self.barrier()
        self.stack.close()
        self.stack = self._outer

    def barrier(self):
        for eng, E in self.E.items():
            for k, s_ in self.sems.items():
                v = self.cnt[k]
                if v > 0 and self.known[eng].get(k, 0) < v and not (k == eng):
                    E.wait_ge(s_, v)
                    self.known[eng][k] = v
                    self.ninst += 1

    def wait_all(self, eng, tokens):
        self._waits(eng, tokens, ())


class Rot:
    def __init__(self, c, name, n, shape, dt, psum=False):
        self.bufs = [(c.ps if psum else c.sb)("%s%d" % (name, i), shape, dt) for i in range(n)]
        self.names = ["%s%d" % (name, i) for i in range(n)]
        self.i = 0

    def next(self):
        j = self.i % len(self.bufs)
        self.i += 1
        return self.bufs[j], self.names[j]


def new_nc():
    return bass.Bass("TRN2", target_bir_lowering=False)


def run(nc, in_maps):
    res = run_bass_kernel_spmd(nc, in_maps, core_ids=list(range(NCORE)))
    return res.results


def emit_rstd(c, rstd, ss, sstok, W, n=D, tok="rstd"):
    c.op("dve", lambda e: e.tensor_scalar(out=rstd[:, :W], in0=ss[:, :W], scalar1=1.0 / n, scalar2=EPS, op0=ALU.mult, op1=ALU.add),
         reads=[sstok], writes=[tok])
    c.op("act", lambda e: e.activation(out=rstd[:, :W], in_=rstd[:, :W], func=AF.Sqrt), reads=[tok], writes=[tok])
    c.op("dve", lambda e: e.reciprocal(out=rstd[:, :W], in_=rstd[:, :W]), reads=[tok], writes=[tok])


def emit_norm_mod(c, K, xt, xtok, W, a_ap, b_ap, h, htok, maskcols=()):
    ss, sstok = K["ps_ss"].next()
    for kc in range(KC):
        sq, sqtok = K["sq"].next()
        c.op("act", lambda e: e.activation(out=sq[:, :W], in_=xt[:, kc, :W], func=AF.Square), reads=[xtok], writes=[sqtok])
        c.op("pe", lambda e: e.matmul(ss[:, :W], lhsT=K["ones"][:, :], rhs=sq[:, :W], start=(kc == 0), stop=(kc == KC - 1)),
             reads=[sqtok, "ones"], writes=[sstok])
    rstd = K["rstd"]
    emit_rstd(c, rstd, ss, sstok, W)
    for kc in range(KC):
        tmp, tmptok = K["tmp"].next()
        c.op("dve", lambda e: e.scalar_tensor_tensor(out=tmp[:, :W], in0=xt[:, kc, :W], scalar=a_ap[:, kc:kc + 1], in1=rstd[:, :W],
                                                     op0=ALU.mult, op1=ALU.mult), reads=[xtok, "rstd", "vec"], writes=[tmptok])
        c.op("act", lambda e: e.activation(out=h[:, kc, :W], in_=tmp[:, :W], func=AF.Identity, bias=b_ap[:, kc:kc + 1], scale=1.0),
             reads=[tmptok, "vec"], writes=[htok])
    for col, sc in maskcols:
        c.op("dve", lambda e: e.tensor_scalar(out=h[:, :, col:col + 1], in0=h[:, :, col:col + 1], scalar1=sc, scalar2=None, op0=ALU.mult),
             reads=[htok, "vec"], writes=[htok])


def make_consts(c):
    K = {}
    K["ones"] = c.sb("ones", [128, 128], BF16)
    c.op("dve", lambda e: e.memset(K["ones"][:, :], 1.0), writes=["ones"])
    K["sq"] = Rot(c, "sq", 2, [128, 512], BF16)
    K["tmp"] = Rot(c, "tmp", 2, [128, 512], F32)
    K["rstd"] = c.sb("rstd", [128, 512], F32)
    K["ps_ss"] = Rot(c, "ps_ss", 1, [128, 512], F32, psum=True)
    return K


MODC = 1536


def build_mod():
    nc = new_nc()
    c2 = nc.dram_tensor("c2", [128, KC, 2], F32, kind="ExternalInput").ap()
    wm = nc.dram_tensor("wm", [DEPTH, D, MODC], F32, kind="ExternalInput").ap()
    bm = nc.dram_tensor("bm", [2, DEPTH, MODC], F32, kind="ExternalInput").ap()
    out = nc.dram_tensor("out", [2, DEPTH, MODC], F32, kind="ExternalOutput").ap()
    with ExitStack() as st:
        c = Ctx(nc, st)
        ct = c.sb("ct", [128, KC, 2], F32)
        cs = c.sb("cs", [128, KC, 2], BF16)
        bt = c.sb("bt", [2, DEPTH, MODC], F32)
        ot = c.sb("ot", [2, DEPTH, MODC], F32)
        wrot = Rot(c, "wt", 2, [128, KC, 512], BF16)
        prot = Rot(c, "pm", 2, [2, 512], F32, psum=True)
        c.dma("sp", "c2", ct[:], c2, writes=["ct"])
        c.dma("sp", "bm", bt[:], bm, writes=["bt"])
        c.op("act", lambda e: e.activation(out=cs[:], in_=ct[:], func=AF.Silu), reads=["ct"], writes=["cs"])
        for l in range(DEPTH):
            for n in range(MODC // 512):
                wt, wtok = wrot.next()
                c.dma("pool", wtok, wt[:], wm[l, :, n * 512:(n + 1) * 512].rearrange("(kc p) m -> p kc m", p=128), writes=[wtok])
                ps, ptok = prot.next()
                for kc in range(KC):
                    c.op("pe", lambda e: e.matmul(ps[:, :], lhsT=cs[:, kc, :], rhs=wt[:, kc, :], start=(kc == 0), stop=(kc == KC - 1)),
                         reads=["cs", wtok], writes=[ptok])
                c.op("dve", lambda e: e.tensor_tensor(out=ot[:, l, n * 512:(n + 1) * 512], in0=ps[:, :], in1=bt[:, l, n * 512:(n + 1) * 512], op=ALU.add),
                     reads=[ptok, "bt"], writes=["ot"])
        c.dma("sp", "out", out, ot[:], reads=["ot"], writes=["out"])
        c.wait_all("sp", ["out"])
    return nc


def tiles_of(total, w):
    return [(s, min(w, total - s)) for s in range(0, total, w)]


def build_pre(ncols, tiles):
    T = sum(w for _, w, _ in tiles)
    nc = new_nc()
    xT = nc.dram_tensor("xT", [D, T], F32, kind="ExternalInput").ap()
    vec = nc.dram_tensor("vec", [128, 5, KC], F32, kind="ExternalInput").ap()
    w = nc.dram_tensor("w", [D, ncols], F32, kind="ExternalInput").ap()
    pT = nc.dram_tensor("pT", [ncols, T], BF16, kind="ExternalOutput").ap()
    with ExitStack() as st:
        c = Ctx(nc, st)
        K = make_consts(c)
        vt = c.sb("vec", [128, 5, KC], F32)
        av = c.sb("av", [128, 2, KC], F32)
        h = c.sb("h", [128, KC, T], BF16)
        xrot = Rot(c, "xt", 2, [128, KC, 512], F32)
        wrot = Rot(c, "wt", 2, [128, KC, 128], BF16)
        prot = Rot(c, "pp", 3, [128, 512], F32, psum=True)
        orot = Rot(c, "po", 3, [128, 512], BF16)
        c.dma("sp", "vec", vt[:], vec, writes=["vec"])
        for s in range(2):
            c.op("dve", lambda e: e.scalar_tensor_tensor(out=av[:, s, :], in0=vt[:, 2 + 2 * s, :], scalar=1.0, in1=vt[:, 0, :],
                                                         op0=ALU.add, op1=ALU.mult), reads=["vec"], writes=["vec"])
        for (s0, wd, stream) in tiles:
            xt, xtok = xrot.next()
            c.dma("sp", xtok, xt[:, :, :wd], xT[:, s0:s0 + wd].rearrange("(kc p) t -> p kc t", p=128), writes=[xtok])
            emit_norm_mod(c, K, xt, xtok, wd, av[:, stream, :], vt[:, 1 + 2 * stream, :], h[:, :, s0:s0 + wd], "h%d" % s0)
        nst = 0
        for cb0 in range(0, ncols, 128):
            m = min(128, ncols - cb0)
            wt, wtok = wrot.next()
            c.dma("pool", wtok, wt[:, :, :m], w[:, cb0:cb0 + m].rearrange("(kc p) m -> p kc m", p=128), writes=[wtok])
            for (s0, wd, stream) in tiles:
                ps, ptok = prot.next()
                for kc in range(KC):
                    c.op("pe", lambda e: e.matmul(ps[:m, :wd], lhsT=wt[:, kc, :m], rhs=h[:, kc, s0:s0 + wd], start=(kc == 0), stop=(kc == KC - 1)),
                         reads=[wtok, "h%d" % s0], writes=[ptok])
                ot, otok = orot.next()
                eng = "act" if nst % 2 == 0 else "dve"
                if eng == "act":
                    c.op("act", lambda e: e.activation(out=ot[:m, :wd], in_=ps[:m, :wd], func=AF.Copy), reads=[ptok], writes=[otok])
                else:
                    c.op("dve", lambda e: e.tensor_copy(out=ot[:m, :wd], in_=ps[:m, :wd]), reads=[ptok], writes=[otok])
                nst += 1
                c.dma("sp", otok, pT[cb0:cb0 + m, s0:s0 + wd], ot[:m, :wd], reads=[otok], writes=["pT"])
        c.wait_all("sp", ["pT"])
    return nc


POST_V = {"n2g": 0, "g1": 16, "sh2": 32, "sc2": 48, "g2": 64, "cg1": 80, "csh2": 96, "csc2": 112, "cg2": 128, "fg": 144,
          "cw": 160, "cb": 160 + 3 * 88, "vl": 160 + 4 * 88, "vr": 161 + 4 * 88, "zero": 162 + 4 * 88}
POST_NV = 163 + 4 * 88


def build_post(tiles, final):
    T = max(s + w for s, w, _, _, _ in tiles)
    TO = sum(w - 2 for _, w, _, _, _ in tiles)
    nc = new_nc()
    xT = nc.dram_tensor("xT", [D, T], F32, kind="ExternalInput").ap()
    mT = nc.dram_tensor("mT", [D, T], BF16, kind="ExternalInput").ap()
    vec = nc.dram_tensor("vec", [128, POST_NV], F32, kind="ExternalInput").ap()
    wo = nc.dram_tensor("wo", [D, D], F32, kind="ExternalInput").ap()
    wu = nc.dram_tensor("wu", [D, 2 * DFF], F32, kind="ExternalInput").ap()
    wdn = nc.dram_tensor("wd", [DFF, D], F32, kind="ExternalInput").ap()
    oT = nc.dram_tensor("oT", [D, TO], F32, kind="ExternalOutput").ap()
    with ExitStack() as st:
        c = Ctx(nc, st)
        K = make_consts(c)
        vt = c.sb("vec", [128, POST_NV], F32)
        av = c.sb("av", [128, 2, KC], F32)
        xrot = Rot(c, "xt", 1, [128, KC, 512], F32)
        mrot = Rot(c, "mt", 1, [128, KC, 512], BF16)
        h2 = c.sb("h2", [128, KC, 512], BF16)
        act = c.sb("actb", [128, FC, 512], BF16)
        worot = Rot(c, "wo", 2, [128, KC, 128], BF16)
        wgrot = Rot(c, "wg", 2, [128, KC, 128], BF16)
        wvrot = Rot(c, "wv", 2, [128, KC, 128], BF16)
        wdrot = Rot(c, "wdn", 2, [128, FC, 128], BF16)
        prot = Rot(c, "pp", 2, [128, 512], F32, psum=True)
        pgrot = Rot(c, "pg", 2, [128, 512], F32, psum=True)
        pvrot = Rot(c, "pv", 2, [128, 512], F32, psum=True)
        cg = Rot(c, "cg", 2, [128, 512], F32)
        cv = Rot(c, "cv", 2, [128, 512], F32)
        sg = Rot(c, "sg", 2, [128, 512], F32)
        yt = Rot(c, "yt", 2, [128, 512], F32)
        c.dma("sp", "vec", vt[:], vec, writes=["vec"])
        V = POST_V
        for s, (scn, gn) in enumerate((("sc2", "n2g"), ("csc2", "n2g"))):
            c.op("dve", lambda e: e.scalar_tensor_tensor(out=av[:, s, :], in0=vt[:, V[scn]:V[scn] + 16], scalar=1.0, in1=vt[:, V[gn]:V[gn] + 16],
                                                         op0=ALU.add, op1=ALU.mult), reads=["vec"], writes=["vec"])
        ocol = 0
        for (s0, wd, stream, lf, rf) in tiles:
            g1 = V["cg1"] if stream else V["g1"]
            g2 = V["cg2"] if stream else V["g2"]
            sh2 = V["csh2"] if stream else V["sh2"]
            wi = wd - 2
            xt, xtok = xrot.next()
            mt, mtok = mrot.next()
            c.dma("sp", xtok, xt[:, :, :wd], xT[:, s0:s0 + wd].rearrange("(kc p) t -> p kc t", p=128), writes=[xtok])
            c.dma("sp", mtok, mt[:, :, :wd], mT[:, s0:s0 + wd].rearrange("(kc p) t -> p kc t", p=128), writes=[mtok])
            for oc in range(KC):
                wt, wtok = worot.next()
                c.dma("pool", wtok, wt[:], wo[:, oc * 128:(oc + 1) * 128].rearrange("(kc p) m -> p kc m", p=128), writes=[wtok])
                ps, ptok = prot.next()
                for kc in range(KC):
                    c.op("pe", lambda e: e.matmul(ps[:, :wd], lhsT=wt[:, kc, :], rhs=mt[:, kc, :wd], start=(kc == 0), stop=(kc == KC - 1)),
                         reads=[wtok, mtok], writes=[ptok])
                c.op("dve", lambda e: e.scalar_tensor_tensor(out=xt[:, oc, :wd], in0=ps[:, :wd], scalar=vt[:, g1 + oc:g1 + oc + 1], in1=xt[:, oc, :wd],
                                                             op0=ALU.mult, op1=ALU.add), reads=[ptok, xtok, "vec"], writes=[xtok])
            masks = []
            if lf != "one":
                masks.append((0, vt[:, V[lf]:V[lf] + 1]))
            if rf != "one":
                masks.append((wd - 1, vt[:, V[rf]:V[rf] + 1]))
            emit_norm_mod(c, K, xt, xtok, wd, av[:, stream, :], vt[:, sh2:sh2 + 16], h2, "h2", maskcols=masks)
            for f in range(FC):
                wg, wgtok = wgrot.next()
                wv, wvtok = wvrot.next()
                c.dma("pool", wgtok, wg[:], wu[:, f * 128:(f + 1) * 128].rearrange("(kc p) m -> p kc m", p=128), writes=[wgtok])
                c.dma("pool", wvtok, wv[:], wu[:, DFF + f * 128:DFF + (f + 1) * 128].rearrange("(kc p) m -> p kc m", p=128), writes=[wvtok])
                pg, pgtok = pgrot.next()
                pv, pvtok = pvrot.next()
                for kc in range(KC):
                    c.op("pe", lambda e: e.matmul(pg[:, :wd], lhsT=wg[:, kc, :], rhs=h2[:, kc, :wd], start=(kc == 0), stop=(kc == KC - 1)),
                         reads=[wgtok, "h2"], writes=[pgtok])
                for kc in range(KC):
                    c.op("pe", lambda e: e.matmul(pv[:, :wd], lhsT=wv[:, kc, :], rhs=h2[:, kc, :wd], start=(kc == 0), stop=(kc == KC - 1)),
                         reads=[wvtok, "h2"], writes=[pvtok])
                outs = []
                for (pp, pptok, rot, fi) in ((pg, pgtok, cg, f), (pv, pvtok, cv, FC + f)):
                    t, ttok = rot.next()
                    cw0 = V["cw"] + 0 * 88 + fi
                    cw1 = V["cw"] + 1 * 88 + fi
                    cw2 = V["cw"] + 2 * 88 + fi
                    c.op("dve", lambda e: e.tensor_scalar(out=t[:, :wi], in0=pp[:, 0:wi], scalar1=vt[:, cw0:cw0 + 1], scalar2=None, op0=ALU.mult),
                         reads=[pptok, "vec"], writes=[ttok])
                    c.op("dve", lambda e: e.scalar_tensor_tensor(out=t[:, :wi], in0=pp[:, 1:wi + 1], scalar=vt[:, cw1:cw1 + 1], in1=t[:, :wi],
                                                                 op0=ALU.mult, op1=ALU.add), reads=[pptok, ttok, "vec"], writes=[ttok])
                    c.op("dve", lambda e: e.scalar_tensor_tensor(out=t[:, :wi], in0=pp[:, 2:wi + 2], scalar=vt[:, cw2:cw2 + 1], in1=t[:, :wi],
                                                                 op0=ALU.mult, op1=ALU.add), reads=[pptok, ttok, "vec"], writes=[ttok])
                    outs.append((t, ttok))
                (tg, tgtok), (tv, tvtok) = outs
                s_, stok = sg.next()
                cbg = V["cb"] + f
                cbv = V["cb"] + FC + f
                c.op("act", lambda e: e.activation(out=s_[:, :wi], in_=tg[:, :wi], func=AF.Silu, bias=vt[:, cbg:cbg + 1], scale=1.0),
                     reads=[tgtok, "vec"], writes=[stok])
                c.op("dve", lambda e: e.scalar_tensor_tensor(out=act[:, f, :wi], in0=tv[:, :wi], scalar=vt[:, cbv:cbv + 1], in1=s_[:, :wi],
                                                              op0=ALU.add, op1=ALU.mult), reads=[tvtok, stok, "vec"], writes=["act%d" % f])
            for oc in range(KC):
                wt, wtok = wdrot.next()
                c.dma("pool", wtok, wt[:], wdn[:, oc * 128:(oc + 1) * 128].rearrange("(f p) m -> p f m", p=128), writes=[wtok])
                ps, ptok = prot.next()
                for f in range(FC):
                    c.op("pe", lambda e: e.matmul(ps[:, :wi], lhsT=wt[:, f, :], rhs=act[:, f, :wi], start=(f == 0), stop=(f == FC - 1)),
                         reads=[wtok, "act%d" % f], writes=[ptok])
                c.op("dve", lambda e: e.scalar_tensor_tensor(out=xt[:, oc, 1:wi + 1], in0=ps[:, :wi], scalar=vt[:, g2 + oc:g2 + oc + 1], in1=xt[:, oc, 1:wi + 1],
                                                             op0=ALU.mult, op1=ALU.add), reads=[ptok, xtok, "vec"], writes=[xtok])
            if not final:
                c.dma("sp", "oT", oT[:, ocol:ocol + wi].rearrange("(kc p) t -> p kc t", p=128), xt[:, :, 1:wi + 1], reads=[xtok], writes=["oT"])
            else:
                ss, sstok = K["ps_ss"].next()
                for kc in range(KC):
                    sq, sqtok = K["sq"].next()
                    c.op("act", lambda e: e.activation(out=sq[:, :wi], in_=xt[:, kc, 1:wi + 1], func=AF.Square), reads=[xtok], writes=[sqtok])
                    c.op("pe", lambda e: e.matmul(ss[:, :wi], lhsT=K["ones"][:, :], rhs=sq[:, :wi], start=(kc == 0), stop=(kc == KC - 1)),
                         reads=[sqtok, "ones"], writes=[sstok])
                rstd = K["rstd"]
                emit_rstd(c, rstd, ss, sstok, wi)
                for kc in range(KC):
                    y, ytok = yt.next()
                    fg = V["fg"] + kc
                    c.op("dve", lambda e: e.scalar_tensor_tensor(out=y[:, :wi], in0=xt[:, kc, 1:wi + 1], scalar=vt[:, fg:fg + 1], in1=rstd[:, :wi],
                                                                 op0=ALU.mult, op1=ALU.mult), reads=[xtok, "rstd", "vec"], writes=[ytok])
                    c.dma("sp", ytok, oT[kc * 128:(kc + 1) * 128, ocol:ocol + wi], y[:, :wi], reads=[ytok], writes=["oT"])
            ocol += wi
        c.wait_all("sp", ["oT"])
    return nc


def build_att(kind, NQL, NKL, NCT=CTX):
    S = 2
    SK = 2 if kind == "B" else 1
    DV = 256 if kind == "B" else 128
    NH = DV // 128
    NK = NCT + NKL
    NKT = NK // 128
    NCKT = NCT // 128
    R = 256
    scale = 128 ** -0.5
    nc = new_nc()
    qT = nc.dram_tensor("qT", [S, 128, NCT + NQL], BF16, kind="ExternalInput").ap()
    kT = nc.dram_tensor("kT", [SK, 128, NK], BF16, kind="ExternalInput").ap()
    v = nc.dram_tensor("v", [NK, DV], BF16, kind="ExternalInput").ap()
    cq = nc.dram_tensor("cq", [2, 128, NQL], F32, kind="ExternalInput").ap()
    ck = nc.dram_tensor("ck", [2, 128, NKL], F32, kind="ExternalInput").ap()
    rt = nc.dram_tensor("rt", [128, 128], BF16, kind="ExternalInput").ap()
    vec = nc.dram_tensor("vec", [128, 8], F32, kind="ExternalInput").ap()
    lp = nc.dram_tensor("lp", [128, 4, 128], F32, kind="ExternalInput").ap()
    oT = nc.dram_tensor("oT", [R, NCT + NQL], BF16, kind="ExternalOutput").ap()
    with ExitStack() as st:
        c = Ctx(nc, st)
        ones = c.sb("ones", [128, 128], BF16)
        c.op("dve", lambda e: e.memset(ones[:, :], 1.0), writes=["ones"])
        rtt = c.sb("rtt", [128, 128], BF16)
        vt = c.sb("vec", [128, 8], F32)
        lpt = c.sb("lpt", [128, 4, 128], F32)
        lam = c.sb("lam", [128, 8], F32)
        Kr = c.sb("Kr", [128, SK, NK], BF16)
        Vt = c.sb("Vt", [128, NKT, DV], BF16)
        raw = Rot(c, "raw", 2, [128, 512], BF16)
        cst = Rot(c, "cst", 2, [128, 2, 512], F32)
        xn = c.sb("xn", [128, 512], F32)
        xnb = c.sb("xnb", [128, 512], BF16)
        sqb = c.sb("sqb", [128, 512], BF16)
        rstd = c.sb("rstd", [128, 512], F32)
        t1 = c.sb("t1", [128, 512], F32)
        t2 = c.sb("t2", [128, 512], F32)
        qr = Rot(c, "qr", 2, [128, 512], BF16)
        E = Rot(c, "E", 4, [128, 512], BF16)
        acc = [c.sb("acc%d" % i, [128, 512], F32) for i in range(2)]
        ones32 = c.sb("ones32", [128, 128], F32)
        c.op("dve", lambda e: e.memset(ones32[:, :], 1.0), writes=["ones32"])
        sqb2 = c.sb("sqb2", [128, 512], BF16)
        rstd2 = c.sb("rstd2", [128, 512], F32)
        on = [[c.sb("on%d%d" % (s, h), [128, 512], F32) for h in range(NH)] for s in range(S)]
        rec = c.sb("rec", [128, 512], F32)
        ob = Rot(c, "ob", 2, [128, 512], BF16)
        ps_s = Rot(c, "pS", 3, [128, 512], F32, psum=True)
        ps_o = [c.ps("pO%d" % h, [128, 512]) for h in range(NH)]
        ps_sum = c.ps("pSum", [128, 512])
        ps_ss2 = ps_sum
        ps_ss = c.ps("pSS", [128, 512])
        ps_rot = c.ps("pRot", [128, 512])
        c.dma("sp", "rtt", rtt[:], rt, writes=["rtt"])
        c.dma("sp", "vec", vt[:], vec, writes=["vec"])
        c.dma("sp", "Vt", Vt[:], v.rearrange("(kt p) d -> p kt d", p=128), writes=["Vt"])
        if kind == "B":
            c.dma("sp", "lpt", lpt[:], lp, writes=["lpt"])
            for i in range(2):
                c.op("dve", lambda e: e.tensor_tensor(out=t1[:, :128], in0=lpt[:, 2 * i, :], in1=lpt[:, 2 * i + 1, :], op=ALU.mult), reads=["lpt"], writes=["t1"])
                c.op("dve", lambda e: e.reduce_sum(out=lam[:, i:i + 1], in_=t1[:, :128], axis=AX.X), reads=["t1"], writes=["lam"])
            c.op("act", lambda e: e.activation(out=lam[:, 0:2], in_=lam[:, 0:2], func=AF.Exp), reads=["lam"], writes=["lam"])
            c.op("dve", lambda e: e.tensor_tensor(out=lam[:, 2:3], in0=lam[:, 0:1], in1=lam[:, 1:2], op=ALU.subtract), reads=["lam"], writes=["lam"])
            c.op("dve", lambda e: e.tensor_tensor(out=lam[:, 2:3], in0=lam[:, 2:3], in1=vt[:, 2:3], op=ALU.add), reads=["lam", "vec"], writes=["lam"])
            c.op("dve", lambda e: e.tensor_scalar(out=lam[:, 3:4], in0=lam[:, 2:3], scalar1=-1.0, scalar2=None, op0=ALU.mult), reads=["lam"], writes=["lam"])
            c.op("dve", lambda e: e.tensor_scalar(out=lam[:, 4:6], in0=vt[:, 4:6], scalar1=vt[:, 3:4], scalar2=None, op0=ALU.mult), reads=["lam", "vec"], writes=["lam"])

        def prep(src, srctok, W, cs, cstok, gain_col, dst, dsttok):
            cur, curtok = src, srctok
            if gain_col is not None:
                c.op("act", lambda e: e.activation(out=sqb[:, :W], in_=src[:, :W], func=AF.Square), reads=[srctok], writes=["sqb"])
                c.op("pe", lambda e: e.matmul(ps_ss[:, :W], lhsT=ones[:, :], rhs=sqb[:, :W], start=True, stop=True), reads=["sqb", "ones"], writes=["pSS"])
                emit_rstd(c, rstd, ps_ss, "pSS", W, n=128)
                c.op("dve", lambda e: e.scalar_tensor_tensor(out=xn[:, :W], in0=src[:, :W], scalar=vt[:, gain_col:gain_col + 1], in1=rstd[:, :W],
                                                             op0=ALU.mult, op1=ALU.mult), reads=[srctok, "rstd", "vec"], writes=["xn"])
                cur, curtok = xn, "xn"
                if cs is None:
                    c.op("act", lambda e: e.activation(out=dst[:, :W], in_=xn[:, :W], func=AF.Copy), reads=["xn"], writes=[dsttok])
                    return
                c.op("act", lambda e: e.activation(out=xnb[:, :W], in_=xn[:, :W], func=AF.Copy), reads=["xn"], writes=["xnb"])
                curb, curbtok = xnb, "xnb"
            else:
                if cs is None:
                    c.op("act", lambda e: e.activation(out=dst[:, :W], in_=src[:, :W], func=AF.Copy), reads=[srctok], writes=[dsttok])
                    return
                curb, curbtok = src, srctok
            c.op("pe", lambda e: e.matmul(ps_rot[:, :W], lhsT=rtt[:, :], rhs=curb[:, :W], start=True, stop=True), reads=["rtt", curbtok], writes=["pRot"])
            c.op("dve", lambda e: e.tensor_tensor(out=t1[:, :W], in0=cur[:, :W], in1=cs[:, 0, :W], op=ALU.mult), reads=[curtok, cstok], writes=["t1"])
            c.op("dve", lambda e: e.tensor_tensor(out=t2[:, :W], in0=ps_rot[:, :W], in1=cs[:, 1, :W], op=ALU.mult), reads=["pRot", cstok], writes=["t2"])
            c.op("dve", lambda e: e.tensor_tensor(out=dst[:, :W], in0=t1[:, :W], in1=t2[:, :W], op=ALU.add), reads=["t1", "t2"], writes=[dsttok])

        kgain = 1 if kind == "C" else None
        qgain = 0 if kind == "C" else None
        ktiles = [(0, NCT, None)] + [(NCT + s0, w, s0) for s0, w in tiles_of(NKL, 512)]
        for sk in range(SK):
            for (c0, w, r0) in ktiles:
                rw, rwtok = raw.next()
                c.dma("sp", rwtok, rw[:, :w], kT[sk, :, c0:c0 + w], writes=[rwtok])
                cs, cstok = None, None
                if r0 is not None:
                    cs, cstok = cst.next()
                    c.dma("sp", cstok, cs[:, :, :w], ck[:, :, r0:r0 + w].rearrange("a p t -> p a t"), writes=[cstok])
                prep(rw, rwtok, w, cs, cstok, kgain, Kr[:, sk, c0:c0 + w], "Kr%d_%d" % (sk, c0))
        ktoks = [["Kr%d_%d" % (sk, c0) for (c0, w, r0) in ktiles] for sk in range(SK)]
        LA = 2
        qtiles = [(0, NCT, None, NCKT)] + [(NCT + s0, w, s0, NKT) for s0, w in tiles_of(NQL, 512)]
        units = [(qi, s) for qi in range(len(qtiles)) for s in range(S)]
        cs_of = {}

        def prep_unit(u):
            qi, s = units[u]
            c0, w, r0, nkt = qtiles[qi]
            if s == 0:
                cs, cstok = None, None
                if r0 is not None:
                    cs, cstok = cst.next()
                    c.dma("sp", cstok, cs[:, :, :w], cq[:, :, r0:r0 + w].rearrange("a p t -> p a t"), writes=[cstok])
                cs_of[qi] = (cs, cstok)
            cs, cstok = cs_of[qi]
            rw, rwtok = raw.next()
            c.dma("sp", rwtok, rw[:, :w], qT[s, :, c0:c0 + w], writes=[rwtok])
            q, qtok = qr.next()
            prep(rw, rwtok, w, cs, cstok, qgain, q, qtok)
            return q, qtok

        nxt = prep_unit(0)
        for u, (qi, s) in enumerate(units):
            c0, w, r0, nkt = qtiles[qi]
            q, qtok = nxt
            if u + 1 < len(units):
                nxt = prep_unit(u + 1)
            sk = s if SK == 2 else 0
            pend = {}

            def score(kt):
                ps, pstok = ps_s.next()
                c.op("pe", lambda e: e.matmul(ps[:, :w], lhsT=Kr[:, sk, kt * 128:(kt + 1) * 128], rhs=q[:, :w], start=True, stop=True),
                     reads=ktoks[sk] + [qtok], writes=[pstok])
                pend[kt] = (ps, pstok)

            for kt in range(min(LA, nkt)):
                score(kt)
            for kt in range(nkt):
                ps, pstok = pend.pop(kt)
                e_, etok = E.next()
                c.op("act", lambda e: e.activation(out=e_[:, :w], in_=ps[:, :w], func=AF.Exp, scale=scale), reads=[pstok], writes=[etok])
                if kt + LA < nkt:
                    score(kt + LA)
                for h in range(NH):
                    c.op("pe", lambda e: e.matmul(ps_o[h][:, :w], lhsT=Vt[:, kt, h * 128:(h + 1) * 128], rhs=e_[:, :w], start=(kt == 0), stop=(kt == nkt - 1)),
                         reads=["Vt", etok], writes=["pO%d" % h])
                eng, ac, actok = ("dve", acc[0], "acc0") if kt % 2 == 0 else ("pool", acc[1], "acc1")
                if kt < 2:
                    c.op(eng, lambda e: e.tensor_copy(out=ac[:, :w], in_=e_[:, :w]), reads=[etok], writes=[actok])
                else:
                    c.op(eng, lambda e: e.tensor_tensor(out=ac[:, :w], in0=ac[:, :w], in1=e_[:, :w], op=ALU.add), reads=[etok, actok], writes=[actok])
            c.op("dve", lambda e: e.tensor_tensor(out=acc[0][:, :w], in0=acc[0][:, :w], in1=acc[1][:, :w], op=ALU.add), reads=["acc0", "acc1"], writes=["acc0"])
            c.op("pe", lambda e: e.matmul(ps_sum[:, :w], lhsT=ones32[:, :], rhs=acc[0][:, :w], start=True, stop=True), reads=["ones32", "acc0"], writes=["pSum"])
            c.op("dve", lambda e: e.reciprocal(out=rec[:, :w], in_=ps_sum[:, :w]), reads=["pSum"], writes=["rec"])
            for h in range(NH):
                c.op("dve", lambda e: e.tensor_tensor(out=on[s][h][:, :w], in0=ps_o[h][:, :w], in1=rec[:, :w], op=ALU.mult),
                     reads=["pO%d" % h, "rec"], writes=["on%d%d" % (s, h)])
            if kind == "C":
                o_, otok = ob.next()
                c.op("act", lambda e: e.activation(out=o_[:, :w], in_=on[s][0][:, :w], func=AF.Copy), reads=["on%d0" % s], writes=[otok])
                c.dma("sp", otok, oT[s * 128:(s + 1) * 128, c0:c0 + w], o_[:, :w], reads=[otok], writes=["oT"])
            if kind == "B" and s == S - 1:
                for h in range(NH):
                    c.op("dve", lambda e: e.scalar_tensor_tensor(out=on[0][h][:, :w], in0=on[1][h][:, :w], scalar=lam[:, 3:4], in1=on[0][h][:, :w],
                                                                 op0=ALU.mult, op1=ALU.add), reads=["on1%d" % h, "on0%d" % h, "lam"], writes=["on0%d" % h])
                    c.op("act", lambda e: e.activation(out=sqb2[:, :w], in_=on[0][h][:, :w], func=AF.Square), reads=["on0%d" % h], writes=["sqb2"])
                    c.op("pe", lambda e: e.matmul(ps_ss2[:, :w], lhsT=ones[:, :], rhs=sqb2[:, :w], start=(h == 0), stop=(h == NH - 1)),
                         reads=["sqb2", "ones"], writes=["pSum"])
                emit_rstd(c, rstd2, ps_ss2, "pSum", w, n=256, tok="rstd2")
                for h in range(NH):
                    o_, otok = ob.next()
                    c.op("dve", lambda e: e.scalar_tensor_tensor(out=o_[:, :w], in0=on[0][h][:, :w], scalar=lam[:, 4 + h:5 + h], in1=rstd2[:, :w],
                                                                 op0=ALU.mult, op1=ALU.mult), reads=["on0%d" % h, "rstd2", "lam"], writes=[otok])
                    c.dma("sp", otok, oT[h * 128:(h + 1) * 128, c0:c0 + w], o_[:, :w], reads=[otok], writes=["oT"])
        c.wait_all("sp", ["oT"])
    return nc


def rope_tables(n):
    freqs = (10000.0 ** (-np.arange(0, 64, 2, dtype=np.float32) / 64)).astype(np.float32)
    t = np.arange(n)
    row = (t // 64).astype(np.float32)
    col = (t % 64).astype(np.float32)
    ang_r = row[:, None] * freqs
    ang_c = col[:, None] * freqs
    ang = np.concatenate([ang_r, ang_r, ang_c, ang_c], -1).astype(np.float32)
    cos = np.cos(ang).astype(np.float32)
    sin = np.sin(ang).astype(np.float32)
    sgn = np.ones(128, np.float32)
    sgn[0:32] = -1
    sgn[64:96] = -1
    return np.ascontiguousarray(np.stack([cos.T, (sin * sgn).T], 0))


def rope_perm():
    m = np.arange(128)
    partner = np.where((m // 32) % 2 == 0, m + 32, m - 32)
    rt = np.zeros((128, 128), np.float32)
    rt[partner, m] = 1.0
    return rt.astype(NPBF)


def build_gdn(NT, NCT=CTX):
    NCH = NT // 128
    CCH = NCT // 128
    nc = new_nc()
    qkvT = nc.dram_tensor("qkvT", [3, 128, NT], BF16, kind="ExternalInput").ap()
    ztm = nc.dram_tensor("ztm", [NT, 128], BF16, kind="ExternalInput").ap()
    gt = nc.dram_tensor("gt", [128, NCH, 4], BF16, kind="ExternalInput").ap()
    vec = nc.dram_tensor("vec", [128, 16], F32, kind="ExternalInput").ap()
    gbc = nc.dram_tensor("gbc", [128, 128], F32, kind="ExternalInput").ap()
    cst = nc.dram_tensor("cst", [6, 128, 128], F32, kind="ExternalInput").ap()
    otm = nc.dram_tensor("otm", [NT, 128], BF16, kind="ExternalOutput").ap()
    with ExitStack() as st:
        c = Ctx(nc, st)
        vt = c.sb("vec", [128, 16], F32)
        gb = c.sb("gbc", [128, 128], F32)
        C32 = c.sb("cst", [128, 6, 128], F32)
        Ib = c.sb("Ib", [128, 128], BF16)
        onesb = c.sb("onesb", [128, 128], BF16)
        qT = c.sb("qT", [128, NT], BF16)
        kT = c.sb("kT", [128, NT], BF16)
        ktm = c.sb("ktm", [128, NCH, 128], BF16)
        vtm = c.sb("vtm", [128, NCH, 128], BF16)
        of = c.sb("of", [128, NCH, 128], BF16)
        G = [c.sb("G%d" % d, [128, NCH], F32) for d in range(2)]
        BT = [c.sb("BT%d" % d, [128, NCH], F32) for d in range(2)]
        NB = [c.sb("NB%d" % d, [128, NCH], F32) for d in range(2)]
        Bc = [c.sb("Bc%d" % d, [128, NCH], F32) for d in range(2)]
        EB = [c.sb("EB%d" % d, [128, NCH], F32) for d in range(2)]
        BEB = [c.sb("BEB%d" % d, [128, NCH], F32) for d in range(2)]
        EKD = [c.sb("EKD%d" % d, [128, NCH], F32) for d in range(2)]
        CD = [c.sb("CD%d" % d, [128, NCH], F32) for d in range(2)]
        tri = c.sb("tri", [128, 2, 128], F32)
        zt = Rot(c, "zt", 2, [128, 128], BF16)
        ot = Rot(c, "ot", 2, [128, 128], BF16)
        banks = [c.ps("bank%d" % i, [128, 512]) for i in range(8)]

        def carve(b, i, n=1):
            return banks[b][:, 128 * i:128 * (i + n)]

        class RotAP:
            def __init__(self, aps, names):
                self.bufs, self.names, self.i = aps, names, 0

            def next(self):
                j = self.i % len(self.bufs)
                self.i += 1
                return self.bufs[j], self.names[j]

        pbig = banks[0]
        pG = banks[1]
        pT = RotAP([carve(1, 0), carve(1, 1)], ["bank1", "bank1"])

        class Chain:
            def __init__(self, d):
                n = "c%d" % d
                self.d = d
                self.oss = c.sb("oss" + n, [128, 4], F32)
                for nm in ("diagb", "nd", "dm", "dmS", "dmI", "Rm", "S32", "o2s", "osum", "osq", "sz"):
                    setattr(self, nm, c.sb(nm + n, [128, 128], F32))
                for nm in ("Rb", "qk", "kb", "vb", "Sb", "u_b"):
                    setattr(self, nm, c.sb(nm + n, [128, 128], BF16))
                self.Pm = Rot(c, "Pm" + n, 2, [128, 128], F32)
                self.Qm = Rot(c, "Qm" + n, 2, [128, 128], F32)
                self.qkT = Rot(c, "qkT" + n, 2, [128, 128], BF16)
                self.kd = Rot(c, "kd" + n, 2, [128, 128], BF16)
                self.ub = Rot(c, "ub" + n, 2, [128, 128], F32)
                self.wT = Rot(c, "wT" + n, 2, [128, 128], BF16)
                b0 = 4 * d
                self.bA, self.bB, self.bC, self.bD = ["bank%d" % (b0 + k) for k in range(4)]
                self.pA, self.pKK, self.pQK, self.pU = [carve(b0, k) for k in range(4)]
                self.pT = RotAP([carve(b0 + 1, 0), carve(b0 + 1, 1)], [self.bB, self.bB])
                self.pW = carve(b0 + 1, 2)
                self.pI = RotAP([carve(b0 + 2, k) for k in range(3)], [self.bC] * 3)
                self.pwS, self.pO1, self.pO2, self.pdS = [carve(b0 + 3, k) for k in range(4)]

            def t(self, nm):
                return "%sc%d" % (nm, self.d)

        c.push_scope()
        gtr = c.sb("gtr", [128, NCH, 4], BF16)
        gx = c.sb("gx", [128, NCH], F32)
        rb = c.sb("rb", [128, 3, 514], BF16)
        cv = c.sb("cv", [128, 514], F32)
        sv = c.sb("sv", [128, 514], F32)
        sqb = c.sb("sqb", [128, 512], BF16)
        rstd = c.sb("rstd", [128, 512], F32)
        vTb = c.sb("vTb", [128, 512], BF16)

        c.dma("sp", "vec", vt[:], vec, writes=["vec"])
        c.dma("sp", "gbc", gb[:], gbc, writes=["gbc"])
        c.dma("sp", "cst", C32[:], cst.rearrange("a p f -> p a f"), writes=["cst"])
        c.dma("sp", "gtr", gtr[:], gt, writes=["gtr"])
        I32 = C32[:, 0, :]
        ones32 = C32[:, 1, :]
        MS = [C32[:, 2, :], C32[:, 4, :]]
        MI = [C32[:, 3, :], C32[:, 5, :]]
        c.op("act", lambda e: e.activation(out=Ib[:, :], in_=C32[:, 0, :], func=AF.Copy), reads=["cst"], writes=["Ib"])
        c.op("act", lambda e: e.activation(out=onesb[:, :], in_=C32[:, 1, :], func=AF.Copy), reads=["cst"], writes=["onesb"])
        c.op("dve", lambda e: e.tensor_copy(out=tri[:, 0, :], in_=C32[:, 5, :]), reads=["cst"], writes=["tri"])
        c.op("dve", lambda e: e.tensor_copy(out=tri[:, 1, :], in_=C32[:, 3, :]), reads=["cst"], writes=["tri"])
        c.op("act", lambda e: e.activation(out=vt[:, 13:15], in_=vt[:, 0:2], func=AF.Exp), reads=["vec"], writes=["vec"])
        c.op("dve", lambda e: e.tensor_scalar(out=vt[:, 13:15], in0=vt[:, 13:15], scalar1=-1.0, scalar2=None, op0=ALU.mult), reads=["vec"], writes=["vec"])
        for d in range(2):
            c.op("act", lambda e: e.activation(out=gx[:, :], in_=gtr[:, :, d], func=AF.Exp, bias=vt[:, 2 + d:3 + d], scale=1.0), reads=["gtr", "vec"], writes=["gx"])
            c.op("act", lambda e: e.activation(out=gx[:, :], in_=gx[:, :], func=AF.Ln, bias=1.0, scale=1.0), reads=["gx"], writes=["gx"])
            c.op("dve", lambda e: e.tensor_scalar(out=G[d][:, :], in0=gx[:, :], scalar1=vt[:, 13 + d:14 + d], scalar2=None, op0=ALU.mult), reads=["gx", "vec"], writes=["G%d" % d])
            c.op("act", lambda e: e.activation(out=BT[d][:, :], in_=gtr[:, :, 2 + d], func=AF.Sigmoid), reads=["gtr"], writes=["BT%d" % d])
            c.op("dve", lambda e: e.tensor_scalar(out=NB[d][:, :], in0=BT[d][:, :], scalar1=-1.0, scalar2=None, op0=ALU.mult), reads=["BT%d" % d], writes=["NB%d" % d])
            c.op("pe", lambda e: e.matmul(pG[:, 0:NCH], lhsT=tri[:, d, :], rhs=G[d][:, :], start=True, stop=True), reads=["tri", "G%d" % d], writes=["bank1"])
            c.op("pe", lambda e: e.matmul(pG[:, 256:256 + NCH], lhsT=C32[:, 1, :], rhs=G[d][:, :], start=True, stop=True), reads=["cst", "G%d" % d], writes=["bank1"])
            c.op("dve", lambda e: e.tensor_copy(out=Bc[d][:, :], in_=pG[:, 0:NCH]), reads=["bank1"], writes=["Bc%d" % d])
            c.op("act", lambda e: e.activation(out=EB[d][:, :], in_=pG[:, 0:NCH], func=AF.Exp), reads=["bank1"], writes=["EB%d" % d])
            c.op("act", lambda e: e.activation(out=CD[d][:, :], in_=pG[:, 256:256 + NCH], func=AF.Exp), reads=["bank1"], writes=["CD%d" % d])
            c.op("dve", lambda e: e.tensor_tensor(out=EKD[d][:, :], in0=pG[:, 256:256 + NCH], in1=Bc[d][:, :], op=ALU.subtract), reads=["bank1", "Bc%d" % d], writes=["EKD%d" % d])
            c.op("act", lambda e: e.activation(out=EKD[d][:, :], in_=EKD[d][:, :], func=AF.Exp), reads=["EKD%d" % d], writes=["EKD%d" % d])
            c.op("dve", lambda e: e.tensor_tensor(out=BEB[d][:, :], in0=BT[d][:, :], in1=EB[d][:, :], op=ALU.mult), reads=["BT%d" % d, "EB%d" % d], writes=["BEB%d" % d])
        gtoks = ["Bc0", "Bc1", "EB0", "EB1", "CD0", "CD1", "EKD0", "EKD1", "BEB0", "BEB1", "BT0", "BT1", "NB0", "NB1"]

        segs = [(0, NCT)] + [(NCT + s0, w) for s0, w in tiles_of(NT - NCT, 512)]
        seq_lo = {0: 0}
        for (c0, w) in segs:
            lo_edge = (c0 == 0) or (c0 == NCT)
            hi_edge = (c0 + w == NCT) or (c0 + w == NT)
            a0 = c0 if lo_edge else c0 - 1
            a1 = c0 + w if hi_edge else c0 + w + 1
            if lo_edge:
                c.op("dve", lambda e: e.memset(rb[:, :, 0:1], 0.0), writes=["rb"])
            if hi_edge:
                c.op("dve", lambda e: e.memset(rb[:, :, w + 1:w + 2], 0.0), writes=["rb"])
            o0 = 1 if lo_edge else 0
            c.dma("sp", "rb", rb[:, :, o0:o0 + (a1 - a0)], qkvT[:, :, a0:a1].rearrange("a p t -> p a t"), writes=["rb"])
            for i in range(3):
                t0 = 4 + 3 * i
                c.op("dve", lambda e: e.tensor_scalar(out=cv[:, :w], in0=rb[:, i, 0:w], scalar1=vt[:, t0:t0 + 1], scalar2=None, op0=ALU.mult), reads=["rb", "vec"], writes=["cv"])
                c.op("dve", lambda e: e.scalar_tensor_tensor(out=cv[:, :w], in0=rb[:, i, 1:w + 1], scalar=vt[:, t0 + 1:t0 + 2], in1=cv[:, :w], op0=ALU.mult, op1=ALU.add),
                     reads=["rb", "vec", "cv"], writes=["cv"])
                c.op("dve", lambda e: e.scalar_tensor_tensor(out=cv[:, :w], in0=rb[:, i, 2:w + 2], scalar=vt[:, t0 + 2:t0 + 3], in1=cv[:, :w], op0=ALU.mult, op1=ALU.add),
                     reads=["rb", "vec", "cv"], writes=["cv"])
                if i == 2:
                    c.op("act", lambda e: e.activation(out=vTb[:, :w], in_=cv[:, :w], func=AF.Silu), reads=["cv"], writes=["vTb"])
                    for s in range(w // 128):
                        p_, ptok = pT.next()
                        c.op("pe", lambda e: e.matmul(p_[:, :], lhsT=vTb[:, s * 128:(s + 1) * 128], rhs=Ib[:, :], start=True, stop=True), reads=["vTb", "Ib"], writes=[ptok])
                        ch = (c0 + s * 128) // 128
                        c.op("act", lambda e: e.activation(out=vtm[:, ch, :], in_=p_[:, :], func=AF.Copy), reads=[ptok], writes=["vtm%d" % ch])
                    continue
                c.op("act", lambda e: e.activation(out=sv[:, :w], in_=cv[:, :w], func=AF.Silu), reads=["cv"], writes=["sv"])
                c.op("act", lambda e: e.activation(out=sqb[:, :w], in_=sv[:, :w], func=AF.Square), reads=["sv"], writes=["sqb"])
                c.op("pe", lambda e: e.matmul(pbig[:, :w], lhsT=onesb[:, :], rhs=sqb[:, :w], start=True, stop=True), reads=["sqb", "onesb"], writes=["bank0"])
                emit_rstd(c, rstd, pbig, "bank0", w, n=1)
                dst = qT if i == 0 else kT
                dtok = ("qT%d" if i == 0 else "kT%d") % c0
                sc = (128 ** -0.5) if i == 0 else 1.0
                c.op("dve", lambda e: e.scalar_tensor_tensor(out=dst[:, c0:c0 + w], in0=sv[:, :w], scalar=sc, in1=rstd[:, :w], op0=ALU.mult, op1=ALU.mult),
                     reads=["sv", "rstd"], writes=[dtok])
                if i == 1:
                    for s in range(w // 128):
                        p_, ptok = pT.next()
                        c.op("pe", lambda e: e.matmul(p_[:, :], lhsT=kT[:, c0 + s * 128:c0 + (s + 1) * 128], rhs=Ib[:, :], start=True, stop=True), reads=[dtok, "Ib"], writes=[ptok])
                        ch = (c0 + s * 128) // 128
                        c.op("dve", lambda e: e.tensor_copy(out=ktm[:, ch, :], in_=p_[:, :]), reads=[ptok], writes=["ktm%d" % ch])

        c.pop_scope()
        chains = [Chain(0), Chain(1)]

        def seg_of(ch):
            t = ch * 128
            if t < NCT:
                return 0
            return NCT + ((t - NCT) // 512) * 512

        def pre(ch, X):
            d = X.d
            col = slice(ch, ch + 1)
            qtok = "qT%d" % seg_of(ch)
            ktok = "kT%d" % seg_of(ch)
            cs = slice(ch * 128, (ch + 1) * 128)
            c.op("dve", lambda e: e.tensor_scalar(out=X.diagb[:, :], in0=C32[:, 0, :], scalar1=Bc[d][:, col], scalar2=None, op0=ALU.mult), reads=["cst"] + gtoks, writes=[X.t("diagb")])
            c.op("pe", lambda e: e.matmul(X.pKK, lhsT=kT[:, cs], rhs=kT[:, cs], start=True, stop=True), reads=[ktok], writes=[X.bA])
            c.op("pe", lambda e: e.matmul(X.pQK, lhsT=qT[:, cs], rhs=kT[:, cs], start=True, stop=True), reads=[qtok, ktok], writes=[X.bA])
            yield
            c.op("pe", lambda e: e.matmul(X.pA, lhsT=C32[:, 1, :], rhs=X.diagb[:, :], start=True, stop=True), reads=["cst", X.t("diagb")], writes=[X.bA])
            yield
            c.op("dve", lambda e: e.tensor_scalar(out=X.nd[:, :], in0=X.pA, scalar1=Bc[d][:, col], scalar2=0.0, op0=ALU.subtract, op1=ALU.max), reads=[X.bA] + gtoks, writes=[X.t("nd")])
            yield
            c.op("act", lambda e: e.activation(out=X.dm[:, :], in_=X.nd[:, :], func=AF.Exp, scale=-1.0), reads=[X.t("nd")], writes=[X.t("dm")])
            yield
            c.op("dve", lambda e: e.tensor_tensor(out=X.dmS[:, :], in0=X.dm[:, :], in1=MS[d], op=ALU.mult), reads=[X.t("dm"), "cst"], writes=[X.t("dmS")])
            c.op("dve", lambda e: e.tensor_tensor(out=X.dmI[:, :], in0=X.dm[:, :], in1=MI[d], op=ALU.mult), reads=[X.t("dm"), "cst"], writes=[X.t("dmI")])
            yield
            Q, Qtok = X.Qm.next()
            c.op("dve", lambda e: e.scalar_tensor_tensor(out=Q[:, :], in0=X.pKK, scalar=NB[d][:, col], in1=X.dmS[:, :], op0=ALU.mult, op1=ALU.mult),
                 reads=[X.bA, X.t("dmS")] + gtoks, writes=[Qtok])
            c.op("dve", lambda e: e.tensor_tensor(out=X.qk[:, :], in0=X.pQK, in1=X.dmI[:, :], op=ALU.mult), reads=[X.bA, X.t("dmI")], writes=[X.t("qk")])
            yield
            p_, ptok = X.pT.next()
            c.op("pe", lambda e: e.matmul(p_, lhsT=Q[:, :], rhs=C32[:, 0, :], start=True, stop=True), reads=[Qtok, "cst"], writes=[ptok])
            p2, p2tok = X.pT.next()
            c.op("pe", lambda e: e.matmul(p2, lhsT=X.qk[:, :], rhs=Ib[:, :], start=True, stop=True), reads=[X.t("qk"), "Ib"], writes=[p2tok])
            yield
            P, Ptok = X.Pm.next()
            c.op("act", lambda e: e.activation(out=P[:, :], in_=p_, func=AF.Copy), reads=[ptok], writes=[Ptok])
            c.op("dve", lambda e: e.tensor_tensor(out=X.Rm[:, :], in0=p_, in1=C32[:, 0, :], op=ALU.add), reads=[ptok, "cst"], writes=[X.t("Rm")])
            qkT_, qkTtok = X.qkT.next()
            c.op("act", lambda e: e.activation(out=qkT_[:, :], in_=p2, func=AF.Copy), reads=[p2tok], writes=[qkTtok])
            yield
            for step in range(6):
                last = step == 5
                pq, pqtok = X.pI.next()
                c.op("pe", lambda e: e.matmul(pq, lhsT=P[:, :], rhs=Q[:, :], start=True, stop=True), reads=[Ptok, Qtok], writes=[pqtok])
                if not last:
                    pp, pptok = X.pI.next()
                    c.op("pe", lambda e: e.matmul(pp, lhsT=Q[:, :], rhs=P[:, :], start=True, stop=True), reads=[Ptok, Qtok], writes=[pptok])
                yield
                Q2, Q2tok = X.Qm.next()
                c.op("dve", lambda e: e.tensor_copy(out=Q2[:, :], in_=pq), reads=[pqtok], writes=[Q2tok])
                if not last:
                    P2, P2tok = X.Pm.next()
                    c.op("act", lambda e: e.activation(out=P2[:, :], in_=pp, func=AF.Copy), reads=[pptok], writes=[P2tok])
                    P, Ptok = P2, P2tok
                Q, Qtok = Q2, Q2tok
                yield
                pr, prtok = X.pI.next()
                c.op("pe", lambda e: e.matmul(pr, lhsT=Q[:, :], rhs=X.Rm[:, :], start=True, stop=True), reads=[Qtok, X.t("Rm")], writes=[prtok])
                yield
                c.op("dve", lambda e: e.tensor_tensor(out=X.Rm[:, :], in0=pr, in1=X.Rm[:, :], op=ALU.add), reads=[prtok, X.t("Rm")], writes=[X.t("Rm")])
                yield
            c.op("act", lambda e: e.activation(out=X.Rb[:, :], in_=X.Rm[:, :], func=AF.Copy), reads=[X.t("Rm")], writes=[X.t("Rb")])
            kd_, kdtok = X.kd.next()
            c.op("pool", lambda e: e.tensor_scalar(out=X.kb[:, :], in0=ktm[:, ch, :], scalar1=BEB[d][:, col], scalar2=None, op0=ALU.mult), reads=["ktm%d" % ch] + gtoks, writes=[X.t("kb")])
            c.op("pool", lambda e: e.tensor_scalar(out=kd_[:, :], in0=ktm[:, ch, :], scalar1=EKD[d][:, col], scalar2=None, op0=ALU.mult), reads=["ktm%d" % ch] + gtoks, writes=[kdtok])
            c.op("pool", lambda e: e.tensor_scalar(out=X.vb[:, :], in0=vtm[:, ch, :], scalar1=BT[d][:, col], scalar2=None, op0=ALU.mult), reads=["vtm%d" % ch] + gtoks, writes=[X.t("vb")])
            yield
            c.op("pe", lambda e: e.matmul(X.pU, lhsT=X.Rb[:, :], rhs=X.vb[:, :], start=True, stop=True), reads=[X.t("Rb"), X.t("vb")], writes=[X.bA])
            c.op("pe", lambda e: e.matmul(X.pW, lhsT=X.kb[:, :], rhs=X.Rb[:, :], start=True, stop=True), reads=[X.t("Rb"), X.t("kb")], writes=[X.bB])
            yield
            ub_, ubtok = X.ub.next()
            wT_, wTtok = X.wT.next()
            c.op("act", lambda e: e.activation(out=ub_[:, :], in_=X.pU, func=AF.Copy), reads=[X.bA], writes=[ubtok])
            c.op("dve", lambda e: e.tensor_copy(out=wT_[:, :], in_=X.pW), reads=[X.bB], writes=[wTtok])
            X.slot_out = (qkT_, qkTtok, kd_, kdtok, ub_, ubtok, wT_, wTtok)
            yield

        def rec(ch, X, slot, fin):
            d = X.d
            qkT_, qkTtok, kd_, kdtok, ub_, ubtok, wT_, wTtok = slot
            col = slice(ch, ch + 1)
            cs = slice(ch * 128, (ch + 1) * 128)
            qtok = "qT%d" % seg_of(ch)
            c.op("pe", lambda e: e.matmul(X.pwS, lhsT=wT_[:, :], rhs=X.Sb[:, :], start=True, stop=True), reads=[wTtok, X.t("Sb")], writes=[X.bD])
            c.op("pe", lambda e: e.matmul(X.pO1, lhsT=qT[:, cs], rhs=X.Sb[:, :], start=True, stop=True), reads=[qtok, X.t("Sb")], writes=[X.bD])
            yield
            c.op("dve", lambda e: e.tensor_tensor(out=X.u_b[:, :], in0=ub_[:, :], in1=X.pwS, op=ALU.subtract), reads=[ubtok, X.bD], writes=[X.t("u_b")])
            yield
            c.op("pe", lambda e: e.matmul(X.pdS, lhsT=kd_[:, :], rhs=X.u_b[:, :], start=True, stop=True), reads=[kdtok, X.t("u_b")], writes=[X.bD])
            c.op("pe", lambda e: e.matmul(X.pO2, lhsT=qkT_[:, :], rhs=X.u_b[:, :], start=True, stop=True), reads=[qkTtok, X.t("u_b")], writes=[X.bD])
            yield
            c.op("dve", lambda e: e.scalar_tensor_tensor(out=X.S32[:, :], in0=X.S32[:, :], scalar=CD[d][:, col], in1=X.pdS, op0=ALU.mult, op1=ALU.add),
                 reads=[X.t("S32"), X.bD] + gtoks, writes=[X.t("S32")])
            yield
            c.op("act", lambda e: e.activation(out=X.Sb[:, :], in_=X.S32[:, :], func=AF.Copy), reads=[X.t("S32")], writes=[X.t("Sb")])
            c.op("act", lambda e: e.activation(out=X.o2s[:, :], in_=X.pO2, func=AF.Copy), reads=[X.bD], writes=[X.t("o2s")])
            yield
            if not fin:
                c.op("dve", lambda e: e.scalar_tensor_tensor(out=of[:, ch, :], in0=X.pO1, scalar=EB[d][:, col], in1=X.o2s[:, :], op0=ALU.mult, op1=ALU.add),
                     reads=[X.bD, X.t("o2s")] + gtoks, writes=["of%d" % ch])
                yield
                return
            c.op("dve", lambda e: e.scalar_tensor_tensor(out=X.osum[:, :], in0=X.pO1, scalar=EB[d][:, col], in1=X.o2s[:, :], op0=ALU.mult, op1=ALU.add),
                 reads=[X.bD, X.t("o2s")] + gtoks, writes=[X.t("osum")])
            yield
            c.op("dve", lambda e: e.tensor_tensor(out=X.osum[:, :], in0=X.osum[:, :], in1=of[:, ch, :], op=ALU.add), reads=[X.t("osum"), "of%d" % ch], writes=[X.t("osum")])
            yield
            c.op("act", lambda e: e.activation(out=X.osq[:, :], in_=X.osum[:, :], func=AF.Square, accum_out=X.oss[:, 0:1]), reads=[X.t("osum")], writes=[X.t("osq"), X.t("oss")])
            yield
            emit_rstd(c, X.oss, X.oss, X.t("oss"), 1, n=128, tok=X.t("oss"))
            z_, ztok = zt.next()
            c.dma("sp", ztok, z_[:, :], ztm[ch * 128:(ch + 1) * 128, :], writes=[ztok])
            c.op("act", lambda e: e.activation(out=X.sz[:, :], in_=z_[:, :], func=AF.Silu), reads=[ztok], writes=[X.t("sz")])
            yield
            c.op("dve", lambda e: e.scalar_tensor_tensor(out=X.osum[:, :], in0=X.osum[:, :], scalar=X.oss[:, 0:1], in1=gb[:, :], op0=ALU.mult, op1=ALU.mult),
                 reads=[X.t("osum"), X.t("oss"), "gbc"], writes=[X.t("osum")])
            o_, otok = ot.next()
            c.op("dve", lambda e: e.tensor_tensor(out=o_[:, :], in0=X.osum[:, :], in1=X.sz[:, :], op=ALU.mult), reads=[X.t("osum"), X.t("sz")], writes=[otok])
            c.dma("sp", otok, otm[ch * 128:(ch + 1) * 128, :], o_[:, :], reads=[otok], writes=["otm"])
            yield

        def drive(gens):
            gens = [g_ for g_ in gens if g_ is not None]
            while gens:
                alive = []
                for g_ in gens:
                    try:
                        next(g_)
                        alive.append(g_)
                    except StopIteration:
                        pass
                gens = alive

        orders = [list(range(NCH)), list(range(CCH - 1, -1, -1)) + list(range(NCH - 1, CCH - 1, -1))]
        pos = [{ch: i for i, ch in enumerate(o)} for o in orders]
        for X in chains:
            c.op("dve", lambda e: e.memset(X.S32[:, :], 0.0), writes=[X.t("S32")])
            c.op("dve", lambda e: e.memset(X.Sb[:, :], 0.0), writes=[X.t("Sb")])
        drive([pre(orders[d][0], chains[d]) for d in range(2)])
        slots = [chains[d].slot_out for d in range(2)]
        for i in range(NCH):
            gens = []
            for d in range(2):
                if i + 1 < NCH:
                    gens.append(pre(orders[d][i + 1], chains[d]))
            for d in range(2):
                ch = orders[d][i]
                gens.append(rec(ch, chains[d], slots[d], pos[d][ch] > pos[1 - d][ch]))
            drive(gens)
            if i + 1 < NCH:
                slots = [chains[d].slot_out for d in range(2)]
        c.wait_all("sp", ["otm"])
    return nc


def gdn_consts():
    i = np.arange(128)
    I = np.eye(128, dtype=np.float32)
    ones = np.ones((128, 128), np.float32)
    LS = (i[:, None] > i[None, :]).astype(np.float32)
    LI = (i[:, None] >= i[None, :]).astype(np.float32)
    return np.ascontiguousarray(np.stack([I, ones, LS, LI, LS.T, LI.T], 0))


def fm(v):
    return np.ascontiguousarray(np.asarray(v, np.float32).reshape(KC, 128).T)


def lambda_init(layer):
    return 0.8 - 0.6 * math.exp(-0.3 * layer)


PRE_TILES = [(0, 32, 1)] + [(32 + i * 512, 512, 0) for i in range(4)]
POST_LAT = [(258 + off, w + 2, 0, "vl" if off == 0 else "one", "vr" if off == 2040 else "one")
            for off, w in ((0, 510), (510, 510), (1020, 510), (1530, 510), (2040, 8))]
POST_TILES = [(0, 258, 1, "zero", "zero")] + POST_LAT
TL = SEQ // NCORE


def kernel(x, c, ctx, c_ctx, w_mod, b_mod, norm1_g, norm2_g, w_in_even, a_conv_w, a_A_log, a_dt_bias,
           a_norm_g, b_lambda, b_norm_g, w_out_even, w_in_odd, c_q_norm, c_k_norm, w_out_odd,
           ffn_up, ffn_conv_w, ffn_conv_b, ffn_down, final_g):
    f32 = np.float32
    x = np.asarray(x, f32)
    ctx = np.asarray(ctx, f32)
    progs = {}

    def prog(key, fn):
        if key not in progs:
            progs[key] = fn()
        return progs[key]

    c2 = np.ascontiguousarray(np.stack([fm(np.asarray(c, f32)[0]), fm(np.asarray(c_ctx, f32))], -1))
    w_mod = np.asarray(w_mod, f32)
    b_mod = np.asarray(b_mod, f32)
    maps = []
    for j in range(NCORE):
        sl = slice(j * MODC, (j + 1) * MODC)
        bm = np.ascontiguousarray(np.broadcast_to(b_mod[None, :, sl], (2, DEPTH, MODC)))
        maps.append({"c2": c2, "wm": np.ascontiguousarray(w_mod[:, :, sl]), "bm": bm})
    res = run(prog("mod", build_mod), maps)
    mod = np.concatenate([r["out"] for r in res], -1)

    def modv(s, l, m):
        return mod[s, l, m * D:(m + 1) * D]

    xlT = np.ascontiguousarray(x[0].T)
    xcT = np.ascontiguousarray(ctx[0].T)
    rope = rope_tables(SEQ)
    rt = rope_perm()
    zcol = np.zeros((D, 1), f32)

    for l in range(DEPTH):
        even = l % 2 == 0
        e = l // 2
        last = l == DEPTH - 1
        w_in = np.asarray(w_in_even[e] if even else w_in_odd[e], f32)
        ncols = w_in.shape[1]
        vec = np.ascontiguousarray(np.stack([fm(norm1_g[l]), fm(modv(0, l, 0)), fm(modv(0, l, 1)), fm(modv(1, l, 0)), fm(modv(1, l, 1))], 1))
        maps = [{"xT": np.ascontiguousarray(np.concatenate([xcT[:, 32 * j:32 * (j + 1)], xlT[:, TL * j:TL * (j + 1)]], 1)), "vec": vec, "w": w_in}
                for j in range(NCORE)]
        res = run(prog(("pre", ncols), lambda: build_pre(ncols, PRE_TILES)), maps)
        pc = np.concatenate([r["pT"][:, :32] for r in res], 1)
        pl = np.concatenate([r["pT"][:, 32:] for r in res], 1)
        p = np.concatenate([pc, pl], 1)
        del res, maps
        mT = np.zeros((D, CTX + SEQ), NPBF)
        if even:
            NT = CTX + SEQ
            cw = np.asarray(a_conv_w[e], f32)
            maps = []
            for j in range(NCORE):
                hs = slice(j * 128, (j + 1) * 128)
                qkvT = np.ascontiguousarray(np.stack([p[j * 128:(j + 1) * 128], p[1024 + j * 128:1024 + (j + 1) * 128], p[2048 + j * 128:2048 + (j + 1) * 128]], 0))
                ztm = np.ascontiguousarray(p[3072 + j * 128:3072 + (j + 1) * 128].T)
                gates = np.stack([p[4096 + gi * 8 + j] for gi in range(4)], 0)
                gt = np.ascontiguousarray(gates.reshape(4, NT // 128, 128).transpose(2, 1, 0))
                vec = np.zeros((128, 16), f32)
                vec[:, 0:2] = np.asarray(a_A_log[e], f32)[:, j]
                vec[:, 2:4] = np.asarray(a_dt_bias[e], f32)[:, j]
                for i, off in enumerate((0, 1024, 2048)):
                    for t in range(3):
                        vec[:, 4 + 3 * i + t] = cw[t, off + j * 128:off + (j + 1) * 128]
                gbc = np.ascontiguousarray(np.broadcast_to(np.asarray(a_norm_g[e], f32), (128, 128)))
                maps.append({"qkvT": qkvT, "ztm": ztm, "gt": gt, "vec": vec, "gbc": gbc, "cst": gdn_consts()})
            res = run(prog("gdn", lambda: build_gdn(NT)), maps)
            for j in range(NCORE):
                mT[j * 128:(j + 1) * 128, :] = res[j]["otm"].T
            del res, maps
            HQ = SEQ // 2
            li = lambda_init(l)
            lp = np.ascontiguousarray(np.broadcast_to(np.asarray(b_lambda[e], f32), (128, 4, 128)))
            bng = np.asarray(b_norm_g[e], f32)
            maps = []
            for j in range(NCORE):
                hb, half = j // 2, j % 2
                qrows = [slice(A_IN + hb * 256 + m * 128, A_IN + hb * 256 + (m + 1) * 128) for m in range(2)]
                krows = [slice(A_IN + 1024 + hb * 256 + m * 128, A_IN + 1024 + hb * 256 + (m + 1) * 128) for m in range(2)]
                qT = np.ascontiguousarray(np.stack([np.concatenate([p[r, :CTX], p[r, CTX + half * HQ:CTX + (half + 1) * HQ]], 1) for r in qrows], 0))
                kT = np.ascontiguousarray(np.stack([p[r] for r in krows], 0))
                v = np.ascontiguousarray(p[A_IN + 2048 + hb * 256:A_IN + 2048 + (hb + 1) * 256].T)
                vec = np.zeros((128, 8), f32)
                vec[:, 2] = li
                vec[:, 3] = 1.0 - li
                vec[:, 4] = bng[:128]
                vec[:, 5] = bng[128:]
                maps.append({"qT": qT, "kT": kT, "v": v, "cq": np.ascontiguousarray(rope[:, :, half * HQ:(half + 1) * HQ]), "ck": rope, "rt": rt, "vec": vec, "lp": lp})
            res = run(prog("attB", lambda: build_att("B", HQ, SEQ)), maps)
            for j in range(NCORE):
                hb, half = j // 2, j % 2
                rows = slice(1024 + hb * 256, 1024 + (hb + 1) * 256)
                if half == 0:
                    mT[rows, :CTX] = res[j]["oT"][:, :CTX]
                mT[rows, CTX + half * HQ:CTX + (half + 1) * HQ] = res[j]["oT"][:, CTX:]
            del res, maps
        else:
            lp0 = np.zeros((128, 4, 128), f32)
            maps = []
            for j in range(NCORE):
                g = j // 2
                qT = np.ascontiguousarray(np.stack([p[(2 * j + s) * 128:(2 * j + s + 1) * 128] for s in range(2)], 0))
                kT = np.ascontiguousarray(p[2048 + g * 128:2048 + (g + 1) * 128][None])
                v = np.ascontiguousarray(p[2560 + g * 128:2560 + (g + 1) * 128].T)
                vec = np.zeros((128, 8), f32)
                vec[:, 0] = np.asarray(c_q_norm[e], f32)
                vec[:, 1] = np.asarray(c_k_norm[e], f32)
                maps.append({"qT": qT, "kT": kT, "v": v, "cq": rope, "ck": rope, "rt": rt, "vec": vec, "lp": lp0})
            res = run(prog("attC", lambda: build_att("C", SEQ, SEQ)), maps)
            for j in range(NCORE):
                mT[2 * j * 128:(2 * j + 2) * 128, :] = res[j]["oT"]
            del res, maps
        del p
        vec = np.zeros((128, POST_NV), f32)
        V = POST_V
        vec[:, V["n2g"]:V["n2g"] + 16] = fm(norm2_g[l])
        for nm, s, m in (("g1", 0, 2), ("sh2", 0, 3), ("sc2", 0, 4), ("g2", 0, 5), ("cg1", 1, 2), ("csh2", 1, 3), ("csc2", 1, 4), ("cg2", 1, 5)):
            vec[:, V[nm]:V[nm] + 16] = fm(modv(s, l, m))
        vec[:, V["fg"]:V["fg"] + 16] = fm(final_g)
        vec[:, V["cw"]:V["cw"] + 3 * 88] = np.asarray(ffn_conv_w[l], f32).reshape(3, 88, 128).transpose(2, 0, 1).reshape(128, 264)
        vec[:, V["cb"]:V["cb"] + 88] = np.asarray(ffn_conv_b[l], f32).reshape(88, 128).T
        wo = np.asarray(w_out_even[e] if even else w_out_odd[e], f32)
        wu = np.asarray(ffn_up[l], f32)
        wd = np.asarray(ffn_down[l], f32)
        mcT, mlT = mT[:, :CTX], mT[:, CTX:]
        zb = np.zeros((D, 1), NPBF)
        maps = []
        for j in range(NCORE):
            lo, hi = TL * j, TL * (j + 1)
            xl_ = [zcol if j == 0 else xlT[:, lo - 1:lo], xlT[:, lo:hi], zcol if j == NCORE - 1 else xlT[:, hi:hi + 1]]
            ml_ = [zb if j == 0 else mlT[:, lo - 1:lo], mlT[:, lo:hi], zb if j == NCORE - 1 else mlT[:, hi:hi + 1]]
            vj = vec.copy()
            vj[:, V["vl"]] = 0.0 if j == 0 else 1.0
            vj[:, V["vr"]] = 0.0 if j == NCORE - 1 else 1.0
            maps.append({"xT": np.ascontiguousarray(np.concatenate([zcol, xcT, zcol] + xl_, 1)),
                         "mT": np.ascontiguousarray(np.concatenate([zb, mcT, zb] + ml_, 1)), "vec": vj, "wo": wo, "wu": wu, "wd": wd})
        if not last:
            res = run(prog("post", lambda: build_post(POST_TILES, False)), maps)
            xcT = np.ascontiguousarray(res[0]["oT"][:, :CTX])
            xlT = np.ascontiguousarray(np.concatenate([r["oT"][:, CTX:] for r in res], 1))
        else:
            res = run(prog("postf", lambda: build_post(POST_LAT, True)), maps)
            xlT = np.concatenate([r["oT"] for r in res], 1)
        del res, maps, mT
    return np.ascontiguousarray(xlT.T)[None].astype(np.float32)
```

```python
import math
from contextlib import ExitStack

import ml_dtypes
import numpy as np
import concourse.bass as bass
import concourse.mybir as mybir
from concourse.bass_utils import run_bass_kernel_spmd

F32 = mybir.dt.float32
BF16 = mybir.dt.bfloat16
AF = mybir.ActivationFunctionType
ALU = mybir.AluOpType
AX = mybir.AxisListType
NPBF = ml_dtypes.bfloat16

D = 2048
KC = 16
NCORE = 8
SEQ = 16384
CTX = 256
DEPTH = 4
EPS = 1e-6
DFF = 5632
FC = 44
A_IN = 4128
EVEN_IN = 7200
ODD_IN = 3072


class Ctx:
    def __init__(self, nc, stack):
        self.nc = nc
        self.stack = stack
        self.E = {"pe": nc.tensor, "act": nc.scalar, "dve": nc.vector, "pool": nc.gpsimd, "sp": nc.sync}
        self.sems = {}
        self.cnt = {}
        self.known = {e: {} for e in self.E}
        self.lastw = {}
        self.readers = {}
        self.ninst = 0

    def sb(self, name, shape, dt):
        return self.stack.enter_context(self.nc.sbuf_tensor("sb_" + name, list(shape), dt))

    def ps(self, name, shape, dt=F32):
        return self.stack.enter_context(self.nc.psum_tensor("ps_" + name, list(shape), dt))

    def sem(self, key):
        if key not in self.sems:
            self.sems[key] = self.stack.enter_context(self.nc.semaphore("s_" + key.replace(":", "_")))
            self.cnt[key] = 0
        return self.sems[key]

    def _waits(self, eng, reads, writes):
        need = {}

        def add(ev):
            if ev is not None and need.get(ev[0], 0) < ev[1]:
                need[ev[0]] = ev[1]

        for t in reads:
            add(self.lastw.get(t))
        for t in writes:
            add(self.lastw.get(t))
            for k, v in self.readers.get(t, {}).items():
                add((k, v))
        E = self.E[eng]
        for k, v in need.items():
            if k == "pe" and eng == "pe":
                continue
            if k.startswith("d:"):
                v = self.cnt[k]
            if self.known[eng].get(k, 0) < v:
                E.wait_ge(self.sems[k], v)
                self.known[eng][k] = v
                self.ninst += 1

    def _record(self, ev, reads, writes):
        k, v = ev
        for t in reads:
            d = self.readers.setdefault(t, {})
            if d.get(k, 0) < v:
                d[k] = v
        for t in writes:
            self.lastw[t] = ev
            self.readers[t] = {}

    def op(self, eng, emit, reads=(), writes=()):
        ex = [t for t in reads if t.startswith("bank")]
        if ex:
            writes = list(writes) + ex
        self._waits(eng, reads, writes)
        s = self.sem(eng)
        ins = emit(self.E[eng])
        ins.then_inc(s, 1)
        self.cnt[eng] += 1
        self.ninst += 1
        self._record((eng, self.cnt[eng]), reads, writes)
        return ins

    def dma(self, eng, stream, out, in_, reads=(), writes=()):
        key = "d:" + stream
        s = self.sem(key)
        self._waits(eng, reads, writes)
        ins = self.E[eng].dma_start(out=out, in_=in_)
        ins.then_inc(s, 16)
        self.cnt[key] += 16
        self.ninst += 1
        self._record((key, self.cnt[key]), reads, writes)
        return ins

    def push_scope(self):
        self._outer = self.stack
        self.stack = ExitStack()

    def pop_scope(self):
        self.barrier()
        self.stack.close()
        self.stack = self._outer

    def barrier(self):
        for eng, E in self.E.items():
            for k, s_ in self.sems.items():
                v = self.cnt[k]
                if v > 0 and self.known[eng].get(k, 0) < v and not (k == eng):
                    E.wait_ge(s_, v)
                    self.known[eng][k] = v
                    self.ninst += 1

    def wait_all(self, eng, tokens):
        self._waits(eng, tokens, ())


class Rot:
    def __init__(self, c, name, n, shape, dt, psum=False):
        self.bufs = [(c.ps if psum else c.sb)("%s%d" % (name, i), shape, dt) for i in range(n)]
        self.names = ["%s%d" % (name, i) for i in range(n)]
        self.i = 0

    def next(self):
        j = self.i % len(self.bufs)
        self.i += 1
        return self.bufs[j], self.names[j]


def new_nc():
    return bass.Bass("TRN2", target_bir_lowering=False)


def run(nc, in_maps):
    res = run_bass_kernel_spmd(nc, in_maps, core_ids=list(range(NCORE)))
    return res.results


def emit_rstd(c, rstd, ss, sstok, W, n=D, tok="rstd"):
    c.op("dve", lambda e: e.tensor_scalar(out=rstd[:, :W], in0=ss[:, :W], scalar1=1.0 / n, scalar2=EPS, op0=ALU.mult, op1=ALU.add),
         reads=[sstok], writes=[tok])
    c.op("act", lambda e: e.activation(out=rstd[:, :W], in_=rstd[:, :W], func=AF.Sqrt), reads=[tok], writes=[tok])
    c.op("dve", lambda e: e.reciprocal(out=rstd[:, :W], in_=rstd[:, :W]), reads=[tok], writes=[tok])


def emit_norm_mod(c, K, xt, xtok, W, a_ap, b_ap, h, htok, maskcols=()):
    ss, sstok = K["ps_ss"].next()
    for kc in range(KC):
        sq, sqtok = K["sq"].next()
        c.op("act", lambda e: e.activation(out=sq[:, :W], in_=xt[:, kc, :W], func=AF.Square), reads=[xtok], writes=[sqtok])
        c.op("pe", lambda e: e.matmul(ss[:, :W], lhsT=K["ones"][:, :], rhs=sq[:, :W], start=(kc == 0), stop=(kc == KC - 1)),
             reads=[sqtok, "ones"], writes=[sstok])
    rstd = K["rstd"]
    emit_rstd(c, rstd, ss, sstok, W)
    for kc in range(KC):
        tmp, tmptok = K["tmp"].next()
        c.op("dve", lambda e: e.scalar_tensor_tensor(out=tmp[:, :W], in0=xt[:, kc, :W], scalar=a_ap[:, kc:kc + 1], in1=rstd[:, :W],
                                                     op0=ALU.mult, op1=ALU.mult), reads=[xtok, "rstd", "vec"], writes=[tmptok])
        c.op("act", lambda e: e.activation(out=h[:, kc, :W], in_=tmp[:, :W], func=AF.Identity, bias=b_ap[:, kc:kc + 1], scale=1.0),
             reads=[tmptok, "vec"], writes=[htok])
    for col, sc in maskcols:
        c.op("dve", lambda e: e.tensor_scalar(out=h[:, :, col:col + 1], in0=h[:, :, col:col + 1], scalar1=sc, scalar2=None, op0=ALU.mult),
             reads=[htok, "vec"], writes=[htok])


def make_consts(c):
    K = {}
    K["ones"] = c.sb("ones", [128, 128], BF16)
    c.op("dve", lambda e: e.memset(K["ones"][:, :], 1.0), writes=["ones"])
    K["sq"] = Rot(c, "sq", 2, [128, 512], BF16)
    K["tmp"] = Rot(c, "tmp", 2, [128, 512], F32)
    K["rstd"] = c.sb("rstd", [128, 512], F32)
    K["ps_ss"] = Rot(c, "ps_ss", 1, [128, 512], F32, psum=True)
    return K


MODC = 1536


def build_mod():
    nc = new_nc()
    c2 = nc.dram_tensor("c2", [128, KC, 2], F32, kind="ExternalInput").ap()
    wm = nc.dram_tensor("wm", [DEPTH, MODC // 512, 128, KC, 512], F32, kind="ExternalInput").ap()
    bm = nc.dram_tensor("bm", [2, DEPTH, MODC], F32, kind="ExternalInput").ap()
    out = nc.dram_tensor("out", [2, DEPTH, MODC], F32, kind="ExternalOutput").ap()
    with ExitStack() as st:
        c = Ctx(nc, st)
        ct = c.sb("ct", [128, KC, 2], F32)
        cs = c.sb("cs", [128, KC, 2], BF16)
        bt = c.sb("bt", [2, DEPTH, MODC], F32)
        ot = c.sb("ot", [2, DEPTH, MODC], F32)
        wrot = Rot(c, "wt", 2, [128, KC, 512], BF16)
        prot = Rot(c, "pm", 2, [2, 512], F32, psum=True)
        c.dma("sp", "c2", ct[:], c2, writes=["ct"])
        c.dma("sp", "bm", bt[:], bm, writes=["bt"])
        c.op("act", lambda e: e.activation(out=cs[:], in_=ct[:], func=AF.Silu), reads=["ct"], writes=["cs"])
        for l in range(DEPTH):
            for n in range(MODC // 512):
                wt, wtok = wrot.next()
                c.dma("pool", wtok, wt[:], wm[l, n], writes=[wtok])
                ps, ptok = prot.next()
                for kc in range(KC):
                    c.op("pe", lambda e: e.matmul(ps[:, :], lhsT=cs[:, kc, :], rhs=wt[:, kc, :], start=(kc == 0), stop=(kc == KC - 1)),
                         reads=["cs", wtok], writes=[ptok])
                c.op("dve", lambda e: e.tensor_tensor(out=ot[:, l, n * 512:(n + 1) * 512], in0=ps[:, :], in1=bt[:, l, n * 512:(n + 1) * 512], op=ALU.add),
                     reads=[ptok, "bt"], writes=["ot"])
        c.dma("sp", "out", out, ot[:], reads=["ot"], writes=["out"])
        c.wait_all("sp", ["out"])
    return nc


def tiles_of(total, w):
    return [(s, min(w, total - s)) for s in range(0, total, w)]


def build_pre(ncols, tiles):
    T = sum(w for _, w, _ in tiles)
    nc = new_nc()
    xT = nc.dram_tensor("xT", [D, T], F32, kind="ExternalInput").ap()
    vec = nc.dram_tensor("vec", [128, 5, KC], F32, kind="ExternalInput").ap()
    w = nc.dram_tensor("w", [(ncols + 127) // 128, 128, KC, 128], F32, kind="ExternalInput").ap()
    pT = nc.dram_tensor("pT", [ncols, T], BF16, kind="ExternalOutput").ap()
    with ExitStack() as st:
        c = Ctx(nc, st)
        K = make_consts(c)
        vt = c.sb("vec", [128, 5, KC], F32)
        av = c.sb("av", [128, 2, KC], F32)
        h = c.sb("h", [128, KC, T], BF16)
        xrot = Rot(c, "xt", 2, [128, KC, 512], F32)
        wrot = Rot(c, "wt", 2, [128, KC, 128], BF16)
        prot = Rot(c, "pp", 3, [128, 512], F32, psum=True)
        orot = Rot(c, "po", 3, [128, 512], BF16)
        c.dma("sp", "vec", vt[:], vec, writes=["vec"])
        for s in range(2):
            c.op("dve", lambda e: e.scalar_tensor_tensor(out=av[:, s, :], in0=vt[:, 2 + 2 * s, :], scalar=1.0, in1=vt[:, 0, :],
                                                         op0=ALU.add, op1=ALU.mult), reads=["vec"], writes=["vec"])
        for (s0, wd, stream) in tiles:
            xt, xtok = xrot.next()
            c.dma("sp", xtok, xt[:, :, :wd], xT[:, s0:s0 + wd].rearrange("(kc p) t -> p kc t", p=128), writes=[xtok])
            emit_norm_mod(c, K, xt, xtok, wd, av[:, stream, :], vt[:, 1 + 2 * stream, :], h[:, :, s0:s0 + wd], "h%d" % s0)
        nst = 0
        for cb0 in range(0, ncols, 128):
            m = min(128, ncols - cb0)
            wt, wtok = wrot.next()
            c.dma("pool", wtok, wt[:], w[cb0 // 128], writes=[wtok])
            for (s0, wd, stream) in tiles:
                ps, ptok = prot.next()
                for kc in range(KC):
                    c.op("pe", lambda e: e.matmul(ps[:m, :wd], lhsT=wt[:, kc, :m], rhs=h[:, kc, s0:s0 + wd], start=(kc == 0), stop=(kc == KC - 1)),
                         reads=[wtok, "h%d" % s0], writes=[ptok])
                ot, otok = orot.next()
                eng = "act" if nst % 2 == 0 else "dve"
                if eng == "act":
                    c.op("act", lambda e: e.activation(out=ot[:m, :wd], in_=ps[:m, :wd], func=AF.Copy), reads=[ptok], writes=[otok])
                else:
                    c.op("dve", lambda e: e.tensor_copy(out=ot[:m, :wd], in_=ps[:m, :wd]), reads=[ptok], writes=[otok])
                nst += 1
                c.dma("sp", otok, pT[cb0:cb0 + m, s0:s0 + wd], ot[:m, :wd], reads=[otok], writes=["pT"])
        c.wait_all("sp", ["pT"])
    return nc


POST_V = {"n2g": 0, "g1": 16, "sh2": 32, "sc2": 48, "g2": 64, "cg1": 80, "csh2": 96, "csc2": 112, "cg2": 128, "fg": 144,
          "cw": 160, "cb": 160 + 3 * 88, "vl": 160 + 4 * 88, "vr": 161 + 4 * 88, "zero": 162 + 4 * 88}
POST_NV = 163 + 4 * 88


def build_post(tiles, final):
    T = max(s + w for s, w, _, _, _ in tiles)
    TO = sum(w - 2 for _, w, _, _, _ in tiles)
    nc = new_nc()
    xT = nc.dram_tensor("xT", [D, T], F32, kind="ExternalInput").ap()
    mT = nc.dram_tensor("mT", [D, T], BF16, kind="ExternalInput").ap()
    vec = nc.dram_tensor("vec", [128, POST_NV], F32, kind="ExternalInput").ap()
    wo = nc.dram_tensor("wo", [KC, 128, KC, 128], F32, kind="ExternalInput").ap()
    wu = nc.dram_tensor("wu", [2 * FC, 128, KC, 128], F32, kind="ExternalInput").ap()
    wdn = nc.dram_tensor("wd", [KC, 128, FC, 128], F32, kind="ExternalInput").ap()
    oT = nc.dram_tensor("oT", [D, TO], F32, kind="ExternalOutput").ap()
    with ExitStack() as st:
        c = Ctx(nc, st)
        K = make_consts(c)
        vt = c.sb("vec", [128, POST_NV], F32)
        av = c.sb("av", [128, 2, KC], F32)
        xrot = Rot(c, "xt", 1, [128, KC, 512], F32)
        mrot = Rot(c, "mt", 1, [128, KC, 512], BF16)
        h2 = c.sb("h2", [128, KC, 512], BF16)
        act = c.sb("actb", [128, FC, 512], BF16)
        worot = Rot(c, "wo", 2, [128, KC, 128], BF16)
        wgrot = Rot(c, "wg", 2, [128, KC, 128], BF16)
        wvrot = Rot(c, "wv", 2, [128, KC, 128], BF16)
        wdrot = Rot(c, "wdn", 2, [128, FC, 128], BF16)
        prot = Rot(c, "pp", 2, [128, 512], F32, psum=True)
        pgrot = Rot(c, "pg", 2, [128, 512], F32, psum=True)
        pvrot = Rot(c, "pv", 2, [128, 512], F32, psum=True)
        cg = Rot(c, "cg", 2, [128, 512], F32)
        cv = Rot(c, "cv", 2, [128, 512], F32)
        sg = Rot(c, "sg", 2, [128, 512], F32)
        yt = Rot(c, "yt", 2, [128, 512], F32)
        c.dma("sp", "vec", vt[:], vec, writes=["vec"])
        V = POST_V
        for s, (scn, gn) in enumerate((("sc2", "n2g"), ("csc2", "n2g"))):
            c.op("dve", lambda e: e.scalar_tensor_tensor(out=av[:, s, :], in0=vt[:, V[scn]:V[scn] + 16], scalar=1.0, in1=vt[:, V[gn]:V[gn] + 16],
                                                         op0=ALU.add, op1=ALU.mult), reads=["vec"], writes=["vec"])
        ocol = 0
        for (s0, wd, stream, lf, rf) in tiles:
            g1 = V["cg1"] if stream else V["g1"]
            g2 = V["cg2"] if stream else V["g2"]
            sh2 = V["csh2"] if stream else V["sh2"]
            wi = wd - 2
            xt, xtok = xrot.next()
            mt, mtok = mrot.next()
            c.dma("sp", xtok, xt[:, :, :wd], xT[:, s0:s0 + wd].rearrange("(kc p) t -> p kc t", p=128), writes=[xtok])
            c.dma("sp", mtok, mt[:, :, :wd], mT[:, s0:s0 + wd].rearrange("(kc p) t -> p kc t", p=128), writes=[mtok])
            for oc in range(KC):
                wt, wtok = worot.next()
                c.dma("pool", wtok, wt[:], wo[oc], writes=[wtok])
                ps, ptok = prot.next()
                for kc in range(KC):
                    c.op("pe", lambda e: e.matmul(ps[:, :wd], lhsT=wt[:, kc, :], rhs=mt[:, kc, :wd], start=(kc == 0), stop=(kc == KC - 1)),
                         reads=[wtok, mtok], writes=[ptok])
                c.op("dve", lambda e: e.scalar_tensor_tensor(out=xt[:, oc, :wd], in0=ps[:, :wd], scalar=vt[:, g1 + oc:g1 + oc + 1], in1=xt[:, oc, :wd],
                                                             op0=ALU.mult, op1=ALU.add), reads=[ptok, xtok, "vec"], writes=[xtok])
            masks = []
            if lf != "one":
                masks.append((0, vt[:, V[lf]:V[lf] + 1]))
            if rf != "one":
                masks.append((wd - 1, vt[:, V[rf]:V[rf] + 1]))
            emit_norm_mod(c, K, xt, xtok, wd, av[:, stream, :], vt[:, sh2:sh2 + 16], h2, "h2", maskcols=masks)
            for f in range(FC):
                wg, wgtok = wgrot.next()
                wv, wvtok = wvrot.next()
                c.dma("pool", wgtok, wg[:], wu[f], writes=[wgtok])
                c.dma("pool", wvtok, wv[:], wu[FC + f], writes=[wvtok])
                pg, pgtok = pgrot.next()
                pv, pvtok = pvrot.next()
                for kc in range(KC):
                    c.op("pe", lambda e: e.matmul(pg[:, :wd], lhsT=wg[:, kc, :], rhs=h2[:, kc, :wd], start=(kc == 0), stop=(kc == KC - 1)),
                         reads=[wgtok, "h2"], writes=[pgtok])
                for kc in range(KC):
                    c.op("pe", lambda e: e.matmul(pv[:, :wd], lhsT=wv[:, kc, :], rhs=h2[:, kc, :wd], start=(kc == 0), stop=(kc == KC - 1)),
                         reads=[wvtok, "h2"], writes=[pvtok])
                outs = []
                for (pp, pptok, rot, fi) in ((pg, pgtok, cg, f), (pv, pvtok, cv, FC + f)):
                    t, ttok = rot.next()
                    cw0 = V["cw"] + 0 * 88 + fi
                    cw1 = V["cw"] + 1 * 88 + fi
                    cw2 = V["cw"] + 2 * 88 + fi
                    c.op("dve", lambda e: e.tensor_scalar(out=t[:, :wi], in0=pp[:, 0:wi], scalar1=vt[:, cw0:cw0 + 1], scalar2=None, op0=ALU.mult),
                         reads=[pptok, "vec"], writes=[ttok])
                    c.op("dve", lambda e: e.scalar_tensor_tensor(out=t[:, :wi], in0=pp[:, 1:wi + 1], scalar=vt[:, cw1:cw1 + 1], in1=t[:, :wi],
                                                                 op0=ALU.mult, op1=ALU.add), reads=[pptok, ttok, "vec"], writes=[ttok])
                    c.op("dve", lambda e: e.scalar_tensor_tensor(out=t[:, :wi], in0=pp[:, 2:wi + 2], scalar=vt[:, cw2:cw2 + 1], in1=t[:, :wi],
                                                                 op0=ALU.mult, op1=ALU.add), reads=[pptok, ttok, "vec"], writes=[ttok])
                    outs.append((t, ttok))
                (tg, tgtok), (tv, tvtok) = outs
                s_, stok = sg.next()
                cbg = V["cb"] + f
                cbv = V["cb"] + FC + f
                c.op("act", lambda e: e.activation(out=s_[:, :wi], in_=tg[:, :wi], func=AF.Silu, bias=vt[:, cbg:cbg + 1], scale=1.0),
                     reads=[tgtok, "vec"], writes=[stok])
                c.op("dve", lambda e: e.scalar_tensor_tensor(out=act[:, f, :wi], in0=tv[:, :wi], scalar=vt[:, cbv:cbv + 1], in1=s_[:, :wi],
                                                              op0=ALU.add, op1=ALU.mult), reads=[tvtok, stok, "vec"], writes=["act%d" % f])
            for oc in range(KC):
                wt, wtok = wdrot.next()
                c.dma("pool", wtok, wt[:], wdn[oc], writes=[wtok])
                ps, ptok = prot.next()
                for f in range(FC):
                    c.op("pe", lambda e: e.matmul(ps[:, :wi], lhsT=wt[:, f, :], rhs=act[:, f, :wi], start=(f == 0), stop=(f == FC - 1)),
                         reads=[wtok, "act%d" % f], writes=[ptok])
                c.op("dve", lambda e: e.scalar_tensor_tensor(out=xt[:, oc, 1:wi + 1], in0=ps[:, :wi], scalar=vt[:, g2 + oc:g2 + oc + 1], in1=xt[:, oc, 1:wi + 1],
                                                             op0=ALU.mult, op1=ALU.add), reads=[ptok, xtok, "vec"], writes=[xtok])
            if not final:
                c.dma("sp", "oT", oT[:, ocol:ocol + wi].rearrange("(kc p) t -> p kc t", p=128), xt[:, :, 1:wi + 1], reads=[xtok], writes=["oT"])
            else:
                ss, sstok = K["ps_ss"].next()
                for kc in range(KC):
                    sq, sqtok = K["sq"].next()
                    c.op("act", lambda e: e.activation(out=sq[:, :wi], in_=xt[:, kc, 1:wi + 1], func=AF.Square), reads=[xtok], writes=[sqtok])
                    c.op("pe", lambda e: e.matmul(ss[:, :wi], lhsT=K["ones"][:, :], rhs=sq[:, :wi], start=(kc == 0), stop=(kc == KC - 1)),
                         reads=[sqtok, "ones"], writes=[sstok])
                rstd = K["rstd"]
                emit_rstd(c, rstd, ss, sstok, wi)
                for kc in range(KC):
                    y, ytok = yt.next()
                    fg = V["fg"] + kc
                    c.op("dve", lambda e: e.scalar_tensor_tensor(out=y[:, :wi], in0=xt[:, kc, 1:wi + 1], scalar=vt[:, fg:fg + 1], in1=rstd[:, :wi],
                                                                 op0=ALU.mult, op1=ALU.mult), reads=[xtok, "rstd", "vec"], writes=[ytok])
                    c.dma("sp", ytok, oT[kc * 128:(kc + 1) * 128, ocol:ocol + wi], y[:, :wi], reads=[ytok], writes=["oT"])
            ocol += wi
        c.wait_all("sp", ["oT"])
    return nc


ATT_ACC2 = "pool"


def build_att(kind, NQL, NKL, NCT=CTX):
    S = 2
    SK = 2 if kind == "B" else 1
    DV = 256 if kind == "B" else 128
    NH = DV // 128
    NK = NCT + NKL
    NKT = NK // 128
    NCKT = NCT // 128
    R = 256
    scale = 128 ** -0.5
    nc = new_nc()
    qT = nc.dram_tensor("qT", [S, 128, NCT + NQL], BF16, kind="ExternalInput").ap()
    kT = nc.dram_tensor("kT", [SK, 128, NK], BF16, kind="ExternalInput").ap()
    v = nc.dram_tensor("v", [NK, DV], BF16, kind="ExternalInput").ap()
    cq = nc.dram_tensor("cq", [2, 128, NQL], F32, kind="ExternalInput").ap()
    ck = nc.dram_tensor("ck", [2, 128, NKL], F32, kind="ExternalInput").ap()
    rt = nc.dram_tensor("rt", [128, 128], BF16, kind="ExternalInput").ap()
    vec = nc.dram_tensor("vec", [128, 8], F32, kind="ExternalInput").ap()
    lp = nc.dram_tensor("lp", [128, 4, 128], F32, kind="ExternalInput").ap()
    oT = nc.dram_tensor("oT", [R, NCT + NQL], BF16, kind="ExternalOutput").ap()
    with ExitStack() as st:
        c = Ctx(nc, st)
        ones = c.sb("ones", [128, 128], BF16)
        c.op("dve", lambda e: e.memset(ones[:, :], 1.0), writes=["ones"])
        rtt = c.sb("rtt", [128, 128], BF16)
        vt = c.sb("vec", [128, 8], F32)
        lpt = c.sb("lpt", [128, 4, 128], F32)
        lam = c.sb("lam", [128, 8], F32)
        Kr = c.sb("Kr", [128, SK, NK], BF16)
        Vt = c.sb("Vt", [128, NKT, DV], BF16)
        raw = Rot(c, "raw", 2, [128, 512], BF16)
        cst = Rot(c, "cst", 2, [128, 2, 512], F32)
        xn = c.sb("xn", [128, 512], F32)
        xnb = c.sb("xnb", [128, 512], BF16)
        sqb = c.sb("sqb", [128, 512], BF16)
        rstd = c.sb("rstd", [128, 512], F32)
        t1 = c.sb("t1", [128, 512], F32)
        t2 = c.sb("t2", [128, 512], F32)
        qr = Rot(c, "qr", 2, [128, 512], BF16)
        E = Rot(c, "E", 3, [128, 2, 512], BF16)
        acc = [c.sb("acc%d" % i, [128, 2, 512], F32) for i in range(2)]
        ones32 = c.sb("ones32", [128, 128], F32)
        c.op("dve", lambda e: e.memset(ones32[:, :], 1.0), writes=["ones32"])
        sqb2 = c.sb("sqb2", [128, 512], BF16)
        rstd2 = c.sb("rstd2", [128, 512], F32)
        on = [[c.sb("on%d%d" % (s, h), [128, 512], F32) for h in range(NH)] for s in range(S)]
        rec = c.sb("rec", [128, 512], F32)
        ob = Rot(c, "ob", 2, [128, 512], BF16)
        ps_s = Rot(c, "pS", 2, [128, 2, 512], F32, psum=True)
        ps_o = [c.ps("pO%d" % h, [128, 512]) for h in range(NH)]
        ps_sum = c.ps("pSum", [128, 512])
        ps_ss2 = ps_sum
        ps_ss = c.ps("pSS", [128, 512])
        ps_rot = ps_ss
        c.dma("sp", "rtt", rtt[:], rt, writes=["rtt"])
        c.dma("sp", "vec", vt[:], vec, writes=["vec"])
        c.dma("sp", "Vt", Vt[:], v.rearrange("(kt p) d -> p kt d", p=128), writes=["Vt"])
        if kind == "B":
            c.dma("sp", "lpt", lpt[:], lp, writes=["lpt"])
            for i in range(2):
                c.op("dve", lambda e: e.tensor_tensor(out=t1[:, :128], in0=lpt[:, 2 * i, :], in1=lpt[:, 2 * i + 1, :], op=ALU.mult), reads=["lpt"], writes=["t1"])
                c.op("dve", lambda e: e.reduce_sum(out=lam[:, i:i + 1], in_=t1[:, :128], axis=AX.X), reads=["t1"], writes=["lam"])
            c.op("act", lambda e: e.activation(out=lam[:, 0:2], in_=lam[:, 0:2], func=AF.Exp), reads=["lam"], writes=["lam"])
            c.op("dve", lambda e: e.tensor_tensor(out=lam[:, 2:3], in0=lam[:, 0:1], in1=lam[:, 1:2], op=ALU.subtract), reads=["lam"], writes=["lam"])
            c.op("dve", lambda e: e.tensor_tensor(out=lam[:, 2:3], in0=lam[:, 2:3], in1=vt[:, 2:3], op=ALU.add), reads=["lam", "vec"], writes=["lam"])
            c.op("dve", lambda e: e.tensor_scalar(out=lam[:, 3:4], in0=lam[:, 2:3], scalar1=-1.0, scalar2=None, op0=ALU.mult), reads=["lam"], writes=["lam"])
            c.op("dve", lambda e: e.tensor_scalar(out=lam[:, 4:6], in0=vt[:, 4:6], scalar1=vt[:, 3:4], scalar2=None, op0=ALU.mult), reads=["lam", "vec"], writes=["lam"])

        def prep(src, srctok, W, cs, cstok, gain_col, dst, dsttok):
            cur, curtok = src, srctok
            if gain_col is not None:
                c.op("act", lambda e: e.activation(out=sqb[:, :W], in_=src[:, :W], func=AF.Square), reads=[srctok], writes=["sqb"])
                c.op("pe", lambda e: e.matmul(ps_ss[:, :W], lhsT=ones[:, :], rhs=sqb[:, :W], start=True, stop=True), reads=["sqb", "ones"], writes=["pSS"])
                emit_rstd(c, rstd, ps_ss, "pSS", W, n=128)
                c.op("dve", lambda e: e.scalar_tensor_tensor(out=xn[:, :W], in0=src[:, :W], scalar=vt[:, gain_col:gain_col + 1], in1=rstd[:, :W],
                                                             op0=ALU.mult, op1=ALU.mult), reads=[srctok, "rstd", "vec"], writes=["xn"])
                cur, curtok = xn, "xn"
                if cs is None:
                    c.op("act", lambda e: e.activation(out=dst[:, :W], in_=xn[:, :W], func=AF.Copy), reads=["xn"], writes=[dsttok])
                    return
                c.op("act", lambda e: e.activation(out=xnb[:, :W], in_=xn[:, :W], func=AF.Copy), reads=["xn"], writes=["xnb"])
                curb, curbtok = xnb, "xnb"
            else:
                if cs is None:
                    c.op("act", lambda e: e.activation(out=dst[:, :W], in_=src[:, :W], func=AF.Copy), reads=[srctok], writes=[dsttok])
                    return
                curb, curbtok = src, srctok
            c.op("pe", lambda e: e.matmul(ps_rot[:, :W], lhsT=rtt[:, :], rhs=curb[:, :W], start=True, stop=True), reads=["rtt", curbtok], writes=["pSS"])
            c.op("dve", lambda e: e.tensor_tensor(out=t1[:, :W], in0=cur[:, :W], in1=cs[:, 0, :W], op=ALU.mult), reads=[curtok, cstok], writes=["t1"])
            c.op("dve", lambda e: e.tensor_tensor(out=t2[:, :W], in0=ps_rot[:, :W], in1=cs[:, 1, :W], op=ALU.mult), reads=["pSS", cstok], writes=["t2"])
            c.op("dve", lambda e: e.tensor_tensor(out=dst[:, :W], in0=t1[:, :W], in1=t2[:, :W], op=ALU.add), reads=["t1", "t2"], writes=[dsttok])

        kgain = 1 if kind == "C" else None
        qgain = 0 if kind == "C" else None
        ktiles = [(0, NCT, None)] + [(NCT + s0, w, s0) for s0, w in tiles_of(NKL, 512)]
        for sk in range(SK):
            for (c0, w, r0) in ktiles:
                rw, rwtok = raw.next()
                c.dma("sp", rwtok, rw[:, :w], kT[sk, :, c0:c0 + w], writes=[rwtok])
                cs, cstok = None, None
                if r0 is not None:
                    cs, cstok = cst.next()
                    c.dma("sp", cstok, cs[:, :, :w], ck[:, :, r0:r0 + w].rearrange("a p t -> p a t"), writes=[cstok])
                prep(rw, rwtok, w, cs, cstok, kgain, Kr[:, sk, c0:c0 + w], "Kr%d_%d" % (sk, c0))
        ktoks = [["Kr%d_%d" % (sk, c0) for (c0, w, r0) in ktiles] for sk in range(SK)]
        qtiles = [(0, NCT, None, NCKT)] + [(NCT + s0, w, s0, NKT) for s0, w in tiles_of(NQL, 512)]
        units = [(qi, s) for qi in range(len(qtiles)) for s in range(S)]
        cs_of = {}

        def prep_unit(u):
            qi, s = units[u]
            c0, w, r0, nkt = qtiles[qi]
            if s == 0:
                cs, cstok = None, None
                if r0 is not None:
                    cs, cstok = cst.next()
                    c.dma("sp", cstok, cs[:, :, :w], cq[:, :, r0:r0 + w].rearrange("a p t -> p a t"), writes=[cstok])
                cs_of[qi] = (cs, cstok)
            cs, cstok = cs_of[qi]
            rw, rwtok = raw.next()
            c.dma("sp", rwtok, rw[:, :w], qT[s, :, c0:c0 + w], writes=[rwtok])
            q, qtok = qr.next()
            prep(rw, rwtok, w, cs, cstok, qgain, q, qtok)
            return q, qtok

        nxt = prep_unit(0)
        for u, (qi, s) in enumerate(units):
            c0, w, r0, nkt = qtiles[qi]
            q, qtok = nxt
            if u + 1 < len(units):
                nxt = prep_unit(u + 1)
            sk = s if SK == 2 else 0
            pend = {}
            npair = nkt // 2

            def score(p_):
                ps, pstok = ps_s.next()
                for j in range(2):
                    kt = 2 * p_ + j
                    c.op("pe", lambda e: e.matmul(ps[:, j, :w], lhsT=Kr[:, sk, kt * 128:(kt + 1) * 128], rhs=q[:, :w], start=True, stop=True),
                         reads=ktoks[sk] + [qtok], writes=[pstok])
                pend[p_] = (ps, pstok)

            score(0)
            for p_ in range(npair):
                ps, pstok = pend.pop(p_)
                e_, etok = E.next()
                c.op("act", lambda e: e.activation(out=e_[:, :, :w], in_=ps[:, :, :w], func=AF.Exp, scale=scale), reads=[pstok], writes=[etok])
                if p_ + 1 < npair:
                    score(p_ + 1)
                for j in range(2):
                    kt = 2 * p_ + j
                    for h in range(NH):
                        c.op("pe", lambda e: e.matmul(ps_o[h][:, :w], lhsT=Vt[:, kt, h * 128:(h + 1) * 128], rhs=e_[:, j, :w], start=(kt == 0), stop=(kt == nkt - 1)),
                             reads=["Vt", etok], writes=["pO%d" % h])
                ac, actok = (acc[0], "acc0") if p_ % 2 == 0 else (acc[1], "acc1")
                if p_ < 2:
                    c.op("dve", lambda e: e.tensor_copy(out=ac[:, :, :w], in_=e_[:, :, :w]), reads=[etok], writes=[actok])
                else:
                    c.op("dve", lambda e: e.tensor_tensor(out=ac[:, :, :w], in0=ac[:, :, :w], in1=e_[:, :, :w], op=ALU.add), reads=[etok, actok], writes=[actok])
            if npair > 1:
                c.op("dve", lambda e: e.tensor_tensor(out=acc[0][:, :, :w], in0=acc[0][:, :, :w], in1=acc[1][:, :, :w], op=ALU.add), reads=["acc0", "acc1"], writes=["acc0"])
            c.op("dve", lambda e: e.tensor_tensor(out=acc[0][:, 0, :w], in0=acc[0][:, 0, :w], in1=acc[0][:, 1, :w], op=ALU.add), reads=["acc0"], writes=["acc0"])
            c.op("pe", lambda e: e.matmul(ps_sum[:, :w], lhsT=ones32[:, :], rhs=acc[0][:, 0, :w], start=True, stop=True), reads=["ones32", "acc0"], writes=["pSum"])
            c.op("dve", lambda e: e.reciprocal(out=rec[:, :w], in_=ps_sum[:, :w]), reads=["pSum"], writes=["rec"])
            for h in range(NH):
                c.op("dve", lambda e: e.tensor_tensor(out=on[s][h][:, :w], in0=ps_o[h][:, :w], in1=rec[:, :w], op=ALU.mult),
                     reads=["pO%d" % h, "rec"], writes=["on%d%d" % (s, h)])
            if kind == "C":
                o_, otok = ob.next()
                c.op("act", lambda e: e.activation(out=o_[:, :w], in_=on[s][0][:, :w], func=AF.Copy), reads=["on%d0" % s], writes=[otok])
                c.dma("sp", otok, oT[s * 128:(s + 1) * 128, c0:c0 + w], o_[:, :w], reads=[otok], writes=["oT"])
            if kind == "B" and s == S - 1:
                for h in range(NH):
                    c.op("dve", lambda e: e.scalar_tensor_tensor(out=on[0][h][:, :w], in0=on[1][h][:, :w], scalar=lam[:, 3:4], in1=on[0][h][:, :w],
                                                                 op0=ALU.mult, op1=ALU.add), reads=["on1%d" % h, "on0%d" % h, "lam"], writes=["on0%d" % h])
                    c.op("act", lambda e: e.activation(out=sqb2[:, :w], in_=on[0][h][:, :w], func=AF.Square), reads=["on0%d" % h], writes=["sqb2"])
                    c.op("pe", lambda e: e.matmul(ps_ss2[:, :w], lhsT=ones[:, :], rhs=sqb2[:, :w], start=(h == 0), stop=(h == NH - 1)),
                         reads=["sqb2", "ones"], writes=["pSum"])
                emit_rstd(c, rstd2, ps_ss2, "pSum", w, n=256, tok="rstd2")
                for h in range(NH):
                    o_, otok = ob.next()
                    c.op("dve", lambda e: e.scalar_tensor_tensor(out=o_[:, :w], in0=on[0][h][:, :w], scalar=lam[:, 4 + h:5 + h], in1=rstd2[:, :w],
                                                                 op0=ALU.mult, op1=ALU.mult), reads=["on0%d" % h, "rstd2", "lam"], writes=[otok])
                    c.dma("sp", otok, oT[h * 128:(h + 1) * 128, c0:c0 + w], o_[:, :w], reads=[otok], writes=["oT"])
        c.wait_all("sp", ["oT"])
    return nc


def rope_tables(n):
    freqs = (10000.0 ** (-np.arange(0, 64, 2, dtype=np.float32) / 64)).astype(np.float32)
    t = np.arange(n)
    row = (t // 64).astype(np.float32)
    col = (t % 64).astype(np.float32)
    ang_r = row[:, None] * freqs
    ang_c = col[:, None] * freqs
    ang = np.concatenate([ang_r, ang_r, ang_c, ang_c], -1).astype(np.float32)
    cos = np.cos(ang).astype(np.float32)
    sin = np.sin(ang).astype(np.float32)
    sgn = np.ones(128, np.float32)
    sgn[0:32] = -1
    sgn[64:96] = -1
    return np.ascontiguousarray(np.stack([cos.T, (sin * sgn).T], 0))


def rope_perm():
    m = np.arange(128)
    partner = np.where((m // 32) % 2 == 0, m + 32, m - 32)
    rt = np.zeros((128, 128), np.float32)
    rt[partner, m] = 1.0
    return rt.astype(NPBF)


def build_gdn(NT, NCT=CTX):
    NCH = NT // 128
    CCH = NCT // 128
    nc = new_nc()
    qkvT = nc.dram_tensor("qkvT", [3, 128, NT], BF16, kind="ExternalInput").ap()
    ztm = nc.dram_tensor("ztm", [NT, 128], BF16, kind="ExternalInput").ap()
    gt = nc.dram_tensor("gt", [128, NCH, 4], BF16, kind="ExternalInput").ap()
    vec = nc.dram_tensor("vec", [128, 16], F32, kind="ExternalInput").ap()
    gbc = nc.dram_tensor("gbc", [128, 128], F32, kind="ExternalInput").ap()
    cst = nc.dram_tensor("cst", [6, 128, 128], F32, kind="ExternalInput").ap()
    otm = nc.dram_tensor("otm", [NT, 128], BF16, kind="ExternalOutput").ap()
    with ExitStack() as st:
        c = Ctx(nc, st)
        vt = c.sb("vec", [128, 16], F32)
        gb = c.sb("gbc", [128, 128], F32)
        C32 = c.sb("cst", [128, 6, 128], F32)
        Ib = c.sb("Ib", [128, 128], BF16)
        onesb = c.sb("onesb", [128, 128], BF16)
        qT = c.sb("qT", [128, NT], BF16)
        kT = c.sb("kT", [128, NT], BF16)
        ktm = c.sb("ktm", [128, NCH, 128], BF16)
        vtm = c.sb("vtm", [128, NCH, 128], BF16)
        of = c.sb("of", [128, NCH, 128], BF16)
        G = [c.sb("G%d" % d, [128, NCH], F32) for d in range(2)]
        BT = [c.sb("BT%d" % d, [128, NCH], F32) for d in range(2)]
        NB = [c.sb("NB%d" % d, [128, NCH], F32) for d in range(2)]
        Bc = [c.sb("Bc%d" % d, [128, NCH], F32) for d in range(2)]
        EB = [c.sb("EB%d" % d, [128, NCH], F32) for d in range(2)]
        BEB = [c.sb("BEB%d" % d, [128, NCH], F32) for d in range(2)]
        EKD = [c.sb("EKD%d" % d, [128, NCH], F32) for d in range(2)]
        CD = [c.sb("CD%d" % d, [128, NCH], F32) for d in range(2)]
        tri = c.sb("tri", [128, 2, 128], F32)
        zt = Rot(c, "zt", 2, [128, 128], BF16)
        ot = Rot(c, "ot", 2, [128, 128], BF16)
        banks = [c.ps("bank%d" % i, [128, 512]) for i in range(8)]

        def carve(b, i, n=1):
            return banks[b][:, 128 * i:128 * (i + n)]

        class RotAP:
            def __init__(self, aps, names):
                self.bufs, self.names, self.i = aps, names, 0

            def next(self):
                j = self.i % len(self.bufs)
                self.i += 1
                return self.bufs[j], self.names[j]

        pbig = banks[0]
        pG = banks[1]
        pT = RotAP([carve(1, 0), carve(1, 1)], ["bank1", "bank1"])

        class Chain:
            def __init__(self, d):
                n = "c%d" % d
                self.d = d
                self.oss = c.sb("oss" + n, [128, 4], F32)
                for nm in ("diagb", "nd", "dm", "dmS", "dmI", "Rm", "S32", "o2s", "osum", "osq", "sz"):
                    setattr(self, nm, c.sb(nm + n, [128, 128], F32))
                for nm in ("Rb", "qk", "kb", "vb", "Sb", "u_b"):
                    setattr(self, nm, c.sb(nm + n, [128, 128], BF16))
                self.Pm = Rot(c, "Pm" + n, 2, [128, 128], F32)
                self.Qm = Rot(c, "Qm" + n, 2, [128, 128], F32)
                self.qkT = Rot(c, "qkT" + n, 2, [128, 128], BF16)
                self.kd = Rot(c, "kd" + n, 2, [128, 128], BF16)
                self.ub = Rot(c, "ub" + n, 2, [128, 128], F32)
                self.wT = Rot(c, "wT" + n, 2, [128, 128], BF16)
                b0 = 4 * d
                self.bA, self.bB, self.bC, self.bD = ["bank%d" % (b0 + k) for k in range(4)]
                self.pA, self.pKK, self.pQK, self.pU = [carve(b0, k) for k in range(4)]
                self.pT = RotAP([carve(b0 + 1, 0), carve(b0 + 1, 1)], [self.bB, self.bB])
                self.pW = carve(b0 + 1, 2)
                self.pI = RotAP([carve(b0 + 2, k) for k in range(3)], [self.bC] * 3)
                self.pwS, self.pO1, self.pO2, self.pdS = [carve(b0 + 3, k) for k in range(4)]

            def t(self, nm):
                return "%sc%d" % (nm, self.d)

        c.push_scope()
        gtr = c.sb("gtr", [128, NCH, 4], BF16)
        gx = c.sb("gx", [128, NCH], F32)
        rb = c.sb("rb", [128, 3, 514], BF16)
        cv = c.sb("cv", [128, 514], F32)
        sv = c.sb("sv", [128, 514], F32)
        sqb = c.sb("sqb", [128, 512], BF16)
        rstd = c.sb("rstd", [128, 512], F32)
        vTb = c.sb("vTb", [128, 512], BF16)

        c.dma("sp", "vec", vt[:], vec, writes=["vec"])
        c.dma("sp", "gbc", gb[:], gbc, writes=["gbc"])
        c.dma("sp", "cst", C32[:], cst.rearrange("a p f -> p a f"), writes=["cst"])
        c.dma("sp", "gtr", gtr[:], gt, writes=["gtr"])
        I32 = C32[:, 0, :]
        ones32 = C32[:, 1, :]
        MS = [C32[:, 2, :], C32[:, 4, :]]
        MI = [C32[:, 3, :], C32[:, 5, :]]
        c.op("act", lambda e: e.activation(out=Ib[:, :], in_=C32[:, 0, :], func=AF.Copy), reads=["cst"], writes=["Ib"])
        c.op("act", lambda e: e.activation(out=onesb[:, :], in_=C32[:, 1, :], func=AF.Copy), reads=["cst"], writes=["onesb"])
        c.op("dve", lambda e: e.tensor_copy(out=tri[:, 0, :], in_=C32[:, 5, :]), reads=["cst"], writes=["tri"])
        c.op("dve", lambda e: e.tensor_copy(out=tri[:, 1, :], in_=C32[:, 3, :]), reads=["cst"], writes=["tri"])
        c.op("act", lambda e: e.activation(out=vt[:, 13:15], in_=vt[:, 0:2], func=AF.Exp), reads=["vec"], writes=["vec"])
        c.op("dve", lambda e: e.tensor_scalar(out=vt[:, 13:15], in0=vt[:, 13:15], scalar1=-1.0, scalar2=None, op0=ALU.mult), reads=["vec"], writes=["vec"])
        for d in range(2):
            c.op("act", lambda e: e.activation(out=gx[:, :], in_=gtr[:, :, d], func=AF.Exp, bias=vt[:, 2 + d:3 + d], scale=1.0), reads=["gtr", "vec"], writes=["gx"])
            c.op("act", lambda e: e.activation(out=gx[:, :], in_=gx[:, :], func=AF.Ln, bias=1.0, scale=1.0), reads=["gx"], writes=["gx"])
            c.op("dve", lambda e: e.tensor_scalar(out=G[d][:, :], in0=gx[:, :], scalar1=vt[:, 13 + d:14 + d], scalar2=None, op0=ALU.mult), reads=["gx", "vec"], writes=["G%d" % d])
            c.op("act", lambda e: e.activation(out=BT[d][:, :], in_=gtr[:, :, 2 + d], func=AF.Sigmoid), reads=["gtr"], writes=["BT%d" % d])
            c.op("dve", lambda e: e.tensor_scalar(out=NB[d][:, :], in0=BT[d][:, :], scalar1=-1.0, scalar2=None, op0=ALU.mult), reads=["BT%d" % d], writes=["NB%d" % d])
            c.op("pe", lambda e: e.matmul(pG[:, 0:NCH], lhsT=tri[:, d, :], rhs=G[d][:, :], start=True, stop=True), reads=["tri", "G%d" % d], writes=["bank1"])
            c.op("pe", lambda e: e.matmul(pG[:, 256:256 + NCH], lhsT=C32[:, 1, :], rhs=G[d][:, :], start=True, stop=True), reads=["cst", "G%d" % d], writes=["bank1"])
            c.op("dve", lambda e: e.tensor_copy(out=Bc[d][:, :], in_=pG[:, 0:NCH]), reads=["bank1"], writes=["Bc%d" % d])
            c.op("act", lambda e: e.activation(out=EB[d][:, :], in_=pG[:, 0:NCH], func=AF.Exp), reads=["bank1"], writes=["EB%d" % d])
            c.op("act", lambda e: e.activation(out=CD[d][:, :], in_=pG[:, 256:256 + NCH], func=AF.Exp), reads=["bank1"], writes=["CD%d" % d])
            c.op("dve", lambda e: e.tensor_tensor(out=EKD[d][:, :], in0=pG[:, 256:256 + NCH], in1=Bc[d][:, :], op=ALU.subtract), reads=["bank1", "Bc%d" % d], writes=["EKD%d" % d])
            c.op("act", lambda e: e.activation(out=EKD[d][:, :], in_=EKD[d][:, :], func=AF.Exp), reads=["EKD%d" % d], writes=["EKD%d" % d])
            c.op("dve", lambda e: e.tensor_tensor(out=BEB[d][:, :], in0=BT[d][:, :], in1=EB[d][:, :], op=ALU.mult), reads=["BT%d" % d, "EB%d" % d], writes=["BEB%d" % d])
        gtoks = ["Bc0", "Bc1", "EB0", "EB1", "CD0", "CD1", "EKD0", "EKD1", "BEB0", "BEB1", "BT0", "BT1", "NB0", "NB1"]

        segs = [(0, NCT)] + [(NCT + s0, w) for s0, w in tiles_of(NT - NCT, 512)]
        seq_lo = {0: 0}
        for (c0, w) in segs:
            lo_edge = (c0 == 0) or (c0 == NCT)
            hi_edge = (c0 + w == NCT) or (c0 + w == NT)
            a0 = c0 if lo_edge else c0 - 1
            a1 = c0 + w if hi_edge else c0 + w + 1
            if lo_edge:
                c.op("dve", lambda e: e.memset(rb[:, :, 0:1], 0.0), writes=["rb"])
            if hi_edge:
                c.op("dve", lambda e: e.memset(rb[:, :, w + 1:w + 2], 0.0), writes=["rb"])
            o0 = 1 if lo_edge else 0
            c.dma("sp", "rb", rb[:, :, o0:o0 + (a1 - a0)], qkvT[:, :, a0:a1].rearrange("a p t -> p a t"), writes=["rb"])
            for i in range(3):
                t0 = 4 + 3 * i
                c.op("dve", lambda e: e.tensor_scalar(out=cv[:, :w], in0=rb[:, i, 0:w], scalar1=vt[:, t0:t0 + 1], scalar2=None, op0=ALU.mult), reads=["rb", "vec"], writes=["cv"])
                c.op("dve", lambda e: e.scalar_tensor_tensor(out=cv[:, :w], in0=rb[:, i, 1:w + 1], scalar=vt[:, t0 + 1:t0 + 2], in1=cv[:, :w], op0=ALU.mult, op1=ALU.add),
                     reads=["rb", "vec", "cv"], writes=["cv"])
                c.op("dve", lambda e: e.scalar_tensor_tensor(out=cv[:, :w], in0=rb[:, i, 2:w + 2], scalar=vt[:, t0 + 2:t0 + 3], in1=cv[:, :w], op0=ALU.mult, op1=ALU.add),
                     reads=["rb", "vec", "cv"], writes=["cv"])
                if i == 2:
                    c.op("act", lambda e: e.activation(out=vTb[:, :w], in_=cv[:, :w], func=AF.Silu), reads=["cv"], writes=["vTb"])
                    for s in range(w // 128):
                        p_, ptok = pT.next()
                        c.op("pe", lambda e: e.matmul(p_[:, :], lhsT=vTb[:, s * 128:(s + 1) * 128], rhs=Ib[:, :], start=True, stop=True), reads=["vTb", "Ib"], writes=[ptok])
                        ch = (c0 + s * 128) // 128
                        c.op("act", lambda e: e.activation(out=vtm[:, ch, :], in_=p_[:, :], func=AF.Copy), reads=[ptok], writes=["vtm%d" % ch])
                    continue
                c.op("act", lambda e: e.activation(out=sv[:, :w], in_=cv[:, :w], func=AF.Silu), reads=["cv"], writes=["sv"])
                c.op("act", lambda e: e.activation(out=sqb[:, :w], in_=sv[:, :w], func=AF.Square), reads=["sv"], writes=["sqb"])
                c.op("pe", lambda e: e.matmul(pbig[:, :w], lhsT=onesb[:, :], rhs=sqb[:, :w], start=True, stop=True), reads=["sqb", "onesb"], writes=["bank0"])
                emit_rstd(c, rstd, pbig, "bank0", w, n=1)
                dst = qT if i == 0 else kT
                dtok = ("qT%d" if i == 0 else "kT%d") % c0
                sc = (128 ** -0.5) if i == 0 else 1.0
                c.op("dve", lambda e: e.scalar_tensor_tensor(out=dst[:, c0:c0 + w], in0=sv[:, :w], scalar=sc, in1=rstd[:, :w], op0=ALU.mult, op1=ALU.mult),
                     reads=["sv", "rstd"], writes=[dtok])
                if i == 1:
                    for s in range(w // 128):
                        p_, ptok = pT.next()
                        c.op("pe", lambda e: e.matmul(p_[:, :], lhsT=kT[:, c0 + s * 128:c0 + (s + 1) * 128], rhs=Ib[:, :], start=True, stop=True), reads=[dtok, "Ib"], writes=[ptok])
                        ch = (c0 + s * 128) // 128
                        c.op("dve", lambda e: e.tensor_copy(out=ktm[:, ch, :], in_=p_[:, :]), reads=[ptok], writes=["ktm%d" % ch])

        c.pop_scope()
        chains = [Chain(0), Chain(1)]

        def seg_of(ch):
            t = ch * 128
            if t < NCT:
                return 0
            return NCT + ((t - NCT) // 512) * 512

        def pre(ch, X):
            d = X.d
            col = slice(ch, ch + 1)
            qtok = "qT%d" % seg_of(ch)
            ktok = "kT%d" % seg_of(ch)
            cs = slice(ch * 128, (ch + 1) * 128)
            c.op("dve", lambda e: e.tensor_scalar(out=X.diagb[:, :], in0=C32[:, 0, :], scalar1=Bc[d][:, col], scalar2=None, op0=ALU.mult), reads=["cst"] + gtoks, writes=[X.t("diagb")])
            c.op("pe", lambda e: e.matmul(X.pKK, lhsT=kT[:, cs], rhs=kT[:, cs], start=True, stop=True), reads=[ktok], writes=[X.bA])
            c.op("pe", lambda e: e.matmul(X.pQK, lhsT=qT[:, cs], rhs=kT[:, cs], start=True, stop=True), reads=[qtok, ktok], writes=[X.bA])
            yield
            c.op("pe", lambda e: e.matmul(X.pA, lhsT=C32[:, 1, :], rhs=X.diagb[:, :], start=True, stop=True), reads=["cst", X.t("diagb")], writes=[X.bA])
            yield
            c.op("dve", lambda e: e.tensor_scalar(out=X.nd[:, :], in0=X.pA, scalar1=Bc[d][:, col], scalar2=0.0, op0=ALU.subtract, op1=ALU.max), reads=[X.bA] + gtoks, writes=[X.t("nd")])
            yield
            c.op("act", lambda e: e.activation(out=X.dm[:, :], in_=X.nd[:, :], func=AF.Exp, scale=-1.0), reads=[X.t("nd")], writes=[X.t("dm")])
            yield
            c.op("dve", lambda e: e.tensor_tensor(out=X.dmS[:, :], in0=X.dm[:, :], in1=MS[d], op=ALU.mult), reads=[X.t("dm"), "cst"], writes=[X.t("dmS")])
            c.op("dve", lambda e: e.tensor_tensor(out=X.dmI[:, :], in0=X.dm[:, :], in1=MI[d], op=ALU.mult), reads=[X.t("dm"), "cst"], writes=[X.t("dmI")])
            yield
            Q, Qtok = X.Qm.next()
            c.op("dve", lambda e: e.scalar_tensor_tensor(out=Q[:, :], in0=X.pKK, scalar=NB[d][:, col], in1=X.dmS[:, :], op0=ALU.mult, op1=ALU.mult),
                 reads=[X.bA, X.t("dmS")] + gtoks, writes=[Qtok])
            c.op("dve", lambda e: e.tensor_tensor(out=X.qk[:, :], in0=X.pQK, in1=X.dmI[:, :], op=ALU.mult), reads=[X.bA, X.t("dmI")], writes=[X.t("qk")])
            yield
            p_, ptok = X.pT.next()
            c.op("pe", lambda e: e.matmul(p_, lhsT=Q[:, :], rhs=C32[:, 0, :], start=True, stop=True), reads=[Qtok, "cst"], writes=[ptok])
            p2, p2tok = X.pT.next()
            c.op("pe", lambda e: e.matmul(p2, lhsT=X.qk[:, :], rhs=Ib[:, :], start=True, stop=True), reads=[X.t("qk"), "Ib"], writes=[p2tok])
            yield
            P, Ptok = X.Pm.next()
            c.op("act", lambda e: e.activation(out=P[:, :], in_=p_, func=AF.Copy), reads=[ptok], writes=[Ptok])
            c.op("dve", lambda e: e.tensor_tensor(out=X.Rm[:, :], in0=p_, in1=C32[:, 0, :], op=ALU.add), reads=[ptok, "cst"], writes=[X.t("Rm")])
            qkT_, qkTtok = X.qkT.next()
            c.op("act", lambda e: e.activation(out=qkT_[:, :], in_=p2, func=AF.Copy), reads=[p2tok], writes=[qkTtok])
            yield
            for step in range(6):
                last = step == 5
                pq, pqtok = X.pI.next()
                c.op("pe", lambda e: e.matmul(pq, lhsT=P[:, :], rhs=Q[:, :], start=True, stop=True), reads=[Ptok, Qtok], writes=[pqtok])
                if not last:
                    pp, pptok = X.pI.next()
                    c.op("pe", lambda e: e.matmul(pp, lhsT=Q[:, :], rhs=P[:, :], start=True, stop=True), reads=[Ptok, Qtok], writes=[pptok])
                yield
                Q2, Q2tok = X.Qm.next()
                c.op("dve", lambda e: e.tensor_copy(out=Q2[:, :], in_=pq), reads=[pqtok], writes=[Q2tok])
                if not last:
                    P2, P2tok = X.Pm.next()
                    c.op("act", lambda e: e.activation(out=P2[:, :], in_=pp, func=AF.Copy), reads=[pptok], writes=[P2tok])
                    P, Ptok = P2, P2tok
                Q, Qtok = Q2, Q2tok
                yield
                pr, prtok = X.pI.next()
                c.op("pe", lambda e: e.matmul(pr, lhsT=Q[:, :], rhs=X.Rm[:, :], start=True, stop=True), reads=[Qtok, X.t("Rm")], writes=[prtok])
                yield
                c.op("dve", lambda e: e.tensor_tensor(out=X.Rm[:, :], in0=pr, in1=X.Rm[:, :], op=ALU.add), reads=[prtok, X.t("Rm")], writes=[X.t("Rm")])
                yield
            c.op("act", lambda e: e.activation(out=X.Rb[:, :], in_=X.Rm[:, :], func=AF.Copy), reads=[X.t("Rm")], writes=[X.t("Rb")])
            kd_, kdtok = X.kd.next()
            c.op("pool", lambda e: e.tensor_scalar(out=X.kb[:, :], in0=ktm[:, ch, :], scalar1=BEB[d][:, col], scalar2=None, op0=ALU.mult), reads=["ktm%d" % ch] + gtoks, writes=[X.t("kb")])
            c.op("pool", lambda e: e.tensor_scalar(out=kd_[:, :], in0=ktm[:, ch, :], scalar1=EKD[d][:, col], scalar2=None, op0=ALU.mult), reads=["ktm%d" % ch] + gtoks, writes=[kdtok])
            c.op("pool", lambda e: e.tensor_scalar(out=X.vb[:, :], in0=vtm[:, ch, :], scalar1=BT[d][:, col], scalar2=None, op0=ALU.mult), reads=["vtm%d" % ch] + gtoks, writes=[X.t("vb")])
            yield
            c.op("pe", lambda e: e.matmul(X.pU, lhsT=X.Rb[:, :], rhs=X.vb[:, :], start=True, stop=True), reads=[X.t("Rb"), X.t("vb")], writes=[X.bA])
            c.op("pe", lambda e: e.matmul(X.pW, lhsT=X.kb[:, :], rhs=X.Rb[:, :], start=True, stop=True), reads=[X.t("Rb"), X.t("kb")], writes=[X.bB])
            yield
            ub_, ubtok = X.ub.next()
            wT_, wTtok = X.wT.next()
            c.op("act", lambda e: e.activation(out=ub_[:, :], in_=X.pU, func=AF.Copy), reads=[X.bA], writes=[ubtok])
            c.op("dve", lambda e: e.tensor_copy(out=wT_[:, :], in_=X.pW), reads=[X.bB], writes=[wTtok])
            X.slot_out = (qkT_, qkTtok, kd_, kdtok, ub_, ubtok, wT_, wTtok)
            yield

        def rec(ch, X, slot, fin):
            d = X.d
            qkT_, qkTtok, kd_, kdtok, ub_, ubtok, wT_, wTtok = slot
            col = slice(ch, ch + 1)
            cs = slice(ch * 128, (ch + 1) * 128)
            qtok = "qT%d" % seg_of(ch)
            c.op("pe", lambda e: e.matmul(X.pwS, lhsT=wT_[:, :], rhs=X.Sb[:, :], start=True, stop=True), reads=[wTtok, X.t("Sb")], writes=[X.bD])
            c.op("pe", lambda e: e.matmul(X.pO1, lhsT=qT[:, cs], rhs=X.Sb[:, :], start=True, stop=True), reads=[qtok, X.t("Sb")], writes=[X.bD])
            yield
            c.op("dve", lambda e: e.tensor_tensor(out=X.u_b[:, :], in0=ub_[:, :], in1=X.pwS, op=ALU.subtract), reads=[ubtok, X.bD], writes=[X.t("u_b")])
            yield
            c.op("pe", lambda e: e.matmul(X.pdS, lhsT=kd_[:, :], rhs=X.u_b[:, :], start=True, stop=True), reads=[kdtok, X.t("u_b")], writes=[X.bD])
            c.op("pe", lambda e: e.matmul(X.pO2, lhsT=qkT_[:, :], rhs=X.u_b[:, :], start=True, stop=True), reads=[qkTtok, X.t("u_b")], writes=[X.bD])
            yield
            c.op("dve", lambda e: e.scalar_tensor_tensor(out=X.S32[:, :], in0=X.S32[:, :], scalar=CD[d][:, col], in1=X.pdS, op0=ALU.mult, op1=ALU.add),
                 reads=[X.t("S32"), X.bD] + gtoks, writes=[X.t("S32")])
            yield
            c.op("act", lambda e: e.activation(out=X.Sb[:, :], in_=X.S32[:, :], func=AF.Copy), reads=[X.t("S32")], writes=[X.t("Sb")])
            c.op("act", lambda e: e.activation(out=X.o2s[:, :], in_=X.pO2, func=AF.Copy), reads=[X.bD], writes=[X.t("o2s")])
            yield
            if not fin:
                c.op("dve", lambda e: e.scalar_tensor_tensor(out=of[:, ch, :], in0=X.pO1, scalar=EB[d][:, col], in1=X.o2s[:, :], op0=ALU.mult, op1=ALU.add),
                     reads=[X.bD, X.t("o2s")] + gtoks, writes=["of%d" % ch])
                yield
                return
            c.op("dve", lambda e: e.scalar_tensor_tensor(out=X.osum[:, :], in0=X.pO1, scalar=EB[d][:, col], in1=X.o2s[:, :], op0=ALU.mult, op1=ALU.add),
                 reads=[X.bD, X.t("o2s")] + gtoks, writes=[X.t("osum")])
            yield
            c.op("dve", lambda e: e.tensor_tensor(out=X.osum[:, :], in0=X.osum[:, :], in1=of[:, ch, :], op=ALU.add), reads=[X.t("osum"), "of%d" % ch], writes=[X.t("osum")])
            yield
            c.op("act", lambda e: e.activation(out=X.osq[:, :], in_=X.osum[:, :], func=AF.Square, accum_out=X.oss[:, 0:1]), reads=[X.t("osum")], writes=[X.t("osq"), X.t("oss")])
            yield
            emit_rstd(c, X.oss, X.oss, X.t("oss"), 1, n=128, tok=X.t("oss"))
            z_, ztok = zt.next()
            c.dma("sp", ztok, z_[:, :], ztm[ch * 128:(ch + 1) * 128, :], writes=[ztok])
            c.op("act", lambda e: e.activation(out=X.sz[:, :], in_=z_[:, :], func=AF.Silu), reads=[ztok], writes=[X.t("sz")])
            yield
            c.op("dve", lambda e: e.scalar_tensor_tensor(out=X.osum[:, :], in0=X.osum[:, :], scalar=X.oss[:, 0:1], in1=gb[:, :], op0=ALU.mult, op1=ALU.mult),
                 reads=[X.t("osum"), X.t("oss"), "gbc"], writes=[X.t("osum")])
            o_, otok = ot.next()
            c.op("dve", lambda e: e.tensor_tensor(out=o_[:, :], in0=X.osum[:, :], in1=X.sz[:, :], op=ALU.mult), reads=[X.t("osum"), X.t("sz")], writes=[otok])
            c.dma("sp", otok, otm[ch * 128:(ch + 1) * 128, :], o_[:, :], reads=[otok], writes=["otm"])
            yield

        def drive(gens):
            gens = [g_ for g_ in gens if g_ is not None]
            while gens:
                alive = []
                for g_ in gens:
                    try:
                        next(g_)
                        alive.append(g_)
                    except StopIteration:
                        pass
                gens = alive

        orders = [list(range(NCH)), list(range(CCH - 1, -1, -1)) + list(range(NCH - 1, CCH - 1, -1))]
        pos = [{ch: i for i, ch in enumerate(o)} for o in orders]
        for X in chains:
            c.op("dve", lambda e: e.memset(X.S32[:, :], 0.0), writes=[X.t("S32")])
            c.op("dve", lambda e: e.memset(X.Sb[:, :], 0.0), writes=[X.t("Sb")])
        drive([pre(orders[d][0], chains[d]) for d in range(2)])
        slots = [chains[d].slot_out for d in range(2)]
        for i in range(NCH):
            gens = []
            for d in range(2):
                if i + 1 < NCH:
                    gens.append(pre(orders[d][i + 1], chains[d]))
            for d in range(2):
                ch = orders[d][i]
                gens.append(rec(ch, chains[d], slots[d], pos[d][ch] > pos[1 - d][ch]))
            drive(gens)
            if i + 1 < NCH:
                slots = [chains[d].slot_out for d in range(2)]
        c.wait_all("sp", ["otm"])
    return nc


def gdn_consts():
    i = np.arange(128)
    I = np.eye(128, dtype=np.float32)
    ones = np.ones((128, 128), np.float32)
    LS = (i[:, None] > i[None, :]).astype(np.float32)
    LI = (i[:, None] >= i[None, :]).astype(np.float32)
    return np.ascontiguousarray(np.stack([I, ones, LS, LI, LS.T, LI.T], 0))


def tile_w(w, m=128):
    w = np.asarray(w, np.float32)
    K, N = w.shape
    nb = (N + m - 1) // m
    if nb * m != N:
        w = np.concatenate([w, np.zeros((K, nb * m - N), np.float32)], 1)
    return np.ascontiguousarray(w.reshape(K // 128, 128, nb, m).transpose(2, 1, 0, 3))


def fm(v):
    return np.ascontiguousarray(np.asarray(v, np.float32).reshape(KC, 128).T)


def lambda_init(layer):
    return 0.8 - 0.6 * math.exp(-0.3 * layer)


PRE_TILES = [(0, 32, 1)] + [(32 + i * 512, 512, 0) for i in range(4)]
POST_LAT = [(258 + off, w + 2, 0, "vl" if off == 0 else "one", "vr" if off == 2040 else "one")
            for off, w in ((0, 510), (510, 510), (1020, 510), (1530, 510), (2040, 8))]
POST_TILES = [(0, 258, 1, "zero", "zero")] + POST_LAT
TL = SEQ // NCORE


def kernel(x, c, ctx, c_ctx, w_mod, b_mod, norm1_g, norm2_g, w_in_even, a_conv_w, a_A_log, a_dt_bias,
           a_norm_g, b_lambda, b_norm_g, w_out_even, w_in_odd, c_q_norm, c_k_norm, w_out_odd,
           ffn_up, ffn_conv_w, ffn_conv_b, ffn_down, final_g):
    f32 = np.float32
    x = np.asarray(x, f32)
    ctx = np.asarray(ctx, f32)
    progs = {}

    def prog(key, fn):
        if key not in progs:
            progs[key] = fn()
        return progs[key]

    c2 = np.ascontiguousarray(np.stack([fm(np.asarray(c, f32)[0]), fm(np.asarray(c_ctx, f32))], -1))
    w_mod = np.asarray(w_mod, f32)
    b_mod = np.asarray(b_mod, f32)
    maps = []
    for j in range(NCORE):
        sl = slice(j * MODC, (j + 1) * MODC)
        bm = np.ascontiguousarray(np.broadcast_to(b_mod[None, :, sl], (2, DEPTH, MODC)))
        maps.append({"c2": c2, "wm": np.stack([tile_w(w_mod[l_][:, sl], 512) for l_ in range(DEPTH)], 0), "bm": bm})
    res = run(prog("mod", build_mod), maps)
    mod = np.concatenate([r["out"] for r in res], -1)

    def modv(s, l, m):
        return mod[s, l, m * D:(m + 1) * D]

    xlT = np.ascontiguousarray(x[0].T)
    xcT = np.ascontiguousarray(ctx[0].T)
    rope = rope_tables(SEQ)
    rt = rope_perm()
    zcol = np.zeros((D, 1), f32)

    for l in range(DEPTH):
        even = l % 2 == 0
        e = l // 2
        last = l == DEPTH - 1
        w_in = np.asarray(w_in_even[e] if even else w_in_odd[e], f32)
        ncols = w_in.shape[1]
        w_in = tile_w(w_in)
        vec = np.ascontiguousarray(np.stack([fm(norm1_g[l]), fm(modv(0, l, 0)), fm(modv(0, l, 1)), fm(modv(1, l, 0)), fm(modv(1, l, 1))], 1))
        maps = [{"xT": np.ascontiguousarray(np.concatenate([xcT[:, 32 * j:32 * (j + 1)], xlT[:, TL * j:TL * (j + 1)]], 1)), "vec": vec, "w": w_in}
                for j in range(NCORE)]
        res = run(prog(("pre", ncols), lambda: build_pre(ncols, PRE_TILES)), maps)
        pc = np.concatenate([r["pT"][:, :32] for r in res], 1)
        pl = np.concatenate([r["pT"][:, 32:] for r in res], 1)
        p = np.concatenate([pc, pl], 1)
        del res, maps
        mT = np.zeros((D, CTX + SEQ), NPBF)
        if even:
            NT = CTX + SEQ
            cw = np.asarray(a_conv_w[e], f32)
            maps = []
            for j in range(NCORE):
                hs = slice(j * 128, (j + 1) * 128)
                qkvT = np.ascontiguousarray(np.stack([p[j * 128:(j + 1) * 128], p[1024 + j * 128:1024 + (j + 1) * 128], p[2048 + j * 128:2048 + (j + 1) * 128]], 0))
                ztm = np.ascontiguousarray(p[3072 + j * 128:3072 + (j + 1) * 128].T)
                gates = np.stack([p[4096 + gi * 8 + j] for gi in range(4)], 0)
                gt = np.ascontiguousarray(gates.reshape(4, NT // 128, 128).transpose(2, 1, 0))
                vec = np.zeros((128, 16), f32)
                vec[:, 0:2] = np.asarray(a_A_log[e], f32)[:, j]
                vec[:, 2:4] = np.asarray(a_dt_bias[e], f32)[:, j]
                for i, off in enumerate((0, 1024, 2048)):
                    for t in range(3):
                        vec[:, 4 + 3 * i + t] = cw[t, off + j * 128:off + (j + 1) * 128]
                gbc = np.ascontiguousarray(np.broadcast_to(np.asarray(a_norm_g[e], f32), (128, 128)))
                maps.append({"qkvT": qkvT, "ztm": ztm, "gt": gt, "vec": vec, "gbc": gbc, "cst": gdn_consts()})
            res = run(prog("gdn", lambda: build_gdn(NT)), maps)
            for j in range(NCORE):
                mT[j * 128:(j + 1) * 128, :] = res[j]["otm"].T
            del res, maps
            HQ = SEQ // 2
            li = lambda_init(l)
            lp = np.ascontiguousarray(np.broadcast_to(np.asarray(b_lambda[e], f32), (128, 4, 128)))
            bng = np.asarray(b_norm_g[e], f32)
            maps = []
            for j in range(NCORE):
                hb, half = j // 2, j % 2
                qrows = [slice(A_IN + hb * 256 + m * 128, A_IN + hb * 256 + (m + 1) * 128) for m in range(2)]
                krows = [slice(A_IN + 1024 + hb * 256 + m * 128, A_IN + 1024 + hb * 256 + (m + 1) * 128) for m in range(2)]
                qT = np.ascontiguousarray(np.stack([np.concatenate([p[r, :CTX], p[r, CTX + half * HQ:CTX + (half + 1) * HQ]], 1) for r in qrows], 0))
                kT = np.ascontiguousarray(np.stack([p[r] for r in krows], 0))
                v = np.ascontiguousarray(p[A_IN + 2048 + hb * 256:A_IN + 2048 + (hb + 1) * 256].T)
                vec = np.zeros((128, 8), f32)
                vec[:, 2] = li
                vec[:, 3] = 1.0 - li
                vec[:, 4] = bng[:128]
                vec[:, 5] = bng[128:]
                maps.append({"qT": qT, "kT": kT, "v": v, "cq": np.ascontiguousarray(rope[:, :, half * HQ:(half + 1) * HQ]), "ck": rope, "rt": rt, "vec": vec, "lp": lp})
            res = run(prog("attB", lambda: build_att("B", HQ, SEQ)), maps)
            for j in range(NCORE):
                hb, half = j // 2, j % 2
                rows = slice(1024 + hb * 256, 1024 + (hb + 1) * 256)
                if half == 0:
                    mT[rows, :CTX] = res[j]["oT"][:, :CTX]
                mT[rows, CTX + half * HQ:CTX + (half + 1) * HQ] = res[j]["oT"][:, CTX:]
            del res, maps
        else:
            lp0 = np.zeros((128, 4, 128), f32)
            maps = []
            for j in range(NCORE):
                g = j // 2
                qT = np.ascontiguousarray(np.stack([p[(2 * j + s) * 128:(2 * j + s + 1) * 128] for s in range(2)], 0))
                kT = np.ascontiguousarray(p[2048 + g * 128:2048 + (g + 1) * 128][None])
                v = np.ascontiguousarray(p[2560 + g * 128:2560 + (g + 1) * 128].T)
                vec = np.zeros((128, 8), f32)
                vec[:, 0] = np.asarray(c_q_norm[e], f32)
                vec[:, 1] = np.asarray(c_k_norm[e], f32)
                maps.append({"qT": qT, "kT": kT, "v": v, "cq": rope, "ck": rope, "rt": rt, "vec": vec, "lp": lp0})
            res = run(prog("attC", lambda: build_att("C", SEQ, SEQ)), maps)
            for j in range(NCORE):
                mT[2 * j * 128:(2 * j + 2) * 128, :] = res[j]["oT"]
            del res, maps
        del p
        vec = np.zeros((128, POST_NV), f32)
        V = POST_V
        vec[:, V["n2g"]:V["n2g"] + 16] = fm(norm2_g[l])
        for nm, s, m in (("g1", 0, 2), ("sh2", 0, 3), ("sc2", 0, 4), ("g2", 0, 5), ("cg1", 1, 2), ("csh2", 1, 3), ("csc2", 1, 4), ("cg2", 1, 5)):
            vec[:, V[nm]:V[nm] + 16] = fm(modv(s, l, m))
        vec[:, V["fg"]:V["fg"] + 16] = fm(final_g)
        vec[:, V["cw"]:V["cw"] + 3 * 88] = np.asarray(ffn_conv_w[l], f32).reshape(3, 88, 128).transpose(2, 0, 1).reshape(128, 264)
        vec[:, V["cb"]:V["cb"] + 88] = np.asarray(ffn_conv_b[l], f32).reshape(88, 128).T
        wo = tile_w(w_out_even[e] if even else w_out_odd[e])
        wu = tile_w(ffn_up[l])
        wd = tile_w(ffn_down[l])
        mcT, mlT = mT[:, :CTX], mT[:, CTX:]
        zb = np.zeros((D, 1), NPBF)
        maps = []
        for j in range(NCORE):
            lo, hi = TL * j, TL * (j + 1)
            xl_ = [zcol if j == 0 else xlT[:, lo - 1:lo], xlT[:, lo:hi], zcol if j == NCORE - 1 else xlT[:, hi:hi + 1]]
            ml_ = [zb if j == 0 else mlT[:, lo - 1:lo], mlT[:, lo:hi], zb if j == NCORE - 1 else mlT[:, hi:hi + 1]]
            vj = vec.copy()
            vj[:, V["vl"]] = 0.0 if j == 0 else 1.0
            vj[:, V["vr"]] = 0.0 if j == NCORE - 1 else 1.0
            maps.append({"xT": np.ascontiguousarray(np.concatenate([zcol, xcT, zcol] + xl_, 1)),
                         "mT": np.ascontiguousarray(np.concatenate([zb, mcT, zb] + ml_, 1)), "vec": vj, "wo": wo, "wu": wu, "wd": wd})
        if not last:
            res = run(prog("post", lambda: build_post(POST_TILES, False)), maps)
            xcT = np.ascontiguousarray(res[0]["oT"][:, :CTX])
            xlT = np.ascontiguousarray(np.concatenate([r["oT"][:, CTX:] for r in res], 1))
        else:
            res = run(prog("postf", lambda: build_post(POST_LAT, True)), maps)
            xlT = np.concatenate([r["oT"] for r in res], 1)
        del res, maps, mT
    return np.ascontiguousarray(xlT.T)[None].astype(np.float32)
```

```python
import math
import os
import sys
from contextlib import ExitStack

import ml_dtypes
import numpy as np
import concourse.bass as bass
import concourse.mybir as mybir
from concourse.bass_utils import run_bass_kernel_spmd

F32 = mybir.dt.float32
BF16 = mybir.dt.bfloat16
AF = mybir.ActivationFunctionType
ALU = mybir.AluOpType
AX = mybir.AxisListType
NPBF = ml_dtypes.bfloat16

D = 2048
KC = 16
NCORE = 8
SEQ = 16384
CTX = 256
DEPTH = 4
EPS = 1e-6
DFF = 5632
FC = 44
A_IN = 4128
EVEN_IN = 7200
ODD_IN = 3072


class Ctx:
    def __init__(self, nc, stack):
        self.nc = nc
        self.stack = stack
        self.E = {"pe": nc.tensor, "act": nc.scalar, "dve": nc.vector, "pool": nc.gpsimd, "sp": nc.sync}
        self.sems = {}
        self.cnt = {}
        self.known = {e: {} for e in self.E}
        self.lastw = {}
        self.readers = {}
        self.ninst = 0

    def sb(self, name, shape, dt):
        return self.stack.enter_context(self.nc.sbuf_tensor("sb_" + name, list(shape), dt))

    def ps(self, name, shape, dt=F32):
        return self.stack.enter_context(self.nc.psum_tensor("ps_" + name, list(shape), dt))

    def sem(self, key):
        if key not in self.sems:
            self.sems[key] = self.stack.enter_context(self.nc.semaphore("s_" + key.replace(":", "_")))
            self.cnt[key] = 0
        return self.sems[key]

    def _waits(self, eng, reads, writes):
        need = {}

        def add(ev):
            if ev is not None and need.get(ev[0], 0) < ev[1]:
                need[ev[0]] = ev[1]

        for t in reads:
            add(self.lastw.get(t))
        for t in writes:
            add(self.lastw.get(t))
            for k, v in self.readers.get(t, {}).items():
                add((k, v))
        E = self.E[eng]
        for k, v in need.items():
            if k == "pe" and eng == "pe":
                continue
            if k.startswith("d:"):
                v = self.cnt[k]
            if self.known[eng].get(k, 0) < v:
                E.wait_ge(self.sems[k], v)
                self.known[eng][k] = v
                self.ninst += 1

    def _record(self, ev, reads, writes):
        k, v = ev
        for t in reads:
            d = self.readers.setdefault(t, {})
            if d.get(k, 0) < v:
                d[k] = v
        for t in writes:
            self.lastw[t] = ev
            self.readers[t] = {}

    def op(self, eng, emit, reads=(), writes=()):
        ex = [t for t in reads if t.startswith("bank")]
        if ex:
            writes = list(writes) + ex
        self._waits(eng, reads, writes)
        s = self.sem(eng)
        ins = emit(self.E[eng])
        ins.then_inc(s, 1)
        self.cnt[eng] += 1
        self.ninst += 1
        self._record((eng, self.cnt[eng]), reads, writes)
        return ins

    def dma(self, eng, stream, out, in_, reads=(), writes=()):
        key = "d:" + stream
        s = self.sem(key)
        self._waits(eng, reads, writes)
        ins = self.E[eng].dma_start(out=out, in_=in_)
        ins.then_inc(s, 16)
        self.cnt[key] += 16
        self.ninst += 1
        self._record((key, self.cnt[key]), reads, writes)
        return ins

    def push_scope(self):
        self._outer = self.stack
        self.stack = ExitStack()

    def pop_scope(self):
        self.barrier()
        self.stack.close()
        self.stack = self._outer

    def barrier(self):
        for eng, E in self.E.items():
            for k, s_ in self.sems.items():
                v = self.cnt[k]
                if v > 0 and self.known[eng].get(k, 0) < v and not (k == eng):
                    E.wait_ge(s_, v)
                    self.known[eng][k] = v
                    self.ninst += 1

    def wait_all(self, eng, tokens):
        self._waits(eng, tokens, ())


class Rot:
    def __init__(self, c, name, n, shape, dt, psum=False):
        self.bufs = [(c.ps if psum else c.sb)("%s%d" % (name, i), shape, dt) for i in range(n)]
        self.names = ["%s%d" % (name, i) for i in range(n)]
        self.i = 0

    def next(self):
        j = self.i % len(self.bufs)
        self.i += 1
        return self.bufs[j], self.names[j]


def new_nc():
    return bass.Bass("TRN2", target_bir_lowering=False)


def run(nc, in_maps, tag=""):
    res = run_bass_kernel_spmd(nc, in_maps, core_ids=list(range(NCORE)))
    if os.environ.get("KDEBUG"):
        for j, r in enumerate(res.results):
            for name, arr in r.items():
                a = np.asarray(arr).astype(np.float32)
                if not np.isfinite(a).all():
                    print("KDEBUG non-finite:", tag, "core", j, name, int((~np.isfinite(a)).sum()), "of", a.size, file=sys.stderr)
    return res.results


def emit_rstd(c, rstd, ss, sstok, W, n=D, tok="rstd"):
    c.op("dve", lambda e: e.tensor_scalar(out=rstd[:, :W], in0=ss[:, :W], scalar1=1.0 / n, scalar2=EPS, op0=ALU.mult, op1=ALU.add),
         reads=[sstok], writes=[tok])
    c.op("act", lambda e: e.activation(out=rstd[:, :W], in_=rstd[:, :W], func=AF.Sqrt), reads=[tok], writes=[tok])
    c.op("dve", lambda e: e.reciprocal(out=rstd[:, :W], in_=rstd[:, :W]), reads=[tok], writes=[tok])


def emit_norm_mod(c, K, xt, xtok, W, a_ap, b_ap, h, htok, maskcols=()):
    ss, sstok = K["ps_ss"].next()
    for kc in range(KC):
        sq, sqtok = K["sq"].next()
        c.op("act", lambda e: e.activation(out=sq[:, :W], in_=xt[:, kc, :W], func=AF.Square), reads=[xtok], writes=[sqtok])
        c.op("pe", lambda e: e.matmul(ss[:, :W], lhsT=K["ones"][:, :], rhs=sq[:, :W], start=(kc == 0), stop=(kc == KC - 1)),
             reads=[sqtok, "ones"], writes=[sstok])
    rstd = K["rstd"]
    emit_rstd(c, rstd, ss, sstok, W)
    for kc in range(KC):
        tmp, tmptok = K["tmp"].next()
        c.op("dve", lambda e: e.scalar_tensor_tensor(out=tmp[:, :W], in0=xt[:, kc, :W], scalar=a_ap[:, kc:kc + 1], in1=rstd[:, :W],
                                                     op0=ALU.mult, op1=ALU.mult), reads=[xtok, "rstd", "vec"], writes=[tmptok])
        c.op("act", lambda e: e.activation(out=h[:, kc, :W], in_=tmp[:, :W], func=AF.Identity, bias=b_ap[:, kc:kc + 1], scale=1.0),
             reads=[tmptok, "vec"], writes=[htok])
    for col, sc in maskcols:
        c.op("dve", lambda e: e.tensor_scalar(out=h[:, :, col:col + 1], in0=h[:, :, col:col + 1], scalar1=sc, scalar2=None, op0=ALU.mult),
             reads=[htok, "vec"], writes=[htok])


def make_consts(c):
    K = {}
    K["ones"] = c.sb("ones", [128, 128], BF16)
    c.op("dve", lambda e: e.memset(K["ones"][:, :], 1.0), writes=["ones"])
    K["sq"] = Rot(c, "sq", 2, [128, 512], BF16)
    K["tmp"] = Rot(c, "tmp", 2, [128, 512], F32)
    K["rstd"] = c.sb("rstd", [128, 512], F32)
    K["ps_ss"] = Rot(c, "ps_ss", 1, [128, 512], F32, psum=True)
    return K


MODC = 1536


def build_mod():
    nc = new_nc()
    c2 = nc.dram_tensor("c2", [128, KC, 2], F32, kind="ExternalInput").ap()
    wm = nc.dram_tensor("wm", [DEPTH, MODC // 512, 128, KC, 512], F32, kind="ExternalInput").ap()
    bm = nc.dram_tensor("bm", [2, DEPTH, MODC], F32, kind="ExternalInput").ap()
    out = nc.dram_tensor("out", [2, DEPTH, MODC], F32, kind="ExternalOutput").ap()
    with ExitStack() as st:
        c = Ctx(nc, st)
        ct = c.sb("ct", [128, KC, 2], F32)
        cs = c.sb("cs", [128, KC, 2], BF16)
        bt = c.sb("bt", [2, DEPTH, MODC], F32)
        ot = c.sb("ot", [2, DEPTH, MODC], F32)
        wrot = Rot(c, "wt", 2, [128, KC, 512], BF16)
        prot = Rot(c, "pm", 2, [2, 512], F32, psum=True)
        c.dma("sp", "c2", ct[:], c2, writes=["ct"])
        c.dma("sp", "bm", bt[:], bm, writes=["bt"])
        c.op("act", lambda e: e.activation(out=cs[:], in_=ct[:], func=AF.Silu), reads=["ct"], writes=["cs"])
        for l in range(DEPTH):
            for n in range(MODC // 512):
                wt, wtok = wrot.next()
                c.dma("pool", wtok, wt[:], wm[l, n], writes=[wtok])
                ps, ptok = prot.next()
                for kc in range(KC):
                    c.op("pe", lambda e: e.matmul(ps[:, :], lhsT=cs[:, kc, :], rhs=wt[:, kc, :], start=(kc == 0), stop=(kc == KC - 1)),
                         reads=["cs", wtok], writes=[ptok])
                c.op("dve", lambda e: e.tensor_tensor(out=ot[:, l, n * 512:(n + 1) * 512], in0=ps[:, :], in1=bt[:, l, n * 512:(n + 1) * 512], op=ALU.add),
                     reads=[ptok, "bt"], writes=["ot"])
        c.dma("sp", "out", out, ot[:], reads=["ot"], writes=["out"])
        c.wait_all("sp", ["out"])
    return nc


def tiles_of(total, w):
    return [(s, min(w, total - s)) for s in range(0, total, w)]


def build_pre(ncols, tiles):
    T = sum(w for _, w, _ in tiles)
    nc = new_nc()
    xT = nc.dram_tensor("xT", [D, T], F32, kind="ExternalInput").ap()
    vec = nc.dram_tensor("vec", [128, 5, KC], F32, kind="ExternalInput").ap()
    w = nc.dram_tensor("w", [(ncols + 127) // 128, 128, KC, 128], F32, kind="ExternalInput").ap()
    pT = nc.dram_tensor("pT", [ncols, T], BF16, kind="ExternalOutput").ap()
    with ExitStack() as st:
        c = Ctx(nc, st)
        K = make_consts(c)
        vt = c.sb("vec", [128, 5, KC], F32)
        av = c.sb("av", [128, 2, KC], F32)
        h = c.sb("h", [128, KC, T], BF16)
        xrot = Rot(c, "xt", 2, [128, KC, 512], F32)
        wrot = Rot(c, "wt", 2, [128, KC, 128], BF16)
        prot = Rot(c, "pp", 3, [128, 512], F32, psum=True)
        orot = Rot(c, "po", 3, [128, 512], BF16)
        c.dma("sp", "vec", vt[:], vec, writes=["vec"])
        for s in range(2):
            c.op("dve", lambda e: e.scalar_tensor_tensor(out=av[:, s, :], in0=vt[:, 2 + 2 * s, :], scalar=1.0, in1=vt[:, 0, :],
                                                         op0=ALU.add, op1=ALU.mult), reads=["vec"], writes=["vec"])
        for (s0, wd, stream) in tiles:
            xt, xtok = xrot.next()
            c.dma("sp", xtok, xt[:, :, :wd], xT[:, s0:s0 + wd].rearrange("(kc p) t -> p kc t", p=128), writes=[xtok])
            emit_norm_mod(c, K, xt, xtok, wd, av[:, stream, :], vt[:, 1 + 2 * stream, :], h[:, :, s0:s0 + wd], "h%d" % s0)
        nst = 0
        for cb0 in range(0, ncols, 128):
            m = min(128, ncols - cb0)
            wt, wtok = wrot.next()
            c.dma("pool", wtok, wt[:], w[cb0 // 128], writes=[wtok])
            for (s0, wd, stream) in tiles:
                ps, ptok = prot.next()
                for kc in range(KC):
                    c.op("pe", lambda e: e.matmul(ps[:m, :wd], lhsT=wt[:, kc, :m], rhs=h[:, kc, s0:s0 + wd], start=(kc == 0), stop=(kc == KC - 1)),
                         reads=[wtok, "h%d" % s0], writes=[ptok])
                ot, otok = orot.next()
                eng = "act" if nst % 2 == 0 else "dve"
                if eng == "act":
                    c.op("act", lambda e: e.activation(out=ot[:m, :wd], in_=ps[:m, :wd], func=AF.Copy), reads=[ptok], writes=[otok])
                else:
                    c.op("dve", lambda e: e.tensor_copy(out=ot[:m, :wd], in_=ps[:m, :wd]), reads=[ptok], writes=[otok])
                nst += 1
                c.dma("sp", otok, pT[cb0:cb0 + m, s0:s0 + wd], ot[:m, :wd], reads=[otok], writes=["pT"])
        c.wait_all("sp", ["pT"])
    return nc


POST_V = {"n2g": 0, "g1": 16, "sh2": 32, "sc2": 48, "g2": 64, "cg1": 80, "csh2": 96, "csc2": 112, "cg2": 128, "fg": 144,
          "cw": 160, "cb": 160 + 3 * 88, "vl": 160 + 4 * 88, "vr": 161 + 4 * 88, "zero": 162 + 4 * 88}
POST_NV = 163 + 4 * 88


def build_post(tiles, final):
    T = max(s + w for s, w, _, _, _ in tiles)
    TO = sum(w - 2 for _, w, _, _, _ in tiles)
    nc = new_nc()
    xT = nc.dram_tensor("xT", [D, T], F32, kind="ExternalInput").ap()
    mT = nc.dram_tensor("mT", [D, T], BF16, kind="ExternalInput").ap()
    vec = nc.dram_tensor("vec", [128, POST_NV], F32, kind="ExternalInput").ap()
    wo = nc.dram_tensor("wo", [KC, 128, KC, 128], F32, kind="ExternalInput").ap()
    wu = nc.dram_tensor("wu", [2 * FC, 128, KC, 128], F32, kind="ExternalInput").ap()
    wdn = nc.dram_tensor("wd", [KC, 128, FC, 128], F32, kind="ExternalInput").ap()
    oT = nc.dram_tensor("oT", [D, TO], F32, kind="ExternalOutput").ap()
    with ExitStack() as st:
        c = Ctx(nc, st)
        K = make_consts(c)
        vt = c.sb("vec", [128, POST_NV], F32)
        av = c.sb("av", [128, 2, KC], F32)
        xrot = Rot(c, "xt", 1, [128, KC, 512], F32)
        mrot = Rot(c, "mt", 1, [128, KC, 512], BF16)
        h2 = c.sb("h2", [128, KC, 512], BF16)
        act = c.sb("actb", [128, FC, 512], BF16)
        worot = Rot(c, "wo", 2, [128, KC, 128], BF16)
        wgrot = Rot(c, "wg", 2, [128, KC, 128], BF16)
        wvrot = Rot(c, "wv", 2, [128, KC, 128], BF16)
        wdrot = Rot(c, "wdn", 2, [128, FC, 128], BF16)
        prot = Rot(c, "pp", 2, [128, 512], F32, psum=True)
        pgrot = Rot(c, "pg", 2, [128, 512], F32, psum=True)
        pvrot = Rot(c, "pv", 2, [128, 512], F32, psum=True)
        cg = Rot(c, "cg", 2, [128, 512], F32)
        cv = Rot(c, "cv", 2, [128, 512], F32)
        sg = Rot(c, "sg", 2, [128, 512], F32)
        yt = Rot(c, "yt", 2, [128, 512], F32)
        c.dma("sp", "vec", vt[:], vec, writes=["vec"])
        V = POST_V
        for s, (scn, gn) in enumerate((("sc2", "n2g"), ("csc2", "n2g"))):
            c.op("dve", lambda e: e.scalar_tensor_tensor(out=av[:, s, :], in0=vt[:, V[scn]:V[scn] + 16], scalar=1.0, in1=vt[:, V[gn]:V[gn] + 16],
                                                         op0=ALU.add, op1=ALU.mult), reads=["vec"], writes=["vec"])
        ocol = 0
        for (s0, wd, stream, lf, rf) in tiles:
            g1 = V["cg1"] if stream else V["g1"]
            g2 = V["cg2"] if stream else V["g2"]
            sh2 = V["csh2"] if stream else V["sh2"]
            wi = wd - 2
            xt, xtok = xrot.next()
            mt, mtok = mrot.next()
            c.dma("sp", xtok, xt[:, :, :wd], xT[:, s0:s0 + wd].rearrange("(kc p) t -> p kc t", p=128), writes=[xtok])
            c.dma("sp", mtok, mt[:, :, :wd], mT[:, s0:s0 + wd].rearrange("(kc p) t -> p kc t", p=128), writes=[mtok])
            for oc in range(KC):
                wt, wtok = worot.next()
                c.dma("pool", wtok, wt[:], wo[oc], writes=[wtok])
                ps, ptok = prot.next()
                for kc in range(KC):
                    c.op("pe", lambda e: e.matmul(ps[:, :wd], lhsT=wt[:, kc, :], rhs=mt[:, kc, :wd], start=(kc == 0), stop=(kc == KC - 1)),
                         reads=[wtok, mtok], writes=[ptok])
                c.op("dve", lambda e: e.scalar_tensor_tensor(out=xt[:, oc, :wd], in0=ps[:, :wd], scalar=vt[:, g1 + oc:g1 + oc + 1], in1=xt[:, oc, :wd],
                                                             op0=ALU.mult, op1=ALU.add), reads=[ptok, xtok, "vec"], writes=[xtok])
            masks = []
            if lf != "one":
                masks.append((0, vt[:, V[lf]:V[lf] + 1]))
            if rf != "one":
                masks.append((wd - 1, vt[:, V[rf]:V[rf] + 1]))
            emit_norm_mod(c, K, xt, xtok, wd, av[:, stream, :], vt[:, sh2:sh2 + 16], h2, "h2", maskcols=masks)
            for f in range(FC):
                wg, wgtok = wgrot.next()
                wv, wvtok = wvrot.next()
                c.dma("pool", wgtok, wg[:], wu[f], writes=[wgtok])
                c.dma("pool", wvtok, wv[:], wu[FC + f], writes=[wvtok])
                pg, pgtok = pgrot.next()
                pv, pvtok = pvrot.next()
                for kc in range(KC):
                    c.op("pe", lambda e: e.matmul(pg[:, :wd], lhsT=wg[:, kc, :], rhs=h2[:, kc, :wd], start=(kc == 0), stop=(kc == KC - 1)),
                         reads=[wgtok, "h2"], writes=[pgtok])
                for kc in range(KC):
                    c.op("pe", lambda e: e.matmul(pv[:, :wd], lhsT=wv[:, kc, :], rhs=h2[:, kc, :wd], start=(kc == 0), stop=(kc == KC - 1)),
                         reads=[wvtok, "h2"], writes=[pvtok])
                outs = []
                for (pp, pptok, rot, fi) in ((pg, pgtok, cg, f), (pv, pvtok, cv, FC + f)):
                    t, ttok = rot.next()
                    cw0 = V["cw"] + 0 * 88 + fi
                    cw1 = V["cw"] + 1 * 88 + fi
                    cw2 = V["cw"] + 2 * 88 + fi
                    c.op("dve", lambda e: e.tensor_scalar(out=t[:, :wi], in0=pp[:, 0:wi], scalar1=vt[:, cw0:cw0 + 1], scalar2=None, op0=ALU.mult),
                         reads=[pptok, "vec"], writes=[ttok])
                    c.op("dve", lambda e: e.scalar_tensor_tensor(out=t[:, :wi], in0=pp[:, 1:wi + 1], scalar=vt[:, cw1:cw1 + 1], in1=t[:, :wi],
                                                                 op0=ALU.mult, op1=ALU.add), reads=[pptok, ttok, "vec"], writes=[ttok])
                    c.op("dve", lambda e: e.scalar_tensor_tensor(out=t[:, :wi], in0=pp[:, 2:wi + 2], scalar=vt[:, cw2:cw2 + 1], in1=t[:, :wi],
                                                                 op0=ALU.mult, op1=ALU.add), reads=[pptok, ttok, "vec"], writes=[ttok])
                    outs.append((t, ttok))
                (tg, tgtok), (tv, tvtok) = outs
                s_, stok = sg.next()
                cbg = V["cb"] + f
                cbv = V["cb"] + FC + f
                c.op("act", lambda e: e.activation(out=s_[:, :wi], in_=tg[:, :wi], func=AF.Silu, bias=vt[:, cbg:cbg + 1], scale=1.0),
                     reads=[tgtok, "vec"], writes=[stok])
                c.op("dve", lambda e: e.scalar_tensor_tensor(out=act[:, f, :wi], in0=tv[:, :wi], scalar=vt[:, cbv:cbv + 1], in1=s_[:, :wi],
                                                              op0=ALU.add, op1=ALU.mult), reads=[tvtok, stok, "vec"], writes=["act%d" % f])
            for oc in range(KC):
                wt, wtok = wdrot.next()
                c.dma("pool", wtok, wt[:], wdn[oc], writes=[wtok])
                ps, ptok = prot.next()
                for f in range(FC):
                    c.op("pe", lambda e: e.matmul(ps[:, :wi], lhsT=wt[:, f, :], rhs=act[:, f, :wi], start=(f == 0), stop=(f == FC - 1)),
                         reads=[wtok, "act%d" % f], writes=[ptok])
                c.op("dve", lambda e: e.scalar_tensor_tensor(out=xt[:, oc, 1:wi + 1], in0=ps[:, :wi], scalar=vt[:, g2 + oc:g2 + oc + 1], in1=xt[:, oc, 1:wi + 1],
                                                             op0=ALU.mult, op1=ALU.add), reads=[ptok, xtok, "vec"], writes=[xtok])
            if not final:
                c.dma("sp", "oT", oT[:, ocol:ocol + wi].rearrange("(kc p) t -> p kc t", p=128), xt[:, :, 1:wi + 1], reads=[xtok], writes=["oT"])
            else:
                ss, sstok = K["ps_ss"].next()
                for kc in range(KC):
                    sq, sqtok = K["sq"].next()
                    c.op("act", lambda e: e.activation(out=sq[:, :wi], in_=xt[:, kc, 1:wi + 1], func=AF.Square), reads=[xtok], writes=[sqtok])
                    c.op("pe", lambda e: e.matmul(ss[:, :wi], lhsT=K["ones"][:, :], rhs=sq[:, :wi], start=(kc == 0), stop=(kc == KC - 1)),
                         reads=[sqtok, "ones"], writes=[sstok])
                rstd = K["rstd"]
                emit_rstd(c, rstd, ss, sstok, wi)
                for kc in range(KC):
                    y, ytok = yt.next()
                    fg = V["fg"] + kc
                    c.op("dve", lambda e: e.scalar_tensor_tensor(out=y[:, :wi], in0=xt[:, kc, 1:wi + 1], scalar=vt[:, fg:fg + 1], in1=rstd[:, :wi],
                                                                 op0=ALU.mult, op1=ALU.mult), reads=[xtok, "rstd", "vec"], writes=[ytok])
                    c.dma("sp", ytok, oT[kc * 128:(kc + 1) * 128, ocol:ocol + wi], y[:, :wi], reads=[ytok], writes=["oT"])
            ocol += wi
        c.wait_all("sp", ["oT"])
    return nc


ATT_ACC2 = "pool"


def build_att(kind, NQL, NKL, NCT=CTX):
    S = 2
    SK = 2 if kind == "B" else 1
    DV = 256 if kind == "B" else 128
    NH = DV // 128
    NK = NCT + NKL
    NKT = NK // 128
    NCKT = NCT // 128
    R = 256
    scale = 128 ** -0.5
    nc = new_nc()
    qT = nc.dram_tensor("qT", [S, 128, NCT + NQL], BF16, kind="ExternalInput").ap()
    kT = nc.dram_tensor("kT", [SK, 128, NK], BF16, kind="ExternalInput").ap()
    v = nc.dram_tensor("v", [NK, DV], BF16, kind="ExternalInput").ap()
    cq = nc.dram_tensor("cq", [2, 128, NQL], F32, kind="ExternalInput").ap()
    ck = nc.dram_tensor("ck", [2, 128, NKL], F32, kind="ExternalInput").ap()
    rt = nc.dram_tensor("rt", [128, 128], BF16, kind="ExternalInput").ap()
    vec = nc.dram_tensor("vec", [128, 8], F32, kind="ExternalInput").ap()
    lp = nc.dram_tensor("lp", [128, 4, 128], F32, kind="ExternalInput").ap()
    oT = nc.dram_tensor("oT", [R, NCT + NQL], BF16, kind="ExternalOutput").ap()
    with ExitStack() as st:
        c = Ctx(nc, st)
        ones = c.sb("ones", [128, 128], BF16)
        c.op("dve", lambda e: e.memset(ones[:, :], 1.0), writes=["ones"])
        rtt = c.sb("rtt", [128, 128], BF16)
        vt = c.sb("vec", [128, 8], F32)
        lpt = c.sb("lpt", [128, 4, 128], F32)
        lam = c.sb("lam", [128, 8], F32)
        Kr = c.sb("Kr", [128, SK, NK], BF16)
        Vt = c.sb("Vt", [128, NKT, DV], BF16)
        raw = Rot(c, "raw", 2, [128, 512], BF16)
        cst = Rot(c, "cst", 2, [128, 2, 512], F32)
        xn = c.sb("xn", [128, 512], F32)
        xnb = c.sb("xnb", [128, 512], BF16)
        sqb = c.sb("sqb", [128, 512], BF16)
        rstd = c.sb("rstd", [128, 512], F32)
        t1 = c.sb("t1", [128, 512], F32)
        t2 = c.sb("t2", [128, 512], F32)
        qr = Rot(c, "qr", 2, [128, 512], BF16)
        E = Rot(c, "E", 3, [128, 2, 512], BF16)
        acc = [c.sb("acc%d" % i, [128, 2, 512], F32) for i in range(2)]
        ones32 = c.sb("ones32", [128, 128], F32)
        c.op("dve", lambda e: e.memset(ones32[:, :], 1.0), writes=["ones32"])
        sqb2 = c.sb("sqb2", [128, 512], BF16)
        rstd2 = c.sb("rstd2", [128, 512], F32)
        on = [[c.sb("on%d%d" % (s, h), [128, 512], F32) for h in range(NH)] for s in range(S)]
        rec = c.sb("rec", [128, 512], F32)
        ob = Rot(c, "ob", 2, [128, 512], BF16)
        ps_s = Rot(c, "pS", 2, [128, 2, 512], F32, psum=True)
        ps_o = [c.ps("pO%d" % h, [128, 512]) for h in range(NH)]
        ps_sum = c.ps("pSum", [128, 512])
        ps_ss2 = ps_sum
        ps_ss = c.ps("pSS", [128, 512])
        ps_rot = ps_ss
        c.dma("sp", "rtt", rtt[:], rt, writes=["rtt"])
        c.dma("sp", "vec", vt[:], vec, writes=["vec"])
        c.dma("sp", "Vt", Vt[:], v.rearrange("(kt p) d -> p kt d", p=128), writes=["Vt"])
        if kind == "B":
            c.dma("sp", "lpt", lpt[:], lp, writes=["lpt"])
            for i in range(2):
                c.op("dve", lambda e: e.tensor_tensor(out=t1[:, :128], in0=lpt[:, 2 * i, :], in1=lpt[:, 2 * i + 1, :], op=ALU.mult), reads=["lpt"], writes=["t1"])
                c.op("dve", lambda e: e.reduce_sum(out=lam[:, i:i + 1], in_=t1[:, :128], axis=AX.X), reads=["t1"], writes=["lam"])
            c.op("act", lambda e: e.activation(out=lam[:, 0:2], in_=lam[:, 0:2], func=AF.Exp), reads=["lam"], writes=["lam"])
            c.op("dve", lambda e: e.tensor_tensor(out=lam[:, 2:3], in0=lam[:, 0:1], in1=lam[:, 1:2], op=ALU.subtract), reads=["lam"], writes=["lam"])
            c.op("dve", lambda e: e.tensor_tensor(out=lam[:, 2:3], in0=lam[:, 2:3], in1=vt[:, 2:3], op=ALU.add), reads=["lam", "vec"], writes=["lam"])
            c.op("dve", lambda e: e.tensor_scalar(out=lam[:, 3:4], in0=lam[:, 2:3], scalar1=-1.0, scalar2=None, op0=ALU.mult), reads=["lam"], writes=["lam"])
            c.op("dve", lambda e: e.tensor_scalar(out=lam[:, 4:6], in0=vt[:, 4:6], scalar1=vt[:, 3:4], scalar2=None, op0=ALU.mult), reads=["lam", "vec"], writes=["lam"])

        def prep(src, srctok, W, cs, cstok, gain_col, dst, dsttok):
            cur, curtok = src, srctok
            if gain_col is not None:
                c.op("act", lambda e: e.activation(out=sqb[:, :W], in_=src[:, :W], func=AF.Square), reads=[srctok], writes=["sqb"])
                c.op("pe", lambda e: e.matmul(ps_ss[:, :W], lhsT=ones[:, :], rhs=sqb[:, :W], start=True, stop=True), reads=["sqb", "ones"], writes=["pSS"])
                emit_rstd(c, rstd, ps_ss, "pSS", W, n=128)
                c.op("dve", lambda e: e.scalar_tensor_tensor(out=xn[:, :W], in0=src[:, :W], scalar=vt[:, gain_col:gain_col + 1], in1=rstd[:, :W],
                                                             op0=ALU.mult, op1=ALU.mult), reads=[srctok, "rstd", "vec"], writes=["xn"])
                cur, curtok = xn, "xn"
                if cs is None:
                    c.op("act", lambda e: e.activation(out=dst[:, :W], in_=xn[:, :W], func=AF.Copy), reads=["xn"], writes=[dsttok])
                    return
                c.op("act", lambda e: e.activation(out=xnb[:, :W], in_=xn[:, :W], func=AF.Copy), reads=["xn"], writes=["xnb"])
                curb, curbtok = xnb, "xnb"
            else:
                if cs is None:
                    c.op("act", lambda e: e.activation(out=dst[:, :W], in_=src[:, :W], func=AF.Copy), reads=[srctok], writes=[dsttok])
                    return
                curb, curbtok = src, srctok
            c.op("pe", lambda e: e.matmul(ps_rot[:, :W], lhsT=rtt[:, :], rhs=curb[:, :W], start=True, stop=True), reads=["rtt", curbtok], writes=["pSS"])
            c.op("dve", lambda e: e.tensor_tensor(out=t1[:, :W], in0=cur[:, :W], in1=cs[:, 0, :W], op=ALU.mult), reads=[curtok, cstok], writes=["t1"])
            c.op("dve", lambda e: e.tensor_tensor(out=t2[:, :W], in0=ps_rot[:, :W], in1=cs[:, 1, :W], op=ALU.mult), reads=["pSS", cstok], writes=["t2"])
            c.op("dve", lambda e: e.tensor_tensor(out=dst[:, :W], in0=t1[:, :W], in1=t2[:, :W], op=ALU.add), reads=["t1", "t2"], writes=[dsttok])

        kgain = 1 if kind == "C" else None
        qgain = 0 if kind == "C" else None
        ktiles = [(0, NCT, None)] + [(NCT + s0, w, s0) for s0, w in tiles_of(NKL, 512)]
        for sk in range(SK):
            for (c0, w, r0) in ktiles:
                rw, rwtok = raw.next()
                c.dma("sp", rwtok, rw[:, :w], kT[sk, :, c0:c0 + w], writes=[rwtok])
                cs, cstok = None, None
                if r0 is not None:
                    cs, cstok = cst.next()
                    c.dma("sp", cstok, cs[:, :, :w], ck[:, :, r0:r0 + w].rearrange("a p t -> p a t"), writes=[cstok])
                prep(rw, rwtok, w, cs, cstok, kgain, Kr[:, sk, c0:c0 + w], "Kr%d_%d" % (sk, c0))
        ktoks = [["Kr%d_%d" % (sk, c0) for (c0, w, r0) in ktiles] for sk in range(SK)]
        qtiles = [(0, NCT, None, NCKT)] + [(NCT + s0, w, s0, NKT) for s0, w in tiles_of(NQL, 512)]
        units = [(qi, s) for qi in range(len(qtiles)) for s in range(S)]
        cs_of = {}

        def prep_unit(u):
            qi, s = units[u]
            c0, w, r0, nkt = qtiles[qi]
            if s == 0:
                cs, cstok = None, None
                if r0 is not None:
                    cs, cstok = cst.next()
                    c.dma("sp", cstok, cs[:, :, :w], cq[:, :, r0:r0 + w].rearrange("a p t -> p a t"), writes=[cstok])
                cs_of[qi] = (cs, cstok)
            cs, cstok = cs_of[qi]
            rw, rwtok = raw.next()
            c.dma("sp", rwtok, rw[:, :w], qT[s, :, c0:c0 + w], writes=[rwtok])
            q, qtok = qr.next()
            prep(rw, rwtok, w, cs, cstok, qgain, q, qtok)
            return q, qtok

        nxt = prep_unit(0)
        for u, (qi, s) in enumerate(units):
            c0, w, r0, nkt = qtiles[qi]
            q, qtok = nxt
            if u + 1 < len(units):
                nxt = prep_unit(u + 1)
            sk = s if SK == 2 else 0
            pend = {}
            npair = nkt // 2

            def score(p_):
                ps, pstok = ps_s.next()
                for j in range(2):
                    kt = 2 * p_ + j
                    c.op("pe", lambda e: e.matmul(ps[:, j, :w], lhsT=Kr[:, sk, kt * 128:(kt + 1) * 128], rhs=q[:, :w], start=True, stop=True),
                         reads=ktoks[sk] + [qtok], writes=[pstok])
                pend[p_] = (ps, pstok)

            score(0)
            for p_ in range(npair):
                ps, pstok = pend.pop(p_)
                e_, etok = E.next()
                c.op("act", lambda e: e.activation(out=e_[:, :, :w], in_=ps[:, :, :w], func=AF.Exp, scale=scale), reads=[pstok], writes=[etok])
                if p_ + 1 < npair:
                    score(p_ + 1)
                for j in range(2):
                    kt = 2 * p_ + j
                    for h in range(NH):
                        c.op("pe", lambda e: e.matmul(ps_o[h][:, :w], lhsT=Vt[:, kt, h * 128:(h + 1) * 128], rhs=e_[:, j, :w], start=(kt == 0), stop=(kt == nkt - 1)),
                             reads=["Vt", etok], writes=["pO%d" % h])
                ac, actok = (acc[0], "acc0") if p_ % 2 == 0 else (acc[1], "acc1")
                if p_ < 2:
                    c.op("dve", lambda e: e.tensor_copy(out=ac[:, :, :w], in_=e_[:, :, :w]), reads=[etok], writes=[actok])
                else:
                    c.op("dve", lambda e: e.tensor_tensor(out=ac[:, :, :w], in0=ac[:, :, :w], in1=e_[:, :, :w], op=ALU.add), reads=[etok, actok], writes=[actok])
            if npair > 1:
                c.op("dve", lambda e: e.tensor_tensor(out=acc[0][:, :, :w], in0=acc[0][:, :, :w], in1=acc[1][:, :, :w], op=ALU.add), reads=["acc0", "acc1"], writes=["acc0"])
            c.op("dve", lambda e: e.tensor_tensor(out=acc[0][:, 0, :w], in0=acc[0][:, 0, :w], in1=acc[0][:, 1, :w], op=ALU.add), reads=["acc0"], writes=["acc0"])
            c.op("pe", lambda e: e.matmul(ps_sum[:, :w], lhsT=ones32[:, :], rhs=acc[0][:, 0, :w], start=True, stop=True), reads=["ones32", "acc0"], writes=["pSum"])
            c.op("dve", lambda e: e.reciprocal(out=rec[:, :w], in_=ps_sum[:, :w]), reads=["pSum"], writes=["rec"])
            for h in range(NH):
                c.op("dve", lambda e: e.tensor_tensor(out=on[s][h][:, :w], in0=ps_o[h][:, :w], in1=rec[:, :w], op=ALU.mult),
                     reads=["pO%d" % h, "rec"], writes=["on%d%d" % (s, h)])
            if kind == "C":
                o_, otok = ob.next()
                c.op("act", lambda e: e.activation(out=o_[:, :w], in_=on[s][0][:, :w], func=AF.Copy), reads=["on%d0" % s], writes=[otok])
                c.dma("sp", otok, oT[s * 128:(s + 1) * 128, c0:c0 + w], o_[:, :w], reads=[otok], writes=["oT"])
            if kind == "B" and s == S - 1:
                for h in range(NH):
                    c.op("dve", lambda e: e.scalar_tensor_tensor(out=on[0][h][:, :w], in0=on[1][h][:, :w], scalar=lam[:, 3:4], in1=on[0][h][:, :w],
                                                                 op0=ALU.mult, op1=ALU.add), reads=["on1%d" % h, "on0%d" % h, "lam"], writes=["on0%d" % h])
                    c.op("act", lambda e: e.activation(out=sqb2[:, :w], in_=on[0][h][:, :w], func=AF.Square), reads=["on0%d" % h], writes=["sqb2"])
                    c.op("pe", lambda e: e.matmul(ps_ss2[:, :w], lhsT=ones[:, :], rhs=sqb2[:, :w], start=(h == 0), stop=(h == NH - 1)),
                         reads=["sqb2", "ones"], writes=["pSum"])
                emit_rstd(c, rstd2, ps_ss2, "pSum", w, n=256, tok="rstd2")
                for h in range(NH):
                    o_, otok = ob.next()
                    c.op("dve", lambda e: e.scalar_tensor_tensor(out=o_[:, :w], in0=on[0][h][:, :w], scalar=lam[:, 4 + h:5 + h], in1=rstd2[:, :w],
                                                                 op0=ALU.mult, op1=ALU.mult), reads=["on0%d" % h, "rstd2", "lam"], writes=[otok])
                    c.dma("sp", otok, oT[h * 128:(h + 1) * 128, c0:c0 + w], o_[:, :w], reads=[otok], writes=["oT"])
        c.wait_all("sp", ["oT"])
    return nc


def rope_tables(n):
    freqs = (10000.0 ** (-np.arange(0, 64, 2, dtype=np.float32) / 64)).astype(np.float32)
    t = np.arange(n)
    row = (t // 64).astype(np.float32)
    col = (t % 64).astype(np.float32)
    ang_r = row[:, None] * freqs
    ang_c = col[:, None] * freqs
    ang = np.concatenate([ang_r, ang_r, ang_c, ang_c], -1).astype(np.float32)
    cos = np.cos(ang).astype(np.float32)
    sin = np.sin(ang).astype(np.float32)
    sgn = np.ones(128, np.float32)
    sgn[0:32] = -1
    sgn[64:96] = -1
    return np.ascontiguousarray(np.stack([cos.T, (sin * sgn).T], 0))


def rope_perm():
    m = np.arange(128)
    partner = np.where((m // 32) % 2 == 0, m + 32, m - 32)
    rt = np.zeros((128, 128), np.float32)
    rt[partner, m] = 1.0
    return rt.astype(NPBF)


GDN_FP32R = False
GDN_INV_BF16 = False


def build_gdn(NT, NCT=CTX):
    NCH = NT // 128
    CCH = NCT // 128
    nc = new_nc()
    qkvT = nc.dram_tensor("qkvT", [3, 128, NT], BF16, kind="ExternalInput").ap()
    ztm = nc.dram_tensor("ztm", [NT, 128], BF16, kind="ExternalInput").ap()
    gt = nc.dram_tensor("gt", [128, NCH, 4], BF16, kind="ExternalInput").ap()
    vec = nc.dram_tensor("vec", [128, 16], F32, kind="ExternalInput").ap()
    gbc = nc.dram_tensor("gbc", [128, 128], F32, kind="ExternalInput").ap()
    cst = nc.dram_tensor("cst", [6, 128, 128], F32, kind="ExternalInput").ap()
    otm = nc.dram_tensor("otm", [NT, 128], BF16, kind="ExternalOutput").ap()
    with ExitStack() as st:
        c = Ctx(nc, st)
        vt = c.sb("vec", [128, 16], F32)
        gb = c.sb("gbc", [128, 128], F32)
        C32 = c.sb("cst", [128, 6, 128], F32)
        Ib = c.sb("Ib", [128, 128], BF16)
        onesb = c.sb("onesb", [128, 128], BF16)
        qT = c.sb("qT", [128, NT], BF16)
        kT = c.sb("kT", [128, NT], BF16)
        ktm = c.sb("ktm", [128, NCH, 128], BF16)
        vtm = c.sb("vtm", [128, NCH, 128], BF16)
        of = c.sb("of", [128, NCH, 128], BF16)
        G = [c.sb("G%d" % d, [128, NCH], F32) for d in range(2)]
        BT = [c.sb("BT%d" % d, [128, NCH], F32) for d in range(2)]
        NB = [c.sb("NB%d" % d, [128, NCH], F32) for d in range(2)]
        Bc = [c.sb("Bc%d" % d, [128, NCH], F32) for d in range(2)]
        EB = [c.sb("EB%d" % d, [128, NCH], F32) for d in range(2)]
        BEB = [c.sb("BEB%d" % d, [128, NCH], F32) for d in range(2)]
        EKD = [c.sb("EKD%d" % d, [128, NCH], F32) for d in range(2)]
        CD = [c.sb("CD%d" % d, [128, NCH], F32) for d in range(2)]
        tri = c.sb("tri", [128, 2, 128], F32)
        zt = Rot(c, "zt", 2, [128, 128], BF16)
        ot = Rot(c, "ot", 2, [128, 128], BF16)
        banks = [c.ps("bank%d" % i, [128, 512]) for i in range(8)]

        def carve(b, i, n=1):
            return banks[b][:, 128 * i:128 * (i + n)]

        class RotAP:
            def __init__(self, aps, names):
                self.bufs, self.names, self.i = aps, names, 0

            def next(self):
                j = self.i % len(self.bufs)
                self.i += 1
                return self.bufs[j], self.names[j]

        pbig = banks[0]
        pG = banks[1]
        pT = RotAP([carve(1, 0), carve(1, 1)], ["bank1", "bank1"])

        class Chain:
            def __init__(self, d):
                n = "c%d" % d
                self.d = d
                self.oss = c.sb("oss" + n, [128, 4], F32)
                IDT = BF16 if GDN_INV_BF16 else (mybir.dt.float32r if GDN_FP32R else F32)
                self.Rm = c.sb("Rm" + n, [128, 128], IDT)
                for nm in ("diagb", "nd", "dm", "dmS", "dmI", "S32", "o2s", "osum", "osq", "sz"):
                    setattr(self, nm, c.sb(nm + n, [128, 128], F32))
                for nm in ("Rb", "qk", "kb", "vb", "Sb", "u_b"):
                    setattr(self, nm, c.sb(nm + n, [128, 128], BF16))
                self.Pm = Rot(c, "Pm" + n, 2, [128, 128], IDT)
                self.Qm = Rot(c, "Qm" + n, 2, [128, 128], IDT)
                self.qkT = Rot(c, "qkT" + n, 2, [128, 128], BF16)
                self.kd = Rot(c, "kd" + n, 2, [128, 128], BF16)
                self.ub = Rot(c, "ub" + n, 2, [128, 128], F32)
                self.wT = Rot(c, "wT" + n, 2, [128, 128], BF16)
                b0 = 4 * d
                self.bA, self.bB, self.bC, self.bD = ["bank%d" % (b0 + k) for k in range(4)]
                self.pA, self.pKK, self.pQK, self.pU = [carve(b0, k) for k in range(4)]
                self.pT = RotAP([carve(b0 + 1, 0), carve(b0 + 1, 1)], [self.bB, self.bB])
                self.pW = carve(b0 + 1, 2)
                self.pI = RotAP([carve(b0 + 2, k) for k in range(3)], [self.bC] * 3)
                self.pwS, self.pO1, self.pO2, self.pdS = [carve(b0 + 3, k) for k in range(4)]

            def t(self, nm):
                return "%sc%d" % (nm, self.d)

        c.push_scope()
        gtr = c.sb("gtr", [128, NCH, 4], BF16)
        gx = c.sb("gx", [128, NCH], F32)
        rb = c.sb("rb", [128, 3, 514], BF16)
        cv = c.sb("cv", [128, 514], F32)
        sv = c.sb("sv", [128, 514], F32)
        sqb = c.sb("sqb", [128, 512], BF16)
        rstd = c.sb("rstd", [128, 512], F32)
        vTb = c.sb("vTb", [128, 512], BF16)

        c.dma("sp", "vec", vt[:], vec, writes=["vec"])
        c.dma("sp", "gbc", gb[:], gbc, writes=["gbc"])
        c.dma("sp", "cst", C32[:], cst.rearrange("a p f -> p a f"), writes=["cst"])
        c.dma("sp", "gtr", gtr[:], gt, writes=["gtr"])
        I32 = C32[:, 0, :]
        ones32 = C32[:, 1, :]
        MS = [C32[:, 2, :], C32[:, 4, :]]
        MI = [C32[:, 3, :], C32[:, 5, :]]
        c.op("act", lambda e: e.activation(out=Ib[:, :], in_=C32[:, 0, :], func=AF.Copy), reads=["cst"], writes=["Ib"])
        c.op("act", lambda e: e.activation(out=onesb[:, :], in_=C32[:, 1, :], func=AF.Copy), reads=["cst"], writes=["onesb"])
        c.op("dve", lambda e: e.tensor_copy(out=tri[:, 0, :], in_=C32[:, 5, :]), reads=["cst"], writes=["tri"])
        c.op("dve", lambda e: e.tensor_copy(out=tri[:, 1, :], in_=C32[:, 3, :]), reads=["cst"], writes=["tri"])
        c.op("act", lambda e: e.activation(out=vt[:, 13:15], in_=vt[:, 0:2], func=AF.Exp), reads=["vec"], writes=["vec"])
        c.op("dve", lambda e: e.tensor_scalar(out=vt[:, 13:15], in0=vt[:, 13:15], scalar1=-1.0, scalar2=None, op0=ALU.mult), reads=["vec"], writes=["vec"])
        for d in range(2):
            c.op("act", lambda e: e.activation(out=gx[:, :], in_=gtr[:, :, d], func=AF.Exp, bias=vt[:, 2 + d:3 + d], scale=1.0), reads=["gtr", "vec"], writes=["gx"])
            c.op("act", lambda e: e.activation(out=gx[:, :], in_=gx[:, :], func=AF.Ln, bias=1.0, scale=1.0), reads=["gx"], writes=["gx"])
            c.op("dve", lambda e: e.tensor_scalar(out=G[d][:, :], in0=gx[:, :], scalar1=vt[:, 13 + d:14 + d], scalar2=None, op0=ALU.mult), reads=["gx", "vec"], writes=["G%d" % d])
            c.op("act", lambda e: e.activation(out=BT[d][:, :], in_=gtr[:, :, 2 + d], func=AF.Sigmoid), reads=["gtr"], writes=["BT%d" % d])
            c.op("dve", lambda e: e.tensor_scalar(out=NB[d][:, :], in0=BT[d][:, :], scalar1=-1.0, scalar2=None, op0=ALU.mult), reads=["BT%d" % d], writes=["NB%d" % d])
            c.op("pe", lambda e: e.matmul(pG[:, 0:NCH], lhsT=tri[:, d, :], rhs=G[d][:, :], start=True, stop=True), reads=["tri", "G%d" % d], writes=["bank1"])
            c.op("pe", lambda e: e.matmul(pG[:, 256:256 + NCH], lhsT=C32[:, 1, :], rhs=G[d][:, :], start=True, stop=True), reads=["cst", "G%d" % d], writes=["bank1"])
            c.op("dve", lambda e: e.tensor_copy(out=Bc[d][:, :], in_=pG[:, 0:NCH]), reads=["bank1"], writes=["Bc%d" % d])
            c.op("act", lambda e: e.activation(out=EB[d][:, :], in_=pG[:, 0:NCH], func=AF.Exp), reads=["bank1"], writes=["EB%d" % d])
            c.op("act", lambda e: e.activation(out=CD[d][:, :], in_=pG[:, 256:256 + NCH], func=AF.Exp), reads=["bank1"], writes=["CD%d" % d])
            c.op("dve", lambda e: e.tensor_tensor(out=EKD[d][:, :], in0=pG[:, 256:256 + NCH], in1=Bc[d][:, :], op=ALU.subtract), reads=["bank1", "Bc%d" % d], writes=["EKD%d" % d])
            c.op("act", lambda e: e.activation(out=EKD[d][:, :], in_=EKD[d][:, :], func=AF.Exp), reads=["EKD%d" % d], writes=["EKD%d" % d])
            c.op("dve", lambda e: e.tensor_tensor(out=BEB[d][:, :], in0=BT[d][:, :], in1=EB[d][:, :], op=ALU.mult), reads=["BT%d" % d, "EB%d" % d], writes=["BEB%d" % d])
        gtoks = ["Bc0", "Bc1", "EB0", "EB1", "CD0", "CD1", "EKD0", "EKD1", "BEB0", "BEB1", "BT0", "BT1", "NB0", "NB1"]

        segs = [(0, NCT)] + [(NCT + s0, w) for s0, w in tiles_of(NT - NCT, 512)]
        seq_lo = {0: 0}
        for (c0, w) in segs:
            lo_edge = (c0 == 0) or (c0 == NCT)
            hi_edge = (c0 + w == NCT) or (c0 + w == NT)
            a0 = c0 if lo_edge else c0 - 1
            a1 = c0 + w if hi_edge else c0 + w + 1
            if lo_edge:
                c.op("dve", lambda e: e.memset(rb[:, :, 0:1], 0.0), writes=["rb"])
            if hi_edge:
                c.op("dve", lambda e: e.memset(rb[:, :, w + 1:w + 2], 0.0), writes=["rb"])
            o0 = 1 if lo_edge else 0
            c.dma("sp", "rb", rb[:, :, o0:o0 + (a1 - a0)], qkvT[:, :, a0:a1].rearrange("a p t -> p a t"), writes=["rb"])
            for i in range(3):
                t0 = 4 + 3 * i
                c.op("dve", lambda e: e.tensor_scalar(out=cv[:, :w], in0=rb[:, i, 0:w], scalar1=vt[:, t0:t0 + 1], scalar2=None, op0=ALU.mult), reads=["rb", "vec"], writes=["cv"])
                c.op("dve", lambda e: e.scalar_tensor_tensor(out=cv[:, :w], in0=rb[:, i, 1:w + 1], scalar=vt[:, t0 + 1:t0 + 2], in1=cv[:, :w], op0=ALU.mult, op1=ALU.add),
                     reads=["rb", "vec", "cv"], writes=["cv"])
                c.op("dve", lambda e: e.scalar_tensor_tensor(out=cv[:, :w], in0=rb[:, i, 2:w + 2], scalar=vt[:, t0 + 2:t0 + 3], in1=cv[:, :w], op0=ALU.mult, op1=ALU.add),
                     reads=["rb", "vec", "cv"], writes=["cv"])
                if i == 2:
                    c.op("act", lambda e: e.activation(out=vTb[:, :w], in_=cv[:, :w], func=AF.Silu), reads=["cv"], writes=["vTb"])
                    for s in range(w // 128):
                        p_, ptok = pT.next()
                        c.op("pe", lambda e: e.matmul(p_[:, :], lhsT=vTb[:, s * 128:(s + 1) * 128], rhs=Ib[:, :], start=True, stop=True), reads=["vTb", "Ib"], writes=[ptok])
                        ch = (c0 + s * 128) // 128
                        c.op("act", lambda e: e.activation(out=vtm[:, ch, :], in_=p_[:, :], func=AF.Copy), reads=[ptok], writes=["vtm%d" % ch])
                    continue
                c.op("act", lambda e: e.activation(out=sv[:, :w], in_=cv[:, :w], func=AF.Silu), reads=["cv"], writes=["sv"])
                c.op("act", lambda e: e.activation(out=sqb[:, :w], in_=sv[:, :w], func=AF.Square), reads=["sv"], writes=["sqb"])
                c.op("pe", lambda e: e.matmul(pbig[:, :w], lhsT=onesb[:, :], rhs=sqb[:, :w], start=True, stop=True), reads=["sqb", "onesb"], writes=["bank0"])
                emit_rstd(c, rstd, pbig, "bank0", w, n=1)
                dst = qT if i == 0 else kT
                dtok = ("qT%d" if i == 0 else "kT%d") % c0
                sc = (128 ** -0.5) if i == 0 else 1.0
                c.op("dve", lambda e: e.scalar_tensor_tensor(out=dst[:, c0:c0 + w], in0=sv[:, :w], scalar=sc, in1=rstd[:, :w], op0=ALU.mult, op1=ALU.mult),
                     reads=["sv", "rstd"], writes=[dtok])
                if i == 1:
                    for s in range(w // 128):
                        p_, ptok = pT.next()
                        c.op("pe", lambda e: e.matmul(p_[:, :], lhsT=kT[:, c0 + s * 128:c0 + (s + 1) * 128], rhs=Ib[:, :], start=True, stop=True), reads=[dtok, "Ib"], writes=[ptok])
                        ch = (c0 + s * 128) // 128
                        c.op("dve", lambda e: e.tensor_copy(out=ktm[:, ch, :], in_=p_[:, :]), reads=[ptok], writes=["ktm%d" % ch])

        c.pop_scope()
        chains = [Chain(0), Chain(1)]

        def INV(ap):
            return ap

        def AS32(ap):
            return ap if GDN_INV_BF16 else (ap.bitcast(F32) if GDN_FP32R else ap)

        def seg_of(ch):
            t = ch * 128
            if t < NCT:
                return 0
            return NCT + ((t - NCT) // 512) * 512

        def pre(ch, X):
            d = X.d
            col = slice(ch, ch + 1)
            qtok = "qT%d" % seg_of(ch)
            ktok = "kT%d" % seg_of(ch)
            cs = slice(ch * 128, (ch + 1) * 128)
            c.op("dve", lambda e: e.tensor_scalar(out=X.diagb[:, :], in0=C32[:, 0, :], scalar1=Bc[d][:, col], scalar2=None, op0=ALU.mult), reads=["cst"] + gtoks, writes=[X.t("diagb")])
            c.op("pe", lambda e: e.matmul(X.pKK, lhsT=kT[:, cs], rhs=kT[:, cs], start=True, stop=True), reads=[ktok], writes=[X.bA])
            c.op("pe", lambda e: e.matmul(X.pQK, lhsT=qT[:, cs], rhs=kT[:, cs], start=True, stop=True), reads=[qtok, ktok], writes=[X.bA])
            yield
            c.op("pe", lambda e: e.matmul(X.pA, lhsT=C32[:, 1, :], rhs=X.diagb[:, :], start=True, stop=True), reads=["cst", X.t("diagb")], writes=[X.bA])
            yield
            c.op("dve", lambda e: e.tensor_scalar(out=X.nd[:, :], in0=X.pA, scalar1=Bc[d][:, col], scalar2=0.0, op0=ALU.subtract, op1=ALU.max), reads=[X.bA] + gtoks, writes=[X.t("nd")])
            yield
            c.op("act", lambda e: e.activation(out=X.dm[:, :], in_=X.nd[:, :], func=AF.Exp, scale=-1.0), reads=[X.t("nd")], writes=[X.t("dm")])
            yield
            c.op("dve", lambda e: e.tensor_tensor(out=X.dmS[:, :], in0=X.dm[:, :], in1=MS[d], op=ALU.mult), reads=[X.t("dm"), "cst"], writes=[X.t("dmS")])
            c.op("dve", lambda e: e.tensor_tensor(out=X.dmI[:, :], in0=X.dm[:, :], in1=MI[d], op=ALU.mult), reads=[X.t("dm"), "cst"], writes=[X.t("dmI")])
            yield
            Q, Qtok = X.Qm.next()
            c.op("dve", lambda e: e.scalar_tensor_tensor(out=Q[:, :], in0=X.pKK, scalar=NB[d][:, col], in1=X.dmS[:, :], op0=ALU.mult, op1=ALU.mult),
                 reads=[X.bA, X.t("dmS")] + gtoks, writes=[Qtok])
            c.op("dve", lambda e: e.tensor_tensor(out=X.qk[:, :], in0=X.pQK, in1=X.dmI[:, :], op=ALU.mult), reads=[X.bA, X.t("dmI")], writes=[X.t("qk")])
            yield
            p_, ptok = X.pT.next()
            c.op("pe", lambda e: e.matmul(p_, lhsT=AS32(Q[:, :]), rhs=(Ib[:, :] if GDN_INV_BF16 else C32[:, 0, :]), start=True, stop=True), reads=[Qtok, "cst", "Ib"], writes=[ptok])
            p2, p2tok = X.pT.next()
            c.op("pe", lambda e: e.matmul(p2, lhsT=X.qk[:, :], rhs=Ib[:, :], start=True, stop=True), reads=[X.t("qk"), "Ib"], writes=[p2tok])
            yield
            P, Ptok = X.Pm.next()
            c.op("act", lambda e: e.activation(out=P[:, :], in_=p_, func=AF.Copy), reads=[ptok], writes=[Ptok])
            c.op("dve", lambda e: e.tensor_tensor(out=X.Rm[:, :], in0=p_, in1=C32[:, 0, :], op=ALU.add), reads=[ptok, "cst"], writes=[X.t("Rm")])
            qkT_, qkTtok = X.qkT.next()
            c.op("act", lambda e: e.activation(out=qkT_[:, :], in_=p2, func=AF.Copy), reads=[p2tok], writes=[qkTtok])
            yield
            for step in range(6):
                last = step == 5
                pq, pqtok = X.pI.next()
                c.op("pe", lambda e: e.matmul(pq, lhsT=INV(P[:, :]), rhs=INV(Q[:, :]), start=True, stop=True), reads=[Ptok, Qtok], writes=[pqtok])
                if not last:
                    pp, pptok = X.pI.next()
                    c.op("pe", lambda e: e.matmul(pp, lhsT=INV(Q[:, :]), rhs=INV(P[:, :]), start=True, stop=True), reads=[Ptok, Qtok], writes=[pptok])
                yield
                Q2, Q2tok = X.Qm.next()
                c.op("dve", lambda e: e.tensor_copy(out=Q2[:, :], in_=pq), reads=[pqtok], writes=[Q2tok])
                if not last:
                    P2, P2tok = X.Pm.next()
                    c.op("act", lambda e: e.activation(out=P2[:, :], in_=pp, func=AF.Copy), reads=[pptok], writes=[P2tok])
                    P, Ptok = P2, P2tok
                Q, Qtok = Q2, Q2tok
                yield
                pr, prtok = X.pI.next()
                c.op("pe", lambda e: e.matmul(pr, lhsT=INV(Q[:, :]), rhs=INV(X.Rm[:, :]), start=True, stop=True), reads=[Qtok, X.t("Rm")], writes=[prtok])
                yield
                c.op("dve", lambda e: e.tensor_tensor(out=X.Rm[:, :], in0=pr, in1=AS32(X.Rm[:, :]), op=ALU.add), reads=[prtok, X.t("Rm")], writes=[X.t("Rm")])
                yield
            c.op("act", lambda e: e.activation(out=X.Rb[:, :], in_=AS32(X.Rm[:, :]), func=AF.Copy), reads=[X.t("Rm")], writes=[X.t("Rb")])
            kd_, kdtok = X.kd.next()
            c.op("pool", lambda e: e.tensor_scalar(out=X.kb[:, :], in0=ktm[:, ch, :], scalar1=BEB[d][:, col], scalar2=None, op0=ALU.mult), reads=["ktm%d" % ch] + gtoks, writes=[X.t("kb")])
            c.op("pool", lambda e: e.tensor_scalar(out=kd_[:, :], in0=ktm[:, ch, :], scalar1=EKD[d][:, col], scalar2=None, op0=ALU.mult), reads=["ktm%d" % ch] + gtoks, writes=[kdtok])
            c.op("pool", lambda e: e.tensor_scalar(out=X.vb[:, :], in0=vtm[:, ch, :], scalar1=BT[d][:, col], scalar2=None, op0=ALU.mult), reads=["vtm%d" % ch] + gtoks, writes=[X.t("vb")])
            yield
            c.op("pe", lambda e: e.matmul(X.pU, lhsT=X.Rb[:, :], rhs=X.vb[:, :], start=True, stop=True), reads=[X.t("Rb"), X.t("vb")], writes=[X.bA])
            c.op("pe", lambda e: e.matmul(X.pW, lhsT=X.kb[:, :], rhs=X.Rb[:, :], start=True, stop=True), reads=[X.t("Rb"), X.t("kb")], writes=[X.bB])
            yield
            ub_, ubtok = X.ub.next()
            wT_, wTtok = X.wT.next()
            c.op("act", lambda e: e.activation(out=ub_[:, :], in_=X.pU, func=AF.Copy), reads=[X.bA], writes=[ubtok])
            c.op("dve", lambda e: e.tensor_copy(out=wT_[:, :], in_=X.pW), reads=[X.bB], writes=[wTtok])
            X.slot_out = (qkT_, qkTtok, kd_, kdtok, ub_, ubtok, wT_, wTtok)
            yield

        def rec(ch, X, slot, fin):
            d = X.d
            qkT_, qkTtok, kd_, kdtok, ub_, ubtok, wT_, wTtok = slot
            col = slice(ch, ch + 1)
            cs = slice(ch * 128, (ch + 1) * 128)
            qtok = "qT%d" % seg_of(ch)
            c.op("pe", lambda e: e.matmul(X.pwS, lhsT=wT_[:, :], rhs=X.Sb[:, :], start=True, stop=True), reads=[wTtok, X.t("Sb")], writes=[X.bD])
            c.op("pe", lambda e: e.matmul(X.pO1, lhsT=qT[:, cs], rhs=X.Sb[:, :], start=True, stop=True), reads=[qtok, X.t("Sb")], writes=[X.bD])
            yield
            c.op("dve", lambda e: e.tensor_tensor(out=X.u_b[:, :], in0=ub_[:, :], in1=X.pwS, op=ALU.subtract), reads=[ubtok, X.bD], writes=[X.t("u_b")])
            yield
            c.op("pe", lambda e: e.matmul(X.pdS, lhsT=kd_[:, :], rhs=X.u_b[:, :], start=True, stop=True), reads=[kdtok, X.t("u_b")], writes=[X.bD])
            c.op("pe", lambda e: e.matmul(X.pO2, lhsT=qkT_[:, :], rhs=X.u_b[:, :], start=True, stop=True), reads=[qkTtok, X.t("u_b")], writes=[X.bD])
            yield
            c.op("dve", lambda e: e.scalar_tensor_tensor(out=X.S32[:, :], in0=X.S32[:, :], scalar=CD[d][:, col], in1=X.pdS, op0=ALU.mult, op1=ALU.add),
                 reads=[X.t("S32"), X.bD] + gtoks, writes=[X.t("S32")])
            yield
            c.op("act", lambda e: e.activation(out=X.Sb[:, :], in_=X.S32[:, :], func=AF.Copy), reads=[X.t("S32")], writes=[X.t("Sb")])
            c.op("act", lambda e: e.activation(out=X.o2s[:, :], in_=X.pO2, func=AF.Copy), reads=[X.bD], writes=[X.t("o2s")])
            yield
            if not fin:
                c.op("dve", lambda e: e.scalar_tensor_tensor(out=of[:, ch, :], in0=X.pO1, scalar=EB[d][:, col], in1=X.o2s[:, :], op0=ALU.mult, op1=ALU.add),
                     reads=[X.bD, X.t("o2s")] + gtoks, writes=["of%d" % ch])
                yield
                return
            c.op("dve", lambda e: e.scalar_tensor_tensor(out=X.osum[:, :], in0=X.pO1, scalar=EB[d][:, col], in1=X.o2s[:, :], op0=ALU.mult, op1=ALU.add),
                 reads=[X.bD, X.t("o2s")] + gtoks, writes=[X.t("osum")])
            yield
            c.op("dve", lambda e: e.tensor_tensor(out=X.osum[:, :], in0=X.osum[:, :], in1=of[:, ch, :], op=ALU.add), reads=[X.t("osum"), "of%d" % ch], writes=[X.t("osum")])
            yield
            c.op("act", lambda e: e.activation(out=X.osq[:, :], in_=X.osum[:, :], func=AF.Square, accum_out=X.oss[:, 0:1]), reads=[X.t("osum")], writes=[X.t("osq"), X.t("oss")])
            yield
            emit_rstd(c, X.oss, X.oss, X.t("oss"), 1, n=128, tok=X.t("oss"))
            z_, ztok = zt.next()
            c.dma("sp", ztok, z_[:, :], ztm[ch * 128:(ch + 1) * 128, :], writes=[ztok])
            c.op("act", lambda e: e.activation(out=X.sz[:, :], in_=z_[:, :], func=AF.Silu), reads=[ztok], writes=[X.t("sz")])
            yield
            c.op("dve", lambda e: e.scalar_tensor_tensor(out=X.osum[:, :], in0=X.osum[:, :], scalar=X.oss[:, 0:1], in1=gb[:, :], op0=ALU.mult, op1=ALU.mult),
                 reads=[X.t("osum"), X.t("oss"), "gbc"], writes=[X.t("osum")])
            o_, otok = ot.next()
            c.op("dve", lambda e: e.tensor_tensor(out=o_[:, :], in0=X.osum[:, :], in1=X.sz[:, :], op=ALU.mult), reads=[X.t("osum"), X.t("sz")], writes=[otok])
            c.dma("sp", otok, otm[ch * 128:(ch + 1) * 128, :], o_[:, :], reads=[otok], writes=["otm"])
            yield

        def drive(gens):
            gens = [g_ for g_ in gens if g_ is not None]
            while gens:
                alive = []
                for g_ in gens:
                    try:
                        next(g_)
                        alive.append(g_)
                    except StopIteration:
                        pass
                gens = alive

        orders = [list(range(NCH)), list(range(CCH - 1, -1, -1)) + list(range(NCH - 1, CCH - 1, -1))]
        pos = [{ch: i for i, ch in enumerate(o)} for o in orders]
        for X in chains:
            c.op("dve", lambda e: e.memset(X.S32[:, :], 0.0), writes=[X.t("S32")])
            c.op("dve", lambda e: e.memset(X.Sb[:, :], 0.0), writes=[X.t("Sb")])
        drive([pre(orders[d][0], chains[d]) for d in range(2)])
        slots = [chains[d].slot_out for d in range(2)]
        for i in range(NCH):
            gens = []
            for d in range(2):
                if i + 1 < NCH:
                    gens.append(pre(orders[d][i + 1], chains[d]))
            for d in range(2):
                ch = orders[d][i]
                gens.append(rec(ch, chains[d], slots[d], pos[d][ch] > pos[1 - d][ch]))
            drive(gens)
            if i + 1 < NCH:
                slots = [chains[d].slot_out for d in range(2)]
        c.wait_all("sp", ["otm"])
    return nc


def gdn_consts():
    i = np.arange(128)
    I = np.eye(128, dtype=np.float32)
    ones = np.ones((128, 128), np.float32)
    LS = (i[:, None] > i[None, :]).astype(np.float32)
    LI = (i[:, None] >= i[None, :]).astype(np.float32)
    return np.ascontiguousarray(np.stack([I, ones, LS, LI, LS.T, LI.T], 0))


def tile_w(w, m=128):
    w = np.asarray(w, np.float32)
    K, N = w.shape
    nb = (N + m - 1) // m
    if nb * m != N:
        w = np.concatenate([w, np.zeros((K, nb * m - N), np.float32)], 1)
    return np.ascontiguousarray(w.reshape(K // 128, 128, nb, m).transpose(2, 1, 0, 3))


def fm(v):
    return np.ascontiguousarray(np.asarray(v, np.float32).reshape(KC, 128).T)


def lambda_init(layer):
    return 0.8 - 0.6 * math.exp(-0.3 * layer)


PRE_TILES = [(0, 32, 1)] + [(32 + i * 512, 512, 0) for i in range(4)]
CTXC = CTX // NCORE
POST_LAT = [(CTXC + 2 + off, w + 2, 0, "vl" if off == 0 else "one", "vr" if off == 2040 else "one")
            for off, w in ((0, 510), (510, 510), (1020, 510), (1530, 510), (2040, 8))]
POST_TILES = [(0, CTXC + 2, 1, "vl", "vr")] + POST_LAT
TL = SEQ // NCORE


def kernel(x, c, ctx, c_ctx, w_mod, b_mod, norm1_g, norm2_g, w_in_even, a_conv_w, a_A_log, a_dt_bias,
           a_norm_g, b_lambda, b_norm_g, w_out_even, w_in_odd, c_q_norm, c_k_norm, w_out_odd,
           ffn_up, ffn_conv_w, ffn_conv_b, ffn_down, final_g):
    f32 = np.float32
    x = np.asarray(x, f32)
    ctx = np.asarray(ctx, f32)
    progs = {}

    def prog(key, fn):
        if key not in progs:
            progs[key] = fn()
        return progs[key]

    c2 = np.ascontiguousarray(np.stack([fm(np.asarray(c, f32)[0]), fm(np.asarray(c_ctx, f32))], -1))
    w_mod = np.asarray(w_mod, f32)
    b_mod = np.asarray(b_mod, f32)
    maps = []
    for j in range(NCORE):
        sl = slice(j * MODC, (j + 1) * MODC)
        bm = np.ascontiguousarray(np.broadcast_to(b_mod[None, :, sl], (2, DEPTH, MODC)))
        maps.append({"c2": c2, "wm": np.stack([tile_w(w_mod[l_][:, sl], 512) for l_ in range(DEPTH)], 0), "bm": bm})
    res = run(prog("mod", build_mod), maps, "mod")
    mod = np.concatenate([r["out"] for r in res], -1)

    def modv(s, l, m):
        return mod[s, l, m * D:(m + 1) * D]

    xlT = np.ascontiguousarray(x[0].T)
    xcT = np.ascontiguousarray(ctx[0].T)
    rope = rope_tables(SEQ)
    rt = rope_perm()
    zcol = np.zeros((D, 1), f32)

    for l in range(DEPTH):
        even = l % 2 == 0
        e = l // 2
        last = l == DEPTH - 1
        w_in = np.asarray(w_in_even[e] if even else w_in_odd[e], f32)
        ncols = w_in.shape[1]
        w_in = tile_w(w_in)
        vec = np.ascontiguousarray(np.stack([fm(norm1_g[l]), fm(modv(0, l, 0)), fm(modv(0, l, 1)), fm(modv(1, l, 0)), fm(modv(1, l, 1))], 1))
        maps = [{"xT": np.ascontiguousarray(np.concatenate([xcT[:, 32 * j:32 * (j + 1)], xlT[:, TL * j:TL * (j + 1)]], 1)), "vec": vec, "w": w_in}
                for j in range(NCORE)]
        res = run(prog(("pre", ncols), lambda: build_pre(ncols, PRE_TILES)), maps, "pre%d" % l)
        pc = np.concatenate([r["pT"][:, :32] for r in res], 1)
        pl = np.concatenate([r["pT"][:, 32:] for r in res], 1)
        p = np.concatenate([pc, pl], 1)
        del res, maps
        mT = np.zeros((D, CTX + SEQ), NPBF)
        if even:
            NT = CTX + SEQ
            cw = np.asarray(a_conv_w[e], f32)
            maps = []
            for j in range(NCORE):
                hs = slice(j * 128, (j + 1) * 128)
                qkvT = np.ascontiguousarray(np.stack([p[j * 128:(j + 1) * 128], p[1024 + j * 128:1024 + (j + 1) * 128], p[2048 + j * 128:2048 + (j + 1) * 128]], 0))
                ztm = np.ascontiguousarray(p[3072 + j * 128:3072 + (j + 1) * 128].T)
                gates = np.stack([p[4096 + gi * 8 + j] for gi in range(4)], 0)
                gt = np.ascontiguousarray(gates.reshape(4, NT // 128, 128).transpose(2, 1, 0))
                vec = np.zeros((128, 16), f32)
                vec[:, 0:2] = np.asarray(a_A_log[e], f32)[:, j]
                vec[:, 2:4] = np.asarray(a_dt_bias[e], f32)[:, j]
                for i, off in enumerate((0, 1024, 2048)):
                    for t in range(3):
                        vec[:, 4 + 3 * i + t] = cw[t, off + j * 128:off + (j + 1) * 128]
                gbc = np.ascontiguousarray(np.broadcast_to(np.asarray(a_norm_g[e], f32), (128, 128)))
                maps.append({"qkvT": qkvT, "ztm": ztm, "gt": gt, "vec": vec, "gbc": gbc, "cst": gdn_consts()})
            res = run(prog("gdn", lambda: build_gdn(NT)), maps, "gdn%d" % l)
            for j in range(NCORE):
                mT[j * 128:(j + 1) * 128, :] = res[j]["otm"].T
            del res, maps
            HQ = SEQ // 2
            li = lambda_init(l)
            lp = np.ascontiguousarray(np.broadcast_to(np.asarray(b_lambda[e], f32), (128, 4, 128)))
            bng = np.asarray(b_norm_g[e], f32)
            maps = []
            for j in range(NCORE):
                hb, half = j // 2, j % 2
                qrows = [slice(A_IN + hb * 256 + m * 128, A_IN + hb * 256 + (m + 1) * 128) for m in range(2)]
                krows = [slice(A_IN + 1024 + hb * 256 + m * 128, A_IN + 1024 + hb * 256 + (m + 1) * 128) for m in range(2)]
                qT = np.ascontiguousarray(np.stack([np.concatenate([p[r, :CTX], p[r, CTX + half * HQ:CTX + (half + 1) * HQ]], 1) for r in qrows], 0))
                kT = np.ascontiguousarray(np.stack([p[r] for r in krows], 0))
                v = np.ascontiguousarray(p[A_IN + 2048 + hb * 256:A_IN + 2048 + (hb + 1) * 256].T)
                vec = np.zeros((128, 8), f32)
                vec[:, 2] = li
                vec[:, 3] = 1.0 - li
                vec[:, 4] = bng[:128]
                vec[:, 5] = bng[128:]
                maps.append({"qT": qT, "kT": kT, "v": v, "cq": np.ascontiguousarray(rope[:, :, half * HQ:(half + 1) * HQ]), "ck": rope, "rt": rt, "vec": vec, "lp": lp})
            res = run(prog("attB", lambda: build_att("B", HQ, SEQ)), maps, "attB%d" % l)
            for j in range(NCORE):
                hb, half = j // 2, j % 2
                rows = slice(1024 + hb * 256, 1024 + (hb + 1) * 256)
                if half == 0:
                    mT[rows, :CTX] = res[j]["oT"][:, :CTX]
                mT[rows, CTX + half * HQ:CTX + (half + 1) * HQ] = res[j]["oT"][:, CTX:]
            del res, maps
        else:
            lp0 = np.zeros((128, 4, 128), f32)
            maps = []
            for j in range(NCORE):
                g = j // 2
                qT = np.ascontiguousarray(np.stack([p[(2 * j + s) * 128:(2 * j + s + 1) * 128] for s in range(2)], 0))
                kT = np.ascontiguousarray(p[2048 + g * 128:2048 + (g + 1) * 128][None])
                v = np.ascontiguousarray(p[2560 + g * 128:2560 + (g + 1) * 128].T)
                vec = np.zeros((128, 8), f32)
                vec[:, 0] = np.asarray(c_q_norm[e], f32)
                vec[:, 1] = np.asarray(c_k_norm[e], f32)
                maps.append({"qT": qT, "kT": kT, "v": v, "cq": rope, "ck": rope, "rt": rt, "vec": vec, "lp": lp0})
            res = run(prog("attC", lambda: build_att("C", SEQ, SEQ)), maps, "attC%d" % l)
            for j in range(NCORE):
                mT[2 * j * 128:(2 * j + 2) * 128, :] = res[j]["oT"]
            del res, maps
        del p
        vec = np.zeros((128, POST_NV), f32)
        V = POST_V
        vec[:, V["n2g"]:V["n2g"] + 16] = fm(norm2_g[l])
        for nm, s, m in (("g1", 0, 2), ("sh2", 0, 3), ("sc2", 0, 4), ("g2", 0, 5), ("cg1", 1, 2), ("csh2", 1, 3), ("csc2", 1, 4), ("cg2", 1, 5)):
            vec[:, V[nm]:V[nm] + 16] = fm(modv(s, l, m))
        vec[:, V["fg"]:V["fg"] + 16] = fm(final_g)
        vec[:, V["cw"]:V["cw"] + 3 * 88] = np.asarray(ffn_conv_w[l], f32).reshape(3, 88, 128).transpose(2, 0, 1).reshape(128, 264)
        vec[:, V["cb"]:V["cb"] + 88] = np.asarray(ffn_conv_b[l], f32).reshape(88, 128).T
        wo = tile_w(w_out_even[e] if even else w_out_odd[e])
        wu = tile_w(ffn_up[l])
        wd = tile_w(ffn_down[l])
        mcT, mlT = mT[:, :CTX], mT[:, CTX:]
        zb = np.zeros((D, 1), NPBF)
        maps = []
        for j in range(NCORE):
            lo, hi = TL * j, TL * (j + 1)
            xl_ = [zcol if j == 0 else xlT[:, lo - 1:lo], xlT[:, lo:hi], zcol if j == NCORE - 1 else xlT[:, hi:hi + 1]]
            ml_ = [zb if j == 0 else mlT[:, lo - 1:lo], mlT[:, lo:hi], zb if j == NCORE - 1 else mlT[:, hi:hi + 1]]
            vj = vec.copy()
            vj[:, V["vl"]] = 0.0 if j == 0 else 1.0
            vj[:, V["vr"]] = 0.0 if j == NCORE - 1 else 1.0
            clo, chi = CTXC * j, CTXC * (j + 1)
            xc_ = [zcol if j == 0 else xcT[:, clo - 1:clo], xcT[:, clo:chi], zcol if j == NCORE - 1 else xcT[:, chi:chi + 1]]
            mc_ = [zb if j == 0 else mcT[:, clo - 1:clo], mcT[:, clo:chi], zb if j == NCORE - 1 else mcT[:, chi:chi + 1]]
            maps.append({"xT": np.ascontiguousarray(np.concatenate(xc_ + xl_, 1)),
                         "mT": np.ascontiguousarray(np.concatenate(mc_ + ml_, 1)), "vec": vj, "wo": wo, "wu": wu, "wd": wd})
        if not last:
            res = run(prog("post", lambda: build_post(POST_TILES, False)), maps, "post%d" % l)
            xcT = np.ascontiguousarray(np.concatenate([r["oT"][:, :CTXC] for r in res], 1))
            xlT = np.ascontiguousarray(np.concatenate([r["oT"][:, CTXC:] for r in res], 1))
        else:
            res = run(prog("postf", lambda: build_post(POST_LAT, True)), maps, "postf%d" % l)
            xlT = np.concatenate([r["oT"] for r in res], 1)
        del res, maps, mT
    return np.ascontiguousarray(xlT.T)[None].astype(np.float32)
```

```python
import math
import os
import sys
from contextlib import ExitStack

import ml_dtypes
import numpy as np
import concourse.bass as bass
import concourse.mybir as mybir
from concourse.bass_utils import run_bass_kernel_spmd

F32 = mybir.dt.float32
BF16 = mybir.dt.bfloat16
AF = mybir.ActivationFunctionType
ALU = mybir.AluOpType
AX = mybir.AxisListType
NPBF = ml_dtypes.bfloat16

D = 2048
KC = 16
NCORE = 8
SEQ = 16384
CTX = 256
DEPTH = 4
EPS = 1e-6
DFF = 5632
FC = 44
A_IN = 4128
EVEN_IN = 7200
ODD_IN = 3072


class Ctx:
    def __init__(self, nc, stack):
        self.nc = nc
        self.stack = stack
        self.E = {"pe": nc.tensor, "act": nc.scalar, "dve": nc.vector, "pool": nc.gpsimd, "sp": nc.sync}
        self.sems = {}
        self.cnt = {}
        self.known = {e: {} for e in self.E}
        self.lastw = {}
        self.readers = {}
        self.ninst = 0

    def sb(self, name, shape, dt):
        return self.stack.enter_context(self.nc.sbuf_tensor("sb_" + name, list(shape), dt))

    def ps(self, name, shape, dt=F32):
        return self.stack.enter_context(self.nc.psum_tensor("ps_" + name, list(shape), dt))

    def sem(self, key):
        if key not in self.sems:
            self.sems[key] = self.stack.enter_context(self.nc.semaphore("s_" + key.replace(":", "_")))
            self.cnt[key] = 0
        return self.sems[key]

    def _waits(self, eng, reads, writes):
        need = {}

        def add(ev):
            if ev is not None and need.get(ev[0], 0) < ev[1]:
                need[ev[0]] = ev[1]

        for t in reads:
            add(self.lastw.get(t))
        for t in writes:
            add(self.lastw.get(t))
            for k, v in self.readers.get(t, {}).items():
                add((k, v))
        E = self.E[eng]
        for k, v in need.items():
            if k == "pe" and eng == "pe":
                continue
            if k.startswith("d:"):
                v = self.cnt[k]
            if self.known[eng].get(k, 0) < v:
                E.wait_ge(self.sems[k], v)
                self.known[eng][k] = v
                self.ninst += 1

    def _record(self, ev, reads, writes):
        k, v = ev
        for t in reads:
            d = self.readers.setdefault(t, {})
            if d.get(k, 0) < v:
                d[k] = v
        for t in writes:
            self.lastw[t] = ev
            self.readers[t] = {}

    def op(self, eng, emit, reads=(), writes=()):
        ex = [t for t in reads if t.startswith("bank")]
        if ex:
            writes = list(writes) + ex
        self._waits(eng, reads, writes)
        s = self.sem(eng)
        ins = emit(self.E[eng])
        ins.then_inc(s, 1)
        self.cnt[eng] += 1
        self.ninst += 1
        self._record((eng, self.cnt[eng]), reads, writes)
        return ins

    def dma(self, eng, stream, out, in_, reads=(), writes=()):
        key = "d:" + stream
        s = self.sem(key)
        self._waits(eng, reads, writes)
        ins = self.E[eng].dma_start(out=out, in_=in_)
        ins.then_inc(s, 16)
        self.cnt[key] += 16
        self.ninst += 1
        self._record((key, self.cnt[key]), reads, writes)
        return ins

    def push_scope(self):
        self._outer = self.stack
        self.stack = ExitStack()

    def pop_scope(self):
        self.barrier()
        self.stack.close()
        self.stack = self._outer

    def barrier(self):
        for eng, E in self.E.items():
            for k, s_ in self.sems.items():
                v = self.cnt[k]
                if v > 0 and self.known[eng].get(k, 0) < v and not (k == eng):
                    E.wait_ge(s_, v)
                    self.known[eng][k] = v
                    self.ninst += 1

    def wait_all(self, eng, tokens):
        self._waits(eng, tokens, ())


class Rot:
    def __init__(self, c, name, n, shape, dt, psum=False):
        self.bufs = [(c.ps if psum else c.sb)("%s%d" % (name, i), shape, dt) for i in range(n)]
        self.names = ["%s%d" % (name, i) for i in range(n)]
        self.i = 0

    def next(self):
        j = self.i % len(self.bufs)
        self.i += 1
        return self.bufs[j], self.names[j]


def new_nc():
    return bass.Bass("TRN2", target_bir_lowering=False)


def run(nc, in_maps, tag=""):
    res = run_bass_kernel_spmd(nc, in_maps, core_ids=list(range(NCORE)))
    if os.environ.get("KDEBUG"):
        for j, r in enumerate(res.results):
            for name, arr in r.items():
                a = np.asarray(arr).astype(np.float32)
                if not np.isfinite(a).all():
                    print("KDEBUG non-finite:", tag, "core", j, name, int((~np.isfinite(a)).sum()), "of", a.size, file=sys.stderr)
    return res.results


def emit_rstd(c, rstd, ss, sstok, W, n=D, tok="rstd"):
    c.op("dve", lambda e: e.tensor_scalar(out=rstd[:, :W], in0=ss[:, :W], scalar1=1.0 / n, scalar2=EPS, op0=ALU.mult, op1=ALU.add),
         reads=[sstok], writes=[tok])
    c.op("act", lambda e: e.activation(out=rstd[:, :W], in_=rstd[:, :W], func=AF.Sqrt), reads=[tok], writes=[tok])
    c.op("dve", lambda e: e.reciprocal(out=rstd[:, :W], in_=rstd[:, :W]), reads=[tok], writes=[tok])


def emit_norm_mod(c, K, xt, xtok, W, segs, h, htok, maskcols=()):
    ss, sstok = K["ps_ss"].next()
    for kc in range(KC):
        sq, sqtok = K["sq"].next()
        c.op("act", lambda e: e.activation(out=sq[:, :W], in_=xt[:, kc, :W], func=AF.Square), reads=[xtok], writes=[sqtok])
        c.op("pe", lambda e: e.matmul(ss[:, :W], lhsT=K["ones"][:, :], rhs=sq[:, :W], start=(kc == 0), stop=(kc == KC - 1)),
             reads=[sqtok, "ones"], writes=[sstok])
    rstd = K["rstd"]
    emit_rstd(c, rstd, ss, sstok, W)
    for kc in range(KC):
        tmp, tmptok = K["tmp"].next()
        for (c0, c1, a_ap, b_ap) in segs:
            c.op("dve", lambda e: e.scalar_tensor_tensor(out=tmp[:, c0:c1], in0=xt[:, kc, c0:c1], scalar=a_ap[:, kc:kc + 1], in1=rstd[:, c0:c1],
                                                         op0=ALU.mult, op1=ALU.mult), reads=[xtok, "rstd", "vec"], writes=[tmptok])
            c.op("act", lambda e: e.activation(out=h[:, kc, c0:c1], in_=tmp[:, c0:c1], func=AF.Identity, bias=b_ap[:, kc:kc + 1], scale=1.0),
                 reads=[tmptok, "vec"], writes=[htok])
    for col, sc in maskcols:
        c.op("dve", lambda e: e.tensor_scalar(out=h[:, :, col:col + 1], in0=h[:, :, col:col + 1], scalar1=sc, scalar2=None, op0=ALU.mult),
             reads=[htok, "vec"], writes=[htok])


def make_consts(c):
    K = {}
    K["ones"] = c.sb("ones", [128, 128], BF16)
    c.op("dve", lambda e: e.memset(K["ones"][:, :], 1.0), writes=["ones"])
    K["sq"] = Rot(c, "sq", 2, [128, 512], BF16)
    K["tmp"] = Rot(c, "tmp", 2, [128, 512], F32)
    K["rstd"] = c.sb("rstd", [128, 512], F32)
    K["ps_ss"] = Rot(c, "ps_ss", 1, [128, 512], F32, psum=True)
    return K


MODC = 1536


def build_mod():
    nc = new_nc()
    c2 = nc.dram_tensor("c2", [128, KC, 2], F32, kind="ExternalInput").ap()
    wm = nc.dram_tensor("wm", [DEPTH, MODC // 512, 128, KC, 512], F32, kind="ExternalInput").ap()
    bm = nc.dram_tensor("bm", [2, DEPTH, MODC], F32, kind="ExternalInput").ap()
    out = nc.dram_tensor("out", [2, DEPTH, MODC], F32, kind="ExternalOutput").ap()
    with ExitStack() as st:
        c = Ctx(nc, st)
        ct = c.sb("ct", [128, KC, 2], F32)
        cs = c.sb("cs", [128, KC, 2], BF16)
        bt = c.sb("bt", [2, DEPTH, MODC], F32)
        ot = c.sb("ot", [2, DEPTH, MODC], F32)
        wrot = Rot(c, "wt", 2, [128, KC, 512], BF16)
        prot = Rot(c, "pm", 2, [2, 512], F32, psum=True)
        c.dma("sp", "c2", ct[:], c2, writes=["ct"])
        c.dma("sp", "bm", bt[:], bm, writes=["bt"])
        c.op("act", lambda e: e.activation(out=cs[:], in_=ct[:], func=AF.Silu), reads=["ct"], writes=["cs"])
        for l in range(DEPTH):
            for n in range(MODC // 512):
                wt, wtok = wrot.next()
                c.dma("pool", wtok, wt[:], wm[l, n], writes=[wtok])
                ps, ptok = prot.next()
                for kc in range(KC):
                    c.op("pe", lambda e: e.matmul(ps[:, :], lhsT=cs[:, kc, :], rhs=wt[:, kc, :], start=(kc == 0), stop=(kc == KC - 1)),
                         reads=["cs", wtok], writes=[ptok])
                c.op("dve", lambda e: e.tensor_tensor(out=ot[:, l, n * 512:(n + 1) * 512], in0=ps[:, :], in1=bt[:, l, n * 512:(n + 1) * 512], op=ALU.add),
                     reads=[ptok, "bt"], writes=["ot"])
        c.dma("sp", "out", out, ot[:], reads=["ot"], writes=["out"])
        c.wait_all("sp", ["out"])
    return nc


def tiles_of(total, w):
    return [(s, min(w, total - s)) for s in range(0, total, w)]


def build_pre(ncols, tiles):
    T = sum(w for _, w, _ in tiles)
    nc = new_nc()
    xT = nc.dram_tensor("xT", [D, T], F32, kind="ExternalInput").ap()
    vec = nc.dram_tensor("vec", [128, 5, KC], F32, kind="ExternalInput").ap()
    w = nc.dram_tensor("w", [(ncols + 127) // 128, 128, KC, 128], F32, kind="ExternalInput").ap()
    pT = nc.dram_tensor("pT", [ncols, T], BF16, kind="ExternalOutput").ap()
    with ExitStack() as st:
        c = Ctx(nc, st)
        K = make_consts(c)
        vt = c.sb("vec", [128, 5, KC], F32)
        av = c.sb("av", [128, 2, KC], F32)
        h = c.sb("h", [128, KC, T], BF16)
        xrot = Rot(c, "xt", 2, [128, KC, 512], F32)
        wrot = Rot(c, "wt", 2, [128, KC, 128], BF16)
        prot = Rot(c, "pp", 3, [128, 512], F32, psum=True)
        orot = Rot(c, "po", 3, [128, 512], BF16)
        c.dma("sp", "vec", vt[:], vec, writes=["vec"])
        for s in range(2):
            c.op("dve", lambda e: e.scalar_tensor_tensor(out=av[:, s, :], in0=vt[:, 2 + 2 * s, :], scalar=1.0, in1=vt[:, 0, :],
                                                         op0=ALU.add, op1=ALU.mult), reads=["vec"], writes=["vec"])
        for (s0, wd, stream) in tiles:
            xt, xtok = xrot.next()
            c.dma("sp", xtok, xt[:, :, :wd], xT[:, s0:s0 + wd].rearrange("(kc p) t -> p kc t", p=128), writes=[xtok])
            emit_norm_mod(c, K, xt, xtok, wd, [(0, wd, av[:, stream, :], vt[:, 1 + 2 * stream, :])], h[:, :, s0:s0 + wd], "h%d" % s0)
        nst = 0
        for cb0 in range(0, ncols, 128):
            m = min(128, ncols - cb0)
            wt, wtok = wrot.next()
            c.dma("pool", wtok, wt[:], w[cb0 // 128], writes=[wtok])
            for (s0, wd, stream) in tiles:
                ps, ptok = prot.next()
                for kc in range(KC):
                    c.op("pe", lambda e: e.matmul(ps[:m, :wd], lhsT=wt[:, kc, :m], rhs=h[:, kc, s0:s0 + wd], start=(kc == 0), stop=(kc == KC - 1)),
                         reads=[wtok, "h%d" % s0], writes=[ptok])
                ot, otok = orot.next()
                eng = "act" if nst % 2 == 0 else "dve"
                if eng == "act":
                    c.op("act", lambda e: e.activation(out=ot[:m, :wd], in_=ps[:m, :wd], func=AF.Copy), reads=[ptok], writes=[otok])
                else:
                    c.op("dve", lambda e: e.tensor_copy(out=ot[:m, :wd], in_=ps[:m, :wd]), reads=[ptok], writes=[otok])
                nst += 1
                c.dma("sp", otok, pT[cb0:cb0 + m, s0:s0 + wd], ot[:m, :wd], reads=[otok], writes=["pT"])
        c.wait_all("sp", ["pT"])
    return nc


POST_V = {"n2g": 0, "g1": 16, "sh2": 32, "sc2": 48, "g2": 64, "cg1": 80, "csh2": 96, "csc2": 112, "cg2": 128, "fg": 144,
          "cw": 160, "cb": 160 + 3 * 88, "vl": 160 + 4 * 88, "vr": 161 + 4 * 88, "zero": 162 + 4 * 88}
POST_NV = 163 + 4 * 88


def build_post(tiles, final):
    T = max(s + w for s, w, _, _ in tiles)
    TO = sum(w - 2 for _, w, _, _ in tiles)
    nc = new_nc()
    xT = nc.dram_tensor("xT", [D, T], F32, kind="ExternalInput").ap()
    mT = nc.dram_tensor("mT", [D, T], BF16, kind="ExternalInput").ap()
    vec = nc.dram_tensor("vec", [128, POST_NV], F32, kind="ExternalInput").ap()
    wo = nc.dram_tensor("wo", [KC, 128, KC, 128], F32, kind="ExternalInput").ap()
    wu = nc.dram_tensor("wu", [2 * FC, 128, KC, 128], F32, kind="ExternalInput").ap()
    wdn = nc.dram_tensor("wd", [KC, 128, FC, 128], F32, kind="ExternalInput").ap()
    oT = nc.dram_tensor("oT", [D, TO], F32, kind="ExternalOutput").ap()
    with ExitStack() as st:
        c = Ctx(nc, st)
        K = make_consts(c)
        vt = c.sb("vec", [128, POST_NV], F32)
        av = c.sb("av", [128, 2, KC], F32)
        xrot = Rot(c, "xt", 1, [128, KC, 512], F32)
        mrot = Rot(c, "mt", 1, [128, KC, 512], BF16)
        h2 = c.sb("h2", [128, KC, 512], BF16)
        act = c.sb("actb", [128, FC, 512], BF16)
        worot = Rot(c, "wo", 2, [128, KC, 128], BF16)
        wgrot = Rot(c, "wg", 2, [128, KC, 128], BF16)
        wvrot = Rot(c, "wv", 2, [128, KC, 128], BF16)
        wdrot = Rot(c, "wdn", 2, [128, FC, 128], BF16)
        prot = Rot(c, "pp", 2, [128, 512], F32, psum=True)
        pgrot = Rot(c, "pg", 2, [128, 512], F32, psum=True)
        pvrot = Rot(c, "pv", 2, [128, 512], F32, psum=True)
        cg = Rot(c, "cg", 2, [128, 512], F32)
        cv = Rot(c, "cv", 2, [128, 512], F32)
        sg = Rot(c, "sg", 2, [128, 512], F32)
        yt = Rot(c, "yt", 2, [128, 512], F32)
        c.dma("sp", "vec", vt[:], vec, writes=["vec"])
        V = POST_V
        for s, (scn, gn) in enumerate((("sc2", "n2g"), ("csc2", "n2g"))):
            c.op("dve", lambda e: e.scalar_tensor_tensor(out=av[:, s, :], in0=vt[:, V[scn]:V[scn] + 16], scalar=1.0, in1=vt[:, V[gn]:V[gn] + 16],
                                                         op0=ALU.add, op1=ALU.mult), reads=["vec"], writes=["vec"])
        ocol = 0
        for (s0, wd, segs, tmasks) in tiles:
            wi = wd - 2
            xt, xtok = xrot.next()
            mt, mtok = mrot.next()
            c.dma("sp", xtok, xt[:, :, :wd], xT[:, s0:s0 + wd].rearrange("(kc p) t -> p kc t", p=128), writes=[xtok])
            c.dma("sp", mtok, mt[:, :, :wd], mT[:, s0:s0 + wd].rearrange("(kc p) t -> p kc t", p=128), writes=[mtok])
            for oc in range(KC):
                wt, wtok = worot.next()
                c.dma("pool", wtok, wt[:], wo[oc], writes=[wtok])
                ps, ptok = prot.next()
                for kc in range(KC):
                    c.op("pe", lambda e: e.matmul(ps[:, :wd], lhsT=wt[:, kc, :], rhs=mt[:, kc, :wd], start=(kc == 0), stop=(kc == KC - 1)),
                         reads=[wtok, mtok], writes=[ptok])
                for (c0, c1, stream) in segs:
                    g1 = V["cg1"] if stream else V["g1"]
                    c.op("dve", lambda e: e.scalar_tensor_tensor(out=xt[:, oc, c0:c1], in0=ps[:, c0:c1], scalar=vt[:, g1 + oc:g1 + oc + 1], in1=xt[:, oc, c0:c1],
                                                                 op0=ALU.mult, op1=ALU.add), reads=[ptok, xtok, "vec"], writes=[xtok])
            masks = [(col, vt[:, V[fl]:V[fl] + 1]) for col, fl in tmasks]
            nsegs = [(c0, c1, av[:, stream, :], vt[:, (V["csh2"] if stream else V["sh2"]):(V["csh2"] if stream else V["sh2"]) + 16]) for (c0, c1, stream) in segs]
            emit_norm_mod(c, K, xt, xtok, wd, nsegs, h2, "h2", maskcols=masks)
            for f in range(FC):
                wg, wgtok = wgrot.next()
                wv, wvtok = wvrot.next()
                c.dma("pool", wgtok, wg[:], wu[f], writes=[wgtok])
                c.dma("pool", wvtok, wv[:], wu[FC + f], writes=[wvtok])
                pg, pgtok = pgrot.next()
                pv, pvtok = pvrot.next()
                for kc in range(KC):
                    c.op("pe", lambda e: e.matmul(pg[:, :wd], lhsT=wg[:, kc, :], rhs=h2[:, kc, :wd], start=(kc == 0), stop=(kc == KC - 1)),
                         reads=[wgtok, "h2"], writes=[pgtok])
                for kc in range(KC):
                    c.op("pe", lambda e: e.matmul(pv[:, :wd], lhsT=wv[:, kc, :], rhs=h2[:, kc, :wd], start=(kc == 0), stop=(kc == KC - 1)),
                         reads=[wvtok, "h2"], writes=[pvtok])
                outs = []
                for (pp, pptok, rot, fi) in ((pg, pgtok, cg, f), (pv, pvtok, cv, FC + f)):
                    t, ttok = rot.next()
                    cw0 = V["cw"] + 0 * 88 + fi
                    cw1 = V["cw"] + 1 * 88 + fi
                    cw2 = V["cw"] + 2 * 88 + fi
                    c.op("dve", lambda e: e.tensor_scalar(out=t[:, :wi], in0=pp[:, 0:wi], scalar1=vt[:, cw0:cw0 + 1], scalar2=None, op0=ALU.mult),
                         reads=[pptok, "vec"], writes=[ttok])
                    c.op("dve", lambda e: e.scalar_tensor_tensor(out=t[:, :wi], in0=pp[:, 1:wi + 1], scalar=vt[:, cw1:cw1 + 1], in1=t[:, :wi],
                                                                 op0=ALU.mult, op1=ALU.add), reads=[pptok, ttok, "vec"], writes=[ttok])
                    c.op("dve", lambda e: e.scalar_tensor_tensor(out=t[:, :wi], in0=pp[:, 2:wi + 2], scalar=vt[:, cw2:cw2 + 1], in1=t[:, :wi],
                                                                 op0=ALU.mult, op1=ALU.add), reads=[pptok, ttok, "vec"], writes=[ttok])
                    outs.append((t, ttok))
                (tg, tgtok), (tv, tvtok) = outs
                s_, stok = sg.next()
                cbg = V["cb"] + f
                cbv = V["cb"] + FC + f
                c.op("act", lambda e: e.activation(out=s_[:, :wi], in_=tg[:, :wi], func=AF.Silu, bias=vt[:, cbg:cbg + 1], scale=1.0),
                     reads=[tgtok, "vec"], writes=[stok])
                c.op("dve", lambda e: e.scalar_tensor_tensor(out=act[:, f, :wi], in0=tv[:, :wi], scalar=vt[:, cbv:cbv + 1], in1=s_[:, :wi],
                                                              op0=ALU.add, op1=ALU.mult), reads=[tvtok, stok, "vec"], writes=["act%d" % f])
            for oc in range(KC):
                wt, wtok = wdrot.next()
                c.dma("pool", wtok, wt[:], wdn[oc], writes=[wtok])
                ps, ptok = prot.next()
                for f in range(FC):
                    c.op("pe", lambda e: e.matmul(ps[:, :wi], lhsT=wt[:, f, :], rhs=act[:, f, :wi], start=(f == 0), stop=(f == FC - 1)),
                         reads=[wtok, "act%d" % f], writes=[ptok])
                for (c0, c1, stream) in segs:
                    g2 = V["cg2"] if stream else V["g2"]
                    a0, a1 = max(c0, 1), min(c1, wd - 1)
                    if a1 <= a0:
                        continue
                    c.op("dve", lambda e: e.scalar_tensor_tensor(out=xt[:, oc, a0:a1], in0=ps[:, a0 - 1:a1 - 1], scalar=vt[:, g2 + oc:g2 + oc + 1], in1=xt[:, oc, a0:a1],
                                                                 op0=ALU.mult, op1=ALU.add), reads=[ptok, xtok, "vec"], writes=[xtok])
            if not final:
                c.dma("sp", "oT", oT[:, ocol:ocol + wi].rearrange("(kc p) t -> p kc t", p=128), xt[:, :, 1:wi + 1], reads=[xtok], writes=["oT"])
            else:
                ss, sstok = K["ps_ss"].next()
                for kc in range(KC):
                    sq, sqtok = K["sq"].next()
                    c.op("act", lambda e: e.activation(out=sq[:, :wi], in_=xt[:, kc, 1:wi + 1], func=AF.Square), reads=[xtok], writes=[sqtok])
                    c.op("pe", lambda e: e.matmul(ss[:, :wi], lhsT=K["ones"][:, :], rhs=sq[:, :wi], start=(kc == 0), stop=(kc == KC - 1)),
                         reads=[sqtok, "ones"], writes=[sstok])
                rstd = K["rstd"]
                emit_rstd(c, rstd, ss, sstok, wi)
                for kc in range(KC):
                    y, ytok = yt.next()
                    fg = V["fg"] + kc
                    c.op("dve", lambda e: e.scalar_tensor_tensor(out=y[:, :wi], in0=xt[:, kc, 1:wi + 1], scalar=vt[:, fg:fg + 1], in1=rstd[:, :wi],
                                                                 op0=ALU.mult, op1=ALU.mult), reads=[xtok, "rstd", "vec"], writes=[ytok])
                    c.dma("sp", ytok, oT[kc * 128:(kc + 1) * 128, ocol:ocol + wi], y[:, :wi], reads=[ytok], writes=["oT"])
            ocol += wi
        c.wait_all("sp", ["oT"])
    return nc


ATT_ACC2 = "pool"


def build_att(kind, NQL, NKL, NCT=CTX):
    S = 2
    SK = 2 if kind == "B" else 1
    DV = 256 if kind == "B" else 128
    NH = DV // 128
    NK = NCT + NKL
    NKT = NK // 128
    NCKT = NCT // 128
    R = 256
    scale = 128 ** -0.5
    nc = new_nc()
    qT = nc.dram_tensor("qT", [S, 128, NCT + NQL], BF16, kind="ExternalInput").ap()
    kT = nc.dram_tensor("kT", [SK, 128, NK], BF16, kind="ExternalInput").ap()
    v = nc.dram_tensor("v", [NK, DV], BF16, kind="ExternalInput").ap()
    cq = nc.dram_tensor("cq", [2, 128, NQL], F32, kind="ExternalInput").ap()
    ck = nc.dram_tensor("ck", [2, 128, NKL], F32, kind="ExternalInput").ap()
    rt = nc.dram_tensor("rt", [128, 128], BF16, kind="ExternalInput").ap()
    vec = nc.dram_tensor("vec", [128, 8], F32, kind="ExternalInput").ap()
    lp = nc.dram_tensor("lp", [128, 4, 128], F32, kind="ExternalInput").ap()
    oT = nc.dram_tensor("oT", [R, NCT + NQL], BF16, kind="ExternalOutput").ap()
    with ExitStack() as st:
        c = Ctx(nc, st)
        ones = c.sb("ones", [128, 128], BF16)
        c.op("dve", lambda e: e.memset(ones[:, :], 1.0), writes=["ones"])
        rtt = c.sb("rtt", [128, 128], BF16)
        vt = c.sb("vec", [128, 8], F32)
        lpt = c.sb("lpt", [128, 4, 128], F32)
        lam = c.sb("lam", [128, 8], F32)
        Kr = c.sb("Kr", [128, SK, NK], BF16)
        Vt = c.sb("Vt", [128, NKT, DV], BF16)
        raw = Rot(c, "raw", 2, [128, 512], BF16)
        cst = Rot(c, "cst", 2, [128, 2, 512], F32)
        xn = c.sb("xn", [128, 512], F32)
        xnb = c.sb("xnb", [128, 512], BF16)
        sqb = c.sb("sqb", [128, 512], BF16)
        rstd = c.sb("rstd", [128, 512], F32)
        t1 = c.sb("t1", [128, 512], F32)
        t2 = c.sb("t2", [128, 512], F32)
        qr = Rot(c, "qr", 2, [128, 512], BF16)
        E = Rot(c, "E", 3, [128, 2, 512], BF16)
        acc = [c.sb("acc%d" % i, [128, 2, 512], F32) for i in range(2)]
        ones32 = c.sb("ones32", [128, 128], F32)
        c.op("dve", lambda e: e.memset(ones32[:, :], 1.0), writes=["ones32"])
        sqb2 = c.sb("sqb2", [128, 512], BF16)
        rstd2 = c.sb("rstd2", [128, 512], F32)
        on = [[c.sb("on%d%d" % (s, h), [128, 512], F32) for h in range(NH)] for s in range(S)]
        rec = c.sb("rec", [128, 512], F32)
        ob = Rot(c, "ob", 2, [128, 512], BF16)
        ps_s = Rot(c, "pS", 2, [128, 2, 512], F32, psum=True)
        ps_o = [c.ps("pO%d" % h, [128, 512]) for h in range(NH)]
        ps_sum = c.ps("pSum", [128, 512])
        ps_ss2 = ps_sum
        ps_ss = c.ps("pSS", [128, 512])
        ps_rot = ps_ss
        c.dma("sp", "rtt", rtt[:], rt, writes=["rtt"])
        c.dma("sp", "vec", vt[:], vec, writes=["vec"])
        c.dma("sp", "Vt", Vt[:], v.rearrange("(kt p) d -> p kt d", p=128), writes=["Vt"])
        if kind == "B":
            c.dma("sp", "lpt", lpt[:], lp, writes=["lpt"])
            for i in range(2):
                c.op("dve", lambda e: e.tensor_tensor(out=t1[:, :128], in0=lpt[:, 2 * i, :], in1=lpt[:, 2 * i + 1, :], op=ALU.mult), reads=["lpt"], writes=["t1"])
                c.op("dve", lambda e: e.reduce_sum(out=lam[:, i:i + 1], in_=t1[:, :128], axis=AX.X), reads=["t1"], writes=["lam"])
            c.op("act", lambda e: e.activation(out=lam[:, 0:2], in_=lam[:, 0:2], func=AF.Exp), reads=["lam"], writes=["lam"])
            c.op("dve", lambda e: e.tensor_tensor(out=lam[:, 2:3], in0=lam[:, 0:1], in1=lam[:, 1:2], op=ALU.subtract), reads=["lam"], writes=["lam"])
            c.op("dve", lambda e: e.tensor_tensor(out=lam[:, 2:3], in0=lam[:, 2:3], in1=vt[:, 2:3], op=ALU.add), reads=["lam", "vec"], writes=["lam"])
            c.op("dve", lambda e: e.tensor_scalar(out=lam[:, 3:4], in0=lam[:, 2:3], scalar1=-1.0, scalar2=None, op0=ALU.mult), reads=["lam"], writes=["lam"])
            c.op("dve", lambda e: e.tensor_scalar(out=lam[:, 4:6], in0=vt[:, 4:6], scalar1=vt[:, 3:4], scalar2=None, op0=ALU.mult), reads=["lam", "vec"], writes=["lam"])

        def prep(src, srctok, W, cs, cstok, gain_col, dst, dsttok):
            cur, curtok = src, srctok
            if gain_col is not None:
                c.op("act", lambda e: e.activation(out=sqb[:, :W], in_=src[:, :W], func=AF.Square), reads=[srctok], writes=["sqb"])
                c.op("pe", lambda e: e.matmul(ps_ss[:, :W], lhsT=ones[:, :], rhs=sqb[:, :W], start=True, stop=True), reads=["sqb", "ones"], writes=["pSS"])
                emit_rstd(c, rstd, ps_ss, "pSS", W, n=128)
                c.op("dve", lambda e: e.scalar_tensor_tensor(out=xn[:, :W], in0=src[:, :W], scalar=vt[:, gain_col:gain_col + 1], in1=rstd[:, :W],
                                                             op0=ALU.mult, op1=ALU.mult), reads=[srctok, "rstd", "vec"], writes=["xn"])
                cur, curtok = xn, "xn"
                if cs is None:
                    c.op("act", lambda e: e.activation(out=dst[:, :W], in_=xn[:, :W], func=AF.Copy), reads=["xn"], writes=[dsttok])
                    return
                c.op("act", lambda e: e.activation(out=xnb[:, :W], in_=xn[:, :W], func=AF.Copy), reads=["xn"], writes=["xnb"])
                curb, curbtok = xnb, "xnb"
            else:
                if cs is None:
                    c.op("act", lambda e: e.activation(out=dst[:, :W], in_=src[:, :W], func=AF.Copy), reads=[srctok], writes=[dsttok])
                    return
                curb, curbtok = src, srctok
            c.op("pe", lambda e: e.matmul(ps_rot[:, :W], lhsT=rtt[:, :], rhs=curb[:, :W], start=True, stop=True), reads=["rtt", curbtok], writes=["pSS"])
            c.op("dve", lambda e: e.tensor_tensor(out=t1[:, :W], in0=cur[:, :W], in1=cs[:, 0, :W], op=ALU.mult), reads=[curtok, cstok], writes=["t1"])
            c.op("dve", lambda e: e.tensor_tensor(out=t2[:, :W], in0=ps_rot[:, :W], in1=cs[:, 1, :W], op=ALU.mult), reads=["pSS", cstok], writes=["t2"])
            c.op("dve", lambda e: e.tensor_tensor(out=dst[:, :W], in0=t1[:, :W], in1=t2[:, :W], op=ALU.add), reads=["t1", "t2"], writes=[dsttok])

        kgain = 1 if kind == "C" else None
        qgain = 0 if kind == "C" else None
        PE_SUM = False
        ktiles = [(0, NCT, None)] + [(NCT + s0, w, s0) for s0, w in tiles_of(NKL, 512)]
        for sk in range(SK):
            for (c0, w, r0) in ktiles:
                rw, rwtok = raw.next()
                c.dma("sp", rwtok, rw[:, :w], kT[sk, :, c0:c0 + w], writes=[rwtok])
                cs, cstok = None, None
                if r0 is not None:
                    cs, cstok = cst.next()
                    c.dma("sp", cstok, cs[:, :, :w], ck[:, :, r0:r0 + w].rearrange("a p t -> p a t"), writes=[cstok])
                prep(rw, rwtok, w, cs, cstok, kgain, Kr[:, sk, c0:c0 + w], "Kr%d_%d" % (sk, c0))
        ktoks = [["Kr%d_%d" % (sk, c0) for (c0, w, r0) in ktiles] for sk in range(SK)]
        qtiles = [(0, NCT, None, NCKT)] + [(NCT + s0, w, s0, NKT) for s0, w in tiles_of(NQL, 512)]
        units = [(qi, s) for qi in range(len(qtiles)) for s in range(S)]
        cs_of = {}

        def prep_unit(u):
            qi, s = units[u]
            c0, w, r0, nkt = qtiles[qi]
            if s == 0:
                cs, cstok = None, None
                if r0 is not None:
                    cs, cstok = cst.next()
                    c.dma("sp", cstok, cs[:, :, :w], cq[:, :, r0:r0 + w].rearrange("a p t -> p a t"), writes=[cstok])
                cs_of[qi] = (cs, cstok)
            cs, cstok = cs_of[qi]
            rw, rwtok = raw.next()
            c.dma("sp", rwtok, rw[:, :w], qT[s, :, c0:c0 + w], writes=[rwtok])
            q, qtok = qr.next()
            prep(rw, rwtok, w, cs, cstok, qgain, q, qtok)
            return q, qtok

        nxt = prep_unit(0)
        for u, (qi, s) in enumerate(units):
            c0, w, r0, nkt = qtiles[qi]
            q, qtok = nxt
            if u + 1 < len(units):
                nxt = prep_unit(u + 1)
            sk = s if SK == 2 else 0
            pend = {}
            npair = nkt // 2
            npe, ndve = [0], [0]

            def score(p_):
                ps, pstok = ps_s.next()
                for j in range(2):
                    kt = 2 * p_ + j
                    c.op("pe", lambda e: e.matmul(ps[:, j, :w], lhsT=Kr[:, sk, kt * 128:(kt + 1) * 128], rhs=q[:, :w], start=True, stop=True),
                         reads=ktoks[sk] + [qtok], writes=[pstok])
                pend[p_] = (ps, pstok)

            score(0)
            for p_ in range(npair):
                ps, pstok = pend.pop(p_)
                e_, etok = E.next()
                c.op("act", lambda e: e.activation(out=e_[:, :, :w], in_=ps[:, :, :w], func=AF.Exp, scale=scale), reads=[pstok], writes=[etok])
                if p_ + 1 < npair:
                    score(p_ + 1)
                for j in range(2):
                    kt = 2 * p_ + j
                    for h in range(NH):
                        c.op("pe", lambda e: e.matmul(ps_o[h][:, :w], lhsT=Vt[:, kt, h * 128:(h + 1) * 128], rhs=e_[:, j, :w], start=(kt == 0), stop=(kt == nkt - 1)),
                             reads=["Vt", etok], writes=["pO%d" % h])
                if PE_SUM and p_ % 3 == 2:
                    for j in range(2):
                        c.op("pe", lambda e: e.matmul(ps_sum[:, :w], lhsT=ones[:, :], rhs=e_[:, j, :w], start=(npe[0] == 0), stop=False),
                             reads=["ones", etok], writes=["pSum"])
                        npe[0] += 1
                else:
                    ac, actok = (acc[0], "acc0") if ndve[0] % 2 == 0 else (acc[1], "acc1")
                    if ndve[0] < 2:
                        c.op("dve", lambda e: e.tensor_copy(out=ac[:, :, :w], in_=e_[:, :, :w]), reads=[etok], writes=[actok])
                    else:
                        c.op("dve", lambda e: e.tensor_tensor(out=ac[:, :, :w], in0=ac[:, :, :w], in1=e_[:, :, :w], op=ALU.add), reads=[etok, actok], writes=[actok])
                    ndve[0] += 1
            if ndve[0] > 1:
                c.op("dve", lambda e: e.tensor_tensor(out=acc[0][:, :, :w], in0=acc[0][:, :, :w], in1=acc[1][:, :, :w], op=ALU.add), reads=["acc0", "acc1"], writes=["acc0"])
            c.op("dve", lambda e: e.tensor_tensor(out=acc[0][:, 0, :w], in0=acc[0][:, 0, :w], in1=acc[0][:, 1, :w], op=ALU.add), reads=["acc0"], writes=["acc0"])
            c.op("pe", lambda e: e.matmul(ps_sum[:, :w], lhsT=ones32[:, :], rhs=acc[0][:, 0, :w], start=(npe[0] == 0), stop=True), reads=["ones32", "acc0"], writes=["pSum"])
            c.op("dve", lambda e: e.reciprocal(out=rec[:, :w], in_=ps_sum[:, :w]), reads=["pSum"], writes=["rec"])
            for h in range(NH):
                c.op("dve", lambda e: e.tensor_tensor(out=on[s][h][:, :w], in0=ps_o[h][:, :w], in1=rec[:, :w], op=ALU.mult),
                     reads=["pO%d" % h, "rec"], writes=["on%d%d" % (s, h)])
            if kind == "C":
                o_, otok = ob.next()
                c.op("act", lambda e: e.activation(out=o_[:, :w], in_=on[s][0][:, :w], func=AF.Copy), reads=["on%d0" % s], writes=[otok])
                c.dma("sp", otok, oT[s * 128:(s + 1) * 128, c0:c0 + w], o_[:, :w], reads=[otok], writes=["oT"])
            if kind == "B" and s == S - 1:
                for h in range(NH):
                    c.op("dve", lambda e: e.scalar_tensor_tensor(out=on[0][h][:, :w], in0=on[1][h][:, :w], scalar=lam[:, 3:4], in1=on[0][h][:, :w],
                                                                 op0=ALU.mult, op1=ALU.add), reads=["on1%d" % h, "on0%d" % h, "lam"], writes=["on0%d" % h])
                    c.op("act", lambda e: e.activation(out=sqb2[:, :w], in_=on[0][h][:, :w], func=AF.Square), reads=["on0%d" % h], writes=["sqb2"])
                    c.op("pe", lambda e: e.matmul(ps_ss2[:, :w], lhsT=ones[:, :], rhs=sqb2[:, :w], start=(h == 0), stop=(h == NH - 1)),
                         reads=["sqb2", "ones"], writes=["pSum"])
                emit_rstd(c, rstd2, ps_ss2, "pSum", w, n=256, tok="rstd2")
                for h in range(NH):
                    o_, otok = ob.next()
                    c.op("dve", lambda e: e.scalar_tensor_tensor(out=o_[:, :w], in0=on[0][h][:, :w], scalar=lam[:, 4 + h:5 + h], in1=rstd2[:, :w],
                                                                 op0=ALU.mult, op1=ALU.mult), reads=["on0%d" % h, "rstd2", "lam"], writes=[otok])
                    c.dma("sp", otok, oT[h * 128:(h + 1) * 128, c0:c0 + w], o_[:, :w], reads=[otok], writes=["oT"])
        c.wait_all("sp", ["oT"])
    return nc


def rope_tables(n):
    freqs = (10000.0 ** (-np.arange(0, 64, 2, dtype=np.float32) / 64)).astype(np.float32)
    t = np.arange(n)
    row = (t // 64).astype(np.float32)
    col = (t % 64).astype(np.float32)
    ang_r = row[:, None] * freqs
    ang_c = col[:, None] * freqs
    ang = np.concatenate([ang_r, ang_r, ang_c, ang_c], -1).astype(np.float32)
    cos = np.cos(ang).astype(np.float32)
    sin = np.sin(ang).astype(np.float32)
    sgn = np.ones(128, np.float32)
    sgn[0:32] = -1
    sgn[64:96] = -1
    return np.ascontiguousarray(np.stack([cos.T, (sin * sgn).T], 0))


def rope_perm():
    m = np.arange(128)
    partner = np.where((m // 32) % 2 == 0, m + 32, m - 32)
    rt = np.zeros((128, 128), np.float32)
    rt[partner, m] = 1.0
    return rt.astype(NPBF)


GDN_FP32R = False
GDN_INV_BF16 = False


def build_gdn(NT, NCT=CTX):
    NCH = NT // 128
    CCH = NCT // 128
    nc = new_nc()
    qkvT = nc.dram_tensor("qkvT", [3, 128, NT], BF16, kind="ExternalInput").ap()
    ztm = nc.dram_tensor("ztm", [NT, 128], BF16, kind="ExternalInput").ap()
    gt = nc.dram_tensor("gt", [128, NCH, 4], BF16, kind="ExternalInput").ap()
    vec = nc.dram_tensor("vec", [128, 16], F32, kind="ExternalInput").ap()
    gbc = nc.dram_tensor("gbc", [128, 128], F32, kind="ExternalInput").ap()
    cst = nc.dram_tensor("cst", [6, 128, 128], F32, kind="ExternalInput").ap()
    otm = nc.dram_tensor("otm", [NT, 128], BF16, kind="ExternalOutput").ap()
    with ExitStack() as st:
        c = Ctx(nc, st)
        vt = c.sb("vec", [128, 16], F32)
        gb = c.sb("gbc", [128, 128], F32)
        C32 = c.sb("cst", [128, 6, 128], F32)
        Ib = c.sb("Ib", [128, 128], BF16)
        onesb = c.sb("onesb", [128, 128], BF16)
        qT = c.sb("qT", [128, NT], BF16)
        kT = c.sb("kT", [128, NT], BF16)
        ktm = c.sb("ktm", [128, NCH, 128], BF16)
        vtm = c.sb("vtm", [128, NCH, 128], BF16)
        of = c.sb("of", [128, NCH, 128], BF16)
        G = [c.sb("G%d" % d, [128, NCH], F32) for d in range(2)]
        BT = [c.sb("BT%d" % d, [128, NCH], F32) for d in range(2)]
        NB = [c.sb("NB%d" % d, [128, NCH], F32) for d in range(2)]
        Bc = [c.sb("Bc%d" % d, [128, NCH], F32) for d in range(2)]
        EB = [c.sb("EB%d" % d, [128, NCH], F32) for d in range(2)]
        BEB = [c.sb("BEB%d" % d, [128, NCH], F32) for d in range(2)]
        EKD = [c.sb("EKD%d" % d, [128, NCH], F32) for d in range(2)]
        CD = [c.sb("CD%d" % d, [128, NCH], F32) for d in range(2)]
        tri = c.sb("tri", [128, 2, 128], F32)
        zt = Rot(c, "zt", 2, [128, 128], BF16)
        ot = Rot(c, "ot", 2, [128, 128], BF16)
        banks = [c.ps("bank%d" % i, [128, 512]) for i in range(8)]

        def carve(b, i, n=1):
            return banks[b][:, 128 * i:128 * (i + n)]

        class RotAP:
            def __init__(self, aps, names):
                self.bufs, self.names, self.i = aps, names, 0

            def next(self):
                j = self.i % len(self.bufs)
                self.i += 1
                return self.bufs[j], self.names[j]

        pbig = banks[0]
        pG = banks[1]
        pT = RotAP([carve(1, 0), carve(1, 1)], ["bank1", "bank1"])

        class Chain:
            def __init__(self, d):
                n = "c%d" % d
                self.d = d
                self.oss = c.sb("oss" + n, [128, 4], F32)
                IDT = BF16 if GDN_INV_BF16 else (mybir.dt.float32r if GDN_FP32R else F32)
                self.Rm = c.sb("Rm" + n, [128, 128], IDT)
                for nm in ("diagb", "nd", "dm", "dmS", "dmI", "S32", "o2s", "osum", "osq", "sz"):
                    setattr(self, nm, c.sb(nm + n, [128, 128], F32))
                for nm in ("Rb", "qk", "kb", "vb", "Sb", "u_b"):
                    setattr(self, nm, c.sb(nm + n, [128, 128], BF16))
                self.Pm = Rot(c, "Pm" + n, 2, [128, 128], IDT)
                self.Qm = Rot(c, "Qm" + n, 2, [128, 128], IDT)
                self.qkT = Rot(c, "qkT" + n, 2, [128, 128], BF16)
                self.kd = Rot(c, "kd" + n, 2, [128, 128], BF16)
                self.ub = Rot(c, "ub" + n, 2, [128, 128], F32)
                self.wT = Rot(c, "wT" + n, 2, [128, 128], BF16)
                b0 = 4 * d
                self.bA, self.bB, self.bC, self.bD = ["bank%d" % (b0 + k) for k in range(4)]
                self.pA, self.pKK, self.pQK, self.pU = [carve(b0, k) for k in range(4)]
                self.pT = RotAP([carve(b0 + 1, 0), carve(b0 + 1, 1)], [self.bB, self.bB])
                self.pW = carve(b0 + 1, 2)
                self.pI = RotAP([carve(b0 + 2, k) for k in range(3)], [self.bC] * 3)
                self.pwS, self.pO1, self.pO2, self.pdS = [carve(b0 + 3, k) for k in range(4)]

            def t(self, nm):
                return "%sc%d" % (nm, self.d)

        c.push_scope()
        gtr = c.sb("gtr", [128, NCH, 4], BF16)
        gx = c.sb("gx", [128, NCH], F32)
        rb = c.sb("rb", [128, 3, 514], BF16)
        cv = c.sb("cv", [128, 514], F32)
        sv = c.sb("sv", [128, 514], F32)
        sqb = c.sb("sqb", [128, 512], BF16)
        rstd = c.sb("rstd", [128, 512], F32)
        vTb = c.sb("vTb", [128, 512], BF16)

        c.dma("sp", "vec", vt[:], vec, writes=["vec"])
        c.dma("sp", "gbc", gb[:], gbc, writes=["gbc"])
        c.dma("sp", "cst", C32[:], cst.rearrange("a p f -> p a f"), writes=["cst"])
        c.dma("sp", "gtr", gtr[:], gt, writes=["gtr"])
        I32 = C32[:, 0, :]
        ones32 = C32[:, 1, :]
        MS = [C32[:, 2, :], C32[:, 4, :]]
        MI = [C32[:, 3, :], C32[:, 5, :]]
        c.op("act", lambda e: e.activation(out=Ib[:, :], in_=C32[:, 0, :], func=AF.Copy), reads=["cst"], writes=["Ib"])
        c.op("act", lambda e: e.activation(out=onesb[:, :], in_=C32[:, 1, :], func=AF.Copy), reads=["cst"], writes=["onesb"])
        c.op("dve", lambda e: e.tensor_copy(out=tri[:, 0, :], in_=C32[:, 5, :]), reads=["cst"], writes=["tri"])
        c.op("dve", lambda e: e.tensor_copy(out=tri[:, 1, :], in_=C32[:, 3, :]), reads=["cst"], writes=["tri"])
        c.op("act", lambda e: e.activation(out=vt[:, 13:15], in_=vt[:, 0:2], func=AF.Exp), reads=["vec"], writes=["vec"])
        c.op("dve", lambda e: e.tensor_scalar(out=vt[:, 13:15], in0=vt[:, 13:15], scalar1=-1.0, scalar2=None, op0=ALU.mult), reads=["vec"], writes=["vec"])
        for d in range(2):
            c.op("act", lambda e: e.activation(out=gx[:, :], in_=gtr[:, :, d], func=AF.Exp, bias=vt[:, 2 + d:3 + d], scale=1.0), reads=["gtr", "vec"], writes=["gx"])
            c.op("act", lambda e: e.activation(out=gx[:, :], in_=gx[:, :], func=AF.Ln, bias=1.0, scale=1.0), reads=["gx"], writes=["gx"])
            c.op("dve", lambda e: e.tensor_scalar(out=G[d][:, :], in0=gx[:, :], scalar1=vt[:, 13 + d:14 + d], scalar2=None, op0=ALU.mult), reads=["gx", "vec"], writes=["G%d" % d])
            c.op("act", lambda e: e.activation(out=BT[d][:, :], in_=gtr[:, :, 2 + d], func=AF.Sigmoid), reads=["gtr"], writes=["BT%d" % d])
            c.op("dve", lambda e: e.tensor_scalar(out=NB[d][:, :], in0=BT[d][:, :], scalar1=-1.0, scalar2=None, op0=ALU.mult), reads=["BT%d" % d], writes=["NB%d" % d])
            c.op("pe", lambda e: e.matmul(pG[:, 0:NCH], lhsT=tri[:, d, :], rhs=G[d][:, :], start=True, stop=True), reads=["tri", "G%d" % d], writes=["bank1"])
            c.op("pe", lambda e: e.matmul(pG[:, 256:256 + NCH], lhsT=C32[:, 1, :], rhs=G[d][:, :], start=True, stop=True), reads=["cst", "G%d" % d], writes=["bank1"])
            c.op("dve", lambda e: e.tensor_copy(out=Bc[d][:, :], in_=pG[:, 0:NCH]), reads=["bank1"], writes=["Bc%d" % d])
            c.op("act", lambda e: e.activation(out=EB[d][:, :], in_=pG[:, 0:NCH], func=AF.Exp), reads=["bank1"], writes=["EB%d" % d])
            c.op("act", lambda e: e.activation(out=CD[d][:, :], in_=pG[:, 256:256 + NCH], func=AF.Exp), reads=["bank1"], writes=["CD%d" % d])
            c.op("dve", lambda e: e.tensor_tensor(out=EKD[d][:, :], in0=pG[:, 256:256 + NCH], in1=Bc[d][:, :], op=ALU.subtract), reads=["bank1", "Bc%d" % d], writes=["EKD%d" % d])
            c.op("act", lambda e: e.activation(out=EKD[d][:, :], in_=EKD[d][:, :], func=AF.Exp), reads=["EKD%d" % d], writes=["EKD%d" % d])
            c.op("dve", lambda e: e.tensor_tensor(out=BEB[d][:, :], in0=BT[d][:, :], in1=EB[d][:, :], op=ALU.mult), reads=["BT%d" % d, "EB%d" % d], writes=["BEB%d" % d])
        gtoks = ["Bc0", "Bc1", "EB0", "EB1", "CD0", "CD1", "EKD0", "EKD1", "BEB0", "BEB1", "BT0", "BT1", "NB0", "NB1"]

        segs = [(0, NCT)] + [(NCT + s0, w) for s0, w in tiles_of(NT - NCT, 512)]
        seq_lo = {0: 0}
        for (c0, w) in segs:
            lo_edge = (c0 == 0) or (c0 == NCT)
            hi_edge = (c0 + w == NCT) or (c0 + w == NT)
            a0 = c0 if lo_edge else c0 - 1
            a1 = c0 + w if hi_edge else c0 + w + 1
            if lo_edge:
                c.op("dve", lambda e: e.memset(rb[:, :, 0:1], 0.0), writes=["rb"])
            if hi_edge:
                c.op("dve", lambda e: e.memset(rb[:, :, w + 1:w + 2], 0.0), writes=["rb"])
            o0 = 1 if lo_edge else 0
            c.dma("sp", "rb", rb[:, :, o0:o0 + (a1 - a0)], qkvT[:, :, a0:a1].rearrange("a p t -> p a t"), writes=["rb"])
            for i in range(3):
                t0 = 4 + 3 * i
                c.op("dve", lambda e: e.tensor_scalar(out=cv[:, :w], in0=rb[:, i, 0:w], scalar1=vt[:, t0:t0 + 1], scalar2=None, op0=ALU.mult), reads=["rb", "vec"], writes=["cv"])
                c.op("dve", lambda e: e.scalar_tensor_tensor(out=cv[:, :w], in0=rb[:, i, 1:w + 1], scalar=vt[:, t0 + 1:t0 + 2], in1=cv[:, :w], op0=ALU.mult, op1=ALU.add),
                     reads=["rb", "vec", "cv"], writes=["cv"])
                c.op("dve", lambda e: e.scalar_tensor_tensor(out=cv[:, :w], in0=rb[:, i, 2:w + 2], scalar=vt[:, t0 + 2:t0 + 3], in1=cv[:, :w], op0=ALU.mult, op1=ALU.add),
                     reads=["rb", "vec", "cv"], writes=["cv"])
                if i == 2:
                    c.op("act", lambda e: e.activation(out=vTb[:, :w], in_=cv[:, :w], func=AF.Silu), reads=["cv"], writes=["vTb"])
                    for s in range(w // 128):
                        p_, ptok = pT.next()
                        c.op("pe", lambda e: e.matmul(p_[:, :], lhsT=vTb[:, s * 128:(s + 1) * 128], rhs=Ib[:, :], start=True, stop=True), reads=["vTb", "Ib"], writes=[ptok])
                        ch = (c0 + s * 128) // 128
                        c.op("act", lambda e: e.activation(out=vtm[:, ch, :], in_=p_[:, :], func=AF.Copy), reads=[ptok], writes=["vtm%d" % ch])
                    continue
                c.op("act", lambda e: e.activation(out=sv[:, :w], in_=cv[:, :w], func=AF.Silu), reads=["cv"], writes=["sv"])
                c.op("act", lambda e: e.activation(out=sqb[:, :w], in_=sv[:, :w], func=AF.Square), reads=["sv"], writes=["sqb"])
                c.op("pe", lambda e: e.matmul(pbig[:, :w], lhsT=onesb[:, :], rhs=sqb[:, :w], start=True, stop=True), reads=["sqb", "onesb"], writes=["bank0"])
                emit_rstd(c, rstd, pbig, "bank0", w, n=1)
                dst = qT if i == 0 else kT
                dtok = ("qT%d" if i == 0 else "kT%d") % c0
                sc = (128 ** -0.5) if i == 0 else 1.0
                c.op("dve", lambda e: e.scalar_tensor_tensor(out=dst[:, c0:c0 + w], in0=sv[:, :w], scalar=sc, in1=rstd[:, :w], op0=ALU.mult, op1=ALU.mult),
                     reads=["sv", "rstd"], writes=[dtok])
                if i == 1:
                    for s in range(w // 128):
                        p_, ptok = pT.next()
                        c.op("pe", lambda e: e.matmul(p_[:, :], lhsT=kT[:, c0 + s * 128:c0 + (s + 1) * 128], rhs=Ib[:, :], start=True, stop=True), reads=[dtok, "Ib"], writes=[ptok])
                        ch = (c0 + s * 128) // 128
                        c.op("dve", lambda e: e.tensor_copy(out=ktm[:, ch, :], in_=p_[:, :]), reads=[ptok], writes=["ktm%d" % ch])

        c.pop_scope()
        chains = [Chain(0), Chain(1)]

        def INV(ap):
            return ap

        def AS32(ap):
            return ap if GDN_INV_BF16 else (ap.bitcast(F32) if GDN_FP32R else ap)

        def seg_of(ch):
            t = ch * 128
            if t < NCT:
                return 0
            return NCT + ((t - NCT) // 512) * 512

        def pre(ch, X):
            d = X.d
            col = slice(ch, ch + 1)
            qtok = "qT%d" % seg_of(ch)
            ktok = "kT%d" % seg_of(ch)
            cs = slice(ch * 128, (ch + 1) * 128)
            c.op("dve", lambda e: e.tensor_scalar(out=X.diagb[:, :], in0=C32[:, 0, :], scalar1=Bc[d][:, col], scalar2=None, op0=ALU.mult), reads=["cst"] + gtoks, writes=[X.t("diagb")])
            c.op("pe", lambda e: e.matmul(X.pKK, lhsT=kT[:, cs], rhs=kT[:, cs], start=True, stop=True), reads=[ktok], writes=[X.bA])
            c.op("pe", lambda e: e.matmul(X.pQK, lhsT=qT[:, cs], rhs=kT[:, cs], start=True, stop=True), reads=[qtok, ktok], writes=[X.bA])
            yield
            c.op("pe", lambda e: e.matmul(X.pA, lhsT=C32[:, 1, :], rhs=X.diagb[:, :], start=True, stop=True), reads=["cst", X.t("diagb")], writes=[X.bA])
            yield
            c.op("dve", lambda e: e.tensor_scalar(out=X.nd[:, :], in0=X.pA, scalar1=Bc[d][:, col], scalar2=0.0, op0=ALU.subtract, op1=ALU.max), reads=[X.bA] + gtoks, writes=[X.t("nd")])
            yield
            c.op("act", lambda e: e.activation(out=X.dm[:, :], in_=X.nd[:, :], func=AF.Exp, scale=-1.0), reads=[X.t("nd")], writes=[X.t("dm")])
            yield
            c.op("dve", lambda e: e.tensor_tensor(out=X.dmS[:, :], in0=X.dm[:, :], in1=MS[d], op=ALU.mult), reads=[X.t("dm"), "cst"], writes=[X.t("dmS")])
            c.op("dve", lambda e: e.tensor_tensor(out=X.dmI[:, :], in0=X.dm[:, :], in1=MI[d], op=ALU.mult), reads=[X.t("dm"), "cst"], writes=[X.t("dmI")])
            yield
            Q, Qtok = X.Qm.next()
            c.op("dve", lambda e: e.scalar_tensor_tensor(out=Q[:, :], in0=X.pKK, scalar=NB[d][:, col], in1=X.dmS[:, :], op0=ALU.mult, op1=ALU.mult),
                 reads=[X.bA, X.t("dmS")] + gtoks, writes=[Qtok])
            c.op("dve", lambda e: e.tensor_tensor(out=X.qk[:, :], in0=X.pQK, in1=X.dmI[:, :], op=ALU.mult), reads=[X.bA, X.t("dmI")], writes=[X.t("qk")])
            yield
            p_, ptok = X.pT.next()
            c.op("pe", lambda e: e.matmul(p_, lhsT=AS32(Q[:, :]), rhs=(Ib[:, :] if GDN_INV_BF16 else C32[:, 0, :]), start=True, stop=True), reads=[Qtok, "cst", "Ib"], writes=[ptok])
            p2, p2tok = X.pT.next()
            c.op("pe", lambda e: e.matmul(p2, lhsT=X.qk[:, :], rhs=Ib[:, :], start=True, stop=True), reads=[X.t("qk"), "Ib"], writes=[p2tok])
            yield
            P, Ptok = X.Pm.next()
            c.op("act", lambda e: e.activation(out=P[:, :], in_=p_, func=AF.Copy), reads=[ptok], writes=[Ptok])
            c.op("dve", lambda e: e.tensor_tensor(out=X.Rm[:, :], in0=p_, in1=C32[:, 0, :], op=ALU.add), reads=[ptok, "cst"], writes=[X.t("Rm")])
            qkT_, qkTtok = X.qkT.next()
            c.op("act", lambda e: e.activation(out=qkT_[:, :], in_=p2, func=AF.Copy), reads=[p2tok], writes=[qkTtok])
            yield
            for step in range(6):
                last = step == 5
                pq, pqtok = X.pI.next()
                c.op("pe", lambda e: e.matmul(pq, lhsT=INV(P[:, :]), rhs=INV(Q[:, :]), start=True, stop=True), reads=[Ptok, Qtok], writes=[pqtok])
                if not last:
                    pp, pptok = X.pI.next()
                    c.op("pe", lambda e: e.matmul(pp, lhsT=INV(Q[:, :]), rhs=INV(P[:, :]), start=True, stop=True), reads=[Ptok, Qtok], writes=[pptok])
                yield
                Q2, Q2tok = X.Qm.next()
                c.op("dve", lambda e: e.tensor_copy(out=Q2[:, :], in_=pq), reads=[pqtok], writes=[Q2tok])
                if not last:
                    P2, P2tok = X.Pm.next()
                    c.op("act", lambda e: e.activation(out=P2[:, :], in_=pp, func=AF.Copy), reads=[pptok], writes=[P2tok])
                    P, Ptok = P2, P2tok
                Q, Qtok = Q2, Q2tok
                yield
                pr, prtok = X.pI.next()
                c.op("pe", lambda e: e.matmul(pr, lhsT=INV(Q[:, :]), rhs=INV(X.Rm[:, :]), start=True, stop=True), reads=[Qtok, X.t("Rm")], writes=[prtok])
                yield
                c.op("dve", lambda e: e.tensor_tensor(out=X.Rm[:, :], in0=pr, in1=AS32(X.Rm[:, :]), op=ALU.add), reads=[prtok, X.t("Rm")], writes=[X.t("Rm")])
                yield
            c.op("act", lambda e: e.activation(out=X.Rb[:, :], in_=AS32(X.Rm[:, :]), func=AF.Copy), reads=[X.t("Rm")], writes=[X.t("Rb")])
            kd_, kdtok = X.kd.next()
            c.op("pool", lambda e: e.tensor_scalar(out=X.kb[:, :], in0=ktm[:, ch, :], scalar1=BEB[d][:, col], scalar2=None, op0=ALU.mult), reads=["ktm%d" % ch] + gtoks, writes=[X.t("kb")])
            c.op("pool", lambda e: e.tensor_scalar(out=kd_[:, :], in0=ktm[:, ch, :], scalar1=EKD[d][:, col], scalar2=None, op0=ALU.mult), reads=["ktm%d" % ch] + gtoks, writes=[kdtok])
            c.op("pool", lambda e: e.tensor_scalar(out=X.vb[:, :], in0=vtm[:, ch, :], scalar1=BT[d][:, col], scalar2=None, op0=ALU.mult), reads=["vtm%d" % ch] + gtoks, writes=[X.t("vb")])
            yield
            c.op("pe", lambda e: e.matmul(X.pU, lhsT=X.Rb[:, :], rhs=X.vb[:, :], start=True, stop=True), reads=[X.t("Rb"), X.t("vb")], writes=[X.bA])
            c.op("pe", lambda e: e.matmul(X.pW, lhsT=X.kb[:, :], rhs=X.Rb[:, :], start=True, stop=True), reads=[X.t("Rb"), X.t("kb")], writes=[X.bB])
            yield
            ub_, ubtok = X.ub.next()
            wT_, wTtok = X.wT.next()
            c.op("act", lambda e: e.activation(out=ub_[:, :], in_=X.pU, func=AF.Copy), reads=[X.bA], writes=[ubtok])
            c.op("dve", lambda e: e.tensor_copy(out=wT_[:, :], in_=X.pW), reads=[X.bB], writes=[wTtok])
            X.slot_out = (qkT_, qkTtok, kd_, kdtok, ub_, ubtok, wT_, wTtok)
            yield

        def rec(ch, X, slot, fin):
            d = X.d
            qkT_, qkTtok, kd_, kdtok, ub_, ubtok, wT_, wTtok = slot
            col = slice(ch, ch + 1)
            cs = slice(ch * 128, (ch + 1) * 128)
            qtok = "qT%d" % seg_of(ch)
            c.op("pe", lambda e: e.matmul(X.pwS, lhsT=wT_[:, :], rhs=X.Sb[:, :], start=True, stop=True), reads=[wTtok, X.t("Sb")], writes=[X.bD])
            c.op("pe", lambda e: e.matmul(X.pO1, lhsT=qT[:, cs], rhs=X.Sb[:, :], start=True, stop=True), reads=[qtok, X.t("Sb")], writes=[X.bD])
            yield
            c.op("dve", lambda e: e.tensor_tensor(out=X.u_b[:, :], in0=ub_[:, :], in1=X.pwS, op=ALU.subtract), reads=[ubtok, X.bD], writes=[X.t("u_b")])
            yield
            c.op("pe", lambda e: e.matmul(X.pdS, lhsT=kd_[:, :], rhs=X.u_b[:, :], start=True, stop=True), reads=[kdtok, X.t("u_b")], writes=[X.bD])
            c.op("pe", lambda e: e.matmul(X.pO2, lhsT=qkT_[:, :], rhs=X.u_b[:, :], start=True, stop=True), reads=[qkTtok, X.t("u_b")], writes=[X.bD])
            yield
            c.op("dve", lambda e: e.scalar_tensor_tensor(out=X.S32[:, :], in0=X.S32[:, :], scalar=CD[d][:, col], in1=X.pdS, op0=ALU.mult, op1=ALU.add),
                 reads=[X.t("S32"), X.bD] + gtoks, writes=[X.t("S32")])
            yield
            c.op("act", lambda e: e.activation(out=X.Sb[:, :], in_=X.S32[:, :], func=AF.Copy), reads=[X.t("S32")], writes=[X.t("Sb")])
            c.op("act", lambda e: e.activation(out=X.o2s[:, :], in_=X.pO2, func=AF.Copy), reads=[X.bD], writes=[X.t("o2s")])
            yield
            if not fin:
                c.op("dve", lambda e: e.scalar_tensor_tensor(out=of[:, ch, :], in0=X.pO1, scalar=EB[d][:, col], in1=X.o2s[:, :], op0=ALU.mult, op1=ALU.add),
                     reads=[X.bD, X.t("o2s")] + gtoks, writes=["of%d" % ch])
                yield
                return
            c.op("dve", lambda e: e.scalar_tensor_tensor(out=X.osum[:, :], in0=X.pO1, scalar=EB[d][:, col], in1=X.o2s[:, :], op0=ALU.mult, op1=ALU.add),
                 reads=[X.bD, X.t("o2s")] + gtoks, writes=[X.t("osum")])
            yield
            c.op("dve", lambda e: e.tensor_tensor(out=X.osum[:, :], in0=X.osum[:, :], in1=of[:, ch, :], op=ALU.add), reads=[X.t("osum"), "of%d" % ch], writes=[X.t("osum")])
            yield
            c.op("act", lambda e: e.activation(out=X.osq[:, :], in_=X.osum[:, :], func=AF.Square, accum_out=X.oss[:, 0:1]), reads=[X.t("osum")], writes=[X.t("osq"), X.t("oss")])
            yield
            emit_rstd(c, X.oss, X.oss, X.t("oss"), 1, n=128, tok=X.t("oss"))
            z_, ztok = zt.next()
            c.dma("sp", ztok, z_[:, :], ztm[ch * 128:(ch + 1) * 128, :], writes=[ztok])
            c.op("act", lambda e: e.activation(out=X.sz[:, :], in_=z_[:, :], func=AF.Silu), reads=[ztok], writes=[X.t("sz")])
            yield
            c.op("dve", lambda e: e.scalar_tensor_tensor(out=X.osum[:, :], in0=X.osum[:, :], scalar=X.oss[:, 0:1], in1=gb[:, :], op0=ALU.mult, op1=ALU.mult),
                 reads=[X.t("osum"), X.t("oss"), "gbc"], writes=[X.t("osum")])
            o_, otok = ot.next()
            c.op("dve", lambda e: e.tensor_tensor(out=o_[:, :], in0=X.osum[:, :], in1=X.sz[:, :], op=ALU.mult), reads=[X.t("osum"), X.t("sz")], writes=[otok])
            c.dma("sp", otok, otm[ch * 128:(ch + 1) * 128, :], o_[:, :], reads=[otok], writes=["otm"])
            yield

        def drive(gens):
            gens = [g_ for g_ in gens if g_ is not None]
            while gens:
                alive = []
                for g_ in gens:
                    try:
                        next(g_)
                        alive.append(g_)
                    except StopIteration:
                        pass
                gens = alive

        orders = [list(range(NCH)), list(range(CCH - 1, -1, -1)) + list(range(NCH - 1, CCH - 1, -1))]
        pos = [{ch: i for i, ch in enumerate(o)} for o in orders]
        for X in chains:
            c.op("dve", lambda e: e.memset(X.S32[:, :], 0.0), writes=[X.t("S32")])
            c.op("dve", lambda e: e.memset(X.Sb[:, :], 0.0), writes=[X.t("Sb")])
        drive([pre(orders[d][0], chains[d]) for d in range(2)])
        slots = [chains[d].slot_out for d in range(2)]
        for i in range(NCH):
            gens = []
            for d in range(2):
                if i + 1 < NCH:
                    gens.append(pre(orders[d][i + 1], chains[d]))
            for d in range(2):
                ch = orders[d][i]
                gens.append(rec(ch, chains[d], slots[d], pos[d][ch] > pos[1 - d][ch]))
            drive(gens)
            if i + 1 < NCH:
                slots = [chains[d].slot_out for d in range(2)]
        c.wait_all("sp", ["otm"])
    return nc


def gdn_consts():
    i = np.arange(128)
    I = np.eye(128, dtype=np.float32)
    ones = np.ones((128, 128), np.float32)
    LS = (i[:, None] > i[None, :]).astype(np.float32)
    LI = (i[:, None] >= i[None, :]).astype(np.float32)
    return np.ascontiguousarray(np.stack([I, ones, LS, LI, LS.T, LI.T], 0))


def tile_w(w, m=128):
    w = np.asarray(w, np.float32)
    K, N = w.shape
    nb = (N + m - 1) // m
    if nb * m != N:
        w = np.concatenate([w, np.zeros((K, nb * m - N), np.float32)], 1)
    return np.ascontiguousarray(w.reshape(K // 128, 128, nb, m).transpose(2, 1, 0, 3))


def fm(v):
    return np.ascontiguousarray(np.asarray(v, np.float32).reshape(KC, 128).T)


def lambda_init(layer):
    return 0.8 - 0.6 * math.exp(-0.3 * layer)


PRE_TILES = [(0, 32, 1)] + [(32 + i * 512, 512, 0) for i in range(4)]
CTXC = CTX // NCORE
POST_T = CTXC + 2 + SEQ // NCORE + 2
POST_SPECIAL = [(0, "vl"), (CTXC + 1, "vr"), (CTXC + 2, "vl"), (POST_T - 1, "vr")]


def post_tiles():
    inner = POST_T - 2
    n = 5
    base, extra = divmod(inner, n)
    tiles, a = [], 1
    for i in range(n):
        wi = base + (1 if i < extra else 0)
        s0, wd = a - 1, wi + 2
        segs = []
        for (g0, g1, stream) in ((0, CTXC + 2, 1), (CTXC + 2, POST_T, 0)):
            c0, c1 = max(g0, s0) - s0, min(g1, s0 + wd) - s0
            if c1 > c0:
                segs.append((c0, c1, stream))
        masks = [(col - s0, fl) for col, fl in POST_SPECIAL if s0 <= col < s0 + wd]
        tiles.append((s0, wd, segs, masks))
        a += wi
    return tiles


POST_TILES = post_tiles()
TL = SEQ // NCORE


def kernel(x, c, ctx, c_ctx, w_mod, b_mod, norm1_g, norm2_g, w_in_even, a_conv_w, a_A_log, a_dt_bias,
           a_norm_g, b_lambda, b_norm_g, w_out_even, w_in_odd, c_q_norm, c_k_norm, w_out_odd,
           ffn_up, ffn_conv_w, ffn_conv_b, ffn_down, final_g):
    f32 = np.float32
    x = np.asarray(x, f32)
    ctx = np.asarray(ctx, f32)
    progs = {}

    def prog(key, fn):
        if key not in progs:
            progs[key] = fn()
        return progs[key]

    c2 = np.ascontiguousarray(np.stack([fm(np.asarray(c, f32)[0]), fm(np.asarray(c_ctx, f32))], -1))
    w_mod = np.asarray(w_mod, f32)
    b_mod = np.asarray(b_mod, f32)
    maps = []
    for j in range(NCORE):
        sl = slice(j * MODC, (j + 1) * MODC)
        bm = np.ascontiguousarray(np.broadcast_to(b_mod[None, :, sl], (2, DEPTH, MODC)))
        maps.append({"c2": c2, "wm": np.stack([tile_w(w_mod[l_][:, sl], 512) for l_ in range(DEPTH)], 0), "bm": bm})
    res = run(prog("mod", build_mod), maps, "mod")
    mod = np.concatenate([r["out"] for r in res], -1)

    def modv(s, l, m):
        return mod[s, l, m * D:(m + 1) * D]

    xlT = np.ascontiguousarray(x[0].T)
    xcT = np.ascontiguousarray(ctx[0].T)
    rope = rope_tables(SEQ)
    rt = rope_perm()
    zcol = np.zeros((D, 1), f32)

    for l in range(DEPTH):
        even = l % 2 == 0
        e = l // 2
        last = l == DEPTH - 1
        w_in = np.asarray(w_in_even[e] if even else w_in_odd[e], f32)
        ncols = w_in.shape[1]
        w_in = tile_w(w_in)
        vec = np.ascontiguousarray(np.stack([fm(norm1_g[l]), fm(modv(0, l, 0)), fm(modv(0, l, 1)), fm(modv(1, l, 0)), fm(modv(1, l, 1))], 1))
        maps = [{"xT": np.ascontiguousarray(np.concatenate([xcT[:, 32 * j:32 * (j + 1)], xlT[:, TL * j:TL * (j + 1)]], 1)), "vec": vec, "w": w_in}
                for j in range(NCORE)]
        res = run(prog(("pre", ncols), lambda: build_pre(ncols, PRE_TILES)), maps, "pre%d" % l)
        pc = np.concatenate([r["pT"][:, :32] for r in res], 1)
        pl = np.concatenate([r["pT"][:, 32:] for r in res], 1)
        p = np.concatenate([pc, pl], 1)
        del res, maps
        mT = np.zeros((D, CTX + SEQ), NPBF)
        if even:
            NT = CTX + SEQ
            cw = np.asarray(a_conv_w[e], f32)
            maps = []
            for j in range(NCORE):
                hs = slice(j * 128, (j + 1) * 128)
                qkvT = np.ascontiguousarray(np.stack([p[j * 128:(j + 1) * 128], p[1024 + j * 128:1024 + (j + 1) * 128], p[2048 + j * 128:2048 + (j + 1) * 128]], 0))
                ztm = np.ascontiguousarray(p[3072 + j * 128:3072 + (j + 1) * 128].T)
                gates = np.stack([p[4096 + gi * 8 + j] for gi in range(4)], 0)
                gt = np.ascontiguousarray(gates.reshape(4, NT // 128, 128).transpose(2, 1, 0))
                vec = np.zeros((128, 16), f32)
                vec[:, 0:2] = np.asarray(a_A_log[e], f32)[:, j]
                vec[:, 2:4] = np.asarray(a_dt_bias[e], f32)[:, j]
                for i, off in enumerate((0, 1024, 2048)):
                    for t in range(3):
                        vec[:, 4 + 3 * i + t] = cw[t, off + j * 128:off + (j + 1) * 128]
                gbc = np.ascontiguousarray(np.broadcast_to(np.asarray(a_norm_g[e], f32), (128, 128)))
                maps.append({"qkvT": qkvT, "ztm": ztm, "gt": gt, "vec": vec, "gbc": gbc, "cst": gdn_consts()})
            res = run(prog("gdn", lambda: build_gdn(NT)), maps, "gdn%d" % l)
            for j in range(NCORE):
                mT[j * 128:(j + 1) * 128, :] = res[j]["otm"].T
            del res, maps
            HQ = SEQ // 2
            li = lambda_init(l)
            lp = np.ascontiguousarray(np.broadcast_to(np.asarray(b_lambda[e], f32), (128, 4, 128)))
            bng = np.asarray(b_norm_g[e], f32)
            maps = []
            for j in range(NCORE):
                hb, half = j // 2, j % 2
                qrows = [slice(A_IN + hb * 256 + m * 128, A_IN + hb * 256 + (m + 1) * 128) for m in range(2)]
                krows = [slice(A_IN + 1024 + hb * 256 + m * 128, A_IN + 1024 + hb * 256 + (m + 1) * 128) for m in range(2)]
                qT = np.ascontiguousarray(np.stack([np.concatenate([p[r, :CTX], p[r, CTX + half * HQ:CTX + (half + 1) * HQ]], 1) for r in qrows], 0))
                kT = np.ascontiguousarray(np.stack([p[r] for r in krows], 0))
                v = np.ascontiguousarray(p[A_IN + 2048 + hb * 256:A_IN + 2048 + (hb + 1) * 256].T)
                vec = np.zeros((128, 8), f32)
                vec[:, 2] = li
                vec[:, 3] = 1.0 - li
                vec[:, 4] = bng[:128]
                vec[:, 5] = bng[128:]
                maps.append({"qT": qT, "kT": kT, "v": v, "cq": np.ascontiguousarray(rope[:, :, half * HQ:(half + 1) * HQ]), "ck": rope, "rt": rt, "vec": vec, "lp": lp})
            res = run(prog("attB", lambda: build_att("B", HQ, SEQ)), maps, "attB%d" % l)
            for j in range(NCORE):
                hb, half = j // 2, j % 2
                rows = slice(1024 + hb * 256, 1024 + (hb + 1) * 256)
                if half == 0:
                    mT[rows, :CTX] = res[j]["oT"][:, :CTX]
                mT[rows, CTX + half * HQ:CTX + (half + 1) * HQ] = res[j]["oT"][:, CTX:]
            del res, maps
        else:
            lp0 = np.zeros((128, 4, 128), f32)
            maps = []
            for j in range(NCORE):
                g = j // 2
                qT = np.ascontiguousarray(np.stack([p[(2 * j + s) * 128:(2 * j + s + 1) * 128] for s in range(2)], 0))
                kT = np.ascontiguousarray(p[2048 + g * 128:2048 + (g + 1) * 128][None])
                v = np.ascontiguousarray(p[2560 + g * 128:2560 + (g + 1) * 128].T)
                vec = np.zeros((128, 8), f32)
                vec[:, 0] = np.asarray(c_q_norm[e], f32)
                vec[:, 1] = np.asarray(c_k_norm[e], f32)
                maps.append({"qT": qT, "kT": kT, "v": v, "cq": rope, "ck": rope, "rt": rt, "vec": vec, "lp": lp0})
            res = run(prog("attC", lambda: build_att("C", SEQ, SEQ)), maps, "attC%d" % l)
            for j in range(NCORE):
                mT[2 * j * 128:(2 * j + 2) * 128, :] = res[j]["oT"]
            del res, maps
        del p
        vec = np.zeros((128, POST_NV), f32)
        V = POST_V
        vec[:, V["n2g"]:V["n2g"] + 16] = fm(norm2_g[l])
        for nm, s, m in (("g1", 0, 2), ("sh2", 0, 3), ("sc2", 0, 4), ("g2", 0, 5), ("cg1", 1, 2), ("csh2", 1, 3), ("csc2", 1, 4), ("cg2", 1, 5)):
            vec[:, V[nm]:V[nm] + 16] = fm(modv(s, l, m))
        vec[:, V["fg"]:V["fg"] + 16] = fm(final_g)
        vec[:, V["cw"]:V["cw"] + 3 * 88] = np.asarray(ffn_conv_w[l], f32).reshape(3, 88, 128).transpose(2, 0, 1).reshape(128, 264)
        vec[:, V["cb"]:V["cb"] + 88] = np.asarray(ffn_conv_b[l], f32).reshape(88, 128).T
        wo = tile_w(w_out_even[e] if even else w_out_odd[e])
        wu = tile_w(ffn_up[l])
        wd = tile_w(ffn_down[l])
        mcT, mlT = mT[:, :CTX], mT[:, CTX:]
        zb = np.zeros((D, 1), NPBF)
        maps = []
        for j in range(NCORE):
            lo, hi = TL * j, TL * (j + 1)
            xl_ = [zcol if j == 0 else xlT[:, lo - 1:lo], xlT[:, lo:hi], zcol if j == NCORE - 1 else xlT[:, hi:hi + 1]]
            ml_ = [zb if j == 0 else mlT[:, lo - 1:lo], mlT[:, lo:hi], zb if j == NCORE - 1 else mlT[:, hi:hi + 1]]
            vj = vec.copy()
            vj[:, V["vl"]] = 0.0 if j == 0 else 1.0
            vj[:, V["vr"]] = 0.0 if j == NCORE - 1 else 1.0
            clo, chi = CTXC * j, CTXC * (j + 1)
            xc_ = [zcol if j == 0 else xcT[:, clo - 1:clo], xcT[:, clo:chi], zcol if j == NCORE - 1 else xcT[:, chi:chi + 1]]
            mc_ = [zb if j == 0 else mcT[:, clo - 1:clo], mcT[:, clo:chi], zb if j == NCORE - 1 else mcT[:, chi:chi + 1]]
            maps.append({"xT": np.ascontiguousarray(np.concatenate(xc_ + xl_, 1)),
                         "mT": np.ascontiguousarray(np.concatenate(mc_ + ml_, 1)), "vec": vj, "wo": wo, "wu": wu, "wd": wd})
        if not last:
            res = run(prog("post", lambda: build_post(POST_TILES, False)), maps, "post%d" % l)
            xcT = np.ascontiguousarray(np.concatenate([r["oT"][:, :CTXC] for r in res], 1))
            xlT = np.ascontiguousarray(np.concatenate([r["oT"][:, CTXC + 2:] for r in res], 1))
        else:
            res = run(prog("postf", lambda: build_post(POST_TILES, True)), maps, "postf%d" % l)
            xlT = np.concatenate([r["oT"][:, CTXC + 2:] for r in res], 1)
        del res, maps, mT
    return np.ascontiguousarray(xlT.T)[None].astype(np.float32)
```

```python
import math
import os
import sys
from contextlib import ExitStack

import ml_dtypes
import numpy as np
import concourse.bass as bass
import concourse.mybir as mybir
from concourse.bass_utils import run_bass_kernel_spmd

F32 = mybir.dt.float32
BF16 = mybir.dt.bfloat16
AF = mybir.ActivationFunctionType
ALU = mybir.AluOpType
AX = mybir.AxisListType
NPBF = ml_dtypes.bfloat16

D = 2048
KC = 16
NCORE = 8
SEQ = 16384
CTX = 256
DEPTH = 4
EPS = 1e-6
DFF = 5632
FC = 44
A_IN = 4128
EVEN_IN = 7200
ODD_IN = 3072


class Ctx:
    def __init__(self, nc, stack):
        self.nc = nc
        self.stack = stack
        self.E = {"pe": nc.tensor, "act": nc.scalar, "dve": nc.vector, "pool": nc.gpsimd, "sp": nc.sync}
        self.sems = {}
        self.cnt = {}
        self.known = {e: {} for e in self.E}
        self.lastw = {}
        self.readers = {}
        self.ninst = 0

    def sb(self, name, shape, dt):
        return self.stack.enter_context(self.nc.sbuf_tensor("sb_" + name, list(shape), dt))

    def ps(self, name, shape, dt=F32):
        return self.stack.enter_context(self.nc.psum_tensor("ps_" + name, list(shape), dt))

    def sem(self, key):
        if key not in self.sems:
            self.sems[key] = self.stack.enter_context(self.nc.semaphore("s_" + key.replace(":", "_")))
            self.cnt[key] = 0
        return self.sems[key]

    def _waits(self, eng, reads, writes):
        need = {}

        def add(ev):
            if ev is not None and need.get(ev[0], 0) < ev[1]:
                need[ev[0]] = ev[1]

        for t in reads:
            add(self.lastw.get(t))
        for t in writes:
            add(self.lastw.get(t))
            for k, v in self.readers.get(t, {}).items():
                add((k, v))
        E = self.E[eng]
        for k, v in need.items():
            if k == "pe" and eng == "pe":
                continue
            if k.startswith("d:"):
                v = self.cnt[k]
            if self.known[eng].get(k, 0) < v:
                E.wait_ge(self.sems[k], v)
                self.known[eng][k] = v
                self.ninst += 1

    def _record(self, ev, reads, writes):
        k, v = ev
        for t in reads:
            d = self.readers.setdefault(t, {})
            if d.get(k, 0) < v:
                d[k] = v
        for t in writes:
            self.lastw[t] = ev
            self.readers[t] = {}

    def op(self, eng, emit, reads=(), writes=()):
        ex = [t for t in reads if t.startswith("bank")]
        if ex:
            writes = list(writes) + ex
        self._waits(eng, reads, writes)
        s = self.sem(eng)
        ins = emit(self.E[eng])
        ins.then_inc(s, 1)
        self.cnt[eng] += 1
        self.ninst += 1
        self._record((eng, self.cnt[eng]), reads, writes)
        return ins

    def dma(self, eng, stream, out, in_, reads=(), writes=()):
        key = "d:" + stream
        s = self.sem(key)
        self._waits(eng, reads, writes)
        ins = self.E[eng].dma_start(out=out, in_=in_)
        ins.then_inc(s, 16)
        self.cnt[key] += 16
        self.ninst += 1
        self._record((key, self.cnt[key]), reads, writes)
        return ins

    def push_scope(self):
        self._outer = self.stack
        self.stack = ExitStack()

    def pop_scope(self):
        self.barrier()
        self.stack.close()
        self.stack = self._outer

    def barrier(self):
        for eng, E in self.E.items():
            for k, s_ in self.sems.items():
                v = self.cnt[k]
                if v > 0 and self.known[eng].get(k, 0) < v and not (k == eng):
                    E.wait_ge(s_, v)
                    self.known[eng][k] = v
                    self.ninst += 1

    def wait_all(self, eng, tokens):
        self._waits(eng, tokens, ())


class Rot:
    def __init__(self, c, name, n, shape, dt, psum=False):
        self.bufs = [(c.ps if psum else c.sb)("%s%d" % (name, i), shape, dt) for i in range(n)]
        self.names = ["%s%d" % (name, i) for i in range(n)]
        self.i = 0

    def next(self):
        j = self.i % len(self.bufs)
        self.i += 1
        return self.bufs[j], self.names[j]


def new_nc():
    return bass.Bass("TRN2", target_bir_lowering=False)


def run(nc, in_maps, tag=""):
    res = run_bass_kernel_spmd(nc, in_maps, core_ids=list(range(NCORE)))
    if os.environ.get("KDEBUG"):
        for j, r in enumerate(res.results):
            for name, arr in r.items():
                a = np.asarray(arr).astype(np.float32)
                if not np.isfinite(a).all():
                    print("KDEBUG non-finite:", tag, "core", j, name, int((~np.isfinite(a)).sum()), "of", a.size, file=sys.stderr)
    return res.results


def emit_rstd(c, rstd, ss, sstok, W, n=D, tok="rstd"):
    c.op("dve", lambda e: e.tensor_scalar(out=rstd[:, :W], in0=ss[:, :W], scalar1=1.0 / n, scalar2=EPS, op0=ALU.mult, op1=ALU.add),
         reads=[sstok], writes=[tok])
    c.op("act", lambda e: e.activation(out=rstd[:, :W], in_=rstd[:, :W], func=AF.Sqrt), reads=[tok], writes=[tok])
    c.op("dve", lambda e: e.reciprocal(out=rstd[:, :W], in_=rstd[:, :W]), reads=[tok], writes=[tok])


def emit_norm_mod(c, K, xt, xtok, W, segs, h, htok, maskcols=()):
    ss, sstok = K["ps_ss"].next()
    for kc in range(KC):
        sq, sqtok = K["sq"].next()
        c.op("act", lambda e: e.activation(out=sq[:, :W], in_=xt[:, kc, :W], func=AF.Square), reads=[xtok], writes=[sqtok])
        c.op("pe", lambda e: e.matmul(ss[:, :W], lhsT=K["ones"][:, :], rhs=sq[:, :W], start=(kc == 0), stop=(kc == KC - 1)),
             reads=[sqtok, "ones"], writes=[sstok])
    rstd = K["rstd"]
    emit_rstd(c, rstd, ss, sstok, W)
    for kc in range(KC):
        tmp, tmptok = K["tmp"].next()
        for (c0, c1, a_ap, b_ap) in segs:
            c.op("dve", lambda e: e.scalar_tensor_tensor(out=tmp[:, c0:c1], in0=xt[:, kc, c0:c1], scalar=a_ap[:, kc:kc + 1], in1=rstd[:, c0:c1],
                                                         op0=ALU.mult, op1=ALU.mult), reads=[xtok, "rstd", "vec"], writes=[tmptok])
            c.op("act", lambda e: e.activation(out=h[:, kc, c0:c1], in_=tmp[:, c0:c1], func=AF.Identity, bias=b_ap[:, kc:kc + 1], scale=1.0),
                 reads=[tmptok, "vec"], writes=[htok])
    for col, sc in maskcols:
        c.op("dve", lambda e: e.tensor_scalar(out=h[:, :, col:col + 1], in0=h[:, :, col:col + 1], scalar1=sc, scalar2=None, op0=ALU.mult),
             reads=[htok, "vec"], writes=[htok])


def make_consts(c):
    K = {}
    K["ones"] = c.sb("ones", [128, 128], BF16)
    c.op("dve", lambda e: e.memset(K["ones"][:, :], 1.0), writes=["ones"])
    K["sq"] = Rot(c, "sq", 2, [128, 512], BF16)
    K["tmp"] = Rot(c, "tmp", 2, [128, 512], F32)
    K["rstd"] = c.sb("rstd", [128, 512], F32)
    K["ps_ss"] = Rot(c, "ps_ss", 1, [128, 512], F32, psum=True)
    return K


MODC = 1536


def build_mod():
    nc = new_nc()
    c2 = nc.dram_tensor("c2", [128, KC, 2], F32, kind="ExternalInput").ap()
    wm = nc.dram_tensor("wm", [DEPTH, MODC // 512, 128, KC, 512], F32, kind="ExternalInput").ap()
    bm = nc.dram_tensor("bm", [2, DEPTH, MODC], F32, kind="ExternalInput").ap()
    out = nc.dram_tensor("out", [2, DEPTH, MODC], F32, kind="ExternalOutput").ap()
    with ExitStack() as st:
        c = Ctx(nc, st)
        ct = c.sb("ct", [128, KC, 2], F32)
        cs = c.sb("cs", [128, KC, 2], BF16)
        bt = c.sb("bt", [2, DEPTH, MODC], F32)
        ot = c.sb("ot", [2, DEPTH, MODC], F32)
        wrot = Rot(c, "wt", 2, [128, KC, 512], BF16)
        prot = Rot(c, "pm", 2, [2, 512], F32, psum=True)
        c.dma("sp", "c2", ct[:], c2, writes=["ct"])
        c.dma("sp", "bm", bt[:], bm, writes=["bt"])
        c.op("act", lambda e: e.activation(out=cs[:], in_=ct[:], func=AF.Silu), reads=["ct"], writes=["cs"])
        for l in range(DEPTH):
            for n in range(MODC // 512):
                wt, wtok = wrot.next()
                c.dma("pool", wtok, wt[:], wm[l, n], writes=[wtok])
                ps, ptok = prot.next()
                for kc in range(KC):
                    c.op("pe", lambda e: e.matmul(ps[:, :], lhsT=cs[:, kc, :], rhs=wt[:, kc, :], start=(kc == 0), stop=(kc == KC - 1)),
                         reads=["cs", wtok], writes=[ptok])
                c.op("dve", lambda e: e.tensor_tensor(out=ot[:, l, n * 512:(n + 1) * 512], in0=ps[:, :], in1=bt[:, l, n * 512:(n + 1) * 512], op=ALU.add),
                     reads=[ptok, "bt"], writes=["ot"])
        c.dma("sp", "out", out, ot[:], reads=["ot"], writes=["out"])
        c.wait_all("sp", ["out"])
    return nc


def tiles_of(total, w):
    return [(s, min(w, total - s)) for s in range(0, total, w)]


def build_pre(ncols, tiles):
    T = sum(w for _, w, _ in tiles)
    nc = new_nc()
    xT = nc.dram_tensor("xT", [D, T], F32, kind="ExternalInput").ap()
    vec = nc.dram_tensor("vec", [128, 5, KC], F32, kind="ExternalInput").ap()
    w = nc.dram_tensor("w", [(ncols + 127) // 128, 128, KC, 128], F32, kind="ExternalInput").ap()
    pT = nc.dram_tensor("pT", [ncols, T], BF16, kind="ExternalOutput").ap()
    with ExitStack() as st:
        c = Ctx(nc, st)
        K = make_consts(c)
        vt = c.sb("vec", [128, 5, KC], F32)
        av = c.sb("av", [128, 2, KC], F32)
        h = c.sb("h", [128, KC, T], BF16)
        xrot = Rot(c, "xt", 2, [128, KC, 512], F32)
        wrot = Rot(c, "wt", 2, [128, KC, 128], BF16)
        prot = Rot(c, "pp", 3, [128, 512], F32, psum=True)
        orot = Rot(c, "po", 3, [128, 512], BF16)
        c.dma("sp", "vec", vt[:], vec, writes=["vec"])
        for s in range(2):
            c.op("dve", lambda e: e.scalar_tensor_tensor(out=av[:, s, :], in0=vt[:, 2 + 2 * s, :], scalar=1.0, in1=vt[:, 0, :],
                                                         op0=ALU.add, op1=ALU.mult), reads=["vec"], writes=["vec"])
        for (s0, wd, stream) in tiles:
            xt, xtok = xrot.next()
            c.dma("sp", xtok, xt[:, :, :wd], xT[:, s0:s0 + wd].rearrange("(kc p) t -> p kc t", p=128), writes=[xtok])
            emit_norm_mod(c, K, xt, xtok, wd, [(0, wd, av[:, stream, :], vt[:, 1 + 2 * stream, :])], h[:, :, s0:s0 + wd], "h%d" % s0)
        nst = 0
        for cb0 in range(0, ncols, 128):
            m = min(128, ncols - cb0)
            wt, wtok = wrot.next()
            c.dma("pool", wtok, wt[:], w[cb0 // 128], writes=[wtok])
            for (s0, wd, stream) in tiles:
                ps, ptok = prot.next()
                for kc in range(KC):
                    c.op("pe", lambda e: e.matmul(ps[:m, :wd], lhsT=wt[:, kc, :m], rhs=h[:, kc, s0:s0 + wd], start=(kc == 0), stop=(kc == KC - 1)),
                         reads=[wtok, "h%d" % s0], writes=[ptok])
                ot, otok = orot.next()
                eng = "act" if nst % 2 == 0 else "dve"
                if eng == "act":
                    c.op("act", lambda e: e.activation(out=ot[:m, :wd], in_=ps[:m, :wd], func=AF.Copy), reads=[ptok], writes=[otok])
                else:
                    c.op("dve", lambda e: e.tensor_copy(out=ot[:m, :wd], in_=ps[:m, :wd]), reads=[ptok], writes=[otok])
                nst += 1
                c.dma("sp", otok, pT[cb0:cb0 + m, s0:s0 + wd], ot[:m, :wd], reads=[otok], writes=["pT"])
        c.wait_all("sp", ["pT"])
    return nc


POST_V = {"n2g": 0, "g1": 16, "sh2": 32, "sc2": 48, "g2": 64, "cg1": 80, "csh2": 96, "csc2": 112, "cg2": 128, "fg": 144,
          "cw": 160, "cb": 160 + 3 * 88, "vl": 160 + 4 * 88, "vr": 161 + 4 * 88, "zero": 162 + 4 * 88}
POST_NV = 163 + 4 * 88


def build_post(tiles, final):
    T = max(s + w for s, w, _, _ in tiles)
    TO = sum(w - 2 for _, w, _, _ in tiles)
    nc = new_nc()
    xT = nc.dram_tensor("xT", [D, T], F32, kind="ExternalInput").ap()
    mT = nc.dram_tensor("mT", [D, T], BF16, kind="ExternalInput").ap()
    vec = nc.dram_tensor("vec", [128, POST_NV], F32, kind="ExternalInput").ap()
    wo = nc.dram_tensor("wo", [KC, 128, KC, 128], F32, kind="ExternalInput").ap()
    wu = nc.dram_tensor("wu", [2 * FC, 128, KC, 128], F32, kind="ExternalInput").ap()
    wdn = nc.dram_tensor("wd", [KC, 128, FC, 128], F32, kind="ExternalInput").ap()
    oT = nc.dram_tensor("oT", [D, TO], F32, kind="ExternalOutput").ap()
    swo = nc.dram_tensor("swo", [KC, 128, KC, 128], BF16, kind="Internal").ap()
    swu = nc.dram_tensor("swu", [2 * FC, 128, KC, 128], BF16, kind="Internal").ap()
    swd = nc.dram_tensor("swd", [KC, 128, FC, 128], BF16, kind="Internal").ap()
    with ExitStack() as st:
        c = Ctx(nc, st)
        K = make_consts(c)
        vt = c.sb("vec", [128, POST_NV], F32)
        av = c.sb("av", [128, 2, KC], F32)
        xrot = Rot(c, "xt", 1, [128, KC, 512], F32)
        mrot = Rot(c, "mt", 1, [128, KC, 512], BF16)
        h2 = c.sb("h2", [128, KC, 512], BF16)
        act = c.sb("actb", [128, FC, 512], BF16)
        worot = Rot(c, "wo", 2, [128, KC, 128], BF16)
        wgrot = Rot(c, "wg", 2, [128, KC, 128], BF16)
        wvrot = Rot(c, "wv", 2, [128, KC, 128], BF16)
        wdrot = Rot(c, "wdn", 2, [128, FC, 128], BF16)
        prot = Rot(c, "pp", 2, [128, 512], F32, psum=True)
        pgrot = Rot(c, "pg", 2, [128, 512], F32, psum=True)
        pvrot = Rot(c, "pv", 2, [128, 512], F32, psum=True)
        cg = Rot(c, "cg", 2, [128, 512], F32)
        cv = Rot(c, "cv", 2, [128, 512], F32)
        sg = Rot(c, "sg", 2, [128, 512], F32)
        yt = Rot(c, "yt", 2, [128, 512], F32)
        c.dma("sp", "vec", vt[:], vec, writes=["vec"])
        V = POST_V
        for s, (scn, gn) in enumerate((("sc2", "n2g"), ("csc2", "n2g"))):
            c.op("dve", lambda e: e.scalar_tensor_tensor(out=av[:, s, :], in0=vt[:, V[scn]:V[scn] + 16], scalar=1.0, in1=vt[:, V[gn]:V[gn] + 16],
                                                         op0=ALU.add, op1=ALU.mult), reads=["vec"], writes=["vec"])
        ocol = 0
        def wload(ti, wt, wtok, src, scr, stok):
            if ti == 0:
                c.dma("pool", wtok, wt[:], src, writes=[wtok])
                c.dma("sp", "scr_" + wtok, scr, wt[:], reads=[wtok], writes=[stok])
            else:
                c.dma("sp", wtok, wt[:], scr, reads=[stok], writes=[wtok])

        for ti, (s0, wd, segs, tmasks) in enumerate(tiles):
            wi = wd - 2
            xt, xtok = xrot.next()
            mt, mtok = mrot.next()
            c.dma("sp", xtok, xt[:, :, :wd], xT[:, s0:s0 + wd].rearrange("(kc p) t -> p kc t", p=128), writes=[xtok])
            c.dma("sp", mtok, mt[:, :, :wd], mT[:, s0:s0 + wd].rearrange("(kc p) t -> p kc t", p=128), writes=[mtok])
            for oc in range(KC):
                wt, wtok = worot.next()
                wload(ti, wt, wtok, wo[oc], swo[oc], "swo%d" % oc)
                ps, ptok = prot.next()
                for kc in range(KC):
                    c.op("pe", lambda e: e.matmul(ps[:, :wd], lhsT=wt[:, kc, :], rhs=mt[:, kc, :wd], start=(kc == 0), stop=(kc == KC - 1)),
                         reads=[wtok, mtok], writes=[ptok])
                for (c0, c1, stream) in segs:
                    g1 = V["cg1"] if stream else V["g1"]
                    c.op("dve", lambda e: e.scalar_tensor_tensor(out=xt[:, oc, c0:c1], in0=ps[:, c0:c1], scalar=vt[:, g1 + oc:g1 + oc + 1], in1=xt[:, oc, c0:c1],
                                                                 op0=ALU.mult, op1=ALU.add), reads=[ptok, xtok, "vec"], writes=[xtok])
            masks = [(col, vt[:, V[fl]:V[fl] + 1]) for col, fl in tmasks]
            nsegs = [(c0, c1, av[:, stream, :], vt[:, (V["csh2"] if stream else V["sh2"]):(V["csh2"] if stream else V["sh2"]) + 16]) for (c0, c1, stream) in segs]
            emit_norm_mod(c, K, xt, xtok, wd, nsegs, h2, "h2", maskcols=masks)
            for f in range(FC):
                wg, wgtok = wgrot.next()
                wv, wvtok = wvrot.next()
                wload(ti, wg, wgtok, wu[f], swu[f], "swu%d" % f)
                wload(ti, wv, wvtok, wu[FC + f], swu[FC + f], "swu%d" % (FC + f))
                pg, pgtok = pgrot.next()
                pv, pvtok = pvrot.next()
                for kc in range(KC):
                    c.op("pe", lambda e: e.matmul(pg[:, :wd], lhsT=wg[:, kc, :], rhs=h2[:, kc, :wd], start=(kc == 0), stop=(kc == KC - 1)),
                         reads=[wgtok, "h2"], writes=[pgtok])
                for kc in range(KC):
                    c.op("pe", lambda e: e.matmul(pv[:, :wd], lhsT=wv[:, kc, :], rhs=h2[:, kc, :wd], start=(kc == 0), stop=(kc == KC - 1)),
                         reads=[wvtok, "h2"], writes=[pvtok])
                outs = []
                for (pp, pptok, rot, fi) in ((pg, pgtok, cg, f), (pv, pvtok, cv, FC + f)):
                    t, ttok = rot.next()
                    cw0 = V["cw"] + 0 * 88 + fi
                    cw1 = V["cw"] + 1 * 88 + fi
                    cw2 = V["cw"] + 2 * 88 + fi
                    c.op("dve", lambda e: e.tensor_scalar(out=t[:, :wi], in0=pp[:, 0:wi], scalar1=vt[:, cw0:cw0 + 1], scalar2=None, op0=ALU.mult),
                         reads=[pptok, "vec"], writes=[ttok])
                    c.op("dve", lambda e: e.scalar_tensor_tensor(out=t[:, :wi], in0=pp[:, 1:wi + 1], scalar=vt[:, cw1:cw1 + 1], in1=t[:, :wi],
                                                                 op0=ALU.mult, op1=ALU.add), reads=[pptok, ttok, "vec"], writes=[ttok])
                    c.op("dve", lambda e: e.scalar_tensor_tensor(out=t[:, :wi], in0=pp[:, 2:wi + 2], scalar=vt[:, cw2:cw2 + 1], in1=t[:, :wi],
                                                                 op0=ALU.mult, op1=ALU.add), reads=[pptok, ttok, "vec"], writes=[ttok])
                    outs.append((t, ttok))
                (tg, tgtok), (tv, tvtok) = outs
                s_, stok = sg.next()
                cbg = V["cb"] + f
                cbv = V["cb"] + FC + f
                c.op("act", lambda e: e.activation(out=s_[:, :wi], in_=tg[:, :wi], func=AF.Silu, bias=vt[:, cbg:cbg + 1], scale=1.0),
                     reads=[tgtok, "vec"], writes=[stok])
                c.op("dve", lambda e: e.scalar_tensor_tensor(out=act[:, f, :wi], in0=tv[:, :wi], scalar=vt[:, cbv:cbv + 1], in1=s_[:, :wi],
                                                              op0=ALU.add, op1=ALU.mult), reads=[tvtok, stok, "vec"], writes=["act%d" % f])
            for oc in range(KC):
                wt, wtok = wdrot.next()
                wload(ti, wt, wtok, wdn[oc], swd[oc], "swd%d" % oc)
                ps, ptok = prot.next()
                for f in range(FC):
                    c.op("pe", lambda e: e.matmul(ps[:, :wi], lhsT=wt[:, f, :], rhs=act[:, f, :wi], start=(f == 0), stop=(f == FC - 1)),
                         reads=[wtok, "act%d" % f], writes=[ptok])
                for (c0, c1, stream) in segs:
                    g2 = V["cg2"] if stream else V["g2"]
                    a0, a1 = max(c0, 1), min(c1, wd - 1)
                    if a1 <= a0:
                        continue
                    c.op("dve", lambda e: e.scalar_tensor_tensor(out=xt[:, oc, a0:a1], in0=ps[:, a0 - 1:a1 - 1], scalar=vt[:, g2 + oc:g2 + oc + 1], in1=xt[:, oc, a0:a1],
                                                                 op0=ALU.mult, op1=ALU.add), reads=[ptok, xtok, "vec"], writes=[xtok])
            if not final:
                c.dma("sp", "oT", oT[:, ocol:ocol + wi].rearrange("(kc p) t -> p kc t", p=128), xt[:, :, 1:wi + 1], reads=[xtok], writes=["oT"])
            else:
                ss, sstok = K["ps_ss"].next()
                for kc in range(KC):
                    sq, sqtok = K["sq"].next()
                    c.op("act", lambda e: e.activation(out=sq[:, :wi], in_=xt[:, kc, 1:wi + 1], func=AF.Square), reads=[xtok], writes=[sqtok])
                    c.op("pe", lambda e: e.matmul(ss[:, :wi], lhsT=K["ones"][:, :], rhs=sq[:, :wi], start=(kc == 0), stop=(kc == KC - 1)),
                         reads=[sqtok, "ones"], writes=[sstok])
                rstd = K["rstd"]
                emit_rstd(c, rstd, ss, sstok, wi)
                for kc in range(KC):
                    y, ytok = yt.next()
                    fg = V["fg"] + kc
                    c.op("dve", lambda e: e.scalar_tensor_tensor(out=y[:, :wi], in0=xt[:, kc, 1:wi + 1], scalar=vt[:, fg:fg + 1], in1=rstd[:, :wi],
                                                                 op0=ALU.mult, op1=ALU.mult), reads=[xtok, "rstd", "vec"], writes=[ytok])
                    c.dma("sp", ytok, oT[kc * 128:(kc + 1) * 128, ocol:ocol + wi], y[:, :wi], reads=[ytok], writes=["oT"])
            ocol += wi
        c.wait_all("sp", ["oT"])
    return nc


ATT_ACC2 = "pool"


def build_att(kind, NQL, NKL, NCT=CTX):
    S = 2
    SK = 2 if kind == "B" else 1
    DV = 256 if kind == "B" else 128
    NH = DV // 128
    NK = NCT + NKL
    NKT = NK // 128
    NCKT = NCT // 128
    R = 256
    scale = 128 ** -0.5
    nc = new_nc()
    qT = nc.dram_tensor("qT", [S, 128, NCT + NQL], BF16, kind="ExternalInput").ap()
    kT = nc.dram_tensor("kT", [SK, 128, NK], BF16, kind="ExternalInput").ap()
    v = nc.dram_tensor("v", [NK, DV], BF16, kind="ExternalInput").ap()
    cq = nc.dram_tensor("cq", [2, 128, NQL], F32, kind="ExternalInput").ap()
    ck = nc.dram_tensor("ck", [2, 128, NKL], F32, kind="ExternalInput").ap()
    rt = nc.dram_tensor("rt", [128, 128], BF16, kind="ExternalInput").ap()
    vec = nc.dram_tensor("vec", [128, 8], F32, kind="ExternalInput").ap()
    lp = nc.dram_tensor("lp", [128, 4, 128], F32, kind="ExternalInput").ap()
    oT = nc.dram_tensor("oT", [R, NCT + NQL], BF16, kind="ExternalOutput").ap()
    with ExitStack() as st:
        c = Ctx(nc, st)
        ones = c.sb("ones", [128, 128], BF16)
        c.op("dve", lambda e: e.memset(ones[:, :], 1.0), writes=["ones"])
        rtt = c.sb("rtt", [128, 128], BF16)
        vt = c.sb("vec", [128, 8], F32)
        lpt = c.sb("lpt", [128, 4, 128], F32)
        lam = c.sb("lam", [128, 8], F32)
        Kr = c.sb("Kr", [128, SK, NK], BF16)
        Vt = c.sb("Vt", [128, NKT, DV], BF16)
        raw = Rot(c, "raw", 2, [128, 512], BF16)
        cst = Rot(c, "cst", 2, [128, 2, 512], F32)
        xn = c.sb("xn", [128, 512], F32)
        xnb = c.sb("xnb", [128, 512], BF16)
        sqb = c.sb("sqb", [128, 512], BF16)
        rstd = c.sb("rstd", [128, 512], F32)
        t1 = c.sb("t1", [128, 512], F32)
        t2 = c.sb("t2", [128, 512], F32)
        qr = Rot(c, "qr", 2, [128, 512], BF16)
        E = Rot(c, "E", 3, [128, 2, 512], BF16)
        acc = [c.sb("acc%d" % i, [128, 2, 512], F32) for i in range(2)]
        ones32 = c.sb("ones32", [128, 128], F32)
        c.op("dve", lambda e: e.memset(ones32[:, :], 1.0), writes=["ones32"])
        sqb2 = c.sb("sqb2", [128, 512], BF16)
        rstd2 = c.sb("rstd2", [128, 512], F32)
        on = [[c.sb("on%d%d" % (s, h), [128, 512], F32) for h in range(NH)] for s in range(S)]
        rec = c.sb("rec", [128, 512], F32)
        ob = Rot(c, "ob", 2, [128, 512], BF16)
        ps_s = Rot(c, "pS", 2, [128, 2, 512], F32, psum=True)
        ps_o = [c.ps("pO%d" % h, [128, 512]) for h in range(NH)]
        ps_sum = c.ps("pSum", [128, 512])
        ps_ss2 = ps_sum
        ps_ss = c.ps("pSS", [128, 512])
        ps_rot = ps_ss
        c.dma("sp", "rtt", rtt[:], rt, writes=["rtt"])
        c.dma("sp", "vec", vt[:], vec, writes=["vec"])
        c.dma("sp", "Vt", Vt[:], v.rearrange("(kt p) d -> p kt d", p=128), writes=["Vt"])
        if kind == "B":
            c.dma("sp", "lpt", lpt[:], lp, writes=["lpt"])
            for i in range(2):
                c.op("dve", lambda e: e.tensor_tensor(out=t1[:, :128], in0=lpt[:, 2 * i, :], in1=lpt[:, 2 * i + 1, :], op=ALU.mult), reads=["lpt"], writes=["t1"])
                c.op("dve", lambda e: e.reduce_sum(out=lam[:, i:i + 1], in_=t1[:, :128], axis=AX.X), reads=["t1"], writes=["lam"])
            c.op("act", lambda e: e.activation(out=lam[:, 0:2], in_=lam[:, 0:2], func=AF.Exp), reads=["lam"], writes=["lam"])
            c.op("dve", lambda e: e.tensor_tensor(out=lam[:, 2:3], in0=lam[:, 0:1], in1=lam[:, 1:2], op=ALU.subtract), reads=["lam"], writes=["lam"])
            c.op("dve", lambda e: e.tensor_tensor(out=lam[:, 2:3], in0=lam[:, 2:3], in1=vt[:, 2:3], op=ALU.add), reads=["lam", "vec"], writes=["lam"])
            c.op("dve", lambda e: e.tensor_scalar(out=lam[:, 3:4], in0=lam[:, 2:3], scalar1=-1.0, scalar2=None, op0=ALU.mult), reads=["lam"], writes=["lam"])
            c.op("dve", lambda e: e.tensor_scalar(out=lam[:, 4:6], in0=vt[:, 4:6], scalar1=vt[:, 3:4], scalar2=None, op0=ALU.mult), reads=["lam", "vec"], writes=["lam"])

        def prep(src, srctok, W, cs, cstok, gain_col, dst, dsttok):
            cur, curtok = src, srctok
            if gain_col is not None:
                c.op("act", lambda e: e.activation(out=sqb[:, :W], in_=src[:, :W], func=AF.Square), reads=[srctok], writes=["sqb"])
                yield
                c.op("pe", lambda e: e.matmul(ps_ss[:, :W], lhsT=ones[:, :], rhs=sqb[:, :W], start=True, stop=True), reads=["sqb", "ones"], writes=["pSS"])
                yield
                c.op("dve", lambda e: e.tensor_scalar(out=rstd[:, :W], in0=ps_ss[:, :W], scalar1=1.0 / 128, scalar2=EPS, op0=ALU.mult, op1=ALU.add),
                     reads=["pSS"], writes=["rstd"])
                yield
                c.op("act", lambda e: e.activation(out=rstd[:, :W], in_=rstd[:, :W], func=AF.Sqrt), reads=["rstd"], writes=["rstd"])
                yield
                c.op("dve", lambda e: e.reciprocal(out=rstd[:, :W], in_=rstd[:, :W]), reads=["rstd"], writes=["rstd"])
                yield
                c.op("dve", lambda e: e.scalar_tensor_tensor(out=xn[:, :W], in0=src[:, :W], scalar=vt[:, gain_col:gain_col + 1], in1=rstd[:, :W],
                                                             op0=ALU.mult, op1=ALU.mult), reads=[srctok, "rstd", "vec"], writes=["xn"])
                yield
                cur, curtok = xn, "xn"
                if cs is None:
                    c.op("act", lambda e: e.activation(out=dst[:, :W], in_=xn[:, :W], func=AF.Copy), reads=["xn"], writes=[dsttok])
                    yield
                    return
                c.op("act", lambda e: e.activation(out=xnb[:, :W], in_=xn[:, :W], func=AF.Copy), reads=["xn"], writes=["xnb"])
                yield
                curb, curbtok = xnb, "xnb"
            else:
                if cs is None:
                    c.op("act", lambda e: e.activation(out=dst[:, :W], in_=src[:, :W], func=AF.Copy), reads=[srctok], writes=[dsttok])
                    yield
                    return
                curb, curbtok = src, srctok
            c.op("dve", lambda e: e.tensor_tensor(out=t1[:, :W], in0=cur[:, :W], in1=cs[:, 0, :W], op=ALU.mult), reads=[curtok, cstok], writes=["t1"])
            yield
            c.op("pe", lambda e: e.matmul(ps_rot[:, :W], lhsT=rtt[:, :], rhs=curb[:, :W], start=True, stop=True), reads=["rtt", curbtok], writes=["pSS"])
            yield
            c.op("dve", lambda e: e.tensor_tensor(out=t2[:, :W], in0=ps_rot[:, :W], in1=cs[:, 1, :W], op=ALU.mult), reads=["pSS", cstok], writes=["t2"])
            yield
            c.op("dve", lambda e: e.tensor_tensor(out=dst[:, :W], in0=t1[:, :W], in1=t2[:, :W], op=ALU.add), reads=["t1", "t2"], writes=[dsttok])
            yield

        def run_all(gen):
            for _ in gen:
                pass

        kgain = 1 if kind == "C" else None
        qgain = 0 if kind == "C" else None
        PE_SUM = False
        ktiles = [(0, NCT, None)] + [(NCT + s0, w, s0) for s0, w in tiles_of(NKL, 512)]
        for sk in range(SK):
            for (c0, w, r0) in ktiles:
                rw, rwtok = raw.next()
                c.dma("sp", rwtok, rw[:, :w], kT[sk, :, c0:c0 + w], writes=[rwtok])
                cs, cstok = None, None
                if r0 is not None:
                    cs, cstok = cst.next()
                    c.dma("sp", cstok, cs[:, :, :w], ck[:, :, r0:r0 + w].rearrange("a p t -> p a t"), writes=[cstok])
                run_all(prep(rw, rwtok, w, cs, cstok, kgain, Kr[:, sk, c0:c0 + w], "Kr%d_%d" % (sk, c0)))
        ktoks = [["Kr%d_%d" % (sk, c0) for (c0, w, r0) in ktiles] for sk in range(SK)]
        qtiles = [(0, NCT, None, NCKT)] + [(NCT + s0, w, s0, NKT) for s0, w in tiles_of(NQL, 512)]
        units = [(qi, s) for qi in range(len(qtiles)) for s in range(S)]
        cs_of = {}

        def prep_unit(u):
            qi, s = units[u]
            c0, w, r0, nkt = qtiles[qi]
            if s == 0:
                cs, cstok = None, None
                if r0 is not None:
                    cs, cstok = cst.next()
                    c.dma("sp", cstok, cs[:, :, :w], cq[:, :, r0:r0 + w].rearrange("a p t -> p a t"), writes=[cstok])
                cs_of[qi] = (cs, cstok)
            cs, cstok = cs_of[qi]
            rw, rwtok = raw.next()
            c.dma("sp", rwtok, rw[:, :w], qT[s, :, c0:c0 + w], writes=[rwtok])
            q, qtok = qr.next()
            qready[u] = (q, qtok)
            yield
            for _ in prep(rw, rwtok, w, cs, cstok, qgain, q, qtok):
                yield

        qready = {}
        run_all(prep_unit(0))
        for u, (qi, s) in enumerate(units):
            c0, w, r0, nkt = qtiles[qi]
            q, qtok = qready.pop(u)
            pgen = prep_unit(u + 1) if u + 1 < len(units) else iter(())
            sk = s if SK == 2 else 0
            pend = {}
            npair = nkt // 2
            npe, ndve = [0], [0]

            def score(p_):
                ps, pstok = ps_s.next()
                for j in range(2):
                    kt = 2 * p_ + j
                    c.op("pe", lambda e: e.matmul(ps[:, j, :w], lhsT=Kr[:, sk, kt * 128:(kt + 1) * 128], rhs=q[:, :w], start=True, stop=True),
                         reads=ktoks[sk] + [qtok], writes=[pstok])
                pend[p_] = (ps, pstok)

            score(0)
            for p_ in range(npair):
                ps, pstok = pend.pop(p_)
                e_, etok = E.next()
                c.op("act", lambda e: e.activation(out=e_[:, :, :w], in_=ps[:, :, :w], func=AF.Exp, scale=scale), reads=[pstok], writes=[etok])
                if p_ + 1 < npair:
                    score(p_ + 1)
                if p_ >= 4 and p_ % 3 == 0:
                    next(pgen, None)
                for j in range(2):
                    kt = 2 * p_ + j
                    for h in range(NH):
                        c.op("pe", lambda e: e.matmul(ps_o[h][:, :w], lhsT=Vt[:, kt, h * 128:(h + 1) * 128], rhs=e_[:, j, :w], start=(kt == 0), stop=(kt == nkt - 1)),
                             reads=["Vt", etok], writes=["pO%d" % h])
                if PE_SUM and p_ % 3 == 2:
                    for j in range(2):
                        c.op("pe", lambda e: e.matmul(ps_sum[:, :w], lhsT=ones[:, :], rhs=e_[:, j, :w], start=(npe[0] == 0), stop=False),
                             reads=["ones", etok], writes=["pSum"])
                        npe[0] += 1
                else:
                    ac, actok = (acc[0], "acc0") if ndve[0] % 2 == 0 else (acc[1], "acc1")
                    if ndve[0] < 2:
                        c.op("dve", lambda e: e.tensor_copy(out=ac[:, :, :w], in_=e_[:, :, :w]), reads=[etok], writes=[actok])
                    else:
                        c.op("dve", lambda e: e.tensor_tensor(out=ac[:, :, :w], in0=ac[:, :, :w], in1=e_[:, :, :w], op=ALU.add), reads=[etok, actok], writes=[actok])
                    ndve[0] += 1
            run_all(pgen)
            if ndve[0] > 1:
                c.op("dve", lambda e: e.tensor_tensor(out=acc[0][:, :, :w], in0=acc[0][:, :, :w], in1=acc[1][:, :, :w], op=ALU.add), reads=["acc0", "acc1"], writes=["acc0"])
            c.op("dve", lambda e: e.tensor_tensor(out=acc[0][:, 0, :w], in0=acc[0][:, 0, :w], in1=acc[0][:, 1, :w], op=ALU.add), reads=["acc0"], writes=["acc0"])
            c.op("pe", lambda e: e.matmul(ps_sum[:, :w], lhsT=ones32[:, :], rhs=acc[0][:, 0, :w], start=(npe[0] == 0), stop=True), reads=["ones32", "acc0"], writes=["pSum"])
            c.op("dve", lambda e: e.reciprocal(out=rec[:, :w], in_=ps_sum[:, :w]), reads=["pSum"], writes=["rec"])
            for h in range(NH):
                c.op("dve", lambda e: e.tensor_tensor(out=on[s][h][:, :w], in0=ps_o[h][:, :w], in1=rec[:, :w], op=ALU.mult),
                     reads=["pO%d" % h, "rec"], writes=["on%d%d" % (s, h)])
            if kind == "C":
                o_, otok = ob.next()
                c.op("act", lambda e: e.activation(out=o_[:, :w], in_=on[s][0][:, :w], func=AF.Copy), reads=["on%d0" % s], writes=[otok])
                c.dma("sp", otok, oT[s * 128:(s + 1) * 128, c0:c0 + w], o_[:, :w], reads=[otok], writes=["oT"])
            if kind == "B" and s == S - 1:
                for h in range(NH):
                    c.op("dve", lambda e: e.scalar_tensor_tensor(out=on[0][h][:, :w], in0=on[1][h][:, :w], scalar=lam[:, 3:4], in1=on[0][h][:, :w],
                                                                 op0=ALU.mult, op1=ALU.add), reads=["on1%d" % h, "on0%d" % h, "lam"], writes=["on0%d" % h])
                    c.op("act", lambda e: e.activation(out=sqb2[:, :w], in_=on[0][h][:, :w], func=AF.Square), reads=["on0%d" % h], writes=["sqb2"])
                    c.op("pe", lambda e: e.matmul(ps_ss2[:, :w], lhsT=ones[:, :], rhs=sqb2[:, :w], start=(h == 0), stop=(h == NH - 1)),
                         reads=["sqb2", "ones"], writes=["pSum"])
                emit_rstd(c, rstd2, ps_ss2, "pSum", w, n=256, tok="rstd2")
                for h in range(NH):
                    o_, otok = ob.next()
                    c.op("dve", lambda e: e.scalar_tensor_tensor(out=o_[:, :w], in0=on[0][h][:, :w], scalar=lam[:, 4 + h:5 + h], in1=rstd2[:, :w],
                                                                 op0=ALU.mult, op1=ALU.mult), reads=["on0%d" % h, "rstd2", "lam"], writes=[otok])
                    c.dma("sp", otok, oT[h * 128:(h + 1) * 128, c0:c0 + w], o_[:, :w], reads=[otok], writes=["oT"])
        c.wait_all("sp", ["oT"])
    return nc


def rope_tables(n):
    freqs = (10000.0 ** (-np.arange(0, 64, 2, dtype=np.float32) / 64)).astype(np.float32)
    t = np.arange(n)
    row = (t // 64).astype(np.float32)
    col = (t % 64).astype(np.float32)
    ang_r = row[:, None] * freqs
    ang_c = col[:, None] * freqs
    ang = np.concatenate([ang_r, ang_r, ang_c, ang_c], -1).astype(np.float32)
    cos = np.cos(ang).astype(np.float32)
    sin = np.sin(ang).astype(np.float32)
    sgn = np.ones(128, np.float32)
    sgn[0:32] = -1
    sgn[64:96] = -1
    return np.ascontiguousarray(np.stack([cos.T, (sin * sgn).T], 0))


def rope_perm():
    m = np.arange(128)
    partner = np.where((m // 32) % 2 == 0, m + 32, m - 32)
    rt = np.zeros((128, 128), np.float32)
    rt[partner, m] = 1.0
    return rt.astype(NPBF)


GDN_FP32R = False
GDN_INV_BF16 = False


def build_gdn(NT, NCT=CTX):
    NCH = NT // 128
    CCH = NCT // 128
    nc = new_nc()
    qkvT = nc.dram_tensor("qkvT", [3, 128, NT], BF16, kind="ExternalInput").ap()
    ztm = nc.dram_tensor("ztm", [NT, 128], BF16, kind="ExternalInput").ap()
    gt = nc.dram_tensor("gt", [128, NCH, 4], BF16, kind="ExternalInput").ap()
    vec = nc.dram_tensor("vec", [128, 16], F32, kind="ExternalInput").ap()
    gbc = nc.dram_tensor("gbc", [128, 128], F32, kind="ExternalInput").ap()
    cst = nc.dram_tensor("cst", [6, 128, 128], F32, kind="ExternalInput").ap()
    otm = nc.dram_tensor("otm", [NT, 128], BF16, kind="ExternalOutput").ap()
    with ExitStack() as st:
        c = Ctx(nc, st)
        vt = c.sb("vec", [128, 16], F32)
        gb = c.sb("gbc", [128, 128], F32)
        C32 = c.sb("cst", [128, 6, 128], F32)
        Ib = c.sb("Ib", [128, 128], BF16)
        onesb = c.sb("onesb", [128, 128], BF16)
        qT = c.sb("qT", [128, NT], BF16)
        kT = c.sb("kT", [128, NT], BF16)
        ktm = c.sb("ktm", [128, NCH, 128], BF16)
        vtm = c.sb("vtm", [128, NCH, 128], BF16)
        of = c.sb("of", [128, NCH, 128], BF16)
        G = [c.sb("G%d" % d, [128, NCH], F32) for d in range(2)]
        BT = [c.sb("BT%d" % d, [128, NCH], F32) for d in range(2)]
        NB = [c.sb("NB%d" % d, [128, NCH], F32) for d in range(2)]
        Bc = [c.sb("Bc%d" % d, [128, NCH], F32) for d in range(2)]
        EB = [c.sb("EB%d" % d, [128, NCH], F32) for d in range(2)]
        BEB = [c.sb("BEB%d" % d, [128, NCH], F32) for d in range(2)]
        EKD = [c.sb("EKD%d" % d, [128, NCH], F32) for d in range(2)]
        CD = [c.sb("CD%d" % d, [128, NCH], F32) for d in range(2)]
        tri = c.sb("tri", [128, 2, 128], F32)
        zt = Rot(c, "zt", 2, [128, 128], BF16)
        ot = Rot(c, "ot", 2, [128, 128], BF16)
        banks = [c.ps("bank%d" % i, [128, 512]) for i in range(8)]

        def carve(b, i, n=1):
            return banks[b][:, 128 * i:128 * (i + n)]

        class RotAP:
            def __init__(self, aps, names):
                self.bufs, self.names, self.i = aps, names, 0

            def next(self):
                j = self.i % len(self.bufs)
                self.i += 1
                return self.bufs[j], self.names[j]

        pbig = banks[0]
        pG = banks[1]
        pT = RotAP([carve(1, 0), carve(1, 1)], ["bank1", "bank1"])

        class Chain:
            def __init__(self, d):
                n = "c%d" % d
                self.d = d
                self.oss = c.sb("oss" + n, [128, 4], F32)
                IDT = BF16 if GDN_INV_BF16 else (mybir.dt.float32r if GDN_FP32R else F32)
                self.Rm = c.sb("Rm" + n, [128, 128], IDT)
                for nm in ("diagb", "nd", "dm", "dmS", "dmI", "S32", "o2s", "osum", "osq", "sz"):
                    setattr(self, nm, c.sb(nm + n, [128, 128], F32))
                for nm in ("Rb", "qk", "kb", "vb", "Sb", "u_b"):
                    setattr(self, nm, c.sb(nm + n, [128, 128], BF16))
                self.Pm = Rot(c, "Pm" + n, 2, [128, 128], IDT)
                self.Qm = Rot(c, "Qm" + n, 2, [128, 128], IDT)
                self.qkT = Rot(c, "qkT" + n, 2, [128, 128], BF16)
                self.kd = Rot(c, "kd" + n, 2, [128, 128], BF16)
                self.ub = Rot(c, "ub" + n, 2, [128, 128], F32)
                self.wT = Rot(c, "wT" + n, 2, [128, 128], BF16)
                b0 = 4 * d
                self.bA, self.bB, self.bC, self.bD = ["bank%d" % (b0 + k) for k in range(4)]
                self.pA, self.pKK, self.pQK, self.pU = [carve(b0, k) for k in range(4)]
                self.pT = RotAP([carve(b0 + 1, 0), carve(b0 + 1, 1)], [self.bB, self.bB])
                self.pW = carve(b0 + 1, 2)
                self.pI = RotAP([carve(b0 + 2, k) for k in range(3)], [self.bC] * 3)
                self.pwS, self.pO1, self.pO2, self.pdS = [carve(b0 + 3, k) for k in range(4)]

            def t(self, nm):
                return "%sc%d" % (nm, self.d)

        c.push_scope()
        gtr = c.sb("gtr", [128, NCH, 4], BF16)
        gx = c.sb("gx", [128, NCH], F32)
        rb = c.sb("rb", [128, 3, 514], BF16)
        cv = c.sb("cv", [128, 514], F32)
        sv = c.sb("sv", [128, 514], F32)
        sqb = c.sb("sqb", [128, 512], BF16)
        rstd = c.sb("rstd", [128, 512], F32)
        vTb = c.sb("vTb", [128, 512], BF16)

        c.dma("sp", "vec", vt[:], vec, writes=["vec"])
        c.dma("sp", "gbc", gb[:], gbc, writes=["gbc"])
        c.dma("sp", "cst", C32[:], cst.rearrange("a p f -> p a f"), writes=["cst"])
        c.dma("sp", "gtr", gtr[:], gt, writes=["gtr"])
        I32 = C32[:, 0, :]
        ones32 = C32[:, 1, :]
        MS = [C32[:, 2, :], C32[:, 4, :]]
        MI = [C32[:, 3, :], C32[:, 5, :]]
        c.op("act", lambda e: e.activation(out=Ib[:, :], in_=C32[:, 0, :], func=AF.Copy), reads=["cst"], writes=["Ib"])
        c.op("act", lambda e: e.activation(out=onesb[:, :], in_=C32[:, 1, :], func=AF.Copy), reads=["cst"], writes=["onesb"])
        c.op("dve", lambda e: e.tensor_copy(out=tri[:, 0, :], in_=C32[:, 5, :]), reads=["cst"], writes=["tri"])
        c.op("dve", lambda e: e.tensor_copy(out=tri[:, 1, :], in_=C32[:, 3, :]), reads=["cst"], writes=["tri"])
        c.op("act", lambda e: e.activation(out=vt[:, 13:15], in_=vt[:, 0:2], func=AF.Exp), reads=["vec"], writes=["vec"])
        c.op("dve", lambda e: e.tensor_scalar(out=vt[:, 13:15], in0=vt[:, 13:15], scalar1=-1.0, scalar2=None, op0=ALU.mult), reads=["vec"], writes=["vec"])
        for d in range(2):
            c.op("act", lambda e: e.activation(out=gx[:, :], in_=gtr[:, :, d], func=AF.Exp, bias=vt[:, 2 + d:3 + d], scale=1.0), reads=["gtr", "vec"], writes=["gx"])
            c.op("act", lambda e: e.activation(out=gx[:, :], in_=gx[:, :], func=AF.Ln, bias=1.0, scale=1.0), reads=["gx"], writes=["gx"])
            c.op("dve", lambda e: e.tensor_scalar(out=G[d][:, :], in0=gx[:, :], scalar1=vt[:, 13 + d:14 + d], scalar2=None, op0=ALU.mult), reads=["gx", "vec"], writes=["G%d" % d])
            c.op("act", lambda e: e.activation(out=BT[d][:, :], in_=gtr[:, :, 2 + d], func=AF.Sigmoid), reads=["gtr"], writes=["BT%d" % d])
            c.op("dve", lambda e: e.tensor_scalar(out=NB[d][:, :], in0=BT[d][:, :], scalar1=-1.0, scalar2=None, op0=ALU.mult), reads=["BT%d" % d], writes=["NB%d" % d])
            c.op("pe", lambda e: e.matmul(pG[:, 0:NCH], lhsT=tri[:, d, :], rhs=G[d][:, :], start=True, stop=True), reads=["tri", "G%d" % d], writes=["bank1"])
            c.op("pe", lambda e: e.matmul(pG[:, 256:256 + NCH], lhsT=C32[:, 1, :], rhs=G[d][:, :], start=True, stop=True), reads=["cst", "G%d" % d], writes=["bank1"])
            c.op("dve", lambda e: e.tensor_copy(out=Bc[d][:, :], in_=pG[:, 0:NCH]), reads=["bank1"], writes=["Bc%d" % d])
            c.op("act", lambda e: e.activation(out=EB[d][:, :], in_=pG[:, 0:NCH], func=AF.Exp), reads=["bank1"], writes=["EB%d" % d])
            c.op("act", lambda e: e.activation(out=CD[d][:, :], in_=pG[:, 256:256 + NCH], func=AF.Exp), reads=["bank1"], writes=["CD%d" % d])
            c.op("dve", lambda e: e.tensor_tensor(out=EKD[d][:, :], in0=pG[:, 256:256 + NCH], in1=Bc[d][:, :], op=ALU.subtract), reads=["bank1", "Bc%d" % d], writes=["EKD%d" % d])
            c.op("act", lambda e: e.activation(out=EKD[d][:, :], in_=EKD[d][:, :], func=AF.Exp), reads=["EKD%d" % d], writes=["EKD%d" % d])
            c.op("dve", lambda e: e.tensor_tensor(out=BEB[d][:, :], in0=BT[d][:, :], in1=EB[d][:, :], op=ALU.mult), reads=["BT%d" % d, "EB%d" % d], writes=["BEB%d" % d])
        gtoks = ["Bc0", "Bc1", "EB0", "EB1", "CD0", "CD1", "EKD0", "EKD1", "BEB0", "BEB1", "BT0", "BT1", "NB0", "NB1"]

        segs = [(0, NCT)] + [(NCT + s0, w) for s0, w in tiles_of(NT - NCT, 512)]
        seq_lo = {0: 0}
        for (c0, w) in segs:
            lo_edge = (c0 == 0) or (c0 == NCT)
            hi_edge = (c0 + w == NCT) or (c0 + w == NT)
            a0 = c0 if lo_edge else c0 - 1
            a1 = c0 + w if hi_edge else c0 + w + 1
            if lo_edge:
                c.op("dve", lambda e: e.memset(rb[:, :, 0:1], 0.0), writes=["rb"])
            if hi_edge:
                c.op("dve", lambda e: e.memset(rb[:, :, w + 1:w + 2], 0.0), writes=["rb"])
            o0 = 1 if lo_edge else 0
            c.dma("sp", "rb", rb[:, :, o0:o0 + (a1 - a0)], qkvT[:, :, a0:a1].rearrange("a p t -> p a t"), writes=["rb"])
            for i in range(3):
                t0 = 4 + 3 * i
                c.op("dve", lambda e: e.tensor_scalar(out=cv[:, :w], in0=rb[:, i, 0:w], scalar1=vt[:, t0:t0 + 1], scalar2=None, op0=ALU.mult), reads=["rb", "vec"], writes=["cv"])
                c.op("dve", lambda e: e.scalar_tensor_tensor(out=cv[:, :w], in0=rb[:, i, 1:w + 1], scalar=vt[:, t0 + 1:t0 + 2], in1=cv[:, :w], op0=ALU.mult, op1=ALU.add),
                     reads=["rb", "vec", "cv"], writes=["cv"])
                c.op("dve", lambda e: e.scalar_tensor_tensor(out=cv[:, :w], in0=rb[:, i, 2:w + 2], scalar=vt[:, t0 + 2:t0 + 3], in1=cv[:, :w], op0=ALU.mult, op1=ALU.add),
                     reads=["rb", "vec", "cv"], writes=["cv"])
                if i == 2:
                    c.op("act", lambda e: e.activation(out=vTb[:, :w], in_=cv[:, :w], func=AF.Silu), reads=["cv"], writes=["vTb"])
                    for s in range(w // 128):
                        p_, ptok = pT.next()
                        c.op("pe", lambda e: e.matmul(p_[:, :], lhsT=vTb[:, s * 128:(s + 1) * 128], rhs=Ib[:, :], start=True, stop=True), reads=["vTb", "Ib"], writes=[ptok])
                        ch = (c0 + s * 128) // 128
                        c.op("act", lambda e: e.activation(out=vtm[:, ch, :], in_=p_[:, :], func=AF.Copy), reads=[ptok], writes=["vtm%d" % ch])
                    continue
                c.op("act", lambda e: e.activation(out=sv[:, :w], in_=cv[:, :w], func=AF.Silu), reads=["cv"], writes=["sv"])
                c.op("act", lambda e: e.activation(out=sqb[:, :w], in_=sv[:, :w], func=AF.Square), reads=["sv"], writes=["sqb"])
                c.op("pe", lambda e: e.matmul(pbig[:, :w], lhsT=onesb[:, :], rhs=sqb[:, :w], start=True, stop=True), reads=["sqb", "onesb"], writes=["bank0"])
                emit_rstd(c, rstd, pbig, "bank0", w, n=1)
                dst = qT if i == 0 else kT
                dtok = ("qT%d" if i == 0 else "kT%d") % c0
                sc = (128 ** -0.5) if i == 0 else 1.0
                c.op("dve", lambda e: e.scalar_tensor_tensor(out=dst[:, c0:c0 + w], in0=sv[:, :w], scalar=sc, in1=rstd[:, :w], op0=ALU.mult, op1=ALU.mult),
                     reads=["sv", "rstd"], writes=[dtok])
                if i == 1:
                    for s in range(w // 128):
                        p_, ptok = pT.next()
                        c.op("pe", lambda e: e.matmul(p_[:, :], lhsT=kT[:, c0 + s * 128:c0 + (s + 1) * 128], rhs=Ib[:, :], start=True, stop=True), reads=[dtok, "Ib"], writes=[ptok])
                        ch = (c0 + s * 128) // 128
                        c.op("dve", lambda e: e.tensor_copy(out=ktm[:, ch, :], in_=p_[:, :]), reads=[ptok], writes=["ktm%d" % ch])

        c.pop_scope()
        chains = [Chain(0), Chain(1)]

        def INV(ap):
            return ap

        def AS32(ap):
            return ap if GDN_INV_BF16 else (ap.bitcast(F32) if GDN_FP32R else ap)

        def seg_of(ch):
            t = ch * 128
            if t < NCT:
                return 0
            return NCT + ((t - NCT) // 512) * 512

        def pre(ch, X):
            d = X.d
            col = slice(ch, ch + 1)
            qtok = "qT%d" % seg_of(ch)
            ktok = "kT%d" % seg_of(ch)
            cs = slice(ch * 128, (ch + 1) * 128)
            c.op("dve", lambda e: e.tensor_scalar(out=X.diagb[:, :], in0=C32[:, 0, :], scalar1=Bc[d][:, col], scalar2=None, op0=ALU.mult), reads=["cst"] + gtoks, writes=[X.t("diagb")])
            c.op("pe", lambda e: e.matmul(X.pKK, lhsT=kT[:, cs], rhs=kT[:, cs], start=True, stop=True), reads=[ktok], writes=[X.bA])
            c.op("pe", lambda e: e.matmul(X.pQK, lhsT=qT[:, cs], rhs=kT[:, cs], start=True, stop=True), reads=[qtok, ktok], writes=[X.bA])
            yield
            c.op("pe", lambda e: e.matmul(X.pA, lhsT=C32[:, 1, :], rhs=X.diagb[:, :], start=True, stop=True), reads=["cst", X.t("diagb")], writes=[X.bA])
            yield
            c.op("dve", lambda e: e.tensor_scalar(out=X.nd[:, :], in0=X.pA, scalar1=Bc[d][:, col], scalar2=0.0, op0=ALU.subtract, op1=ALU.max), reads=[X.bA] + gtoks, writes=[X.t("nd")])
            yield
            c.op("act", lambda e: e.activation(out=X.dm[:, :], in_=X.nd[:, :], func=AF.Exp, scale=-1.0), reads=[X.t("nd")], writes=[X.t("dm")])
            yield
            c.op("dve", lambda e: e.tensor_tensor(out=X.dmS[:, :], in0=X.dm[:, :], in1=MS[d], op=ALU.mult), reads=[X.t("dm"), "cst"], writes=[X.t("dmS")])
            c.op("dve", lambda e: e.tensor_tensor(out=X.dmI[:, :], in0=X.dm[:, :], in1=MI[d], op=ALU.mult), reads=[X.t("dm"), "cst"], writes=[X.t("dmI")])
            yield
            Q, Qtok = X.Qm.next()
            c.op("dve", lambda e: e.scalar_tensor_tensor(out=Q[:, :], in0=X.pKK, scalar=NB[d][:, col], in1=X.dmS[:, :], op0=ALU.mult, op1=ALU.mult),
                 reads=[X.bA, X.t("dmS")] + gtoks, writes=[Qtok])
            c.op("dve", lambda e: e.tensor_tensor(out=X.qk[:, :], in0=X.pQK, in1=X.dmI[:, :], op=ALU.mult), reads=[X.bA, X.t("dmI")], writes=[X.t("qk")])
            yield
            p_, ptok = X.pT.next()
            c.op("pe", lambda e: e.matmul(p_, lhsT=AS32(Q[:, :]), rhs=(Ib[:, :] if GDN_INV_BF16 else C32[:, 0, :]), start=True, stop=True), reads=[Qtok, "cst", "Ib"], writes=[ptok])
            p2, p2tok = X.pT.next()
            c.op("pe", lambda e: e.matmul(p2, lhsT=X.qk[:, :], rhs=Ib[:, :], start=True, stop=True), reads=[X.t("qk"), "Ib"], writes=[p2tok])
            yield
            P, Ptok = X.Pm.next()
            c.op("act", lambda e: e.activation(out=P[:, :], in_=p_, func=AF.Copy), reads=[ptok], writes=[Ptok])
            c.op("dve", lambda e: e.tensor_tensor(out=X.Rm[:, :], in0=p_, in1=C32[:, 0, :], op=ALU.add), reads=[ptok, "cst"], writes=[X.t("Rm")])
            qkT_, qkTtok = X.qkT.next()
            c.op("act", lambda e: e.activation(out=qkT_[:, :], in_=p2, func=AF.Copy), reads=[p2tok], writes=[qkTtok])
            yield
            for step in range(6):
                last = step == 5
                pq, pqtok = X.pI.next()
                c.op("pe", lambda e: e.matmul(pq, lhsT=INV(P[:, :]), rhs=INV(Q[:, :]), start=True, stop=True), reads=[Ptok, Qtok], writes=[pqtok])
                if not last:
                    pp, pptok = X.pI.next()
                    c.op("pe", lambda e: e.matmul(pp, lhsT=INV(Q[:, :]), rhs=INV(P[:, :]), start=True, stop=True), reads=[Ptok, Qtok], writes=[pptok])
                yield
                Q2, Q2tok = X.Qm.next()
                c.op("dve", lambda e: e.tensor_copy(out=Q2[:, :], in_=pq), reads=[pqtok], writes=[Q2tok])
                if not last:
                    P2, P2tok = X.Pm.next()
                    c.op("act", lambda e: e.activation(out=P2[:, :], in_=pp, func=AF.Copy), reads=[pptok], writes=[P2tok])
                    P, Ptok = P2, P2tok
                Q, Qtok = Q2, Q2tok
                yield
                pr, prtok = X.pI.next()
                c.op("pe", lambda e: e.matmul(pr, lhsT=INV(Q[:, :]), rhs=INV(X.Rm[:, :]), start=True, stop=True), reads=[Qtok, X.t("Rm")], writes=[prtok])
                yield
                c.op("dve", lambda e: e.tensor_tensor(out=X.Rm[:, :], in0=pr, in1=AS32(X.Rm[:, :]), op=ALU.add), reads=[prtok, X.t("Rm")], writes=[X.t("Rm")])
                yield
            c.op("act", lambda e: e.activation(out=X.Rb[:, :], in_=AS32(X.Rm[:, :]), func=AF.Copy), reads=[X.t("Rm")], writes=[X.t("Rb")])
            kd_, kdtok = X.kd.next()
            c.op("pool", lambda e: e.tensor_scalar(out=X.kb[:, :], in0=ktm[:, ch, :], scalar1=BEB[d][:, col], scalar2=None, op0=ALU.mult), reads=["ktm%d" % ch] + gtoks, writes=[X.t("kb")])
            c.op("pool", lambda e: e.tensor_scalar(out=kd_[:, :], in0=ktm[:, ch, :], scalar1=EKD[d][:, col], scalar2=None, op0=ALU.mult), reads=["ktm%d" % ch] + gtoks, writes=[kdtok])
            c.op("pool", lambda e: e.tensor_scalar(out=X.vb[:, :], in0=vtm[:, ch, :], scalar1=BT[d][:, col], scalar2=None, op0=ALU.mult), reads=["vtm%d" % ch] + gtoks, writes=[X.t("vb")])
            yield
            c.op("pe", lambda e: e.matmul(X.pU, lhsT=X.Rb[:, :], rhs=X.vb[:, :], start=True, stop=True), reads=[X.t("Rb"), X.t("vb")], writes=[X.bA])
            c.op("pe", lambda e: e.matmul(X.pW, lhsT=X.kb[:, :], rhs=X.Rb[:, :], start=True, stop=True), reads=[X.t("Rb"), X.t("kb")], writes=[X.bB])
            yield
            ub_, ubtok = X.ub.next()
            wT_, wTtok = X.wT.next()
            c.op("act", lambda e: e.activation(out=ub_[:, :], in_=X.pU, func=AF.Copy), reads=[X.bA], writes=[ubtok])
            c.op("dve", lambda e: e.tensor_copy(out=wT_[:, :], in_=X.pW), reads=[X.bB], writes=[wTtok])
            X.slot_out = (qkT_, qkTtok, kd_, kdtok, ub_, ubtok, wT_, wTtok)
            yield

        def rec(ch, X, slot, fin):
            d = X.d
            qkT_, qkTtok, kd_, kdtok, ub_, ubtok, wT_, wTtok = slot
            col = slice(ch, ch + 1)
            cs = slice(ch * 128, (ch + 1) * 128)
            qtok = "qT%d" % seg_of(ch)
            c.op("pe", lambda e: e.matmul(X.pwS, lhsT=wT_[:, :], rhs=X.Sb[:, :], start=True, stop=True), reads=[wTtok, X.t("Sb")], writes=[X.bD])
            c.op("pe", lambda e: e.matmul(X.pO1, lhsT=qT[:, cs], rhs=X.Sb[:, :], start=True, stop=True), reads=[qtok, X.t("Sb")], writes=[X.bD])
            yield
            c.op("dve", lambda e: e.tensor_tensor(out=X.u_b[:, :], in0=ub_[:, :], in1=X.pwS, op=ALU.subtract), reads=[ubtok, X.bD], writes=[X.t("u_b")])
            yield
            c.op("pe", lambda e: e.matmul(X.pdS, lhsT=kd_[:, :], rhs=X.u_b[:, :], start=True, stop=True), reads=[kdtok, X.t("u_b")], writes=[X.bD])
            c.op("pe", lambda e: e.matmul(X.pO2, lhsT=qkT_[:, :], rhs=X.u_b[:, :], start=True, stop=True), reads=[qkTtok, X.t("u_b")], writes=[X.bD])
            yield
            c.op("dve", lambda e: e.scalar_tensor_tensor(out=X.S32[:, :], in0=X.S32[:, :], scalar=CD[d][:, col], in1=X.pdS, op0=ALU.mult, op1=ALU.add),
                 reads=[X.t("S32"), X.bD] + gtoks, writes=[X.t("S32")])
            yield
            c.op("act", lambda e: e.activation(out=X.Sb[:, :], in_=X.S32[:, :], func=AF.Copy), reads=[X.t("S32")], writes=[X.t("Sb")])
            c.op("act", lambda e: e.activation(out=X.o2s[:, :], in_=X.pO2, func=AF.Copy), reads=[X.bD], writes=[X.t("o2s")])
            yield
            if not fin:
                c.op("dve", lambda e: e.scalar_tensor_tensor(out=of[:, ch, :], in0=X.pO1, scalar=EB[d][:, col], in1=X.o2s[:, :], op0=ALU.mult, op1=ALU.add),
                     reads=[X.bD, X.t("o2s")] + gtoks, writes=["of%d" % ch])
                yield
                return
            c.op("dve", lambda e: e.scalar_tensor_tensor(out=X.osum[:, :], in0=X.pO1, scalar=EB[d][:, col], in1=X.o2s[:, :], op0=ALU.mult, op1=ALU.add),
                 reads=[X.bD, X.t("o2s")] + gtoks, writes=[X.t("osum")])
            yield
            c.op("dve", lambda e: e.tensor_tensor(out=X.osum[:, :], in0=X.osum[:, :], in1=of[:, ch, :], op=ALU.add), reads=[X.t("osum"), "of%d" % ch], writes=[X.t("osum")])
            yield
            c.op("act", lambda e: e.activation(out=X.osq[:, :], in_=X.osum[:, :], func=AF.Square, accum_out=X.oss[:, 0:1]), reads=[X.t("osum")], writes=[X.t("osq"), X.t("oss")])
            yield
            emit_rstd(c, X.oss, X.oss, X.t("oss"), 1, n=128, tok=X.t("oss"))
            z_, ztok = zt.next()
            c.dma("sp", ztok, z_[:, :], ztm[ch * 128:(ch + 1) * 128, :], writes=[ztok])
            c.op("act", lambda e: e.activation(out=X.sz[:, :], in_=z_[:, :], func=AF.Silu), reads=[ztok], writes=[X.t("sz")])
            yield
            c.op("dve", lambda e: e.scalar_tensor_tensor(out=X.osum[:, :], in0=X.osum[:, :], scalar=X.oss[:, 0:1], in1=gb[:, :], op0=ALU.mult, op1=ALU.mult),
                 reads=[X.t("osum"), X.t("oss"), "gbc"], writes=[X.t("osum")])
            o_, otok = ot.next()
            c.op("dve", lambda e: e.tensor_tensor(out=o_[:, :], in0=X.osum[:, :], in1=X.sz[:, :], op=ALU.mult), reads=[X.t("osum"), X.t("sz")], writes=[otok])
            c.dma("sp", otok, otm[ch * 128:(ch + 1) * 128, :], o_[:, :], reads=[otok], writes=["otm"])
            yield

        def drive(gens):
            gens = [g_ for g_ in gens if g_ is not None]
            while gens:
                alive = []
                for g_ in gens:
                    try:
                        next(g_)
                        alive.append(g_)
                    except StopIteration:
                        pass
                gens = alive

        orders = [list(range(NCH)), list(range(CCH - 1, -1, -1)) + list(range(NCH - 1, CCH - 1, -1))]
        pos = [{ch: i for i, ch in enumerate(o)} for o in orders]
        for X in chains:
            c.op("dve", lambda e: e.memset(X.S32[:, :], 0.0), writes=[X.t("S32")])
            c.op("dve", lambda e: e.memset(X.Sb[:, :], 0.0), writes=[X.t("Sb")])
        drive([pre(orders[d][0], chains[d]) for d in range(2)])
        slots = [chains[d].slot_out for d in range(2)]
        for i in range(NCH):
            gens = []
            for d in range(2):
                if i + 1 < NCH:
                    gens.append(pre(orders[d][i + 1], chains[d]))
            for d in range(2):
                ch = orders[d][i]
                gens.append(rec(ch, chains[d], slots[d], pos[d][ch] > pos[1 - d][ch]))
            drive(gens)
            if i + 1 < NCH:
                slots = [chains[d].slot_out for d in range(2)]
        c.wait_all("sp", ["otm"])
    return nc


def gdn_consts():
    i = np.arange(128)
    I = np.eye(128, dtype=np.float32)
    ones = np.ones((128, 128), np.float32)
    LS = (i[:, None] > i[None, :]).astype(np.float32)
    LI = (i[:, None] >= i[None, :]).astype(np.float32)
    return np.ascontiguousarray(np.stack([I, ones, LS, LI, LS.T, LI.T], 0))


def tile_w(w, m=128):
    w = np.asarray(w, np.float32)
    K, N = w.shape
    nb = (N + m - 1) // m
    if nb * m != N:
        w = np.concatenate([w, np.zeros((K, nb * m - N), np.float32)], 1)
    return np.ascontiguousarray(w.reshape(K // 128, 128, nb, m).transpose(2, 1, 0, 3))


def fm(v):
    return np.ascontiguousarray(np.asarray(v, np.float32).reshape(KC, 128).T)


def lambda_init(layer):
    return 0.8 - 0.6 * math.exp(-0.3 * layer)


PRE_TILES = [(0, 32, 1)] + [(32 + i * 512, 512, 0) for i in range(4)]
CTXC = CTX // NCORE
POST_T = CTXC + 2 + SEQ // NCORE + 2
POST_SPECIAL = [(0, "vl"), (CTXC + 1, "vr"), (CTXC + 2, "vl"), (POST_T - 1, "vr")]


def post_tiles():
    inner = POST_T - 2
    n = 5
    base, extra = divmod(inner, n)
    tiles, a = [], 1
    for i in range(n):
        wi = base + (1 if i < extra else 0)
        s0, wd = a - 1, wi + 2
        segs = []
        for (g0, g1, stream) in ((0, CTXC + 2, 1), (CTXC + 2, POST_T, 0)):
            c0, c1 = max(g0, s0) - s0, min(g1, s0 + wd) - s0
            if c1 > c0:
                segs.append((c0, c1, stream))
        masks = [(col - s0, fl) for col, fl in POST_SPECIAL if s0 <= col < s0 + wd]
        tiles.append((s0, wd, segs, masks))
        a += wi
    return tiles


POST_TILES = post_tiles()
TL = SEQ // NCORE


def kernel(x, c, ctx, c_ctx, w_mod, b_mod, norm1_g, norm2_g, w_in_even, a_conv_w, a_A_log, a_dt_bias,
           a_norm_g, b_lambda, b_norm_g, w_out_even, w_in_odd, c_q_norm, c_k_norm, w_out_odd,
           ffn_up, ffn_conv_w, ffn_conv_b, ffn_down, final_g):
    f32 = np.float32
    x = np.asarray(x, f32)
    ctx = np.asarray(ctx, f32)
    progs = {}

    def prog(key, fn):
        if key not in progs:
            progs[key] = fn()
        return progs[key]

    c2 = np.ascontiguousarray(np.stack([fm(np.asarray(c, f32)[0]), fm(np.asarray(c_ctx, f32))], -1))
    w_mod = np.asarray(w_mod, f32)
    b_mod = np.asarray(b_mod, f32)
    maps = []
    for j in range(NCORE):
        sl = slice(j * MODC, (j + 1) * MODC)
        bm = np.ascontiguousarray(np.broadcast_to(b_mod[None, :, sl], (2, DEPTH, MODC)))
        maps.append({"c2": c2, "wm": np.stack([tile_w(w_mod[l_][:, sl], 512) for l_ in range(DEPTH)], 0), "bm": bm})
    res = run(prog("mod", build_mod), maps, "mod")
    mod = np.concatenate([r["out"] for r in res], -1)

    def modv(s, l, m):
        return mod[s, l, m * D:(m + 1) * D]

    xlT = np.ascontiguousarray(x[0].T)
    xcT = np.ascontiguousarray(ctx[0].T)
    rope = rope_tables(SEQ)
    rt = rope_perm()
    zcol = np.zeros((D, 1), f32)

    for l in range(DEPTH):
        even = l % 2 == 0
        e = l // 2
        last = l == DEPTH - 1
        w_in = np.asarray(w_in_even[e] if even else w_in_odd[e], f32)
        ncols = w_in.shape[1]
        w_in = tile_w(w_in)
        vec = np.ascontiguousarray(np.stack([fm(norm1_g[l]), fm(modv(0, l, 0)), fm(modv(0, l, 1)), fm(modv(1, l, 0)), fm(modv(1, l, 1))], 1))
        maps = [{"xT": np.ascontiguousarray(np.concatenate([xcT[:, 32 * j:32 * (j + 1)], xlT[:, TL * j:TL * (j + 1)]], 1)), "vec": vec, "w": w_in}
                for j in range(NCORE)]
        res = run(prog(("pre", ncols), lambda: build_pre(ncols, PRE_TILES)), maps, "pre%d" % l)
        pc = np.concatenate([r["pT"][:, :32] for r in res], 1)
        pl = np.concatenate([r["pT"][:, 32:] for r in res], 1)
        p = np.concatenate([pc, pl], 1)
        del res, maps
        mT = np.zeros((D, CTX + SEQ), NPBF)
        if even:
            NT = CTX + SEQ
            cw = np.asarray(a_conv_w[e], f32)
            maps = []
            for j in range(NCORE):
                hs = slice(j * 128, (j + 1) * 128)
                qkvT = np.ascontiguousarray(np.stack([p[j * 128:(j + 1) * 128], p[1024 + j * 128:1024 + (j + 1) * 128], p[2048 + j * 128:2048 + (j + 1) * 128]], 0))
                ztm = np.ascontiguousarray(p[3072 + j * 128:3072 + (j + 1) * 128].T)
                gates = np.stack([p[4096 + gi * 8 + j] for gi in range(4)], 0)
                gt = np.ascontiguousarray(gates.reshape(4, NT // 128, 128).transpose(2, 1, 0))
                vec = np.zeros((128, 16), f32)
                vec[:, 0:2] = np.asarray(a_A_log[e], f32)[:, j]
                vec[:, 2:4] = np.asarray(a_dt_bias[e], f32)[:, j]
                for i, off in enumerate((0, 1024, 2048)):
                    for t in range(3):
                        vec[:, 4 + 3 * i + t] = cw[t, off + j * 128:off + (j + 1) * 128]
                gbc = np.ascontiguousarray(np.broadcast_to(np.asarray(a_norm_g[e], f32), (128, 128)))
                maps.append({"qkvT": qkvT, "ztm": ztm, "gt": gt, "vec": vec, "gbc": gbc, "cst": gdn_consts()})
            res = run(prog("gdn", lambda: build_gdn(NT)), maps, "gdn%d" % l)
            for j in range(NCORE):
                mT[j * 128:(j + 1) * 128, :] = res[j]["otm"].T
            del res, maps
            HQ = SEQ // 2
            li = lambda_init(l)
            lp = np.ascontiguousarray(np.broadcast_to(np.asarray(b_lambda[e], f32), (128, 4, 128)))
            bng = np.asarray(b_norm_g[e], f32)
            maps = []
            for j in range(NCORE):
                hb, half = j // 2, j % 2
                qrows = [slice(A_IN + hb * 256 + m * 128, A_IN + hb * 256 + (m + 1) * 128) for m in range(2)]
                krows = [slice(A_IN + 1024 + hb * 256 + m * 128, A_IN + 1024 + hb * 256 + (m + 1) * 128) for m in range(2)]
                qT = np.ascontiguousarray(np.stack([np.concatenate([p[r, :CTX], p[r, CTX + half * HQ:CTX + (half + 1) * HQ]], 1) for r in qrows], 0))
                kT = np.ascontiguousarray(np.stack([p[r] for r in krows], 0))
                v = np.ascontiguousarray(p[A_IN + 2048 + hb * 256:A_IN + 2048 + (hb + 1) * 256].T)
                vec = np.zeros((128, 8), f32)
                vec[:, 2] = li
                vec[:, 3] = 1.0 - li
                vec[:, 4] = bng[:128]
                vec[:, 5] = bng[128:]
                maps.append({"qT": qT, "kT": kT, "v": v, "cq": np.ascontiguousarray(rope[:, :, half * HQ:(half + 1) * HQ]), "ck": rope, "rt": rt, "vec": vec, "lp": lp})
            res = run(prog("attB", lambda: build_att("B", HQ, SEQ)), maps, "attB%d" % l)
            for j in range(NCORE):
                hb, half = j // 2, j % 2
                rows = slice(1024 + hb * 256, 1024 + (hb + 1) * 256)
                if half == 0:
                    mT[rows, :CTX] = res[j]["oT"][:, :CTX]
                mT[rows, CTX + half * HQ:CTX + (half + 1) * HQ] = res[j]["oT"][:, CTX:]
            del res, maps
        else:
            lp0 = np.zeros((128, 4, 128), f32)
            maps = []
            for j in range(NCORE):
                g = j // 2
                qT = np.ascontiguousarray(np.stack([p[(2 * j + s) * 128:(2 * j + s + 1) * 128] for s in range(2)], 0))
                kT = np.ascontiguousarray(p[2048 + g * 128:2048 + (g + 1) * 128][None])
                v = np.ascontiguousarray(p[2560 + g * 128:2560 + (g + 1) * 128].T)
                vec = np.zeros((128, 8), f32)
                vec[:, 0] = np.asarray(c_q_norm[e], f32)
                vec[:, 1] = np.asarray(c_k_norm[e], f32)
                maps.append({"qT": qT, "kT": kT, "v": v, "cq": rope, "ck": rope, "rt": rt, "vec": vec, "lp": lp0})
            res = run(prog("attC", lambda: build_att("C", SEQ, SEQ)), maps, "attC%d" % l)
            for j in range(NCORE):
                mT[2 * j * 128:(2 * j + 2) * 128, :] = res[j]["oT"]
            del res, maps
        del p
        vec = np.zeros((128, POST_NV), f32)
        V = POST_V
        vec[:, V["n2g"]:V["n2g"] + 16] = fm(norm2_g[l])
        for nm, s, m in (("g1", 0, 2), ("sh2", 0, 3), ("sc2", 0, 4), ("g2", 0, 5), ("cg1", 1, 2), ("csh2", 1, 3), ("csc2", 1, 4), ("cg2", 1, 5)):
            vec[:, V[nm]:V[nm] + 16] = fm(modv(s, l, m))
        vec[:, V["fg"]:V["fg"] + 16] = fm(final_g)
        vec[:, V["cw"]:V["cw"] + 3 * 88] = np.asarray(ffn_conv_w[l], f32).reshape(3, 88, 128).transpose(2, 0, 1).reshape(128, 264)
        vec[:, V["cb"]:V["cb"] + 88] = np.asarray(ffn_conv_b[l], f32).reshape(88, 128).T
        wo = tile_w(w_out_even[e] if even else w_out_odd[e])
        wu = tile_w(ffn_up[l])
        wd = tile_w(ffn_down[l])
        mcT, mlT = mT[:, :CTX], mT[:, CTX:]
        zb = np.zeros((D, 1), NPBF)
        maps = []
        for j in range(NCORE):
            lo, hi = TL * j, TL * (j + 1)
            xl_ = [zcol if j == 0 else xlT[:, lo - 1:lo], xlT[:, lo:hi], zcol if j == NCORE - 1 else xlT[:, hi:hi + 1]]
            ml_ = [zb if j == 0 else mlT[:, lo - 1:lo], mlT[:, lo:hi], zb if j == NCORE - 1 else mlT[:, hi:hi + 1]]
            vj = vec.copy()
            vj[:, V["vl"]] = 0.0 if j == 0 else 1.0
            vj[:, V["vr"]] = 0.0 if j == NCORE - 1 else 1.0
            clo, chi = CTXC * j, CTXC * (j + 1)
            xc_ = [zcol if j == 0 else xcT[:, clo - 1:clo], xcT[:, clo:chi], zcol if j == NCORE - 1 else xcT[:, chi:chi + 1]]
            mc_ = [zb if j == 0 else mcT[:, clo - 1:clo], mcT[:, clo:chi], zb if j == NCORE - 1 else mcT[:, chi:chi + 1]]
            maps.append({"xT": np.ascontiguousarray(np.concatenate(xc_ + xl_, 1)),
                         "mT": np.ascontiguousarray(np.concatenate(mc_ + ml_, 1)), "vec": vj, "wo": wo, "wu": wu, "wd": wd})
        if not last:
            res = run(prog("post", lambda: build_post(POST_TILES, False)), maps, "post%d" % l)
            xcT = np.ascontiguousarray(np.concatenate([r["oT"][:, :CTXC] for r in res], 1))
            xlT = np.ascontiguousarray(np.concatenate([r["oT"][:, CTXC + 2:] for r in res], 1))
        else:
            res = run(prog("postf", lambda: build_post(POST_TILES, True)), maps, "postf%d" % l)
            xlT = np.concatenate([r["oT"][:, CTXC + 2:] for r in res], 1)
        del res, maps, mT
    return np.ascontiguousarray(xlT.T)[None].astype(np.float32)
```

```python
import math
import os
import sys
from contextlib import ExitStack

import ml_dtypes
import numpy as np
import concourse.bass as bass
import concourse.mybir as mybir
from concourse.bass_utils import run_bass_kernel_spmd

F32 = mybir.dt.float32
BF16 = mybir.dt.bfloat16
AF = mybir.ActivationFunctionType
ALU = mybir.AluOpType
AX = mybir.AxisListType
NPBF = ml_dtypes.bfloat16

D = 2048
KC = 16
NCORE = 8
SEQ = 16384
CTX = 256
DEPTH = 4
EPS = 1e-6
DFF = 5632
FC = 44
A_IN = 4128
EVEN_IN = 7200
ODD_IN = 3072


class Ctx:
    def __init__(self, nc, stack):
        self.nc = nc
        self.stack = stack
        self.E = {"pe": nc.tensor, "act": nc.scalar, "dve": nc.vector, "pool": nc.gpsimd, "sp": nc.sync}
        self.sems = {}
        self.cnt = {}
        self.known = {e: {} for e in self.E}
        self.lastw = {}
        self.readers = {}
        self.ninst = 0

    def sb(self, name, shape, dt):
        return self.stack.enter_context(self.nc.sbuf_tensor("sb_" + name, list(shape), dt))

    def ps(self, name, shape, dt=F32):
        return self.stack.enter_context(self.nc.psum_tensor("ps_" + name, list(shape), dt))

    def sem(self, key):
        if key not in self.sems:
            self.sems[key] = self.stack.enter_context(self.nc.semaphore("s_" + key.replace(":", "_")))
            self.cnt[key] = 0
        return self.sems[key]

    def _waits(self, eng, reads, writes):
        need = {}

        def add(ev):
            if ev is not None and need.get(ev[0], 0) < ev[1]:
                need[ev[0]] = ev[1]

        for t in reads:
            add(self.lastw.get(t))
        for t in writes:
            add(self.lastw.get(t))
            for k, v in self.readers.get(t, {}).items():
                add((k, v))
        E = self.E[eng]
        for k, v in need.items():
            if k == "pe" and eng == "pe":
                continue
            if k.startswith("d:"):
                v = self.cnt[k]
            if self.known[eng].get(k, 0) < v:
                E.wait_ge(self.sems[k], v)
                self.known[eng][k] = v
                self.ninst += 1

    def _record(self, ev, reads, writes):
        k, v = ev
        for t in reads:
            d = self.readers.setdefault(t, {})
            if d.get(k, 0) < v:
                d[k] = v
        for t in writes:
            self.lastw[t] = ev
            self.readers[t] = {}

    def op(self, eng, emit, reads=(), writes=()):
        ex = [t for t in reads if t.startswith("bank")]
        if ex:
            writes = list(writes) + ex
        self._waits(eng, reads, writes)
        s = self.sem(eng)
        ins = emit(self.E[eng])
        ins.then_inc(s, 1)
        self.cnt[eng] += 1
        self.ninst += 1
        self._record((eng, self.cnt[eng]), reads, writes)
        return ins

    def dma(self, eng, stream, out, in_, reads=(), writes=()):
        key = "d:" + stream
        s = self.sem(key)
        self._waits(eng, reads, writes)
        ins = self.E[eng].dma_start(out=out, in_=in_)
        ins.then_inc(s, 16)
        self.cnt[key] += 16
        self.ninst += 1
        self._record((key, self.cnt[key]), reads, writes)
        return ins

    def push_scope(self):
        self._outer = self.stack
        self.stack = ExitStack()

    def pop_scope(self):
        self.barrier()
        self.stack.close()
        self.stack = self._outer

    def barrier(self):
        for eng, E in self.E.items():
            for k, s_ in self.sems.items():
                v = self.cnt[k]
                if v > 0 and self.known[eng].get(k, 0) < v and not (k == eng):
                    E.wait_ge(s_, v)
                    self.known[eng][k] = v
                    self.ninst += 1

    def wait_all(self, eng, tokens):
        self._waits(eng, tokens, ())


class Rot:
    def __init__(self, c, name, n, shape, dt, psum=False):
        self.bufs = [(c.ps if psum else c.sb)("%s%d" % (name, i), shape, dt) for i in range(n)]
        self.names = ["%s%d" % (name, i) for i in range(n)]
        self.i = 0

    def next(self):
        j = self.i % len(self.bufs)
        self.i += 1
        return self.bufs[j], self.names[j]


def new_nc():
    return bass.Bass("TRN2", target_bir_lowering=False)


def run(nc, in_maps, tag=""):
    res = run_bass_kernel_spmd(nc, in_maps, core_ids=list(range(NCORE)))
    if os.environ.get("KDEBUG"):
        for j, r in enumerate(res.results):
            for name, arr in r.items():
                a = np.asarray(arr).astype(np.float32)
                if not np.isfinite(a).all():
                    print("KDEBUG non-finite:", tag, "core", j, name, int((~np.isfinite(a)).sum()), "of", a.size, file=sys.stderr)
    return res.results


def emit_rstd(c, rstd, ss, sstok, W, n=D, tok="rstd"):
    c.op("dve", lambda e: e.tensor_scalar(out=rstd[:, :W], in0=ss[:, :W], scalar1=1.0 / n, scalar2=EPS, op0=ALU.mult, op1=ALU.add),
         reads=[sstok], writes=[tok])
    c.op("act", lambda e: e.activation(out=rstd[:, :W], in_=rstd[:, :W], func=AF.Ln), reads=[tok], writes=[tok])
    c.op("act", lambda e: e.activation(out=rstd[:, :W], in_=rstd[:, :W], func=AF.Exp, scale=-0.5), reads=[tok], writes=[tok])


def emit_norm_mod(c, K, xt, xtok, W, segs, h, htok, maskcols=()):
    ss, sstok = K["ps_ss"].next()
    for kc in range(KC):
        sq, sqtok = K["sq"].next()
        c.op("act", lambda e: e.activation(out=sq[:, :W], in_=xt[:, kc, :W], func=AF.Square), reads=[xtok], writes=[sqtok])
        c.op("pe", lambda e: e.matmul(ss[:, :W], lhsT=K["ones"][:, :], rhs=sq[:, :W], start=(kc == 0), stop=(kc == KC - 1)),
             reads=[sqtok, "ones"], writes=[sstok])
    rstd = K["rstd"]
    emit_rstd(c, rstd, ss, sstok, W)
    for kc in range(KC):
        tmp, tmptok = K["tmp"].next()
        for (c0, c1, a_ap, b_ap) in segs:
            c.op("dve", lambda e: e.scalar_tensor_tensor(out=tmp[:, c0:c1], in0=xt[:, kc, c0:c1], scalar=a_ap[:, kc:kc + 1], in1=rstd[:, c0:c1],
                                                         op0=ALU.mult, op1=ALU.mult), reads=[xtok, "rstd", "vec"], writes=[tmptok])
            c.op("act", lambda e: e.activation(out=h[:, kc, c0:c1], in_=tmp[:, c0:c1], func=AF.Identity, bias=b_ap[:, kc:kc + 1], scale=1.0),
                 reads=[tmptok, "vec"], writes=[htok])
    for col, sc in maskcols:
        c.op("dve", lambda e: e.tensor_scalar(out=h[:, :, col:col + 1], in0=h[:, :, col:col + 1], scalar1=sc, scalar2=None, op0=ALU.mult),
             reads=[htok, "vec"], writes=[htok])


def make_consts(c):
    K = {}
    K["ones"] = c.sb("ones", [128, 128], BF16)
    c.op("dve", lambda e: e.memset(K["ones"][:, :], 1.0), writes=["ones"])
    K["sq"] = Rot(c, "sq", 2, [128, 512], BF16)
    K["tmp"] = Rot(c, "tmp", 2, [128, 512], F32)
    K["rstd"] = c.sb("rstd", [128, 512], F32)
    K["ps_ss"] = Rot(c, "ps_ss", 1, [128, 512], F32, psum=True)
    return K


MODC = 1536


def build_mod():
    nc = new_nc()
    c2 = nc.dram_tensor("c2", [128, KC, 2], F32, kind="ExternalInput").ap()
    wm = nc.dram_tensor("wm", [DEPTH, MODC // 512, 128, KC, 512], F32, kind="ExternalInput").ap()
    bm = nc.dram_tensor("bm", [2, DEPTH, MODC], F32, kind="ExternalInput").ap()
    out = nc.dram_tensor("out", [2, DEPTH, MODC], F32, kind="ExternalOutput").ap()
    with ExitStack() as st:
        c = Ctx(nc, st)
        ct = c.sb("ct", [128, KC, 2], F32)
        cs = c.sb("cs", [128, KC, 2], BF16)
        bt = c.sb("bt", [2, DEPTH, MODC], F32)
        ot = c.sb("ot", [2, DEPTH, MODC], F32)
        wrot = Rot(c, "wt", 2, [128, KC, 512], BF16)
        prot = Rot(c, "pm", 2, [2, 512], F32, psum=True)
        c.dma("sp", "c2", ct[:], c2, writes=["ct"])
        c.dma("sp", "bm", bt[:], bm, writes=["bt"])
        c.op("act", lambda e: e.activation(out=cs[:], in_=ct[:], func=AF.Silu), reads=["ct"], writes=["cs"])
        for l in range(DEPTH):
            for n in range(MODC // 512):
                wt, wtok = wrot.next()
                c.dma("pool", wtok, wt[:], wm[l, n], writes=[wtok])
                ps, ptok = prot.next()
                for kc in range(KC):
                    c.op("pe", lambda e: e.matmul(ps[:, :], lhsT=cs[:, kc, :], rhs=wt[:, kc, :], start=(kc == 0), stop=(kc == KC - 1)),
                         reads=["cs", wtok], writes=[ptok])
                c.op("dve", lambda e: e.tensor_tensor(out=ot[:, l, n * 512:(n + 1) * 512], in0=ps[:, :], in1=bt[:, l, n * 512:(n + 1) * 512], op=ALU.add),
                     reads=[ptok, "bt"], writes=["ot"])
        c.dma("sp", "out", out, ot[:], reads=["ot"], writes=["out"])
        c.wait_all("sp", ["out"])
    return nc


def tiles_of(total, w):
    return [(s, min(w, total - s)) for s in range(0, total, w)]


def build_pre(ncols, tiles):
    T = sum(w for _, w, _ in tiles)
    nc = new_nc()
    xT = nc.dram_tensor("xT", [D, T], F32, kind="ExternalInput").ap()
    vec = nc.dram_tensor("vec", [128, 5, KC], F32, kind="ExternalInput").ap()
    w = nc.dram_tensor("w", [(ncols + 127) // 128, 128, KC, 128], F32, kind="ExternalInput").ap()
    pT = nc.dram_tensor("pT", [ncols, T], BF16, kind="ExternalOutput").ap()
    with ExitStack() as st:
        c = Ctx(nc, st)
        K = make_consts(c)
        vt = c.sb("vec", [128, 5, KC], F32)
        av = c.sb("av", [128, 2, KC], F32)
        h = c.sb("h", [128, KC, T], BF16)
        xrot = Rot(c, "xt", 2, [128, KC, 512], F32)
        wrot = Rot(c, "wt", 2, [128, KC, 128], BF16)
        prot = Rot(c, "pp", 3, [128, 512], F32, psum=True)
        orot = Rot(c, "po", 3, [128, 512], BF16)
        c.dma("sp", "vec", vt[:], vec, writes=["vec"])
        for s in range(2):
            c.op("dve", lambda e: e.scalar_tensor_tensor(out=av[:, s, :], in0=vt[:, 2 + 2 * s, :], scalar=1.0, in1=vt[:, 0, :],
                                                         op0=ALU.add, op1=ALU.mult), reads=["vec"], writes=["vec"])
        for (s0, wd, stream) in tiles:
            xt, xtok = xrot.next()
            c.dma("sp", xtok, xt[:, :, :wd], xT[:, s0:s0 + wd].rearrange("(kc p) t -> p kc t", p=128), writes=[xtok])
            emit_norm_mod(c, K, xt, xtok, wd, [(0, wd, av[:, stream, :], vt[:, 1 + 2 * stream, :])], h[:, :, s0:s0 + wd], "h%d" % s0)
        nst = 0
        for cb0 in range(0, ncols, 128):
            m = min(128, ncols - cb0)
            wt, wtok = wrot.next()
            c.dma("pool", wtok, wt[:], w[cb0 // 128], writes=[wtok])
            for (s0, wd, stream) in tiles:
                ps, ptok = prot.next()
                for kc in range(KC):
                    c.op("pe", lambda e: e.matmul(ps[:m, :wd], lhsT=wt[:, kc, :m], rhs=h[:, kc, s0:s0 + wd], start=(kc == 0), stop=(kc == KC - 1)),
                         reads=[wtok, "h%d" % s0], writes=[ptok])
                ot, otok = orot.next()
                eng = "act" if nst % 2 == 0 else "dve"
                if eng == "act":
                    c.op("act", lambda e: e.activation(out=ot[:m, :wd], in_=ps[:m, :wd], func=AF.Copy), reads=[ptok], writes=[otok])
                else:
                    c.op("dve", lambda e: e.tensor_copy(out=ot[:m, :wd], in_=ps[:m, :wd]), reads=[ptok], writes=[otok])
                nst += 1
                c.dma("sp", otok, pT[cb0:cb0 + m, s0:s0 + wd], ot[:m, :wd], reads=[otok], writes=["pT"])
        c.wait_all("sp", ["pT"])
    return nc


POST_V = {"n2g": 0, "g1": 16, "sh2": 32, "sc2": 48, "g2": 64, "cg1": 80, "csh2": 96, "csc2": 112, "cg2": 128, "fg": 144,
          "cw": 160, "cb": 160 + 3 * 88, "vl": 160 + 4 * 88, "vr": 161 + 4 * 88, "zero": 162 + 4 * 88}
POST_NV = 163 + 4 * 88


def build_post(tiles, final):
    T = max(s + w for s, w, _, _ in tiles)
    TO = sum(w - 2 for _, w, _, _ in tiles)
    nc = new_nc()
    xT = nc.dram_tensor("xT", [D, T], F32, kind="ExternalInput").ap()
    mT = nc.dram_tensor("mT", [D, T], BF16, kind="ExternalInput").ap()
    vec = nc.dram_tensor("vec", [128, POST_NV], F32, kind="ExternalInput").ap()
    wo = nc.dram_tensor("wo", [KC, 128, KC, 128], F32, kind="ExternalInput").ap()
    wu = nc.dram_tensor("wu", [2 * FC, 128, KC, 128], F32, kind="ExternalInput").ap()
    wdn = nc.dram_tensor("wd", [KC, 128, FC, 128], F32, kind="ExternalInput").ap()
    oT = nc.dram_tensor("oT", [D, TO], F32, kind="ExternalOutput").ap()
    swo = nc.dram_tensor("swo", [KC, 128, KC, 128], BF16, kind="Internal").ap()
    swu = nc.dram_tensor("swu", [2 * FC, 128, KC, 128], BF16, kind="Internal").ap()
    swd = nc.dram_tensor("swd", [KC, 128, FC, 128], BF16, kind="Internal").ap()
    with ExitStack() as st:
        c = Ctx(nc, st)
        K = make_consts(c)
        vt = c.sb("vec", [128, POST_NV], F32)
        av = c.sb("av", [128, 2, KC], F32)
        xrot = Rot(c, "xt", 1, [128, KC, 512], F32)
        mrot = Rot(c, "mt", 1, [128, KC, 512], BF16)
        h2 = c.sb("h2", [128, KC, 512], BF16)
        act = c.sb("actb", [128, FC, 512], BF16)
        worot = Rot(c, "wo", 2, [128, KC, 128], BF16)
        wgrot = Rot(c, "wg", 2, [128, KC, 128], BF16)
        wvrot = Rot(c, "wv", 2, [128, KC, 128], BF16)
        wdrot = Rot(c, "wdn", 2, [128, FC, 128], BF16)
        prot = Rot(c, "pp", 2, [128, 512], F32, psum=True)
        pgrot = Rot(c, "pg", 2, [128, 512], F32, psum=True)
        pvrot = Rot(c, "pv", 2, [128, 512], F32, psum=True)
        cg = Rot(c, "cg", 2, [128, 512], F32)
        cv = Rot(c, "cv", 2, [128, 512], F32)
        sg = Rot(c, "sg", 2, [128, 512], F32)
        yt = Rot(c, "yt", 2, [128, 512], F32)
        c.dma("sp", "vec", vt[:], vec, writes=["vec"])
        V = POST_V
        for s, (scn, gn) in enumerate((("sc2", "n2g"), ("csc2", "n2g"))):
            c.op("dve", lambda e: e.scalar_tensor_tensor(out=av[:, s, :], in0=vt[:, V[scn]:V[scn] + 16], scalar=1.0, in1=vt[:, V[gn]:V[gn] + 16],
                                                         op0=ALU.add, op1=ALU.mult), reads=["vec"], writes=["vec"])
        ocol = 0
        def wload(ti, wt, wtok, src, scr, stok):
            if ti == 0:
                c.dma("pool", wtok, wt[:], src, writes=[wtok])
                c.dma("sp", "scr_" + wtok, scr, wt[:], reads=[wtok], writes=[stok])
            else:
                c.dma("sp", wtok, wt[:], scr, reads=[stok], writes=[wtok])

        for ti, (s0, wd, segs, tmasks) in enumerate(tiles):
            wi = wd - 2
            xt, xtok = xrot.next()
            mt, mtok = mrot.next()
            c.dma("sp", xtok, xt[:, :, :wd], xT[:, s0:s0 + wd].rearrange("(kc p) t -> p kc t", p=128), writes=[xtok])
            c.dma("sp", mtok, mt[:, :, :wd], mT[:, s0:s0 + wd].rearrange("(kc p) t -> p kc t", p=128), writes=[mtok])
            for oc in range(KC):
                wt, wtok = worot.next()
                wload(ti, wt, wtok, wo[oc], swo[oc], "swo%d" % oc)
                ps, ptok = prot.next()
                for kc in range(KC):
                    c.op("pe", lambda e: e.matmul(ps[:, :wd], lhsT=wt[:, kc, :], rhs=mt[:, kc, :wd], start=(kc == 0), stop=(kc == KC - 1)),
                         reads=[wtok, mtok], writes=[ptok])
                for (c0, c1, stream) in segs:
                    g1 = V["cg1"] if stream else V["g1"]
                    c.op("dve", lambda e: e.scalar_tensor_tensor(out=xt[:, oc, c0:c1], in0=ps[:, c0:c1], scalar=vt[:, g1 + oc:g1 + oc + 1], in1=xt[:, oc, c0:c1],
                                                                 op0=ALU.mult, op1=ALU.add), reads=[ptok, xtok, "vec"], writes=[xtok])
            masks = [(col, vt[:, V[fl]:V[fl] + 1]) for col, fl in tmasks]
            nsegs = [(c0, c1, av[:, stream, :], vt[:, (V["csh2"] if stream else V["sh2"]):(V["csh2"] if stream else V["sh2"]) + 16]) for (c0, c1, stream) in segs]
            emit_norm_mod(c, K, xt, xtok, wd, nsegs, h2, "h2", maskcols=masks)
            for f in range(FC):
                wg, wgtok = wgrot.next()
                wv, wvtok = wvrot.next()
                wload(ti, wg, wgtok, wu[f], swu[f], "swu%d" % f)
                wload(ti, wv, wvtok, wu[FC + f], swu[FC + f], "swu%d" % (FC + f))
                pg, pgtok = pgrot.next()
                pv, pvtok = pvrot.next()
                for kc in range(KC):
                    c.op("pe", lambda e: e.matmul(pg[:, :wd], lhsT=wg[:, kc, :], rhs=h2[:, kc, :wd], start=(kc == 0), stop=(kc == KC - 1)),
                         reads=[wgtok, "h2"], writes=[pgtok])
                for kc in range(KC):
                    c.op("pe", lambda e: e.matmul(pv[:, :wd], lhsT=wv[:, kc, :], rhs=h2[:, kc, :wd], start=(kc == 0), stop=(kc == KC - 1)),
                         reads=[wvtok, "h2"], writes=[pvtok])
                outs = []
                for (pp, pptok, rot, fi) in ((pg, pgtok, cg, f), (pv, pvtok, cv, FC + f)):
                    t, ttok = rot.next()
                    cw0 = V["cw"] + 0 * 88 + fi
                    cw1 = V["cw"] + 1 * 88 + fi
                    cw2 = V["cw"] + 2 * 88 + fi
                    c.op("dve", lambda e: e.tensor_scalar(out=t[:, :wi], in0=pp[:, 0:wi], scalar1=vt[:, cw0:cw0 + 1], scalar2=None, op0=ALU.mult),
                         reads=[pptok, "vec"], writes=[ttok])
                    c.op("dve", lambda e: e.scalar_tensor_tensor(out=t[:, :wi], in0=pp[:, 1:wi + 1], scalar=vt[:, cw1:cw1 + 1], in1=t[:, :wi],
                                                                 op0=ALU.mult, op1=ALU.add), reads=[pptok, ttok, "vec"], writes=[ttok])
                    c.op("dve", lambda e: e.scalar_tensor_tensor(out=t[:, :wi], in0=pp[:, 2:wi + 2], scalar=vt[:, cw2:cw2 + 1], in1=t[:, :wi],
                                                                 op0=ALU.mult, op1=ALU.add), reads=[pptok, ttok, "vec"], writes=[ttok])
                    outs.append((t, ttok))
                (tg, tgtok), (tv, tvtok) = outs
                s_, stok = sg.next()
                cbg = V["cb"] + f
                cbv = V["cb"] + FC + f
                c.op("act", lambda e: e.activation(out=s_[:, :wi], in_=tg[:, :wi], func=AF.Silu, bias=vt[:, cbg:cbg + 1], scale=1.0),
                     reads=[tgtok, "vec"], writes=[stok])
                c.op("dve", lambda e: e.scalar_tensor_tensor(out=act[:, f, :wi], in0=tv[:, :wi], scalar=vt[:, cbv:cbv + 1], in1=s_[:, :wi],
                                                              op0=ALU.add, op1=ALU.mult), reads=[tvtok, stok, "vec"], writes=["act%d" % f])
            for oc in range(KC):
                wt, wtok = wdrot.next()
                wload(ti, wt, wtok, wdn[oc], swd[oc], "swd%d" % oc)
                ps, ptok = prot.next()
                for f in range(FC):
                    c.op("pe", lambda e: e.matmul(ps[:, :wi], lhsT=wt[:, f, :], rhs=act[:, f, :wi], start=(f == 0), stop=(f == FC - 1)),
                         reads=[wtok, "act%d" % f], writes=[ptok])
                for (c0, c1, stream) in segs:
                    g2 = V["cg2"] if stream else V["g2"]
                    a0, a1 = max(c0, 1), min(c1, wd - 1)
                    if a1 <= a0:
                        continue
                    c.op("dve", lambda e: e.scalar_tensor_tensor(out=xt[:, oc, a0:a1], in0=ps[:, a0 - 1:a1 - 1], scalar=vt[:, g2 + oc:g2 + oc + 1], in1=xt[:, oc, a0:a1],
                                                                 op0=ALU.mult, op1=ALU.add), reads=[ptok, xtok, "vec"], writes=[xtok])
            if not final:
                c.dma("sp", "oT", oT[:, ocol:ocol + wi].rearrange("(kc p) t -> p kc t", p=128), xt[:, :, 1:wi + 1], reads=[xtok], writes=["oT"])
            else:
                ss, sstok = K["ps_ss"].next()
                for kc in range(KC):
                    sq, sqtok = K["sq"].next()
                    c.op("act", lambda e: e.activation(out=sq[:, :wi], in_=xt[:, kc, 1:wi + 1], func=AF.Square), reads=[xtok], writes=[sqtok])
                    c.op("pe", lambda e: e.matmul(ss[:, :wi], lhsT=K["ones"][:, :], rhs=sq[:, :wi], start=(kc == 0), stop=(kc == KC - 1)),
                         reads=[sqtok, "ones"], writes=[sstok])
                rstd = K["rstd"]
                emit_rstd(c, rstd, ss, sstok, wi)
                for kc in range(KC):
                    y, ytok = yt.next()
                    fg = V["fg"] + kc
                    c.op("dve", lambda e: e.scalar_tensor_tensor(out=y[:, :wi], in0=xt[:, kc, 1:wi + 1], scalar=vt[:, fg:fg + 1], in1=rstd[:, :wi],
                                                                 op0=ALU.mult, op1=ALU.mult), reads=[xtok, "rstd", "vec"], writes=[ytok])
                    c.dma("sp", ytok, oT[kc * 128:(kc + 1) * 128, ocol:ocol + wi], y[:, :wi], reads=[ytok], writes=["oT"])
            ocol += wi
        c.wait_all("sp", ["oT"])
    return nc


ATT_ACC2 = "pool"


def build_att(kind, NQL, NKL, NCT=CTX):
    S = 2
    SK = 2 if kind == "B" else 1
    DV = 256 if kind == "B" else 128
    NH = DV // 128
    NK = NCT + NKL
    NKT = NK // 128
    NCKT = NCT // 128
    R = 256
    scale = 128 ** -0.5
    nc = new_nc()
    qT = nc.dram_tensor("qT", [S, 128, NCT + NQL], BF16, kind="ExternalInput").ap()
    kT = nc.dram_tensor("kT", [SK, 128, NK], BF16, kind="ExternalInput").ap()
    v = nc.dram_tensor("v", [NK, DV], BF16, kind="ExternalInput").ap()
    cq = nc.dram_tensor("cq", [2, 128, NQL], F32, kind="ExternalInput").ap()
    ck = nc.dram_tensor("ck", [2, 128, NKL], F32, kind="ExternalInput").ap()
    rt = nc.dram_tensor("rt", [128, 128], BF16, kind="ExternalInput").ap()
    vec = nc.dram_tensor("vec", [128, 8], F32, kind="ExternalInput").ap()
    lp = nc.dram_tensor("lp", [128, 4, 128], F32, kind="ExternalInput").ap()
    oT = nc.dram_tensor("oT", [R, NCT + NQL], BF16, kind="ExternalOutput").ap()
    with ExitStack() as st:
        c = Ctx(nc, st)
        ones = c.sb("ones", [128, 128], BF16)
        c.op("dve", lambda e: e.memset(ones[:, :], 1.0), writes=["ones"])
        rtt = c.sb("rtt", [128, 128], BF16)
        vt = c.sb("vec", [128, 8], F32)
        lpt = c.sb("lpt", [128, 4, 128], F32)
        lam = c.sb("lam", [128, 8], F32)
        Kr = c.sb("Kr", [128, SK, NK], BF16)
        Vt = c.sb("Vt", [128, NKT, DV], BF16)
        raw = Rot(c, "raw", 2, [128, 512], BF16)
        cst = Rot(c, "cst", 2, [128, 2, 512], F32)
        xn = c.sb("xn", [128, 512], F32)
        xnb = c.sb("xnb", [128, 512], BF16)
        sqb = c.sb("sqb", [128, 512], BF16)
        rstd = c.sb("rstd", [128, 512], F32)
        t1 = c.sb("t1", [128, 512], F32)
        t2 = c.sb("t2", [128, 512], F32)
        qr = Rot(c, "qr", 2, [128, 512], BF16)
        E = Rot(c, "E", 3, [128, 2, 512], BF16)
        acc = [c.sb("acc%d" % i, [128, 2, 512], F32) for i in range(2)]
        ones32 = c.sb("ones32", [128, 128], F32)
        c.op("dve", lambda e: e.memset(ones32[:, :], 1.0), writes=["ones32"])
        sqb2 = c.sb("sqb2", [128, 512], BF16)
        rstd2 = c.sb("rstd2", [128, 512], F32)
        on = [[c.sb("on%d%d" % (s, h), [128, 512], F32) for h in range(NH)] for s in range(S)]
        rec = c.sb("rec", [128, 512], F32)
        ob = Rot(c, "ob", 2, [128, 512], BF16)
        ps_s = Rot(c, "pS", 2, [128, 2, 512], F32, psum=True)
        ps_o = [c.ps("pO%d" % h, [128, 512]) for h in range(NH)]
        ps_sum = c.ps("pSum", [128, 512])
        ps_ss2 = ps_sum
        ps_ss = c.ps("pSS", [128, 512])
        ps_rot = ps_ss
        c.dma("sp", "rtt", rtt[:], rt, writes=["rtt"])
        c.dma("sp", "vec", vt[:], vec, writes=["vec"])
        c.dma("sp", "Vt", Vt[:], v.rearrange("(kt p) d -> p kt d", p=128), writes=["Vt"])
        if kind == "B":
            c.dma("sp", "lpt", lpt[:], lp, writes=["lpt"])
            for i in range(2):
                c.op("dve", lambda e: e.tensor_tensor(out=t1[:, :128], in0=lpt[:, 2 * i, :], in1=lpt[:, 2 * i + 1, :], op=ALU.mult), reads=["lpt"], writes=["t1"])
                c.op("dve", lambda e: e.reduce_sum(out=lam[:, i:i + 1], in_=t1[:, :128], axis=AX.X), reads=["t1"], writes=["lam"])
            c.op("act", lambda e: e.activation(out=lam[:, 0:2], in_=lam[:, 0:2], func=AF.Exp), reads=["lam"], writes=["lam"])
            c.op("dve", lambda e: e.tensor_tensor(out=lam[:, 2:3], in0=lam[:, 0:1], in1=lam[:, 1:2], op=ALU.subtract), reads=["lam"], writes=["lam"])
            c.op("dve", lambda e: e.tensor_tensor(out=lam[:, 2:3], in0=lam[:, 2:3], in1=vt[:, 2:3], op=ALU.add), reads=["lam", "vec"], writes=["lam"])
            c.op("dve", lambda e: e.tensor_scalar(out=lam[:, 3:4], in0=lam[:, 2:3], scalar1=-1.0, scalar2=None, op0=ALU.mult), reads=["lam"], writes=["lam"])
            c.op("dve", lambda e: e.tensor_scalar(out=lam[:, 4:6], in0=vt[:, 4:6], scalar1=vt[:, 3:4], scalar2=None, op0=ALU.mult), reads=["lam", "vec"], writes=["lam"])

        def prep(src, srctok, W, cs, cstok, gain_col, dst, dsttok):
            cur, curtok = src, srctok
            if gain_col is not None:
                c.op("act", lambda e: e.activation(out=sqb[:, :W], in_=src[:, :W], func=AF.Square), reads=[srctok], writes=["sqb"])
                yield
                c.op("pe", lambda e: e.matmul(ps_ss[:, :W], lhsT=ones[:, :], rhs=sqb[:, :W], start=True, stop=True), reads=["sqb", "ones"], writes=["pSS"])
                yield
                c.op("dve", lambda e: e.tensor_scalar(out=rstd[:, :W], in0=ps_ss[:, :W], scalar1=1.0 / 128, scalar2=EPS, op0=ALU.mult, op1=ALU.add),
                     reads=["pSS"], writes=["rstd"])
                yield
                c.op("act", lambda e: e.activation(out=rstd[:, :W], in_=rstd[:, :W], func=AF.Ln), reads=["rstd"], writes=["rstd"])
                yield
                c.op("act", lambda e: e.activation(out=rstd[:, :W], in_=rstd[:, :W], func=AF.Exp, scale=-0.5), reads=["rstd"], writes=["rstd"])
                yield
                c.op("dve", lambda e: e.scalar_tensor_tensor(out=xn[:, :W], in0=src[:, :W], scalar=vt[:, gain_col:gain_col + 1], in1=rstd[:, :W],
                                                             op0=ALU.mult, op1=ALU.mult), reads=[srctok, "rstd", "vec"], writes=["xn"])
                yield
                cur, curtok = xn, "xn"
                if cs is None:
                    c.op("act", lambda e: e.activation(out=dst[:, :W], in_=xn[:, :W], func=AF.Copy), reads=["xn"], writes=[dsttok])
                    yield
                    return
                c.op("act", lambda e: e.activation(out=xnb[:, :W], in_=xn[:, :W], func=AF.Copy), reads=["xn"], writes=["xnb"])
                yield
                curb, curbtok = xnb, "xnb"
            else:
                if cs is None:
                    c.op("act", lambda e: e.activation(out=dst[:, :W], in_=src[:, :W], func=AF.Copy), reads=[srctok], writes=[dsttok])
                    yield
                    return
                curb, curbtok = src, srctok
            c.op("dve", lambda e: e.tensor_tensor(out=t1[:, :W], in0=cur[:, :W], in1=cs[:, 0, :W], op=ALU.mult), reads=[curtok, cstok], writes=["t1"])
            yield
            c.op("pe", lambda e: e.matmul(ps_rot[:, :W], lhsT=rtt[:, :], rhs=curb[:, :W], start=True, stop=True), reads=["rtt", curbtok], writes=["pSS"])
            yield
            c.op("dve", lambda e: e.tensor_tensor(out=t2[:, :W], in0=ps_rot[:, :W], in1=cs[:, 1, :W], op=ALU.mult), reads=["pSS", cstok], writes=["t2"])
            yield
            c.op("dve", lambda e: e.tensor_tensor(out=dst[:, :W], in0=t1[:, :W], in1=t2[:, :W], op=ALU.add), reads=["t1", "t2"], writes=[dsttok])
            yield

        def run_all(gen):
            for _ in gen:
                pass

        kgain = 1 if kind == "C" else None
        qgain = 0 if kind == "C" else None
        PE_SUM = False
        ktiles = [(0, NCT, None)] + [(NCT + s0, w, s0) for s0, w in tiles_of(NKL, 512)]
        for sk in range(SK):
            for (c0, w, r0) in ktiles:
                rw, rwtok = raw.next()
                c.dma("sp", rwtok, rw[:, :w], kT[sk, :, c0:c0 + w], writes=[rwtok])
                cs, cstok = None, None
                if r0 is not None:
                    cs, cstok = cst.next()
                    c.dma("sp", cstok, cs[:, :, :w], ck[:, :, r0:r0 + w].rearrange("a p t -> p a t"), writes=[cstok])
                run_all(prep(rw, rwtok, w, cs, cstok, kgain, Kr[:, sk, c0:c0 + w], "Kr%d_%d" % (sk, c0)))
        ktoks = [["Kr%d_%d" % (sk, c0) for (c0, w, r0) in ktiles] for sk in range(SK)]
        qtiles = [(0, NCT, None, NCKT)] + [(NCT + s0, w, s0, NKT) for s0, w in tiles_of(NQL, 512)]
        units = [(qi, s) for qi in range(len(qtiles)) for s in range(S)]
        cs_of = {}

        def prep_unit(u):
            qi, s = units[u]
            c0, w, r0, nkt = qtiles[qi]
            if s == 0:
                cs, cstok = None, None
                if r0 is not None:
                    cs, cstok = cst.next()
                    c.dma("sp", cstok, cs[:, :, :w], cq[:, :, r0:r0 + w].rearrange("a p t -> p a t"), writes=[cstok])
                cs_of[qi] = (cs, cstok)
            cs, cstok = cs_of[qi]
            rw, rwtok = raw.next()
            c.dma("sp", rwtok, rw[:, :w], qT[s, :, c0:c0 + w], writes=[rwtok])
            q, qtok = qr.next()
            qready[u] = (q, qtok)
            yield
            for _ in prep(rw, rwtok, w, cs, cstok, qgain, q, qtok):
                yield

        qready = {}
        run_all(prep_unit(0))
        for u, (qi, s) in enumerate(units):
            c0, w, r0, nkt = qtiles[qi]
            q, qtok = qready.pop(u)
            pgen = prep_unit(u + 1) if u + 1 < len(units) else iter(())
            sk = s if SK == 2 else 0
            pend = {}
            npair = nkt // 2
            npe, ndve = [0], [0]

            def score(p_):
                ps, pstok = ps_s.next()
                for j in range(2):
                    kt = 2 * p_ + j
                    c.op("pe", lambda e: e.matmul(ps[:, j, :w], lhsT=Kr[:, sk, kt * 128:(kt + 1) * 128], rhs=q[:, :w], start=True, stop=True),
                         reads=ktoks[sk] + [qtok], writes=[pstok])
                pend[p_] = (ps, pstok)

            score(0)
            for p_ in range(npair):
                ps, pstok = pend.pop(p_)
                e_, etok = E.next()
                c.op("act", lambda e: e.activation(out=e_[:, :, :w], in_=ps[:, :, :w], func=AF.Exp, scale=scale), reads=[pstok], writes=[etok])
                if p_ + 1 < npair:
                    score(p_ + 1)
                if p_ >= 4 and p_ % 3 == 0:
                    next(pgen, None)
                for j in range(2):
                    kt = 2 * p_ + j
                    for h in range(NH):
                        c.op("pe", lambda e: e.matmul(ps_o[h][:, :w], lhsT=Vt[:, kt, h * 128:(h + 1) * 128], rhs=e_[:, j, :w], start=(kt == 0), stop=(kt == nkt - 1)),
                             reads=["Vt", etok], writes=["pO%d" % h])
                if PE_SUM and p_ % 3 == 2:
                    for j in range(2):
                        c.op("pe", lambda e: e.matmul(ps_sum[:, :w], lhsT=ones[:, :], rhs=e_[:, j, :w], start=(npe[0] == 0), stop=False),
                             reads=["ones", etok], writes=["pSum"])
                        npe[0] += 1
                else:
                    ac, actok = (acc[0], "acc0") if ndve[0] % 2 == 0 else (acc[1], "acc1")
                    if ndve[0] < 2:
                        c.op("dve", lambda e: e.tensor_copy(out=ac[:, :, :w], in_=e_[:, :, :w]), reads=[etok], writes=[actok])
                    else:
                        c.op("dve", lambda e: e.tensor_tensor(out=ac[:, :, :w], in0=ac[:, :, :w], in1=e_[:, :, :w], op=ALU.add), reads=[etok, actok], writes=[actok])
                    ndve[0] += 1
            run_all(pgen)
            if ndve[0] > 1:
                c.op("dve", lambda e: e.tensor_tensor(out=acc[0][:, :, :w], in0=acc[0][:, :, :w], in1=acc[1][:, :, :w], op=ALU.add), reads=["acc0", "acc1"], writes=["acc0"])
            c.op("dve", lambda e: e.tensor_tensor(out=acc[0][:, 0, :w], in0=acc[0][:, 0, :w], in1=acc[0][:, 1, :w], op=ALU.add), reads=["acc0"], writes=["acc0"])
            c.op("pe", lambda e: e.matmul(ps_sum[:, :w], lhsT=ones32[:, :], rhs=acc[0][:, 0, :w], start=(npe[0] == 0), stop=True), reads=["ones32", "acc0"], writes=["pSum"])
            c.op("act", lambda e: e.activation(out=rec[:, :w], in_=ps_sum[:, :w], func=AF.Ln), reads=["pSum"], writes=["rec"])
            c.op("act", lambda e: e.activation(out=rec[:, :w], in_=rec[:, :w], func=AF.Exp, scale=-1.0), reads=["rec"], writes=["rec"])
            for h in range(NH):
                c.op("dve", lambda e: e.tensor_tensor(out=on[s][h][:, :w], in0=ps_o[h][:, :w], in1=rec[:, :w], op=ALU.mult),
                     reads=["pO%d" % h, "rec"], writes=["on%d%d" % (s, h)])
            if kind == "C":
                o_, otok = ob.next()
                c.op("act", lambda e: e.activation(out=o_[:, :w], in_=on[s][0][:, :w], func=AF.Copy), reads=["on%d0" % s], writes=[otok])
                c.dma("sp", otok, oT[s * 128:(s + 1) * 128, c0:c0 + w], o_[:, :w], reads=[otok], writes=["oT"])
            if kind == "B" and s == S - 1:
                for h in range(NH):
                    c.op("dve", lambda e: e.scalar_tensor_tensor(out=on[0][h][:, :w], in0=on[1][h][:, :w], scalar=lam[:, 3:4], in1=on[0][h][:, :w],
                                                                 op0=ALU.mult, op1=ALU.add), reads=["on1%d" % h, "on0%d" % h, "lam"], writes=["on0%d" % h])
                    c.op("act", lambda e: e.activation(out=sqb2[:, :w], in_=on[0][h][:, :w], func=AF.Square), reads=["on0%d" % h], writes=["sqb2"])
                    c.op("pe", lambda e: e.matmul(ps_ss2[:, :w], lhsT=ones[:, :], rhs=sqb2[:, :w], start=(h == 0), stop=(h == NH - 1)),
                         reads=["sqb2", "ones"], writes=["pSum"])
                emit_rstd(c, rstd2, ps_ss2, "pSum", w, n=256, tok="rstd2")
                for h in range(NH):
                    o_, otok = ob.next()
                    c.op("dve", lambda e: e.scalar_tensor_tensor(out=o_[:, :w], in0=on[0][h][:, :w], scalar=lam[:, 4 + h:5 + h], in1=rstd2[:, :w],
                                                                 op0=ALU.mult, op1=ALU.mult), reads=["on0%d" % h, "rstd2", "lam"], writes=[otok])
                    c.dma("sp", otok, oT[h * 128:(h + 1) * 128, c0:c0 + w], o_[:, :w], reads=[otok], writes=["oT"])
        c.wait_all("sp", ["oT"])
    return nc


def rope_tables(n):
    freqs = (10000.0 ** (-np.arange(0, 64, 2, dtype=np.float32) / 64)).astype(np.float32)
    t = np.arange(n)
    row = (t // 64).astype(np.float32)
    col = (t % 64).astype(np.float32)
    ang_r = row[:, None] * freqs
    ang_c = col[:, None] * freqs
    ang = np.concatenate([ang_r, ang_r, ang_c, ang_c], -1).astype(np.float32)
    cos = np.cos(ang).astype(np.float32)
    sin = np.sin(ang).astype(np.float32)
    sgn = np.ones(128, np.float32)
    sgn[0:32] = -1
    sgn[64:96] = -1
    return np.ascontiguousarray(np.stack([cos.T, (sin * sgn).T], 0))


def rope_perm():
    m = np.arange(128)
    partner = np.where((m // 32) % 2 == 0, m + 32, m - 32)
    rt = np.zeros((128, 128), np.float32)
    rt[partner, m] = 1.0
    return rt.astype(NPBF)


GDN_FP32R = False
GDN_INV_BF16 = False


def build_gdn(NT, NCT=CTX):
    NCH = NT // 128
    CCH = NCT // 128
    nc = new_nc()
    qkvT = nc.dram_tensor("qkvT", [3, 128, NT], BF16, kind="ExternalInput").ap()
    ztm = nc.dram_tensor("ztm", [NT, 128], BF16, kind="ExternalInput").ap()
    gt = nc.dram_tensor("gt", [128, NCH, 4], BF16, kind="ExternalInput").ap()
    vec = nc.dram_tensor("vec", [128, 16], F32, kind="ExternalInput").ap()
    gbc = nc.dram_tensor("gbc", [128, 128], F32, kind="ExternalInput").ap()
    cst = nc.dram_tensor("cst", [6, 128, 128], F32, kind="ExternalInput").ap()
    otm = nc.dram_tensor("otm", [NT, 128], BF16, kind="ExternalOutput").ap()
    with ExitStack() as st:
        c = Ctx(nc, st)
        vt = c.sb("vec", [128, 16], F32)
        gb = c.sb("gbc", [128, 128], F32)
        C32 = c.sb("cst", [128, 6, 128], F32)
        Ib = c.sb("Ib", [128, 128], BF16)
        onesb = c.sb("onesb", [128, 128], BF16)
        qT = c.sb("qT", [128, NT], BF16)
        kT = c.sb("kT", [128, NT], BF16)
        ktm = c.sb("ktm", [128, NCH, 128], BF16)
        vtm = c.sb("vtm", [128, NCH, 128], BF16)
        of = c.sb("of", [128, NCH, 128], BF16)
        G = [c.sb("G%d" % d, [128, NCH], F32) for d in range(2)]
        BT = [c.sb("BT%d" % d, [128, NCH], F32) for d in range(2)]
        NB = [c.sb("NB%d" % d, [128, NCH], F32) for d in range(2)]
        Bc = [c.sb("Bc%d" % d, [128, NCH], F32) for d in range(2)]
        EB = [c.sb("EB%d" % d, [128, NCH], F32) for d in range(2)]
        BEB = [c.sb("BEB%d" % d, [128, NCH], F32) for d in range(2)]
        EKD = [c.sb("EKD%d" % d, [128, NCH], F32) for d in range(2)]
        CD = [c.sb("CD%d" % d, [128, NCH], F32) for d in range(2)]
        tri = c.sb("tri", [128, 2, 128], F32)
        zt = Rot(c, "zt", 2, [128, 128], BF16)
        ot = Rot(c, "ot", 2, [128, 128], BF16)
        banks = [c.ps("bank%d" % i, [128, 512]) for i in range(8)]

        def carve(b, i, n=1):
            return banks[b][:, 128 * i:128 * (i + n)]

        class RotAP:
            def __init__(self, aps, names):
                self.bufs, self.names, self.i = aps, names, 0

            def next(self):
                j = self.i % len(self.bufs)
                self.i += 1
                return self.bufs[j], self.names[j]

        pbig = banks[0]
        pG = banks[1]
        pT = RotAP([carve(1, 0), carve(1, 1)], ["bank1", "bank1"])

        class Chain:
            def __init__(self, d):
                n = "c%d" % d
                self.d = d
                self.oss = c.sb("oss" + n, [128, 4], F32)
                IDT = BF16 if GDN_INV_BF16 else (mybir.dt.float32r if GDN_FP32R else F32)
                self.Rm = c.sb("Rm" + n, [128, 128], IDT)
                for nm in ("diagb", "nd", "dm", "dmS", "dmI", "S32", "o2s", "osum", "osq", "sz"):
                    setattr(self, nm, c.sb(nm + n, [128, 128], F32))
                for nm in ("Rb", "qk", "kb", "vb", "Sb", "u_b"):
                    setattr(self, nm, c.sb(nm + n, [128, 128], BF16))
                self.Pm = Rot(c, "Pm" + n, 2, [128, 128], IDT)
                self.Qm = Rot(c, "Qm" + n, 2, [128, 128], IDT)
                self.qkT = Rot(c, "qkT" + n, 2, [128, 128], BF16)
                self.kd = Rot(c, "kd" + n, 2, [128, 128], BF16)
                self.ub = Rot(c, "ub" + n, 2, [128, 128], F32)
                self.wT = Rot(c, "wT" + n, 2, [128, 128], BF16)
                b0 = 4 * d
                self.bA, self.bB, self.bC, self.bD = ["bank%d" % (b0 + k) for k in range(4)]
                self.pA, self.pKK, self.pQK, self.pU = [carve(b0, k) for k in range(4)]
                self.pT = RotAP([carve(b0 + 1, 0), carve(b0 + 1, 1)], [self.bB, self.bB])
                self.pW = carve(b0 + 1, 2)
                self.pI = RotAP([carve(b0 + 2, k) for k in range(3)], [self.bC] * 3)
                self.pwS, self.pO1, self.pO2, self.pdS = [carve(b0 + 3, k) for k in range(4)]

            def t(self, nm):
                return "%sc%d" % (nm, self.d)

        c.push_scope()
        gtr = c.sb("gtr", [128, NCH, 4], BF16)
        gx = c.sb("gx", [128, NCH], F32)
        rb = c.sb("rb", [128, 3, 514], BF16)
        cv = c.sb("cv", [128, 514], F32)
        sv = c.sb("sv", [128, 514], F32)
        sqb = c.sb("sqb", [128, 512], BF16)
        rstd = c.sb("rstd", [128, 512], F32)
        vTb = c.sb("vTb", [128, 512], BF16)

        c.dma("sp", "vec", vt[:], vec, writes=["vec"])
        c.dma("sp", "gbc", gb[:], gbc, writes=["gbc"])
        c.dma("sp", "cst", C32[:], cst.rearrange("a p f -> p a f"), writes=["cst"])
        c.dma("sp", "gtr", gtr[:], gt, writes=["gtr"])
        I32 = C32[:, 0, :]
        ones32 = C32[:, 1, :]
        MS = [C32[:, 2, :], C32[:, 4, :]]
        MI = [C32[:, 3, :], C32[:, 5, :]]
        c.op("act", lambda e: e.activation(out=Ib[:, :], in_=C32[:, 0, :], func=AF.Copy), reads=["cst"], writes=["Ib"])
        c.op("act", lambda e: e.activation(out=onesb[:, :], in_=C32[:, 1, :], func=AF.Copy), reads=["cst"], writes=["onesb"])
        c.op("dve", lambda e: e.tensor_copy(out=tri[:, 0, :], in_=C32[:, 5, :]), reads=["cst"], writes=["tri"])
        c.op("dve", lambda e: e.tensor_copy(out=tri[:, 1, :], in_=C32[:, 3, :]), reads=["cst"], writes=["tri"])
        c.op("act", lambda e: e.activation(out=vt[:, 13:15], in_=vt[:, 0:2], func=AF.Exp), reads=["vec"], writes=["vec"])
        c.op("dve", lambda e: e.tensor_scalar(out=vt[:, 13:15], in0=vt[:, 13:15], scalar1=-1.0, scalar2=None, op0=ALU.mult), reads=["vec"], writes=["vec"])
        for d in range(2):
            c.op("act", lambda e: e.activation(out=gx[:, :], in_=gtr[:, :, d], func=AF.Exp, bias=vt[:, 2 + d:3 + d], scale=1.0), reads=["gtr", "vec"], writes=["gx"])
            c.op("act", lambda e: e.activation(out=gx[:, :], in_=gx[:, :], func=AF.Ln, bias=1.0, scale=1.0), reads=["gx"], writes=["gx"])
            c.op("dve", lambda e: e.tensor_scalar(out=G[d][:, :], in0=gx[:, :], scalar1=vt[:, 13 + d:14 + d], scalar2=None, op0=ALU.mult), reads=["gx", "vec"], writes=["G%d" % d])
            c.op("act", lambda e: e.activation(out=BT[d][:, :], in_=gtr[:, :, 2 + d], func=AF.Sigmoid), reads=["gtr"], writes=["BT%d" % d])
            c.op("dve", lambda e: e.tensor_scalar(out=NB[d][:, :], in0=BT[d][:, :], scalar1=-1.0, scalar2=None, op0=ALU.mult), reads=["BT%d" % d], writes=["NB%d" % d])
            c.op("pe", lambda e: e.matmul(pG[:, 0:NCH], lhsT=tri[:, d, :], rhs=G[d][:, :], start=True, stop=True), reads=["tri", "G%d" % d], writes=["bank1"])
            c.op("pe", lambda e: e.matmul(pG[:, 256:256 + NCH], lhsT=C32[:, 1, :], rhs=G[d][:, :], start=True, stop=True), reads=["cst", "G%d" % d], writes=["bank1"])
            c.op("dve", lambda e: e.tensor_copy(out=Bc[d][:, :], in_=pG[:, 0:NCH]), reads=["bank1"], writes=["Bc%d" % d])
            c.op("act", lambda e: e.activation(out=EB[d][:, :], in_=pG[:, 0:NCH], func=AF.Exp), reads=["bank1"], writes=["EB%d" % d])
            c.op("act", lambda e: e.activation(out=CD[d][:, :], in_=pG[:, 256:256 + NCH], func=AF.Exp), reads=["bank1"], writes=["CD%d" % d])
            c.op("dve", lambda e: e.tensor_tensor(out=EKD[d][:, :], in0=pG[:, 256:256 + NCH], in1=Bc[d][:, :], op=ALU.subtract), reads=["bank1", "Bc%d" % d], writes=["EKD%d" % d])
            c.op("act", lambda e: e.activation(out=EKD[d][:, :], in_=EKD[d][:, :], func=AF.Exp), reads=["EKD%d" % d], writes=["EKD%d" % d])
            c.op("dve", lambda e: e.tensor_tensor(out=BEB[d][:, :], in0=BT[d][:, :], in1=EB[d][:, :], op=ALU.mult), reads=["BT%d" % d, "EB%d" % d], writes=["BEB%d" % d])
        gtoks = ["Bc0", "Bc1", "EB0", "EB1", "CD0", "CD1", "EKD0", "EKD1", "BEB0", "BEB1", "BT0", "BT1", "NB0", "NB1"]

        segs = [(0, NCT)] + [(NCT + s0, w) for s0, w in tiles_of(NT - NCT, 512)]
        seq_lo = {0: 0}
        for (c0, w) in segs:
            lo_edge = (c0 == 0) or (c0 == NCT)
            hi_edge = (c0 + w == NCT) or (c0 + w == NT)
            a0 = c0 if lo_edge else c0 - 1
            a1 = c0 + w if hi_edge else c0 + w + 1
            if lo_edge:
                c.op("dve", lambda e: e.memset(rb[:, :, 0:1], 0.0), writes=["rb"])
            if hi_edge:
                c.op("dve", lambda e: e.memset(rb[:, :, w + 1:w + 2], 0.0), writes=["rb"])
            o0 = 1 if lo_edge else 0
            c.dma("sp", "rb", rb[:, :, o0:o0 + (a1 - a0)], qkvT[:, :, a0:a1].rearrange("a p t -> p a t"), writes=["rb"])
            for i in range(3):
                t0 = 4 + 3 * i
                c.op("dve", lambda e: e.tensor_scalar(out=cv[:, :w], in0=rb[:, i, 0:w], scalar1=vt[:, t0:t0 + 1], scalar2=None, op0=ALU.mult), reads=["rb", "vec"], writes=["cv"])
                c.op("dve", lambda e: e.scalar_tensor_tensor(out=cv[:, :w], in0=rb[:, i, 1:w + 1], scalar=vt[:, t0 + 1:t0 + 2], in1=cv[:, :w], op0=ALU.mult, op1=ALU.add),
                     reads=["rb", "vec", "cv"], writes=["cv"])
                c.op("dve", lambda e: e.scalar_tensor_tensor(out=cv[:, :w], in0=rb[:, i, 2:w + 2], scalar=vt[:, t0 + 2:t0 + 3], in1=cv[:, :w], op0=ALU.mult, op1=ALU.add),
                     reads=["rb", "vec", "cv"], writes=["cv"])
                if i == 2:
                    c.op("act", lambda e: e.activation(out=vTb[:, :w], in_=cv[:, :w], func=AF.Silu), reads=["cv"], writes=["vTb"])
                    for s in range(w // 128):
                        p_, ptok = pT.next()
                        c.op("pe", lambda e: e.matmul(p_[:, :], lhsT=vTb[:, s * 128:(s + 1) * 128], rhs=Ib[:, :], start=True, stop=True), reads=["vTb", "Ib"], writes=[ptok])
                        ch = (c0 + s * 128) // 128
                        c.op("act", lambda e: e.activation(out=vtm[:, ch, :], in_=p_[:, :], func=AF.Copy), reads=[ptok], writes=["vtm%d" % ch])
                    continue
                c.op("act", lambda e: e.activation(out=sv[:, :w], in_=cv[:, :w], func=AF.Silu), reads=["cv"], writes=["sv"])
                c.op("act", lambda e: e.activation(out=sqb[:, :w], in_=sv[:, :w], func=AF.Square), reads=["sv"], writes=["sqb"])
                c.op("pe", lambda e: e.matmul(pbig[:, :w], lhsT=onesb[:, :], rhs=sqb[:, :w], start=True, stop=True), reads=["sqb", "onesb"], writes=["bank0"])
                emit_rstd(c, rstd, pbig, "bank0", w, n=1)
                dst = qT if i == 0 else kT
                dtok = ("qT%d" if i == 0 else "kT%d") % c0
                sc = (128 ** -0.5) if i == 0 else 1.0
                c.op("dve", lambda e: e.scalar_tensor_tensor(out=dst[:, c0:c0 + w], in0=sv[:, :w], scalar=sc, in1=rstd[:, :w], op0=ALU.mult, op1=ALU.mult),
                     reads=["sv", "rstd"], writes=[dtok])
                if i == 1:
                    for s in range(w // 128):
                        p_, ptok = pT.next()
                        c.op("pe", lambda e: e.matmul(p_[:, :], lhsT=kT[:, c0 + s * 128:c0 + (s + 1) * 128], rhs=Ib[:, :], start=True, stop=True), reads=[dtok, "Ib"], writes=[ptok])
                        ch = (c0 + s * 128) // 128
                        c.op("dve", lambda e: e.tensor_copy(out=ktm[:, ch, :], in_=p_[:, :]), reads=[ptok], writes=["ktm%d" % ch])

        c.pop_scope()
        chains = [Chain(0), Chain(1)]

        def INV(ap):
            return ap

        def AS32(ap):
            return ap if GDN_INV_BF16 else (ap.bitcast(F32) if GDN_FP32R else ap)

        def seg_of(ch):
            t = ch * 128
            if t < NCT:
                return 0
            return NCT + ((t - NCT) // 512) * 512

        def pre(ch, X):
            d = X.d
            col = slice(ch, ch + 1)
            qtok = "qT%d" % seg_of(ch)
            ktok = "kT%d" % seg_of(ch)
            cs = slice(ch * 128, (ch + 1) * 128)
            c.op("dve", lambda e: e.tensor_scalar(out=X.diagb[:, :], in0=C32[:, 0, :], scalar1=Bc[d][:, col], scalar2=None, op0=ALU.mult), reads=["cst"] + gtoks, writes=[X.t("diagb")])
            c.op("pe", lambda e: e.matmul(X.pKK, lhsT=kT[:, cs], rhs=kT[:, cs], start=True, stop=True), reads=[ktok], writes=[X.bA])
            c.op("pe", lambda e: e.matmul(X.pQK, lhsT=qT[:, cs], rhs=kT[:, cs], start=True, stop=True), reads=[qtok, ktok], writes=[X.bA])
            yield
            c.op("pe", lambda e: e.matmul(X.pA, lhsT=C32[:, 1, :], rhs=X.diagb[:, :], start=True, stop=True), reads=["cst", X.t("diagb")], writes=[X.bA])
            yield
            c.op("dve", lambda e: e.tensor_scalar(out=X.nd[:, :], in0=X.pA, scalar1=Bc[d][:, col], scalar2=0.0, op0=ALU.subtract, op1=ALU.max), reads=[X.bA] + gtoks, writes=[X.t("nd")])
            yield
            c.op("act", lambda e: e.activation(out=X.dm[:, :], in_=X.nd[:, :], func=AF.Exp, scale=-1.0), reads=[X.t("nd")], writes=[X.t("dm")])
            yield
            c.op("dve", lambda e: e.tensor_tensor(out=X.dmS[:, :], in0=X.dm[:, :], in1=MS[d], op=ALU.mult), reads=[X.t("dm"), "cst"], writes=[X.t("dmS")])
            c.op("dve", lambda e: e.tensor_tensor(out=X.dmI[:, :], in0=X.dm[:, :], in1=MI[d], op=ALU.mult), reads=[X.t("dm"), "cst"], writes=[X.t("dmI")])
            yield
            Q, Qtok = X.Qm.next()
            c.op("dve", lambda e: e.scalar_tensor_tensor(out=Q[:, :], in0=X.pKK, scalar=NB[d][:, col], in1=X.dmS[:, :], op0=ALU.mult, op1=ALU.mult),
                 reads=[X.bA, X.t("dmS")] + gtoks, writes=[Qtok])
            c.op("dve", lambda e: e.tensor_tensor(out=X.qk[:, :], in0=X.pQK, in1=X.dmI[:, :], op=ALU.mult), reads=[X.bA, X.t("dmI")], writes=[X.t("qk")])
            yield
            p_, ptok = X.pT.next()
            c.op("pe", lambda e: e.matmul(p_, lhsT=AS32(Q[:, :]), rhs=(Ib[:, :] if GDN_INV_BF16 else C32[:, 0, :]), start=True, stop=True), reads=[Qtok, "cst", "Ib"], writes=[ptok])
            p2, p2tok = X.pT.next()
            c.op("pe", lambda e: e.matmul(p2, lhsT=X.qk[:, :], rhs=Ib[:, :], start=True, stop=True), reads=[X.t("qk"), "Ib"], writes=[p2tok])
            yield
            P, Ptok = X.Pm.next()
            c.op("act", lambda e: e.activation(out=P[:, :], in_=p_, func=AF.Copy), reads=[ptok], writes=[Ptok])
            c.op("dve", lambda e: e.tensor_tensor(out=X.Rm[:, :], in0=p_, in1=C32[:, 0, :], op=ALU.add), reads=[ptok, "cst"], writes=[X.t("Rm")])
            qkT_, qkTtok = X.qkT.next()
            c.op("act", lambda e: e.activation(out=qkT_[:, :], in_=p2, func=AF.Copy), reads=[p2tok], writes=[qkTtok])
            yield
            for step in range(6):
                last = step == 5
                pq, pqtok = X.pI.next()
                c.op("pe", lambda e: e.matmul(pq, lhsT=INV(P[:, :]), rhs=INV(Q[:, :]), start=True, stop=True), reads=[Ptok, Qtok], writes=[pqtok])
                if not last:
                    pp, pptok = X.pI.next()
                    c.op("pe", lambda e: e.matmul(pp, lhsT=INV(Q[:, :]), rhs=INV(P[:, :]), start=True, stop=True), reads=[Ptok, Qtok], writes=[pptok])
                yield
                Q2, Q2tok = X.Qm.next()
                c.op("dve", lambda e: e.tensor_copy(out=Q2[:, :], in_=pq), reads=[pqtok], writes=[Q2tok])
                if not last:
                    P2, P2tok = X.Pm.next()
                    c.op("act", lambda e: e.activation(out=P2[:, :], in_=pp, func=AF.Copy), reads=[pptok], writes=[P2tok])
                    P, Ptok = P2, P2tok
                Q, Qtok = Q2, Q2tok
                yield
                pr, prtok = X.pI.next()
                c.op("pe", lambda e: e.matmul(pr, lhsT=INV(Q[:, :]), rhs=INV(X.Rm[:, :]), start=True, stop=True), reads=[Qtok, X.t("Rm")], writes=[prtok])
                yield
                c.op("dve", lambda e: e.tensor_tensor(out=X.Rm[:, :], in0=pr, in1=AS32(X.Rm[:, :]), op=ALU.add), reads=[prtok, X.t("Rm")], writes=[X.t("Rm")])
                yield
            c.op("act", lambda e: e.activation(out=X.Rb[:, :], in_=AS32(X.Rm[:, :]), func=AF.Copy), reads=[X.t("Rm")], writes=[X.t("Rb")])
            kd_, kdtok = X.kd.next()
            c.op("pool", lambda e: e.tensor_scalar(out=X.kb[:, :], in0=ktm[:, ch, :], scalar1=BEB[d][:, col], scalar2=None, op0=ALU.mult), reads=["ktm%d" % ch] + gtoks, writes=[X.t("kb")])
            c.op("pool", lambda e: e.tensor_scalar(out=kd_[:, :], in0=ktm[:, ch, :], scalar1=EKD[d][:, col], scalar2=None, op0=ALU.mult), reads=["ktm%d" % ch] + gtoks, writes=[kdtok])
            c.op("pool", lambda e: e.tensor_scalar(out=X.vb[:, :], in0=vtm[:, ch, :], scalar1=BT[d][:, col], scalar2=None, op0=ALU.mult), reads=["vtm%d" % ch] + gtoks, writes=[X.t("vb")])
            yield
            c.op("pe", lambda e: e.matmul(X.pU, lhsT=X.Rb[:, :], rhs=X.vb[:, :], start=True, stop=True), reads=[X.t("Rb"), X.t("vb")], writes=[X.bA])
            c.op("pe", lambda e: e.matmul(X.pW, lhsT=X.kb[:, :], rhs=X.Rb[:, :], start=True, stop=True), reads=[X.t("Rb"), X.t("kb")], writes=[X.bB])
            yield
            ub_, ubtok = X.ub.next()
            wT_, wTtok = X.wT.next()
            c.op("act", lambda e: e.activation(out=ub_[:, :], in_=X.pU, func=AF.Copy), reads=[X.bA], writes=[ubtok])
            c.op("dve", lambda e: e.tensor_copy(out=wT_[:, :], in_=X.pW), reads=[X.bB], writes=[wTtok])
            X.slot_out = (qkT_, qkTtok, kd_, kdtok, ub_, ubtok, wT_, wTtok)
            yield

        def rec(ch, X, slot, fin):
            d = X.d
            qkT_, qkTtok, kd_, kdtok, ub_, ubtok, wT_, wTtok = slot
            col = slice(ch, ch + 1)
            cs = slice(ch * 128, (ch + 1) * 128)
            qtok = "qT%d" % seg_of(ch)
            c.op("pe", lambda e: e.matmul(X.pwS, lhsT=wT_[:, :], rhs=X.Sb[:, :], start=True, stop=True), reads=[wTtok, X.t("Sb")], writes=[X.bD])
            c.op("pe", lambda e: e.matmul(X.pO1, lhsT=qT[:, cs], rhs=X.Sb[:, :], start=True, stop=True), reads=[qtok, X.t("Sb")], writes=[X.bD])
            yield
            c.op("dve", lambda e: e.tensor_tensor(out=X.u_b[:, :], in0=ub_[:, :], in1=X.pwS, op=ALU.subtract), reads=[ubtok, X.bD], writes=[X.t("u_b")])
            yield
            c.op("pe", lambda e: e.matmul(X.pdS, lhsT=kd_[:, :], rhs=X.u_b[:, :], start=True, stop=True), reads=[kdtok, X.t("u_b")], writes=[X.bD])
            c.op("pe", lambda e: e.matmul(X.pO2, lhsT=qkT_[:, :], rhs=X.u_b[:, :], start=True, stop=True), reads=[qkTtok, X.t("u_b")], writes=[X.bD])
            yield
            c.op("dve", lambda e: e.scalar_tensor_tensor(out=X.S32[:, :], in0=X.S32[:, :], scalar=CD[d][:, col], in1=X.pdS, op0=ALU.mult, op1=ALU.add),
                 reads=[X.t("S32"), X.bD] + gtoks, writes=[X.t("S32")])
            yield
            c.op("act", lambda e: e.activation(out=X.Sb[:, :], in_=X.S32[:, :], func=AF.Copy), reads=[X.t("S32")], writes=[X.t("Sb")])
            c.op("act", lambda e: e.activation(out=X.o2s[:, :], in_=X.pO2, func=AF.Copy), reads=[X.bD], writes=[X.t("o2s")])
            yield
            if not fin:
                c.op("dve", lambda e: e.scalar_tensor_tensor(out=of[:, ch, :], in0=X.pO1, scalar=EB[d][:, col], in1=X.o2s[:, :], op0=ALU.mult, op1=ALU.add),
                     reads=[X.bD, X.t("o2s")] + gtoks, writes=["of%d" % ch])
                yield
                return
            c.op("dve", lambda e: e.scalar_tensor_tensor(out=X.osum[:, :], in0=X.pO1, scalar=EB[d][:, col], in1=X.o2s[:, :], op0=ALU.mult, op1=ALU.add),
                 reads=[X.bD, X.t("o2s")] + gtoks, writes=[X.t("osum")])
            yield
            c.op("dve", lambda e: e.tensor_tensor(out=X.osum[:, :], in0=X.osum[:, :], in1=of[:, ch, :], op=ALU.add), reads=[X.t("osum"), "of%d" % ch], writes=[X.t("osum")])
            yield
            c.op("act", lambda e: e.activation(out=X.osq[:, :], in_=X.osum[:, :], func=AF.Square, accum_out=X.oss[:, 0:1]), reads=[X.t("osum")], writes=[X.t("osq"), X.t("oss")])
            yield
            emit_rstd(c, X.oss, X.oss, X.t("oss"), 1, n=128, tok=X.t("oss"))
            z_, ztok = zt.next()
            c.dma("sp", ztok, z_[:, :], ztm[ch * 128:(ch + 1) * 128, :], writes=[ztok])
            c.op("act", lambda e: e.activation(out=X.sz[:, :], in_=z_[:, :], func=AF.Silu), reads=[ztok], writes=[X.t("sz")])
            yield
            c.op("dve", lambda e: e.scalar_tensor_tensor(out=X.osum[:, :], in0=X.osum[:, :], scalar=X.oss[:, 0:1], in1=gb[:, :], op0=ALU.mult, op1=ALU.mult),
                 reads=[X.t("osum"), X.t("oss"), "gbc"], writes=[X.t("osum")])
            o_, otok = ot.next()
            c.op("dve", lambda e: e.tensor_tensor(out=o_[:, :], in0=X.osum[:, :], in1=X.sz[:, :], op=ALU.mult), reads=[X.t("osum"), X.t("sz")], writes=[otok])
            c.dma("sp", otok, otm[ch * 128:(ch + 1) * 128, :], o_[:, :], reads=[otok], writes=["otm"])
            yield

        def drive(gens):
            gens = [g_ for g_ in gens if g_ is not None]
            while gens:
                alive = []
                for g_ in gens:
                    try:
                        next(g_)
                        alive.append(g_)
                    except StopIteration:
                        pass
                gens = alive

        orders = [list(range(NCH)), list(range(CCH - 1, -1, -1)) + list(range(NCH - 1, CCH - 1, -1))]
        pos = [{ch: i for i, ch in enumerate(o)} for o in orders]
        for X in chains:
            c.op("dve", lambda e: e.memset(X.S32[:, :], 0.0), writes=[X.t("S32")])
            c.op("dve", lambda e: e.memset(X.Sb[:, :], 0.0), writes=[X.t("Sb")])
        drive([pre(orders[d][0], chains[d]) for d in range(2)])
        slots = [chains[d].slot_out for d in range(2)]
        for i in range(NCH):
            gens = []
            for d in range(2):
                if i + 1 < NCH:
                    gens.append(pre(orders[d][i + 1], chains[d]))
            for d in range(2):
                ch = orders[d][i]
                gens.append(rec(ch, chains[d], slots[d], pos[d][ch] > pos[1 - d][ch]))
            drive(gens)
            if i + 1 < NCH:
                slots = [chains[d].slot_out for d in range(2)]
        c.wait_all("sp", ["otm"])
    return nc


def gdn_consts():
    i = np.arange(128)
    I = np.eye(128, dtype=np.float32)
    ones = np.ones((128, 128), np.float32)
    LS = (i[:, None] > i[None, :]).astype(np.float32)
    LI = (i[:, None] >= i[None, :]).astype(np.float32)
    return np.ascontiguousarray(np.stack([I, ones, LS, LI, LS.T, LI.T], 0))


def tile_w(w, m=128):
    w = np.asarray(w, np.float32)
    K, N = w.shape
    nb = (N + m - 1) // m
    if nb * m != N:
        w = np.concatenate([w, np.zeros((K, nb * m - N), np.float32)], 1)
    return np.ascontiguousarray(w.reshape(K // 128, 128, nb, m).transpose(2, 1, 0, 3))


def fm(v):
    return np.ascontiguousarray(np.asarray(v, np.float32).reshape(KC, 128).T)


def lambda_init(layer):
    return 0.8 - 0.6 * math.exp(-0.3 * layer)


PRE_TILES = [(0, 32, 1)] + [(32 + i * 512, 512, 0) for i in range(4)]
CTXC = CTX // NCORE
POST_T = CTXC + 2 + SEQ // NCORE + 2
POST_SPECIAL = [(0, "vl"), (CTXC + 1, "vr"), (CTXC + 2, "vl"), (POST_T - 1, "vr")]


def post_tiles():
    inner = POST_T - 2
    n = 5
    base, extra = divmod(inner, n)
    tiles, a = [], 1
    for i in range(n):
        wi = base + (1 if i < extra else 0)
        s0, wd = a - 1, wi + 2
        segs = []
        for (g0, g1, stream) in ((0, CTXC + 2, 1), (CTXC + 2, POST_T, 0)):
            c0, c1 = max(g0, s0) - s0, min(g1, s0 + wd) - s0
            if c1 > c0:
                segs.append((c0, c1, stream))
        masks = [(col - s0, fl) for col, fl in POST_SPECIAL if s0 <= col < s0 + wd]
        tiles.append((s0, wd, segs, masks))
        a += wi
    return tiles


POST_TILES = post_tiles()
TL = SEQ // NCORE


def kernel(x, c, ctx, c_ctx, w_mod, b_mod, norm1_g, norm2_g, w_in_even, a_conv_w, a_A_log, a_dt_bias,
           a_norm_g, b_lambda, b_norm_g, w_out_even, w_in_odd, c_q_norm, c_k_norm, w_out_odd,
           ffn_up, ffn_conv_w, ffn_conv_b, ffn_down, final_g):
    f32 = np.float32
    x = np.asarray(x, f32)
    ctx = np.asarray(ctx, f32)
    progs = {}

    def prog(key, fn):
        if key not in progs:
            progs[key] = fn()
        return progs[key]

    c2 = np.ascontiguousarray(np.stack([fm(np.asarray(c, f32)[0]), fm(np.asarray(c_ctx, f32))], -1))
    w_mod = np.asarray(w_mod, f32)
    b_mod = np.asarray(b_mod, f32)
    maps = []
    for j in range(NCORE):
        sl = slice(j * MODC, (j + 1) * MODC)
        bm = np.ascontiguousarray(np.broadcast_to(b_mod[None, :, sl], (2, DEPTH, MODC)))
        maps.append({"c2": c2, "wm": np.stack([tile_w(w_mod[l_][:, sl], 512) for l_ in range(DEPTH)], 0), "bm": bm})
    res = run(prog("mod", build_mod), maps, "mod")
    mod = np.concatenate([r["out"] for r in res], -1)

    def modv(s, l, m):
        return mod[s, l, m * D:(m + 1) * D]

    xlT = np.ascontiguousarray(x[0].T)
    xcT = np.ascontiguousarray(ctx[0].T)
    rope = rope_tables(SEQ)
    rt = rope_perm()
    zcol = np.zeros((D, 1), f32)

    for l in range(DEPTH):
        even = l % 2 == 0
        e = l // 2
        last = l == DEPTH - 1
        w_in = np.asarray(w_in_even[e] if even else w_in_odd[e], f32)
        ncols = w_in.shape[1]
        w_in = tile_w(w_in)
        vec = np.ascontiguousarray(np.stack([fm(norm1_g[l]), fm(modv(0, l, 0)), fm(modv(0, l, 1)), fm(modv(1, l, 0)), fm(modv(1, l, 1))], 1))
        maps = [{"xT": np.ascontiguousarray(np.concatenate([xcT[:, 32 * j:32 * (j + 1)], xlT[:, TL * j:TL * (j + 1)]], 1)), "vec": vec, "w": w_in}
                for j in range(NCORE)]
        res = run(prog(("pre", ncols), lambda: build_pre(ncols, PRE_TILES)), maps, "pre%d" % l)
        pc = np.concatenate([r["pT"][:, :32] for r in res], 1)
        pl = np.concatenate([r["pT"][:, 32:] for r in res], 1)
        p = np.concatenate([pc, pl], 1)
        del res, maps
        mT = np.zeros((D, CTX + SEQ), NPBF)
        if even:
            NT = CTX + SEQ
            cw = np.asarray(a_conv_w[e], f32)
            maps = []
            for j in range(NCORE):
                hs = slice(j * 128, (j + 1) * 128)
                qkvT = np.ascontiguousarray(np.stack([p[j * 128:(j + 1) * 128], p[1024 + j * 128:1024 + (j + 1) * 128], p[2048 + j * 128:2048 + (j + 1) * 128]], 0))
                ztm = np.ascontiguousarray(p[3072 + j * 128:3072 + (j + 1) * 128].T)
                gates = np.stack([p[4096 + gi * 8 + j] for gi in range(4)], 0)
                gt = np.ascontiguousarray(gates.reshape(4, NT // 128, 128).transpose(2, 1, 0))
                vec = np.zeros((128, 16), f32)
                vec[:, 0:2] = np.asarray(a_A_log[e], f32)[:, j]
                vec[:, 2:4] = np.asarray(a_dt_bias[e], f32)[:, j]
                for i, off in enumerate((0, 1024, 2048)):
                    for t in range(3):
                        vec[:, 4 + 3 * i + t] = cw[t, off + j * 128:off + (j + 1) * 128]
                gbc = np.ascontiguousarray(np.broadcast_to(np.asarray(a_norm_g[e], f32), (128, 128)))
                maps.append({"qkvT": qkvT, "ztm": ztm, "gt": gt, "vec": vec, "gbc": gbc, "cst": gdn_consts()})
            res = run(prog("gdn", lambda: build_gdn(NT)), maps, "gdn%d" % l)
            for j in range(NCORE):
                mT[j * 128:(j + 1) * 128, :] = res[j]["otm"].T
            del res, maps
            HQ = SEQ // 2
            li = lambda_init(l)
            lp = np.ascontiguousarray(np.broadcast_to(np.asarray(b_lambda[e], f32), (128, 4, 128)))
            bng = np.asarray(b_norm_g[e], f32)
            maps = []
            for j in range(NCORE):
                hb, half = j // 2, j % 2
                qrows = [slice(A_IN + hb * 256 + m * 128, A_IN + hb * 256 + (m + 1) * 128) for m in range(2)]
                krows = [slice(A_IN + 1024 + hb * 256 + m * 128, A_IN + 1024 + hb * 256 + (m + 1) * 128) for m in range(2)]
                qT = np.ascontiguousarray(np.stack([np.concatenate([p[r, :CTX], p[r, CTX + half * HQ:CTX + (half + 1) * HQ]], 1) for r in qrows], 0))
                kT = np.ascontiguousarray(np.stack([p[r] for r in krows], 0))
                v = np.ascontiguousarray(p[A_IN + 2048 + hb * 256:A_IN + 2048 + (hb + 1) * 256].T)
                vec = np.zeros((128, 8), f32)
                vec[:, 2] = li
                vec[:, 3] = 1.0 - li
                vec[:, 4] = bng[:128]
                vec[:, 5] = bng[128:]
                maps.append({"qT": qT, "kT": kT, "v": v, "cq": np.ascontiguousarray(rope[:, :, half * HQ:(half + 1) * HQ]), "ck": rope, "rt": rt, "vec": vec, "lp": lp})
            res = run(prog("attB", lambda: build_att("B", HQ, SEQ)), maps, "attB%d" % l)
            for j in range(NCORE):
                hb, half = j // 2, j % 2
                rows = slice(1024 + hb * 256, 1024 + (hb + 1) * 256)
                if half == 0:
                    mT[rows, :CTX] = res[j]["oT"][:, :CTX]
                mT[rows, CTX + half * HQ:CTX + (half + 1) * HQ] = res[j]["oT"][:, CTX:]
            del res, maps
        else:
            lp0 = np.zeros((128, 4, 128), f32)
            maps = []
            for j in range(NCORE):
                g = j // 2
                qT = np.ascontiguousarray(np.stack([p[(2 * j + s) * 128:(2 * j + s + 1) * 128] for s in range(2)], 0))
                kT = np.ascontiguousarray(p[2048 + g * 128:2048 + (g + 1) * 128][None])
                v = np.ascontiguousarray(p[2560 + g * 128:2560 + (g + 1) * 128].T)
                vec = np.zeros((128, 8), f32)
                vec[:, 0] = np.asarray(c_q_norm[e], f32)
                vec[:, 1] = np.asarray(c_k_norm[e], f32)
                maps.append({"qT": qT, "kT": kT, "v": v, "cq": rope, "ck": rope, "rt": rt, "vec": vec, "lp": lp0})
            res = run(prog("attC", lambda: build_att("C", SEQ, SEQ)), maps, "attC%d" % l)
            for j in range(NCORE):
                mT[2 * j * 128:(2 * j + 2) * 128, :] = res[j]["oT"]
            del res, maps
        del p
        vec = np.zeros((128, POST_NV), f32)
        V = POST_V
        vec[:, V["n2g"]:V["n2g"] + 16] = fm(norm2_g[l])
        for nm, s, m in (("g1", 0, 2), ("sh2", 0, 3), ("sc2", 0, 4), ("g2", 0, 5), ("cg1", 1, 2), ("csh2", 1, 3), ("csc2", 1, 4), ("cg2", 1, 5)):
            vec[:, V[nm]:V[nm] + 16] = fm(modv(s, l, m))
        vec[:, V["fg"]:V["fg"] + 16] = fm(final_g)
        vec[:, V["cw"]:V["cw"] + 3 * 88] = np.asarray(ffn_conv_w[l], f32).reshape(3, 88, 128).transpose(2, 0, 1).reshape(128, 264)
        vec[:, V["cb"]:V["cb"] + 88] = np.asarray(ffn_conv_b[l], f32).reshape(88, 128).T
        wo = tile_w(w_out_even[e] if even else w_out_odd[e])
        wu = tile_w(ffn_up[l])
        wd = tile_w(ffn_down[l])
        mcT, mlT = mT[:, :CTX], mT[:, CTX:]
        zb = np.zeros((D, 1), NPBF)
        maps = []
        for j in range(NCORE):
            lo, hi = TL * j, TL * (j + 1)
            xl_ = [zcol if j == 0 else xlT[:, lo - 1:lo], xlT[:, lo:hi], zcol if j == NCORE - 1 else xlT[:, hi:hi + 1]]
            ml_ = [zb if j == 0 else mlT[:, lo - 1:lo], mlT[:, lo:hi], zb if j == NCORE - 1 else mlT[:, hi:hi + 1]]
            vj = vec.copy()
            vj[:, V["vl"]] = 0.0 if j == 0 else 1.0
            vj[:, V["vr"]] = 0.0 if j == NCORE - 1 else 1.0
            clo, chi = CTXC * j, CTXC * (j + 1)
            xc_ = [zcol if j == 0 else xcT[:, clo - 1:clo], xcT[:, clo:chi], zcol if j == NCORE - 1 else xcT[:, chi:chi + 1]]
            mc_ = [zb if j == 0 else mcT[:, clo - 1:clo], mcT[:, clo:chi], zb if j == NCORE - 1 else mcT[:, chi:chi + 1]]
            maps.append({"xT": np.ascontiguousarray(np.concatenate(xc_ + xl_, 1)),
                         "mT": np.ascontiguousarray(np.concatenate(mc_ + ml_, 1)), "vec": vj, "wo": wo, "wu": wu, "wd": wd})
        if not last:
            res = run(prog("post", lambda: build_post(POST_TILES, False)), maps, "post%d" % l)
            xcT = np.ascontiguousarray(np.concatenate([r["oT"][:, :CTXC] for r in res], 1))
            xlT = np.ascontiguousarray(np.concatenate([r["oT"][:, CTXC + 2:] for r in res], 1))
        else:
            res = run(prog("postf", lambda: build_post(POST_TILES, True)), maps, "postf%d" % l)
            xlT = np.concatenate([r["oT"][:, CTXC + 2:] for r in res], 1)
        del res, maps, mT
    return np.ascontiguousarray(xlT.T)[None].astype(np.float32)
```

```python
import math
import os
import sys
from contextlib import ExitStack

import ml_dtypes
import numpy as np
import concourse.bass as bass
import concourse.mybir as mybir
from concourse.bass_utils import run_bass_kernel_spmd

F32 = mybir.dt.float32
BF16 = mybir.dt.bfloat16
AF = mybir.ActivationFunctionType
ALU = mybir.AluOpType
AX = mybir.AxisListType
NPBF = ml_dtypes.bfloat16

D = 2048
KC = 16
NCORE = 8
SEQ = 16384
CTX = 256
DEPTH = 4
EPS = 1e-6
DFF = 5632
FC = 44
A_IN = 4128
EVEN_IN = 7200
ODD_IN = 3072


class Ctx:
    def __init__(self, nc, stack):
        self.nc = nc
        self.stack = stack
        self.E = {"pe": nc.tensor, "act": nc.scalar, "dve": nc.vector, "pool": nc.gpsimd, "sp": nc.sync}
        self.sems = {}
        self.cnt = {}
        self.known = {e: {} for e in self.E}
        self.lastw = {}
        self.readers = {}
        self.ninst = 0

    def sb(self, name, shape, dt):
        return self.stack.enter_context(self.nc.sbuf_tensor("sb_" + name, list(shape), dt))

    def ps(self, name, shape, dt=F32):
        return self.stack.enter_context(self.nc.psum_tensor("ps_" + name, list(shape), dt))

    def sem(self, key):
        if key not in self.sems:
            self.sems[key] = self.stack.enter_context(self.nc.semaphore("s_" + key.replace(":", "_")))
            self.cnt[key] = 0
        return self.sems[key]

    def _waits(self, eng, reads, writes):
        need = {}

        def add(ev):
            if ev is not None and need.get(ev[0], 0) < ev[1]:
                need[ev[0]] = ev[1]

        for t in reads:
            add(self.lastw.get(t))
        for t in writes:
            add(self.lastw.get(t))
            for k, v in self.readers.get(t, {}).items():
                add((k, v))
        E = self.E[eng]
        for k, v in need.items():
            if k == "pe" and eng == "pe":
                continue
            if k.startswith("d:"):
                v = self.cnt[k]
            if self.known[eng].get(k, 0) < v:
                E.wait_ge(self.sems[k], v)
                self.known[eng][k] = v
                self.ninst += 1

    def _record(self, ev, reads, writes):
        k, v = ev
        for t in reads:
            d = self.readers.setdefault(t, {})
            if d.get(k, 0) < v:
                d[k] = v
        for t in writes:
            self.lastw[t] = ev
            self.readers[t] = {}

    def op(self, eng, emit, reads=(), writes=()):
        ex = [t for t in reads if t.startswith("bank")]
        if ex:
            writes = list(writes) + ex
        self._waits(eng, reads, writes)
        s = self.sem(eng)
        ins = emit(self.E[eng])
        ins.then_inc(s, 1)
        self.cnt[eng] += 1
        self.ninst += 1
        self._record((eng, self.cnt[eng]), reads, writes)
        return ins

    def dma(self, eng, stream, out, in_, reads=(), writes=()):
        key = "d:" + stream
        s = self.sem(key)
        self._waits(eng, reads, writes)
        ins = self.E[eng].dma_start(out=out, in_=in_)
        ins.then_inc(s, 16)
        self.cnt[key] += 16
        self.ninst += 1
        self._record((key, self.cnt[key]), reads, writes)
        return ins

    def push_scope(self):
        self._outer = self.stack
        self.stack = ExitStack()

    def pop_scope(self):
        self.barrier()
        self.stack.close()
        self.stack = self._outer

    def barrier(self):
        for eng, E in self.E.items():
            for k, s_ in self.sems.items():
                v = self.cnt[k]
                if v > 0 and self.known[eng].get(k, 0) < v and not (k == eng):
                    E.wait_ge(s_, v)
                    self.known[eng][k] = v
                    self.ninst += 1

    def wait_all(self, eng, tokens):
        self._waits(eng, tokens, ())


class Rot:
    def __init__(self, c, name, n, shape, dt, psum=False):
        self.bufs = [(c.ps if psum else c.sb)("%s%d" % (name, i), shape, dt) for i in range(n)]
        self.names = ["%s%d" % (name, i) for i in range(n)]
        self.i = 0

    def next(self):
        j = self.i % len(self.bufs)
        self.i += 1
        return self.bufs[j], self.names[j]


def new_nc():
    return bass.Bass("TRN2", target_bir_lowering=False)


def run(nc, in_maps, tag=""):
    res = run_bass_kernel_spmd(nc, in_maps, core_ids=list(range(NCORE)))
    if os.environ.get("KDEBUG"):
        for j, r in enumerate(res.results):
            for name, arr in r.items():
                a = np.asarray(arr).astype(np.float32)
                if not np.isfinite(a).all():
                    print("KDEBUG non-finite:", tag, "core", j, name, int((~np.isfinite(a)).sum()), "of", a.size, file=sys.stderr)
    return res.results


def emit_rstd(c, rstd, ss, sstok, W, n=D, tok="rstd"):
    c.op("dve", lambda e: e.tensor_scalar(out=rstd[:, :W], in0=ss[:, :W], scalar1=1.0 / n, scalar2=EPS, op0=ALU.mult, op1=ALU.add),
         reads=[sstok], writes=[tok])
    c.op("act", lambda e: e.activation(out=rstd[:, :W], in_=rstd[:, :W], func=AF.Ln), reads=[tok], writes=[tok])
    c.op("act", lambda e: e.activation(out=rstd[:, :W], in_=rstd[:, :W], func=AF.Exp, scale=-0.5), reads=[tok], writes=[tok])


def emit_norm_mod(c, K, xt, xtok, W, segs, h, htok, maskcols=()):
    ss, sstok = K["ps_ss"].next()
    for kc in range(KC):
        sq, sqtok = K["sq"].next()
        c.op("act", lambda e: e.activation(out=sq[:, :W], in_=xt[:, kc, :W], func=AF.Square), reads=[xtok], writes=[sqtok])
        c.op("pe", lambda e: e.matmul(ss[:, :W], lhsT=K["ones"][:, :], rhs=sq[:, :W], start=(kc == 0), stop=(kc == KC - 1)),
             reads=[sqtok, "ones"], writes=[sstok])
    rstd = K["rstd"]
    emit_rstd(c, rstd, ss, sstok, W)
    for kc in range(KC):
        tmp, tmptok = K["tmp"].next()
        for (c0, c1, a_ap, b_ap) in segs:
            c.op("dve", lambda e: e.scalar_tensor_tensor(out=tmp[:, c0:c1], in0=xt[:, kc, c0:c1], scalar=a_ap[:, kc:kc + 1], in1=rstd[:, c0:c1],
                                                         op0=ALU.mult, op1=ALU.mult), reads=[xtok, "rstd", "vec"], writes=[tmptok])
            c.op("act", lambda e: e.activation(out=h[:, kc, c0:c1], in_=tmp[:, c0:c1], func=AF.Identity, bias=b_ap[:, kc:kc + 1], scale=1.0),
                 reads=[tmptok, "vec"], writes=[htok])
    for col, sc in maskcols:
        c.op("dve", lambda e: e.tensor_scalar(out=h[:, :, col:col + 1], in0=h[:, :, col:col + 1], scalar1=sc, scalar2=None, op0=ALU.mult),
             reads=[htok, "vec"], writes=[htok])


def make_consts(c):
    K = {}
    K["ones"] = c.sb("ones", [128, 128], BF16)
    c.op("dve", lambda e: e.memset(K["ones"][:, :], 1.0), writes=["ones"])
    K["sq"] = Rot(c, "sq", 2, [128, 512], BF16)
    K["tmp"] = Rot(c, "tmp", 2, [128, 512], F32)
    K["rstd"] = c.sb("rstd", [128, 512], F32)
    K["ps_ss"] = Rot(c, "ps_ss", 1, [128, 512], F32, psum=True)
    return K


MODC = 1536


def build_mod():
    nc = new_nc()
    c2 = nc.dram_tensor("c2", [128, KC, 2], F32, kind="ExternalInput").ap()
    wm = nc.dram_tensor("wm", [DEPTH, MODC // 512, 128, KC, 512], F32, kind="ExternalInput").ap()
    bm = nc.dram_tensor("bm", [2, DEPTH, MODC], F32, kind="ExternalInput").ap()
    out = nc.dram_tensor("out", [2, DEPTH, MODC], F32, kind="ExternalOutput").ap()
    with ExitStack() as st:
        c = Ctx(nc, st)
        ct = c.sb("ct", [128, KC, 2], F32)
        cs = c.sb("cs", [128, KC, 2], BF16)
        bt = c.sb("bt", [2, DEPTH, MODC], F32)
        ot = c.sb("ot", [2, DEPTH, MODC], F32)
        wrot = Rot(c, "wt", 2, [128, KC, 512], BF16)
        prot = Rot(c, "pm", 2, [2, 512], F32, psum=True)
        c.dma("sp", "c2", ct[:], c2, writes=["ct"])
        c.dma("sp", "bm", bt[:], bm, writes=["bt"])
        c.op("act", lambda e: e.activation(out=cs[:], in_=ct[:], func=AF.Silu), reads=["ct"], writes=["cs"])
        for l in range(DEPTH):
            for n in range(MODC // 512):
                wt, wtok = wrot.next()
                c.dma("pool", wtok, wt[:], wm[l, n], writes=[wtok])
                ps, ptok = prot.next()
                for kc in range(KC):
                    c.op("pe", lambda e: e.matmul(ps[:, :], lhsT=cs[:, kc, :], rhs=wt[:, kc, :], start=(kc == 0), stop=(kc == KC - 1)),
                         reads=["cs", wtok], writes=[ptok])
                c.op("dve", lambda e: e.tensor_tensor(out=ot[:, l, n * 512:(n + 1) * 512], in0=ps[:, :], in1=bt[:, l, n * 512:(n + 1) * 512], op=ALU.add),
                     reads=[ptok, "bt"], writes=["ot"])
        c.dma("sp", "out", out, ot[:], reads=["ot"], writes=["out"])
        c.wait_all("sp", ["out"])
    return nc


def tiles_of(total, w):
    return [(s, min(w, total - s)) for s in range(0, total, w)]


def build_pre(ncols, tiles):
    T = sum(w for _, w, _ in tiles)
    nc = new_nc()
    xT = nc.dram_tensor("xT", [D, T], F32, kind="ExternalInput").ap()
    vec = nc.dram_tensor("vec", [128, 5, KC], F32, kind="ExternalInput").ap()
    w = nc.dram_tensor("w", [(ncols + 127) // 128, 128, KC, 128], F32, kind="ExternalInput").ap()
    pT = nc.dram_tensor("pT", [ncols, T], BF16, kind="ExternalOutput").ap()
    with ExitStack() as st:
        c = Ctx(nc, st)
        K = make_consts(c)
        vt = c.sb("vec", [128, 5, KC], F32)
        av = c.sb("av", [128, 2, KC], F32)
        h = c.sb("h", [128, KC, T], BF16)
        xrot = Rot(c, "xt", 2, [128, KC, 512], F32)
        wrot = Rot(c, "wt", 2, [128, KC, 128], BF16)
        prot = Rot(c, "pp", 3, [128, 512], F32, psum=True)
        orot = Rot(c, "po", 3, [128, 512], BF16)
        c.dma("sp", "vec", vt[:], vec, writes=["vec"])
        for s in range(2):
            c.op("dve", lambda e: e.scalar_tensor_tensor(out=av[:, s, :], in0=vt[:, 2 + 2 * s, :], scalar=1.0, in1=vt[:, 0, :],
                                                         op0=ALU.add, op1=ALU.mult), reads=["vec"], writes=["vec"])
        for (s0, wd, stream) in tiles:
            xt, xtok = xrot.next()
            c.dma("sp", xtok, xt[:, :, :wd], xT[:, s0:s0 + wd].rearrange("(kc p) t -> p kc t", p=128), writes=[xtok])
            emit_norm_mod(c, K, xt, xtok, wd, [(0, wd, av[:, stream, :], vt[:, 1 + 2 * stream, :])], h[:, :, s0:s0 + wd], "h%d" % s0)
        nst = 0
        for cb0 in range(0, ncols, 128):
            m = min(128, ncols - cb0)
            wt, wtok = wrot.next()
            c.dma("pool", wtok, wt[:], w[cb0 // 128], writes=[wtok])
            for (s0, wd, stream) in tiles:
                ps, ptok = prot.next()
                for kc in range(KC):
                    c.op("pe", lambda e: e.matmul(ps[:m, :wd], lhsT=wt[:, kc, :m], rhs=h[:, kc, s0:s0 + wd], start=(kc == 0), stop=(kc == KC - 1)),
                         reads=[wtok, "h%d" % s0], writes=[ptok])
                ot, otok = orot.next()
                eng = "act" if nst % 2 == 0 else "dve"
                if eng == "act":
                    c.op("act", lambda e: e.activation(out=ot[:m, :wd], in_=ps[:m, :wd], func=AF.Copy), reads=[ptok], writes=[otok])
                else:
                    c.op("dve", lambda e: e.tensor_copy(out=ot[:m, :wd], in_=ps[:m, :wd]), reads=[ptok], writes=[otok])
                nst += 1
                c.dma("sp", otok, pT[cb0:cb0 + m, s0:s0 + wd], ot[:m, :wd], reads=[otok], writes=["pT"])
        c.wait_all("sp", ["pT"])
    return nc


POST_V = {"n2g": 0, "g1": 16, "sh2": 32, "sc2": 48, "g2": 64, "cg1": 80, "csh2": 96, "csc2": 112, "cg2": 128, "fg": 144,
          "cw": 160, "cb": 160 + 3 * 88, "vl": 160 + 4 * 88, "vr": 161 + 4 * 88, "zero": 162 + 4 * 88}
POST_NV = 163 + 4 * 88


def build_post(tiles, final):
    T = max(s + w for s, w, _, _ in tiles)
    TO = sum(w - 2 for _, w, _, _ in tiles)
    nc = new_nc()
    xT = nc.dram_tensor("xT", [D, T], F32, kind="ExternalInput").ap()
    mT = nc.dram_tensor("mT", [D, T], BF16, kind="ExternalInput").ap()
    vec = nc.dram_tensor("vec", [128, POST_NV], F32, kind="ExternalInput").ap()
    wo = nc.dram_tensor("wo", [KC, 128, KC, 128], F32, kind="ExternalInput").ap()
    wu = nc.dram_tensor("wu", [2 * FC, 128, KC, 128], F32, kind="ExternalInput").ap()
    wdn = nc.dram_tensor("wd", [KC, 128, FC, 128], F32, kind="ExternalInput").ap()
    oT = nc.dram_tensor("oT", [D, TO], F32, kind="ExternalOutput").ap()
    swo = nc.dram_tensor("swo", [KC, 128, KC, 128], BF16, kind="Internal").ap()
    swu = nc.dram_tensor("swu", [2 * FC, 128, KC, 128], BF16, kind="Internal").ap()
    swd = nc.dram_tensor("swd", [KC, 128, FC, 128], BF16, kind="Internal").ap()
    with ExitStack() as st:
        c = Ctx(nc, st)
        K = make_consts(c)
        vt = c.sb("vec", [128, POST_NV], F32)
        av = c.sb("av", [128, 2, KC], F32)
        xrot = Rot(c, "xt", 1, [128, KC, 512], F32)
        mrot = Rot(c, "mt", 1, [128, KC, 512], BF16)
        h2 = c.sb("h2", [128, KC, 512], BF16)
        act = c.sb("actb", [128, FC, 512], BF16)
        worot = Rot(c, "wo", 2, [128, KC, 128], BF16)
        wgrot = Rot(c, "wg", 2, [128, KC, 128], BF16)
        wvrot = Rot(c, "wv", 2, [128, KC, 128], BF16)
        wdrot = Rot(c, "wdn", 2, [128, FC, 128], BF16)
        prot = Rot(c, "pp", 2, [128, 512], F32, psum=True)
        pgrot = Rot(c, "pg", 2, [128, 512], F32, psum=True)
        pvrot = Rot(c, "pv", 2, [128, 512], F32, psum=True)
        cg = Rot(c, "cg", 2, [128, 512], F32)
        cv = Rot(c, "cv", 2, [128, 512], F32)
        sg = Rot(c, "sg", 2, [128, 512], F32)
        yt = Rot(c, "yt", 2, [128, 512], F32)
        c.dma("sp", "vec", vt[:], vec, writes=["vec"])
        V = POST_V
        for s, (scn, gn) in enumerate((("sc2", "n2g"), ("csc2", "n2g"))):
            c.op("dve", lambda e: e.scalar_tensor_tensor(out=av[:, s, :], in0=vt[:, V[scn]:V[scn] + 16], scalar=1.0, in1=vt[:, V[gn]:V[gn] + 16],
                                                         op0=ALU.add, op1=ALU.mult), reads=["vec"], writes=["vec"])
        ocol = 0
        def wload(ti, wt, wtok, src, scr, stok):
            if ti == 0:
                c.dma("pool", wtok, wt[:], src, writes=[wtok])
                c.dma("sp", "scr_" + wtok, scr, wt[:], reads=[wtok], writes=[stok])
            else:
                c.dma("sp", wtok, wt[:], scr, reads=[stok], writes=[wtok])

        for ti, (s0, wd, segs, tmasks) in enumerate(tiles):
            wi = wd - 2
            xt, xtok = xrot.next()
            mt, mtok = mrot.next()
            c.dma("sp", xtok, xt[:, :, :wd], xT[:, s0:s0 + wd].rearrange("(kc p) t -> p kc t", p=128), writes=[xtok])
            c.dma("sp", mtok, mt[:, :, :wd], mT[:, s0:s0 + wd].rearrange("(kc p) t -> p kc t", p=128), writes=[mtok])
            for oc in range(KC):
                wt, wtok = worot.next()
                wload(ti, wt, wtok, wo[oc], swo[oc], "swo%d" % oc)
                ps, ptok = prot.next()
                for kc in range(KC):
                    c.op("pe", lambda e: e.matmul(ps[:, :wd], lhsT=wt[:, kc, :], rhs=mt[:, kc, :wd], start=(kc == 0), stop=(kc == KC - 1)),
                         reads=[wtok, mtok], writes=[ptok])
                for (c0, c1, stream) in segs:
                    g1 = V["cg1"] if stream else V["g1"]
                    c.op("dve", lambda e: e.scalar_tensor_tensor(out=xt[:, oc, c0:c1], in0=ps[:, c0:c1], scalar=vt[:, g1 + oc:g1 + oc + 1], in1=xt[:, oc, c0:c1],
                                                                 op0=ALU.mult, op1=ALU.add), reads=[ptok, xtok, "vec"], writes=[xtok])
            masks = [(col, vt[:, V[fl]:V[fl] + 1]) for col, fl in tmasks]
            nsegs = [(c0, c1, av[:, stream, :], vt[:, (V["csh2"] if stream else V["sh2"]):(V["csh2"] if stream else V["sh2"]) + 16]) for (c0, c1, stream) in segs]
            emit_norm_mod(c, K, xt, xtok, wd, nsegs, h2, "h2", maskcols=masks)
            for f in range(FC):
                wg, wgtok = wgrot.next()
                wv, wvtok = wvrot.next()
                wload(ti, wg, wgtok, wu[f], swu[f], "swu%d" % f)
                wload(ti, wv, wvtok, wu[FC + f], swu[FC + f], "swu%d" % (FC + f))
                pg, pgtok = pgrot.next()
                pv, pvtok = pvrot.next()
                for kc in range(KC):
                    c.op("pe", lambda e: e.matmul(pg[:, :wd], lhsT=wg[:, kc, :], rhs=h2[:, kc, :wd], start=(kc == 0), stop=(kc == KC - 1)),
                         reads=[wgtok, "h2"], writes=[pgtok])
                for kc in range(KC):
                    c.op("pe", lambda e: e.matmul(pv[:, :wd], lhsT=wv[:, kc, :], rhs=h2[:, kc, :wd], start=(kc == 0), stop=(kc == KC - 1)),
                         reads=[wvtok, "h2"], writes=[pvtok])
                outs = []
                for (pp, pptok, rot, fi) in ((pg, pgtok, cg, f), (pv, pvtok, cv, FC + f)):
                    t, ttok = rot.next()
                    cw0 = V["cw"] + 0 * 88 + fi
                    cw1 = V["cw"] + 1 * 88 + fi
                    cw2 = V["cw"] + 2 * 88 + fi
                    c.op("dve", lambda e: e.tensor_scalar(out=t[:, :wi], in0=pp[:, 0:wi], scalar1=vt[:, cw0:cw0 + 1], scalar2=None, op0=ALU.mult),
                         reads=[pptok, "vec"], writes=[ttok])
                    c.op("dve", lambda e: e.scalar_tensor_tensor(out=t[:, :wi], in0=pp[:, 1:wi + 1], scalar=vt[:, cw1:cw1 + 1], in1=t[:, :wi],
                                                                 op0=ALU.mult, op1=ALU.add), reads=[pptok, ttok, "vec"], writes=[ttok])
                    c.op("dve", lambda e: e.scalar_tensor_tensor(out=t[:, :wi], in0=pp[:, 2:wi + 2], scalar=vt[:, cw2:cw2 + 1], in1=t[:, :wi],
                                                                 op0=ALU.mult, op1=ALU.add), reads=[pptok, ttok, "vec"], writes=[ttok])
                    outs.append((t, ttok))
                (tg, tgtok), (tv, tvtok) = outs
                s_, stok = sg.next()
                cbg = V["cb"] + f
                cbv = V["cb"] + FC + f
                c.op("act", lambda e: e.activation(out=s_[:, :wi], in_=tg[:, :wi], func=AF.Silu, bias=vt[:, cbg:cbg + 1], scale=1.0),
                     reads=[tgtok, "vec"], writes=[stok])
                c.op("dve", lambda e: e.scalar_tensor_tensor(out=act[:, f, :wi], in0=tv[:, :wi], scalar=vt[:, cbv:cbv + 1], in1=s_[:, :wi],
                                                              op0=ALU.add, op1=ALU.mult), reads=[tvtok, stok, "vec"], writes=["act%d" % f])
            for oc in range(KC):
                wt, wtok = wdrot.next()
                wload(ti, wt, wtok, wdn[oc], swd[oc], "swd%d" % oc)
                ps, ptok = prot.next()
                for f in range(FC):
                    c.op("pe", lambda e: e.matmul(ps[:, :wi], lhsT=wt[:, f, :], rhs=act[:, f, :wi], start=(f == 0), stop=(f == FC - 1)),
                         reads=[wtok, "act%d" % f], writes=[ptok])
                for (c0, c1, stream) in segs:
                    g2 = V["cg2"] if stream else V["g2"]
                    a0, a1 = max(c0, 1), min(c1, wd - 1)
                    if a1 <= a0:
                        continue
                    c.op("dve", lambda e: e.scalar_tensor_tensor(out=xt[:, oc, a0:a1], in0=ps[:, a0 - 1:a1 - 1], scalar=vt[:, g2 + oc:g2 + oc + 1], in1=xt[:, oc, a0:a1],
                                                                 op0=ALU.mult, op1=ALU.add), reads=[ptok, xtok, "vec"], writes=[xtok])
            if not final:
                c.dma("sp", "oT", oT[:, ocol:ocol + wi].rearrange("(kc p) t -> p kc t", p=128), xt[:, :, 1:wi + 1], reads=[xtok], writes=["oT"])
            else:
                ss, sstok = K["ps_ss"].next()
                for kc in range(KC):
                    sq, sqtok = K["sq"].next()
                    c.op("act", lambda e: e.activation(out=sq[:, :wi], in_=xt[:, kc, 1:wi + 1], func=AF.Square), reads=[xtok], writes=[sqtok])
                    c.op("pe", lambda e: e.matmul(ss[:, :wi], lhsT=K["ones"][:, :], rhs=sq[:, :wi], start=(kc == 0), stop=(kc == KC - 1)),
                         reads=[sqtok, "ones"], writes=[sstok])
                rstd = K["rstd"]
                emit_rstd(c, rstd, ss, sstok, wi)
                for kc in range(KC):
                    y, ytok = yt.next()
                    fg = V["fg"] + kc
                    c.op("dve", lambda e: e.scalar_tensor_tensor(out=y[:, :wi], in0=xt[:, kc, 1:wi + 1], scalar=vt[:, fg:fg + 1], in1=rstd[:, :wi],
                                                                 op0=ALU.mult, op1=ALU.mult), reads=[xtok, "rstd", "vec"], writes=[ytok])
                    c.dma("sp", ytok, oT[kc * 128:(kc + 1) * 128, ocol:ocol + wi], y[:, :wi], reads=[ytok], writes=["oT"])
            ocol += wi
        c.wait_all("sp", ["oT"])
    return nc


ATT_ACC2 = "pool"


def build_att(kind, NQL, NKL, NCT=CTX):
    S = 2
    SK = 2 if kind == "B" else 1
    DV = 256 if kind == "B" else 128
    NH = DV // 128
    NK = NCT + NKL
    NKT = NK // 128
    NCKT = NCT // 128
    R = 256
    scale = 128 ** -0.5
    nc = new_nc()
    qT = nc.dram_tensor("qT", [S, 128, NCT + NQL], BF16, kind="ExternalInput").ap()
    kT = nc.dram_tensor("kT", [SK, 128, NK], BF16, kind="ExternalInput").ap()
    v = nc.dram_tensor("v", [NK, DV], BF16, kind="ExternalInput").ap()
    cq = nc.dram_tensor("cq", [2, 128, NQL], F32, kind="ExternalInput").ap()
    ck = nc.dram_tensor("ck", [2, 128, NKL], F32, kind="ExternalInput").ap()
    rt = nc.dram_tensor("rt", [128, 128], BF16, kind="ExternalInput").ap()
    vec = nc.dram_tensor("vec", [128, 8], F32, kind="ExternalInput").ap()
    lp = nc.dram_tensor("lp", [128, 4, 128], F32, kind="ExternalInput").ap()
    oT = nc.dram_tensor("oT", [R, NCT + NQL], BF16, kind="ExternalOutput").ap()
    with ExitStack() as st:
        c = Ctx(nc, st)
        ones = c.sb("ones", [128, 128], BF16)
        c.op("dve", lambda e: e.memset(ones[:, :], 1.0), writes=["ones"])
        rtt = c.sb("rtt", [128, 128], BF16)
        vt = c.sb("vec", [128, 8], F32)
        lpt = c.sb("lpt", [128, 4, 128], F32)
        lam = c.sb("lam", [128, 8], F32)
        Kr = c.sb("Kr", [128, SK, NK], BF16)
        Vt = c.sb("Vt", [128, NKT, DV], BF16)
        raw = Rot(c, "raw", 2, [128, 512], BF16)
        cst = Rot(c, "cst", 2, [128, 2, 512], F32)
        xn = c.sb("xn", [128, 512], F32)
        xnb = c.sb("xnb", [128, 512], BF16)
        sqb = c.sb("sqb", [128, 512], BF16)
        rstd = c.sb("rstd", [128, 512], F32)
        t1 = c.sb("t1", [128, 512], F32)
        t2 = c.sb("t2", [128, 512], F32)
        qr = Rot(c, "qr", 2, [128, 512], BF16)
        E = Rot(c, "E", 3, [128, 2, 512], BF16)
        acc = [c.sb("acc%d" % i, [128, 2, 512], F32) for i in range(2)]
        ones32 = c.sb("ones32", [128, 128], F32)
        c.op("dve", lambda e: e.memset(ones32[:, :], 1.0), writes=["ones32"])
        sqb2 = c.sb("sqb2", [128, 512], BF16)
        rstd2 = c.sb("rstd2", [128, 512], F32)
        on = [[c.sb("on%d%d" % (s, h), [128, 512], F32) for h in range(NH)] for s in range(S)]
        rec = c.sb("rec", [128, 512], F32)
        ob = Rot(c, "ob", 2, [128, 512], BF16)
        ps_s = Rot(c, "pS", 2, [128, 2, 512], F32, psum=True)
        ps_o = [c.ps("pO%d" % h, [128, 512]) for h in range(NH)]
        ps_sum = c.ps("pSum", [128, 512])
        ps_ss2 = ps_sum
        ps_ss = c.ps("pSS", [128, 512])
        ps_rot = ps_ss
        c.dma("sp", "rtt", rtt[:], rt, writes=["rtt"])
        c.dma("sp", "vec", vt[:], vec, writes=["vec"])
        c.dma("sp", "Vt", Vt[:], v.rearrange("(kt p) d -> p kt d", p=128), writes=["Vt"])
        if kind == "B":
            c.dma("sp", "lpt", lpt[:], lp, writes=["lpt"])
            for i in range(2):
                c.op("dve", lambda e: e.tensor_tensor(out=t1[:, :128], in0=lpt[:, 2 * i, :], in1=lpt[:, 2 * i + 1, :], op=ALU.mult), reads=["lpt"], writes=["t1"])
                c.op("dve", lambda e: e.reduce_sum(out=lam[:, i:i + 1], in_=t1[:, :128], axis=AX.X), reads=["t1"], writes=["lam"])
            c.op("act", lambda e: e.activation(out=lam[:, 0:2], in_=lam[:, 0:2], func=AF.Exp), reads=["lam"], writes=["lam"])
            c.op("dve", lambda e: e.tensor_tensor(out=lam[:, 2:3], in0=lam[:, 0:1], in1=lam[:, 1:2], op=ALU.subtract), reads=["lam"], writes=["lam"])
            c.op("dve", lambda e: e.tensor_tensor(out=lam[:, 2:3], in0=lam[:, 2:3], in1=vt[:, 2:3], op=ALU.add), reads=["lam", "vec"], writes=["lam"])
            c.op("dve", lambda e: e.tensor_scalar(out=lam[:, 3:4], in0=lam[:, 2:3], scalar1=-1.0, scalar2=None, op0=ALU.mult), reads=["lam"], writes=["lam"])
            c.op("dve", lambda e: e.tensor_scalar(out=lam[:, 4:6], in0=vt[:, 4:6], scalar1=vt[:, 3:4], scalar2=None, op0=ALU.mult), reads=["lam", "vec"], writes=["lam"])

        def prep(src, srctok, W, cs, cstok, gain_col, dst, dsttok):
            cur, curtok = src, srctok
            if gain_col is not None:
                c.op("act", lambda e: e.activation(out=sqb[:, :W], in_=src[:, :W], func=AF.Square), reads=[srctok], writes=["sqb"])
                yield
                c.op("pe", lambda e: e.matmul(ps_ss[:, :W], lhsT=ones[:, :], rhs=sqb[:, :W], start=True, stop=True), reads=["sqb", "ones"], writes=["pSS"])
                yield
                c.op("dve", lambda e: e.tensor_scalar(out=rstd[:, :W], in0=ps_ss[:, :W], scalar1=1.0 / 128, scalar2=EPS, op0=ALU.mult, op1=ALU.add),
                     reads=["pSS"], writes=["rstd"])
                yield
                c.op("act", lambda e: e.activation(out=rstd[:, :W], in_=rstd[:, :W], func=AF.Ln), reads=["rstd"], writes=["rstd"])
                yield
                c.op("act", lambda e: e.activation(out=rstd[:, :W], in_=rstd[:, :W], func=AF.Exp, scale=-0.5), reads=["rstd"], writes=["rstd"])
                yield
                c.op("dve", lambda e: e.scalar_tensor_tensor(out=xn[:, :W], in0=src[:, :W], scalar=vt[:, gain_col:gain_col + 1], in1=rstd[:, :W],
                                                             op0=ALU.mult, op1=ALU.mult), reads=[srctok, "rstd", "vec"], writes=["xn"])
                yield
                cur, curtok = xn, "xn"
                if cs is None:
                    c.op("act", lambda e: e.activation(out=dst[:, :W], in_=xn[:, :W], func=AF.Copy), reads=["xn"], writes=[dsttok])
                    yield
                    return
                c.op("act", lambda e: e.activation(out=xnb[:, :W], in_=xn[:, :W], func=AF.Copy), reads=["xn"], writes=["xnb"])
                yield
                curb, curbtok = xnb, "xnb"
            else:
                if cs is None:
                    c.op("act", lambda e: e.activation(out=dst[:, :W], in_=src[:, :W], func=AF.Copy), reads=[srctok], writes=[dsttok])
                    yield
                    return
                curb, curbtok = src, srctok
            c.op("dve", lambda e: e.tensor_tensor(out=t1[:, :W], in0=cur[:, :W], in1=cs[:, 0, :W], op=ALU.mult), reads=[curtok, cstok], writes=["t1"])
            yield
            c.op("pe", lambda e: e.matmul(ps_rot[:, :W], lhsT=rtt[:, :], rhs=curb[:, :W], start=True, stop=True), reads=["rtt", curbtok], writes=["pSS"])
            yield
            c.op("dve", lambda e: e.tensor_tensor(out=t2[:, :W], in0=ps_rot[:, :W], in1=cs[:, 1, :W], op=ALU.mult), reads=["pSS", cstok], writes=["t2"])
            yield
            c.op("dve", lambda e: e.tensor_tensor(out=dst[:, :W], in0=t1[:, :W], in1=t2[:, :W], op=ALU.add), reads=["t1", "t2"], writes=[dsttok])
            yield

        def run_all(gen):
            for _ in gen:
                pass

        kgain = 1 if kind == "C" else None
        qgain = 0 if kind == "C" else None
        PE_SUM = False
        ktiles = [(0, NCT, None)] + [(NCT + s0, w, s0) for s0, w in tiles_of(NKL, 512)]
        for sk in range(SK):
            for (c0, w, r0) in ktiles:
                rw, rwtok = raw.next()
                c.dma("sp", rwtok, rw[:, :w], kT[sk, :, c0:c0 + w], writes=[rwtok])
                cs, cstok = None, None
                if r0 is not None:
                    cs, cstok = cst.next()
                    c.dma("sp", cstok, cs[:, :, :w], ck[:, :, r0:r0 + w].rearrange("a p t -> p a t"), writes=[cstok])
                run_all(prep(rw, rwtok, w, cs, cstok, kgain, Kr[:, sk, c0:c0 + w], "Kr%d_%d" % (sk, c0)))
        ktoks = [["Kr%d_%d" % (sk, c0) for (c0, w, r0) in ktiles] for sk in range(SK)]
        qtiles = [(0, NCT, None, NCKT)] + [(NCT + s0, w, s0, NKT) for s0, w in tiles_of(NQL, 512)]
        units = [(qi, s) for qi in range(len(qtiles)) for s in range(S)]
        cs_of = {}

        def prep_unit(u):
            qi, s = units[u]
            c0, w, r0, nkt = qtiles[qi]
            if s == 0:
                cs, cstok = None, None
                if r0 is not None:
                    cs, cstok = cst.next()
                    c.dma("sp", cstok, cs[:, :, :w], cq[:, :, r0:r0 + w].rearrange("a p t -> p a t"), writes=[cstok])
                cs_of[qi] = (cs, cstok)
            cs, cstok = cs_of[qi]
            rw, rwtok = raw.next()
            c.dma("sp", rwtok, rw[:, :w], qT[s, :, c0:c0 + w], writes=[rwtok])
            q, qtok = qr.next()
            qready[u] = (q, qtok)
            yield
            for _ in prep(rw, rwtok, w, cs, cstok, qgain, q, qtok):
                yield

        qready = {}
        run_all(prep_unit(0))
        for u, (qi, s) in enumerate(units):
            c0, w, r0, nkt = qtiles[qi]
            q, qtok = qready.pop(u)
            pgen = prep_unit(u + 1) if u + 1 < len(units) else iter(())
            sk = s if SK == 2 else 0
            pend = {}
            npair = nkt // 2
            npe, ndve = [0], [0]

            def score(p_):
                ps, pstok = ps_s.next()
                for j in range(2):
                    kt = 2 * p_ + j
                    c.op("pe", lambda e: e.matmul(ps[:, j, :w], lhsT=Kr[:, sk, kt * 128:(kt + 1) * 128], rhs=q[:, :w], start=True, stop=True),
                         reads=ktoks[sk] + [qtok], writes=[pstok])
                pend[p_] = (ps, pstok)

            score(0)
            for p_ in range(npair):
                ps, pstok = pend.pop(p_)
                e_, etok = E.next()
                c.op("act", lambda e: e.activation(out=e_[:, :, :w], in_=ps[:, :, :w], func=AF.Exp, scale=scale), reads=[pstok], writes=[etok])
                if p_ + 1 < npair:
                    score(p_ + 1)
                if p_ >= 4 and p_ % 3 == 0:
                    next(pgen, None)
                for j in range(2):
                    kt = 2 * p_ + j
                    for h in range(NH):
                        c.op("pe", lambda e: e.matmul(ps_o[h][:, :w], lhsT=Vt[:, kt, h * 128:(h + 1) * 128], rhs=e_[:, j, :w], start=(kt == 0), stop=(kt == nkt - 1)),
                             reads=["Vt", etok], writes=["pO%d" % h])
                if PE_SUM and p_ % 3 == 2:
                    for j in range(2):
                        c.op("pe", lambda e: e.matmul(ps_sum[:, :w], lhsT=ones[:, :], rhs=e_[:, j, :w], start=(npe[0] == 0), stop=False),
                             reads=["ones", etok], writes=["pSum"])
                        npe[0] += 1
                else:
                    ac, actok = (acc[0], "acc0") if ndve[0] % 2 == 0 else (acc[1], "acc1")
                    if ndve[0] < 2:
                        c.op("dve", lambda e: e.tensor_copy(out=ac[:, :, :w], in_=e_[:, :, :w]), reads=[etok], writes=[actok])
                    else:
                        c.op("dve", lambda e: e.tensor_tensor(out=ac[:, :, :w], in0=ac[:, :, :w], in1=e_[:, :, :w], op=ALU.add), reads=[etok, actok], writes=[actok])
                    ndve[0] += 1
            run_all(pgen)
            if ndve[0] > 1:
                c.op("dve", lambda e: e.tensor_tensor(out=acc[0][:, :, :w], in0=acc[0][:, :, :w], in1=acc[1][:, :, :w], op=ALU.add), reads=["acc0", "acc1"], writes=["acc0"])
            c.op("dve", lambda e: e.tensor_tensor(out=acc[0][:, 0, :w], in0=acc[0][:, 0, :w], in1=acc[0][:, 1, :w], op=ALU.add), reads=["acc0"], writes=["acc0"])
            c.op("pe", lambda e: e.matmul(ps_sum[:, :w], lhsT=ones32[:, :], rhs=acc[0][:, 0, :w], start=(npe[0] == 0), stop=True), reads=["ones32", "acc0"], writes=["pSum"])
            c.op("act", lambda e: e.activation(out=rec[:, :w], in_=ps_sum[:, :w], func=AF.Ln), reads=["pSum"], writes=["rec"])
            c.op("act", lambda e: e.activation(out=rec[:, :w], in_=rec[:, :w], func=AF.Exp, scale=-1.0), reads=["rec"], writes=["rec"])
            for h in range(NH):
                c.op("dve", lambda e: e.tensor_tensor(out=on[s][h][:, :w], in0=ps_o[h][:, :w], in1=rec[:, :w], op=ALU.mult),
                     reads=["pO%d" % h, "rec"], writes=["on%d%d" % (s, h)])
            if kind == "C":
                o_, otok = ob.next()
                c.op("act", lambda e: e.activation(out=o_[:, :w], in_=on[s][0][:, :w], func=AF.Copy), reads=["on%d0" % s], writes=[otok])
                c.dma("sp", otok, oT[s * 128:(s + 1) * 128, c0:c0 + w], o_[:, :w], reads=[otok], writes=["oT"])
            if kind == "B" and s == S - 1:
                for h in range(NH):
                    c.op("dve", lambda e: e.scalar_tensor_tensor(out=on[0][h][:, :w], in0=on[1][h][:, :w], scalar=lam[:, 3:4], in1=on[0][h][:, :w],
                                                                 op0=ALU.mult, op1=ALU.add), reads=["on1%d" % h, "on0%d" % h, "lam"], writes=["on0%d" % h])
                    c.op("act", lambda e: e.activation(out=sqb2[:, :w], in_=on[0][h][:, :w], func=AF.Square), reads=["on0%d" % h], writes=["sqb2"])
                    c.op("pe", lambda e: e.matmul(ps_ss2[:, :w], lhsT=ones[:, :], rhs=sqb2[:, :w], start=(h == 0), stop=(h == NH - 1)),
                         reads=["sqb2", "ones"], writes=["pSum"])
                emit_rstd(c, rstd2, ps_ss2, "pSum", w, n=256, tok="rstd2")
                for h in range(NH):
                    o_, otok = ob.next()
                    c.op("dve", lambda e: e.scalar_tensor_tensor(out=o_[:, :w], in0=on[0][h][:, :w], scalar=lam[:, 4 + h:5 + h], in1=rstd2[:, :w],
                                                                 op0=ALU.mult, op1=ALU.mult), reads=["on0%d" % h, "rstd2", "lam"], writes=[otok])
                    c.dma("sp", otok, oT[h * 128:(h + 1) * 128, c0:c0 + w], o_[:, :w], reads=[otok], writes=["oT"])
        c.wait_all("sp", ["oT"])
    return nc


def rope_tables(n):
    freqs = (10000.0 ** (-np.arange(0, 64, 2, dtype=np.float32) / 64)).astype(np.float32)
    t = np.arange(n)
    row = (t // 64).astype(np.float32)
    col = (t % 64).astype(np.float32)
    ang_r = row[:, None] * freqs
    ang_c = col[:, None] * freqs
    ang = np.concatenate([ang_r, ang_r, ang_c, ang_c], -1).astype(np.float32)
    cos = np.cos(ang).astype(np.float32)
    sin = np.sin(ang).astype(np.float32)
    sgn = np.ones(128, np.float32)
    sgn[0:32] = -1
    sgn[64:96] = -1
    return np.ascontiguousarray(np.stack([cos.T, (sin * sgn).T], 0))


def rope_perm():
    m = np.arange(128)
    partner = np.where((m // 32) % 2 == 0, m + 32, m - 32)
    rt = np.zeros((128, 128), np.float32)
    rt[partner, m] = 1.0
    return rt.astype(NPBF)


GDN_FP32R = False
GDN_INV_BF16 = False


def build_gdn(NT, NCT=CTX):
    NCH = NT // 128
    CCH = NCT // 128
    nc = new_nc()
    qkvT = nc.dram_tensor("qkvT", [3, 128, NT], BF16, kind="ExternalInput").ap()
    ztm = nc.dram_tensor("ztm", [NT, 128], BF16, kind="ExternalInput").ap()
    gt = nc.dram_tensor("gt", [128, NCH, 4], BF16, kind="ExternalInput").ap()
    vec = nc.dram_tensor("vec", [128, 16], F32, kind="ExternalInput").ap()
    gbc = nc.dram_tensor("gbc", [128, 128], F32, kind="ExternalInput").ap()
    cst = nc.dram_tensor("cst", [6, 128, 128], F32, kind="ExternalInput").ap()
    otm = nc.dram_tensor("otm", [NT, 128], BF16, kind="ExternalOutput").ap()
    szd = nc.dram_tensor("szd", [NT, 128], F32, kind="Internal").ap()
    with ExitStack() as st:
        c = Ctx(nc, st)
        vt = c.sb("vec", [128, 16], F32)
        gb = c.sb("gbc", [128, 128], F32)
        C32 = c.sb("cst", [128, 6, 128], F32)
        Ib = c.sb("Ib", [128, 128], BF16)
        onesb = c.sb("onesb", [128, 128], BF16)
        qT = c.sb("qT", [128, NT], BF16)
        kT = c.sb("kT", [128, NT], BF16)
        ktm = c.sb("ktm", [128, NCH, 128], BF16)
        vtm = c.sb("vtm", [128, NCH, 128], BF16)
        of = c.sb("of", [128, NCH, 128], BF16)
        G = [c.sb("G%d" % d, [128, NCH], F32) for d in range(2)]
        BT = [c.sb("BT%d" % d, [128, NCH], F32) for d in range(2)]
        NB = [c.sb("NB%d" % d, [128, NCH], F32) for d in range(2)]
        Bc = [c.sb("Bc%d" % d, [128, NCH], F32) for d in range(2)]
        EB = [c.sb("EB%d" % d, [128, NCH], F32) for d in range(2)]
        BEB = [c.sb("BEB%d" % d, [128, NCH], F32) for d in range(2)]
        EKD = [c.sb("EKD%d" % d, [128, NCH], F32) for d in range(2)]
        CD = [c.sb("CD%d" % d, [128, NCH], F32) for d in range(2)]
        tri = c.sb("tri", [128, 2, 128], F32)
        zt = Rot(c, "zt", 2, [128, 128], F32)
        ot = Rot(c, "ot", 2, [128, 128], BF16)
        banks = [c.ps("bank%d" % i, [128, 512]) for i in range(8)]

        def carve(b, i, n=1):
            return banks[b][:, 128 * i:128 * (i + n)]

        class RotAP:
            def __init__(self, aps, names):
                self.bufs, self.names, self.i = aps, names, 0

            def next(self):
                j = self.i % len(self.bufs)
                self.i += 1
                return self.bufs[j], self.names[j]

        pbig = banks[0]
        pG = banks[1]
        pT = RotAP([carve(1, 0), carve(1, 1)], ["bank1", "bank1"])

        class Chain:
            def __init__(self, d):
                n = "c%d" % d
                self.d = d
                self.oss = c.sb("oss" + n, [128, 4], F32)
                IDT = BF16 if GDN_INV_BF16 else (mybir.dt.float32r if GDN_FP32R else F32)
                self.Rm = c.sb("Rm" + n, [128, 128], IDT)
                for nm in ("diagb", "nd", "dm", "dmS", "dmI", "S32", "o2s", "osum", "osq", "sz"):
                    setattr(self, nm, c.sb(nm + n, [128, 128], F32))
                for nm in ("Rb", "qk", "kb", "vb", "Sb", "u_b"):
                    setattr(self, nm, c.sb(nm + n, [128, 128], BF16))
                self.Pm = Rot(c, "Pm" + n, 2, [128, 128], IDT)
                self.Qm = Rot(c, "Qm" + n, 2, [128, 128], IDT)
                self.qkT = Rot(c, "qkT" + n, 2, [128, 128], BF16)
                self.kd = Rot(c, "kd" + n, 2, [128, 128], BF16)
                self.ub = Rot(c, "ub" + n, 2, [128, 128], F32)
                self.wT = Rot(c, "wT" + n, 2, [128, 128], BF16)
                b0 = 4 * d
                self.bA, self.bB, self.bC, self.bD = ["bank%d" % (b0 + k) for k in range(4)]
                self.pA, self.pKK, self.pQK, self.pU = [carve(b0, k) for k in range(4)]
                self.pT = RotAP([carve(b0 + 1, 0), carve(b0 + 1, 1)], [self.bB, self.bB])
                self.pW = carve(b0 + 1, 2)
                self.pI = RotAP([carve(b0 + 2, k) for k in range(3)], [self.bC] * 3)
                self.pwS, self.pO1, self.pO2, self.pdS = [carve(b0 + 3, k) for k in range(4)]

            def t(self, nm):
                return "%sc%d" % (nm, self.d)

        c.push_scope()
        gtr = c.sb("gtr", [128, NCH, 4], BF16)
        gx = c.sb("gx", [128, NCH], F32)
        rb = c.sb("rb", [128, 3, 514], BF16)
        cv = c.sb("cv", [128, 514], F32)
        sv = c.sb("sv", [128, 514], F32)
        sqb = c.sb("sqb", [128, 512], BF16)
        rstd = c.sb("rstd", [128, 512], F32)
        vTb = c.sb("vTb", [128, 512], BF16)

        c.dma("sp", "vec", vt[:], vec, writes=["vec"])
        c.dma("sp", "gbc", gb[:], gbc, writes=["gbc"])
        c.dma("sp", "cst", C32[:], cst.rearrange("a p f -> p a f"), writes=["cst"])
        c.dma("sp", "gtr", gtr[:], gt, writes=["gtr"])
        I32 = C32[:, 0, :]
        ones32 = C32[:, 1, :]
        MS = [C32[:, 2, :], C32[:, 4, :]]
        MI = [C32[:, 3, :], C32[:, 5, :]]
        c.op("act", lambda e: e.activation(out=Ib[:, :], in_=C32[:, 0, :], func=AF.Copy), reads=["cst"], writes=["Ib"])
        c.op("act", lambda e: e.activation(out=onesb[:, :], in_=C32[:, 1, :], func=AF.Copy), reads=["cst"], writes=["onesb"])
        c.op("dve", lambda e: e.tensor_copy(out=tri[:, 0, :], in_=C32[:, 5, :]), reads=["cst"], writes=["tri"])
        c.op("dve", lambda e: e.tensor_copy(out=tri[:, 1, :], in_=C32[:, 3, :]), reads=["cst"], writes=["tri"])
        c.op("act", lambda e: e.activation(out=vt[:, 13:15], in_=vt[:, 0:2], func=AF.Exp), reads=["vec"], writes=["vec"])
        c.op("dve", lambda e: e.tensor_scalar(out=vt[:, 13:15], in0=vt[:, 13:15], scalar1=-1.0, scalar2=None, op0=ALU.mult), reads=["vec"], writes=["vec"])
        for d in range(2):
            c.op("act", lambda e: e.activation(out=gx[:, :], in_=gtr[:, :, d], func=AF.Exp, bias=vt[:, 2 + d:3 + d], scale=1.0), reads=["gtr", "vec"], writes=["gx"])
            c.op("act", lambda e: e.activation(out=gx[:, :], in_=gx[:, :], func=AF.Ln, bias=1.0, scale=1.0), reads=["gx"], writes=["gx"])
            c.op("dve", lambda e: e.tensor_scalar(out=G[d][:, :], in0=gx[:, :], scalar1=vt[:, 13 + d:14 + d], scalar2=None, op0=ALU.mult), reads=["gx", "vec"], writes=["G%d" % d])
            c.op("act", lambda e: e.activation(out=BT[d][:, :], in_=gtr[:, :, 2 + d], func=AF.Sigmoid), reads=["gtr"], writes=["BT%d" % d])
            c.op("dve", lambda e: e.tensor_scalar(out=NB[d][:, :], in0=BT[d][:, :], scalar1=-1.0, scalar2=None, op0=ALU.mult), reads=["BT%d" % d], writes=["NB%d" % d])
            c.op("pe", lambda e: e.matmul(pG[:, 0:NCH], lhsT=tri[:, d, :], rhs=G[d][:, :], start=True, stop=True), reads=["tri", "G%d" % d], writes=["bank1"])
            c.op("pe", lambda e: e.matmul(pG[:, 256:256 + NCH], lhsT=C32[:, 1, :], rhs=G[d][:, :], start=True, stop=True), reads=["cst", "G%d" % d], writes=["bank1"])
            c.op("dve", lambda e: e.tensor_copy(out=Bc[d][:, :], in_=pG[:, 0:NCH]), reads=["bank1"], writes=["Bc%d" % d])
            c.op("act", lambda e: e.activation(out=EB[d][:, :], in_=pG[:, 0:NCH], func=AF.Exp), reads=["bank1"], writes=["EB%d" % d])
            c.op("act", lambda e: e.activation(out=CD[d][:, :], in_=pG[:, 256:256 + NCH], func=AF.Exp), reads=["bank1"], writes=["CD%d" % d])
            c.op("dve", lambda e: e.tensor_tensor(out=EKD[d][:, :], in0=pG[:, 256:256 + NCH], in1=Bc[d][:, :], op=ALU.subtract), reads=["bank1", "Bc%d" % d], writes=["EKD%d" % d])
            c.op("act", lambda e: e.activation(out=EKD[d][:, :], in_=EKD[d][:, :], func=AF.Exp), reads=["EKD%d" % d], writes=["EKD%d" % d])
            c.op("dve", lambda e: e.tensor_tensor(out=BEB[d][:, :], in0=BT[d][:, :], in1=EB[d][:, :], op=ALU.mult), reads=["BT%d" % d, "EB%d" % d], writes=["BEB%d" % d])
        gtoks = ["Bc0", "Bc1", "EB0", "EB1", "CD0", "CD1", "EKD0", "EKD1", "BEB0", "BEB1", "BT0", "BT1", "NB0", "NB1"]

        segs = [(0, NCT)] + [(NCT + s0, w) for s0, w in tiles_of(NT - NCT, 512)]
        seq_lo = {0: 0}
        for (c0, w) in segs:
            lo_edge = (c0 == 0) or (c0 == NCT)
            hi_edge = (c0 + w == NCT) or (c0 + w == NT)
            a0 = c0 if lo_edge else c0 - 1
            a1 = c0 + w if hi_edge else c0 + w + 1
            if lo_edge:
                c.op("dve", lambda e: e.memset(rb[:, :, 0:1], 0.0), writes=["rb"])
            if hi_edge:
                c.op("dve", lambda e: e.memset(rb[:, :, w + 1:w + 2], 0.0), writes=["rb"])
            o0 = 1 if lo_edge else 0
            c.dma("sp", "rb", rb[:, :, o0:o0 + (a1 - a0)], qkvT[:, :, a0:a1].rearrange("a p t -> p a t"), writes=["rb"])
            for i in range(3):
                t0 = 4 + 3 * i
                c.op("dve", lambda e: e.tensor_scalar(out=cv[:, :w], in0=rb[:, i, 0:w], scalar1=vt[:, t0:t0 + 1], scalar2=None, op0=ALU.mult), reads=["rb", "vec"], writes=["cv"])
                c.op("dve", lambda e: e.scalar_tensor_tensor(out=cv[:, :w], in0=rb[:, i, 1:w + 1], scalar=vt[:, t0 + 1:t0 + 2], in1=cv[:, :w], op0=ALU.mult, op1=ALU.add),
                     reads=["rb", "vec", "cv"], writes=["cv"])
                c.op("dve", lambda e: e.scalar_tensor_tensor(out=cv[:, :w], in0=rb[:, i, 2:w + 2], scalar=vt[:, t0 + 2:t0 + 3], in1=cv[:, :w], op0=ALU.mult, op1=ALU.add),
                     reads=["rb", "vec", "cv"], writes=["cv"])
                if i == 2:
                    c.op("act", lambda e: e.activation(out=vTb[:, :w], in_=cv[:, :w], func=AF.Silu), reads=["cv"], writes=["vTb"])
                    for s in range(w // 128):
                        p_, ptok = pT.next()
                        c.op("pe", lambda e: e.matmul(p_[:, :], lhsT=vTb[:, s * 128:(s + 1) * 128], rhs=Ib[:, :], start=True, stop=True), reads=["vTb", "Ib"], writes=[ptok])
                        ch = (c0 + s * 128) // 128
                        c.op("act", lambda e: e.activation(out=vtm[:, ch, :], in_=p_[:, :], func=AF.Copy), reads=[ptok], writes=["vtm%d" % ch])
                    continue
                c.op("act", lambda e: e.activation(out=sv[:, :w], in_=cv[:, :w], func=AF.Silu), reads=["cv"], writes=["sv"])
                c.op("act", lambda e: e.activation(out=sqb[:, :w], in_=sv[:, :w], func=AF.Square), reads=["sv"], writes=["sqb"])
                c.op("pe", lambda e: e.matmul(pbig[:, :w], lhsT=onesb[:, :], rhs=sqb[:, :w], start=True, stop=True), reads=["sqb", "onesb"], writes=["bank0"])
                emit_rstd(c, rstd, pbig, "bank0", w, n=1)
                dst = qT if i == 0 else kT
                dtok = ("qT%d" if i == 0 else "kT%d") % c0
                sc = (128 ** -0.5) if i == 0 else 1.0
                c.op("dve", lambda e: e.scalar_tensor_tensor(out=dst[:, c0:c0 + w], in0=sv[:, :w], scalar=sc, in1=rstd[:, :w], op0=ALU.mult, op1=ALU.mult),
                     reads=["sv", "rstd"], writes=[dtok])
                if i == 1:
                    for s in range(w // 128):
                        p_, ptok = pT.next()
                        c.op("pe", lambda e: e.matmul(p_[:, :], lhsT=kT[:, c0 + s * 128:c0 + (s + 1) * 128], rhs=Ib[:, :], start=True, stop=True), reads=[dtok, "Ib"], writes=[ptok])
                        ch = (c0 + s * 128) // 128
                        c.op("dve", lambda e: e.tensor_copy(out=ktm[:, ch, :], in_=p_[:, :]), reads=[ptok], writes=["ktm%d" % ch])

        zb = c.sb("zb", [128, 8, 128], BF16)
        zf = c.sb("zf", [128, 8, 128], F32)
        for b0 in range(0, NCH, 8):
            nb = min(8, NCH - b0)
            c.dma("sp", "zb", zb[:, :nb, :], ztm[b0 * 128:(b0 + nb) * 128, :].rearrange("(ch p) d -> p ch d", p=128), writes=["zb"])
            c.op("act", lambda e: e.activation(out=zf[:, :nb, :], in_=zb[:, :nb, :], func=AF.Silu), reads=["zb"], writes=["zf"])
            c.dma("sp", "szd", szd[b0 * 128:(b0 + nb) * 128, :].rearrange("(ch p) d -> p ch d", p=128), zf[:, :nb, :], reads=["zf"], writes=["szd"])
        c.pop_scope()
        chains = [Chain(0), Chain(1)]

        def INV(ap):
            return ap

        def AS32(ap):
            return ap if GDN_INV_BF16 else (ap.bitcast(F32) if GDN_FP32R else ap)

        def seg_of(ch):
            t = ch * 128
            if t < NCT:
                return 0
            return NCT + ((t - NCT) // 512) * 512

        def pre(ch, X):
            d = X.d
            col = slice(ch, ch + 1)
            qtok = "qT%d" % seg_of(ch)
            ktok = "kT%d" % seg_of(ch)
            cs = slice(ch * 128, (ch + 1) * 128)
            c.op("dve", lambda e: e.tensor_scalar(out=X.diagb[:, :], in0=C32[:, 0, :], scalar1=Bc[d][:, col], scalar2=None, op0=ALU.mult), reads=["cst"] + gtoks, writes=[X.t("diagb")])
            c.op("pe", lambda e: e.matmul(X.pKK, lhsT=kT[:, cs], rhs=kT[:, cs], start=True, stop=True), reads=[ktok], writes=[X.bA])
            c.op("pe", lambda e: e.matmul(X.pQK, lhsT=qT[:, cs], rhs=kT[:, cs], start=True, stop=True), reads=[qtok, ktok], writes=[X.bA])
            yield
            c.op("pe", lambda e: e.matmul(X.pA, lhsT=C32[:, 1, :], rhs=X.diagb[:, :], start=True, stop=True), reads=["cst", X.t("diagb")], writes=[X.bA])
            yield
            c.op("dve", lambda e: e.tensor_scalar(out=X.nd[:, :], in0=X.pA, scalar1=Bc[d][:, col], scalar2=0.0, op0=ALU.subtract, op1=ALU.max), reads=[X.bA] + gtoks, writes=[X.t("nd")])
            yield
            c.op("act", lambda e: e.activation(out=X.dm[:, :], in_=X.nd[:, :], func=AF.Exp, scale=-1.0), reads=[X.t("nd")], writes=[X.t("dm")])
            yield
            c.op("dve", lambda e: e.tensor_tensor(out=X.dmS[:, :], in0=X.dm[:, :], in1=MS[d], op=ALU.mult), reads=[X.t("dm"), "cst"], writes=[X.t("dmS")])
            c.op("dve", lambda e: e.tensor_tensor(out=X.dmI[:, :], in0=X.dm[:, :], in1=MI[d], op=ALU.mult), reads=[X.t("dm"), "cst"], writes=[X.t("dmI")])
            yield
            Q, Qtok = X.Qm.next()
            c.op("dve", lambda e: e.scalar_tensor_tensor(out=Q[:, :], in0=X.pKK, scalar=NB[d][:, col], in1=X.dmS[:, :], op0=ALU.mult, op1=ALU.mult),
                 reads=[X.bA, X.t("dmS")] + gtoks, writes=[Qtok])
            c.op("dve", lambda e: e.tensor_tensor(out=X.qk[:, :], in0=X.pQK, in1=X.dmI[:, :], op=ALU.mult), reads=[X.bA, X.t("dmI")], writes=[X.t("qk")])
            yield
            p_, ptok = X.pT.next()
            c.op("pe", lambda e: e.matmul(p_, lhsT=AS32(Q[:, :]), rhs=(Ib[:, :] if GDN_INV_BF16 else C32[:, 0, :]), start=True, stop=True), reads=[Qtok, "cst", "Ib"], writes=[ptok])
            p2, p2tok = X.pT.next()
            c.op("pe", lambda e: e.matmul(p2, lhsT=X.qk[:, :], rhs=Ib[:, :], start=True, stop=True), reads=[X.t("qk"), "Ib"], writes=[p2tok])
            yield
            P, Ptok = X.Pm.next()
            c.op("act", lambda e: e.activation(out=P[:, :], in_=p_, func=AF.Copy), reads=[ptok], writes=[Ptok])
            c.op("dve", lambda e: e.tensor_tensor(out=X.Rm[:, :], in0=p_, in1=C32[:, 0, :], op=ALU.add), reads=[ptok, "cst"], writes=[X.t("Rm")])
            qkT_, qkTtok = X.qkT.next()
            c.op("act", lambda e: e.activation(out=qkT_[:, :], in_=p2, func=AF.Copy), reads=[p2tok], writes=[qkTtok])
            yield
            for step in range(6):
                last = step == 5
                pq, pqtok = X.pI.next()
                c.op("pe", lambda e: e.matmul(pq, lhsT=INV(P[:, :]), rhs=INV(Q[:, :]), start=True, stop=True), reads=[Ptok, Qtok], writes=[pqtok])
                if not last:
                    pp, pptok = X.pI.next()
                    c.op("pe", lambda e: e.matmul(pp, lhsT=INV(Q[:, :]), rhs=INV(P[:, :]), start=True, stop=True), reads=[Ptok, Qtok], writes=[pptok])
                yield
                Q2, Q2tok = X.Qm.next()
                c.op("dve", lambda e: e.tensor_copy(out=Q2[:, :], in_=pq), reads=[pqtok], writes=[Q2tok])
                if not last:
                    P2, P2tok = X.Pm.next()
                    c.op("act", lambda e: e.activation(out=P2[:, :], in_=pp, func=AF.Copy), reads=[pptok], writes=[P2tok])
                    P, Ptok = P2, P2tok
                Q, Qtok = Q2, Q2tok
                yield
                pr, prtok = X.pI.next()
                c.op("pe", lambda e: e.matmul(pr, lhsT=INV(Q[:, :]), rhs=INV(X.Rm[:, :]), start=True, stop=True), reads=[Qtok, X.t("Rm")], writes=[prtok])
                yield
                c.op("dve", lambda e: e.tensor_tensor(out=X.Rm[:, :], in0=pr, in1=AS32(X.Rm[:, :]), op=ALU.add), reads=[prtok, X.t("Rm")], writes=[X.t("Rm")])
                yield
            c.op("act", lambda e: e.activation(out=X.Rb[:, :], in_=AS32(X.Rm[:, :]), func=AF.Copy), reads=[X.t("Rm")], writes=[X.t("Rb")])
            kd_, kdtok = X.kd.next()
            c.op("pool", lambda e: e.tensor_scalar(out=X.kb[:, :], in0=ktm[:, ch, :], scalar1=BEB[d][:, col], scalar2=None, op0=ALU.mult), reads=["ktm%d" % ch] + gtoks, writes=[X.t("kb")])
            c.op("pool", lambda e: e.tensor_scalar(out=kd_[:, :], in0=ktm[:, ch, :], scalar1=EKD[d][:, col], scalar2=None, op0=ALU.mult), reads=["ktm%d" % ch] + gtoks, writes=[kdtok])
            c.op("pool", lambda e: e.tensor_scalar(out=X.vb[:, :], in0=vtm[:, ch, :], scalar1=BT[d][:, col], scalar2=None, op0=ALU.mult), reads=["vtm%d" % ch] + gtoks, writes=[X.t("vb")])
            yield
            c.op("pe", lambda e: e.matmul(X.pU, lhsT=X.Rb[:, :], rhs=X.vb[:, :], start=True, stop=True), reads=[X.t("Rb"), X.t("vb")], writes=[X.bA])
            c.op("pe", lambda e: e.matmul(X.pW, lhsT=X.kb[:, :], rhs=X.Rb[:, :], start=True, stop=True), reads=[X.t("Rb"), X.t("kb")], writes=[X.bB])
            yield
            ub_, ubtok = X.ub.next()
            wT_, wTtok = X.wT.next()
            c.op("act", lambda e: e.activation(out=ub_[:, :], in_=X.pU, func=AF.Copy), reads=[X.bA], writes=[ubtok])
            c.op("dve", lambda e: e.tensor_copy(out=wT_[:, :], in_=X.pW), reads=[X.bB], writes=[wTtok])
            X.slot_out = (qkT_, qkTtok, kd_, kdtok, ub_, ubtok, wT_, wTtok)
            yield

        def rec(ch, X, slot, fin):
            d = X.d
            qkT_, qkTtok, kd_, kdtok, ub_, ubtok, wT_, wTtok = slot
            col = slice(ch, ch + 1)
            cs = slice(ch * 128, (ch + 1) * 128)
            qtok = "qT%d" % seg_of(ch)
            c.op("pe", lambda e: e.matmul(X.pwS, lhsT=wT_[:, :], rhs=X.Sb[:, :], start=True, stop=True), reads=[wTtok, X.t("Sb")], writes=[X.bD])
            c.op("pe", lambda e: e.matmul(X.pO1, lhsT=qT[:, cs], rhs=X.Sb[:, :], start=True, stop=True), reads=[qtok, X.t("Sb")], writes=[X.bD])
            yield
            c.op("dve", lambda e: e.tensor_tensor(out=X.u_b[:, :], in0=ub_[:, :], in1=X.pwS, op=ALU.subtract), reads=[ubtok, X.bD], writes=[X.t("u_b")])
            yield
            c.op("pe", lambda e: e.matmul(X.pdS, lhsT=kd_[:, :], rhs=X.u_b[:, :], start=True, stop=True), reads=[kdtok, X.t("u_b")], writes=[X.bD])
            c.op("pe", lambda e: e.matmul(X.pO2, lhsT=qkT_[:, :], rhs=X.u_b[:, :], start=True, stop=True), reads=[qkTtok, X.t("u_b")], writes=[X.bD])
            yield
            c.op("dve", lambda e: e.scalar_tensor_tensor(out=X.S32[:, :], in0=X.S32[:, :], scalar=CD[d][:, col], in1=X.pdS, op0=ALU.mult, op1=ALU.add),
                 reads=[X.t("S32"), X.bD] + gtoks, writes=[X.t("S32")])
            yield
            c.op("act", lambda e: e.activation(out=X.Sb[:, :], in_=X.S32[:, :], func=AF.Copy), reads=[X.t("S32")], writes=[X.t("Sb")])
            c.op("act", lambda e: e.activation(out=X.o2s[:, :], in_=X.pO2, func=AF.Copy), reads=[X.bD], writes=[X.t("o2s")])
            yield
            if not fin:
                c.op("dve", lambda e: e.scalar_tensor_tensor(out=of[:, ch, :], in0=X.pO1, scalar=EB[d][:, col], in1=X.o2s[:, :], op0=ALU.mult, op1=ALU.add),
                     reads=[X.bD, X.t("o2s")] + gtoks, writes=["of%d" % ch])
                yield
                return
            c.op("dve", lambda e: e.scalar_tensor_tensor(out=X.osum[:, :], in0=X.pO1, scalar=EB[d][:, col], in1=X.o2s[:, :], op0=ALU.mult, op1=ALU.add),
                 reads=[X.bD, X.t("o2s")] + gtoks, writes=[X.t("osum")])
            yield
            c.op("dve", lambda e: e.tensor_tensor(out=X.osum[:, :], in0=X.osum[:, :], in1=of[:, ch, :], op=ALU.add), reads=[X.t("osum"), "of%d" % ch], writes=[X.t("osum")])
            yield
            c.op("act", lambda e: e.activation(out=X.osq[:, :], in_=X.osum[:, :], func=AF.Square, accum_out=X.oss[:, 0:1]), reads=[X.t("osum")], writes=[X.t("osq"), X.t("oss")])
            yield
            emit_rstd(c, X.oss, X.oss, X.t("oss"), 1, n=128, tok=X.t("oss"))
            z_, ztok = zt.next()
            c.dma("sp", ztok, z_[:, :], szd[ch * 128:(ch + 1) * 128, :], reads=["szd"], writes=[ztok])
            yield
            c.op("dve", lambda e: e.scalar_tensor_tensor(out=X.osum[:, :], in0=X.osum[:, :], scalar=X.oss[:, 0:1], in1=gb[:, :], op0=ALU.mult, op1=ALU.mult),
                 reads=[X.t("osum"), X.t("oss"), "gbc"], writes=[X.t("osum")])
            o_, otok = ot.next()
            c.op("dve", lambda e: e.tensor_tensor(out=o_[:, :], in0=X.osum[:, :], in1=z_[:, :], op=ALU.mult), reads=[X.t("osum"), ztok], writes=[otok])
            c.dma("sp", otok, otm[ch * 128:(ch + 1) * 128, :], o_[:, :], reads=[otok], writes=["otm"])
            yield

        def drive(gens):
            gens = [g_ for g_ in gens if g_ is not None]
            while gens:
                alive = []
                for g_ in gens:
                    try:
                        next(g_)
                        alive.append(g_)
                    except StopIteration:
                        pass
                gens = alive

        orders = [list(range(NCH)), list(range(CCH - 1, -1, -1)) + list(range(NCH - 1, CCH - 1, -1))]
        pos = [{ch: i for i, ch in enumerate(o)} for o in orders]
        for X in chains:
            c.op("dve", lambda e: e.memset(X.S32[:, :], 0.0), writes=[X.t("S32")])
            c.op("dve", lambda e: e.memset(X.Sb[:, :], 0.0), writes=[X.t("Sb")])
        drive([pre(orders[d][0], chains[d]) for d in range(2)])
        slots = [chains[d].slot_out for d in range(2)]
        for i in range(NCH):
            gens = []
            for d in range(2):
                if i + 1 < NCH:
                    gens.append(pre(orders[d][i + 1], chains[d]))
            for d in range(2):
                ch = orders[d][i]
                gens.append(rec(ch, chains[d], slots[d], pos[d][ch] > pos[1 - d][ch]))
            drive(gens)
            if i + 1 < NCH:
                slots = [chains[d].slot_out for d in range(2)]
        c.wait_all("sp", ["otm"])
    return nc


def gdn_consts():
    i = np.arange(128)
    I = np.eye(128, dtype=np.float32)
    ones = np.ones((128, 128), np.float32)
    LS = (i[:, None] > i[None, :]).astype(np.float32)
    LI = (i[:, None] >= i[None, :]).astype(np.float32)
    return np.ascontiguousarray(np.stack([I, ones, LS, LI, LS.T, LI.T], 0))


def tile_w(w, m=128):
    w = np.asarray(w, np.float32)
    K, N = w.shape
    nb = (N + m - 1) // m
    if nb * m != N:
        w = np.concatenate([w, np.zeros((K, nb * m - N), np.float32)], 1)
    return np.ascontiguousarray(w.reshape(K // 128, 128, nb, m).transpose(2, 1, 0, 3))


def fm(v):
    return np.ascontiguousarray(np.asarray(v, np.float32).reshape(KC, 128).T)


def lambda_init(layer):
    return 0.8 - 0.6 * math.exp(-0.3 * layer)


PRE_TILES = [(0, 32, 1)] + [(32 + i * 512, 512, 0) for i in range(4)]
CTXC = CTX // NCORE
POST_T = CTXC + 2 + SEQ // NCORE + 2
POST_SPECIAL = [(0, "vl"), (CTXC + 1, "vr"), (CTXC + 2, "vl"), (POST_T - 1, "vr")]


def post_tiles():
    inner = POST_T - 2
    n = 5
    base, extra = divmod(inner, n)
    tiles, a = [], 1
    for i in range(n):
        wi = base + (1 if i < extra else 0)
        s0, wd = a - 1, wi + 2
        segs = []
        for (g0, g1, stream) in ((0, CTXC + 2, 1), (CTXC + 2, POST_T, 0)):
            c0, c1 = max(g0, s0) - s0, min(g1, s0 + wd) - s0
            if c1 > c0:
                segs.append((c0, c1, stream))
        masks = [(col - s0, fl) for col, fl in POST_SPECIAL if s0 <= col < s0 + wd]
        tiles.append((s0, wd, segs, masks))
        a += wi
    return tiles


POST_TILES = post_tiles()
TL = SEQ // NCORE


def kernel(x, c, ctx, c_ctx, w_mod, b_mod, norm1_g, norm2_g, w_in_even, a_conv_w, a_A_log, a_dt_bias,
           a_norm_g, b_lambda, b_norm_g, w_out_even, w_in_odd, c_q_norm, c_k_norm, w_out_odd,
           ffn_up, ffn_conv_w, ffn_conv_b, ffn_down, final_g):
    f32 = np.float32
    x = np.asarray(x, f32)
    ctx = np.asarray(ctx, f32)
    progs = {}

    def prog(key, fn):
        if key not in progs:
            progs[key] = fn()
        return progs[key]

    c2 = np.ascontiguousarray(np.stack([fm(np.asarray(c, f32)[0]), fm(np.asarray(c_ctx, f32))], -1))
    w_mod = np.asarray(w_mod, f32)
    b_mod = np.asarray(b_mod, f32)
    maps = []
    for j in range(NCORE):
        sl = slice(j * MODC, (j + 1) * MODC)
        bm = np.ascontiguousarray(np.broadcast_to(b_mod[None, :, sl], (2, DEPTH, MODC)))
        maps.append({"c2": c2, "wm": np.stack([tile_w(w_mod[l_][:, sl], 512) for l_ in range(DEPTH)], 0), "bm": bm})
    res = run(prog("mod", build_mod), maps, "mod")
    mod = np.concatenate([r["out"] for r in res], -1)

    def modv(s, l, m):
        return mod[s, l, m * D:(m + 1) * D]

    xlT = np.ascontiguousarray(x[0].T)
    xcT = np.ascontiguousarray(ctx[0].T)
    rope = rope_tables(SEQ)
    rt = rope_perm()
    zcol = np.zeros((D, 1), f32)

    for l in range(DEPTH):
        even = l % 2 == 0
        e = l // 2
        last = l == DEPTH - 1
        w_in = np.asarray(w_in_even[e] if even else w_in_odd[e], f32)
        ncols = w_in.shape[1]
        w_in = tile_w(w_in)
        vec = np.ascontiguousarray(np.stack([fm(norm1_g[l]), fm(modv(0, l, 0)), fm(modv(0, l, 1)), fm(modv(1, l, 0)), fm(modv(1, l, 1))], 1))
        maps = [{"xT": np.ascontiguousarray(np.concatenate([xcT[:, 32 * j:32 * (j + 1)], xlT[:, TL * j:TL * (j + 1)]], 1)), "vec": vec, "w": w_in}
                for j in range(NCORE)]
        res = run(prog(("pre", ncols), lambda: build_pre(ncols, PRE_TILES)), maps, "pre%d" % l)
        pc = np.concatenate([r["pT"][:, :32] for r in res], 1)
        pl = np.concatenate([r["pT"][:, 32:] for r in res], 1)
        p = np.concatenate([pc, pl], 1)
        del res, maps
        mT = np.zeros((D, CTX + SEQ), NPBF)
        if even:
            NT = CTX + SEQ
            cw = np.asarray(a_conv_w[e], f32)
            maps = []
            for j in range(NCORE):
                hs = slice(j * 128, (j + 1) * 128)
                qkvT = np.ascontiguousarray(np.stack([p[j * 128:(j + 1) * 128], p[1024 + j * 128:1024 + (j + 1) * 128], p[2048 + j * 128:2048 + (j + 1) * 128]], 0))
                ztm = np.ascontiguousarray(p[3072 + j * 128:3072 + (j + 1) * 128].T)
                gates = np.stack([p[4096 + gi * 8 + j] for gi in range(4)], 0)
                gt = np.ascontiguousarray(gates.reshape(4, NT // 128, 128).transpose(2, 1, 0))
                vec = np.zeros((128, 16), f32)
                vec[:, 0:2] = np.asarray(a_A_log[e], f32)[:, j]
                vec[:, 2:4] = np.asarray(a_dt_bias[e], f32)[:, j]
                for i, off in enumerate((0, 1024, 2048)):
                    for t in range(3):
                        vec[:, 4 + 3 * i + t] = cw[t, off + j * 128:off + (j + 1) * 128]
                gbc = np.ascontiguousarray(np.broadcast_to(np.asarray(a_norm_g[e], f32), (128, 128)))
                maps.append({"qkvT": qkvT, "ztm": ztm, "gt": gt, "vec": vec, "gbc": gbc, "cst": gdn_consts()})
            res = run(prog("gdn", lambda: build_gdn(NT)), maps, "gdn%d" % l)
            for j in range(NCORE):
                mT[j * 128:(j + 1) * 128, :] = res[j]["otm"].T
            del res, maps
            HQ = SEQ // 2
            li = lambda_init(l)
            lp = np.ascontiguousarray(np.broadcast_to(np.asarray(b_lambda[e], f32), (128, 4, 128)))
            bng = np.asarray(b_norm_g[e], f32)
            maps = []
            for j in range(NCORE):
                hb, half = j // 2, j % 2
                qrows = [slice(A_IN + hb * 256 + m * 128, A_IN + hb * 256 + (m + 1) * 128) for m in range(2)]
                krows = [slice(A_IN + 1024 + hb * 256 + m * 128, A_IN + 1024 + hb * 256 + (m + 1) * 128) for m in range(2)]
                qT = np.ascontiguousarray(np.stack([np.concatenate([p[r, :CTX], p[r, CTX + half * HQ:CTX + (half + 1) * HQ]], 1) for r in qrows], 0))
                kT = np.ascontiguousarray(np.stack([p[r] for r in krows], 0))
                v = np.ascontiguousarray(p[A_IN + 2048 + hb * 256:A_IN + 2048 + (hb + 1) * 256].T)
                vec = np.zeros((128, 8), f32)
                vec[:, 2] = li
                vec[:, 3] = 1.0 - li
                vec[:, 4] = bng[:128]
                vec[:, 5] = bng[128:]
                maps.append({"qT": qT, "kT": kT, "v": v, "cq": np.ascontiguousarray(rope[:, :, half * HQ:(half + 1) * HQ]), "ck": rope, "rt": rt, "vec": vec, "lp": lp})
            res = run(prog("attB", lambda: build_att("B", HQ, SEQ)), maps, "attB%d" % l)
            for j in range(NCORE):
                hb, half = j // 2, j % 2
                rows = slice(1024 + hb * 256, 1024 + (hb + 1) * 256)
                if half == 0:
                    mT[rows, :CTX] = res[j]["oT"][:, :CTX]
                mT[rows, CTX + half * HQ:CTX + (half + 1) * HQ] = res[j]["oT"][:, CTX:]
            del res, maps
        else:
            lp0 = np.zeros((128, 4, 128), f32)
            maps = []
            for j in range(NCORE):
                g = j // 2
                qT = np.ascontiguousarray(np.stack([p[(2 * j + s) * 128:(2 * j + s + 1) * 128] for s in range(2)], 0))
                kT = np.ascontiguousarray(p[2048 + g * 128:2048 + (g + 1) * 128][None])
                v = np.ascontiguousarray(p[2560 + g * 128:2560 + (g + 1) * 128].T)
                vec = np.zeros((128, 8), f32)
                vec[:, 0] = np.asarray(c_q_norm[e], f32)
                vec[:, 1] = np.asarray(c_k_norm[e], f32)
                maps.append({"qT": qT, "kT": kT, "v": v, "cq": rope, "ck": rope, "rt": rt, "vec": vec, "lp": lp0})
            res = run(prog("attC", lambda: build_att("C", SEQ, SEQ)), maps, "attC%d" % l)
            for j in range(NCORE):
                mT[2 * j * 128:(2 * j + 2) * 128, :] = res[j]["oT"]
            del res, maps
        del p
        vec = np.zeros((128, POST_NV), f32)
        V = POST_V
        vec[:, V["n2g"]:V["n2g"] + 16] = fm(norm2_g[l])
        for nm, s, m in (("g1", 0, 2), ("sh2", 0, 3), ("sc2", 0, 4), ("g2", 0, 5), ("cg1", 1, 2), ("csh2", 1, 3), ("csc2", 1, 4), ("cg2", 1, 5)):
            vec[:, V[nm]:V[nm] + 16] = fm(modv(s, l, m))
        vec[:, V["fg"]:V["fg"] + 16] = fm(final_g)
        vec[:, V["cw"]:V["cw"] + 3 * 88] = np.asarray(ffn_conv_w[l], f32).reshape(3, 88, 128).transpose(2, 0, 1).reshape(128, 264)
        vec[:, V["cb"]:V["cb"] + 88] = np.asarray(ffn_conv_b[l], f32).reshape(88, 128).T
        wo = tile_w(w_out_even[e] if even else w_out_odd[e])
        wu = tile_w(ffn_up[l])
        wd = tile_w(ffn_down[l])
        mcT, mlT = mT[:, :CTX], mT[:, CTX:]
        zb = np.zeros((D, 1), NPBF)
        maps = []
        for j in range(NCORE):
            lo, hi = TL * j, TL * (j + 1)
            xl_ = [zcol if j == 0 else xlT[:, lo - 1:lo], xlT[:, lo:hi], zcol if j == NCORE - 1 else xlT[:, hi:hi + 1]]
            ml_ = [zb if j == 0 else mlT[:, lo - 1:lo], mlT[:, lo:hi], zb if j == NCORE - 1 else mlT[:, hi:hi + 1]]
            vj = vec.copy()
            vj[:, V["vl"]] = 0.0 if j == 0 else 1.0
            vj[:, V["vr"]] = 0.0 if j == NCORE - 1 else 1.0
            clo, chi = CTXC * j, CTXC * (j + 1)
            xc_ = [zcol if j == 0 else xcT[:, clo - 1:clo], xcT[:, clo:chi], zcol if j == NCORE - 1 else xcT[:, chi:chi + 1]]
            mc_ = [zb if j == 0 else mcT[:, clo - 1:clo], mcT[:, clo:chi], zb if j == NCORE - 1 else mcT[:, chi:chi + 1]]
            maps.append({"xT": np.ascontiguousarray(np.concatenate(xc_ + xl_, 1)),
                         "mT": np.ascontiguousarray(np.concatenate(mc_ + ml_, 1)), "vec": vj, "wo": wo, "wu": wu, "wd": wd})
        if not last:
            res = run(prog("post", lambda: build_post(POST_TILES, False)), maps, "post%d" % l)
            xcT = np.ascontiguousarray(np.concatenate([r["oT"][:, :CTXC] for r in res], 1))
            xlT = np.ascontiguousarray(np.concatenate([r["oT"][:, CTXC + 2:] for r in res], 1))
        else:
            res = run(prog("postf", lambda: build_post(POST_TILES, True)), maps, "postf%d" % l)
            xlT = np.concatenate([r["oT"][:, CTXC + 2:] for r in res], 1)
        del res, maps, mT
    return np.ascontiguousarray(xlT.T)[None].astype(np.float32)
```
